# Optimizing a Trainium2 kernel written in Bass

```python
import math
import jax, jax.numpy as jnp
from jax import lax
import numpy as np

D_MODEL = 1024
BATCH = 32
SEQ = 256
DEPTH = 4
DEC_BATCH = 2
DEC_SEQ = 2048
PAST_LEN = 256

GRID_W = 64
NORM_EPS = 1e-6
NEG_INF = -1e30

DN_HEADS = 4
DN_DK = 128
DN_DV = 128
DN_WIDTH = DN_HEADS * DN_DV
DN_CONV = 3
DN_CHUNK = 64
NA_HEADS = 8
NA_HD = 64
NA_WIDTH = NA_HEADS * NA_HD
NA_WIN_R = 8
NA_WIN_C = 16
NA_QBLK_C = 16
NA_KBLK_C = 2 * NA_WIN_C
CTX_QBLK = 128
POOL_WINDOWS = (2, 4, 8, 16)
POOL_GROUPS = len(POOL_WINDOWS)
POOL_GC = 128
POOL_WIDTH = POOL_GROUPS * POOL_GC
N_BRANCH = 3

OFF_DN_QKV = 0
OFF_DN_Z = OFF_DN_QKV + 3 * DN_WIDTH
OFF_DN_BETA = OFF_DN_Z + DN_WIDTH
OFF_DN_A = OFF_DN_BETA + 2 * DN_HEADS
OFF_NA_Q = OFF_DN_A + 2 * DN_HEADS
OFF_NA_K = OFF_NA_Q + NA_WIDTH
OFF_NA_V = OFF_NA_K + NA_WIDTH
OFF_NA_Z = OFF_NA_V + NA_WIDTH
OFF_PL_U = OFF_NA_Z + NA_WIDTH
OFF_PL_Z = OFF_PL_U + POOL_WIDTH
OFF_GATE = OFF_PL_Z + POOL_WIDTH
N_IN = OFF_GATE + N_BRANCH * D_MODEL

kernel_name = 'hybrid_dit_deltanet_na_pool_step'

F32 = jnp.float32


def rmsnorm(x, g):
    xf = x.astype(F32)
    y = xf * lax.rsqrt(jnp.mean(xf * xf, axis=-1, keepdims=True) + NORM_EPS)
    return (y * g.astype(F32)).astype(x.dtype)


def l2norm(x):
    return x * lax.rsqrt(jnp.sum(x * x, axis=-1, keepdims=True) + NORM_EPS)


def dwconv_centred(x, w):
    ch = x.shape[-1]
    return lax.conv_general_dilated(x, w[:, None, :].astype(x.dtype), window_strides=(1,),
                                    padding=[(DN_CONV // 2, DN_CONV // 2)],
                                    dimension_numbers=('NWC', 'WIO', 'NWC'), feature_group_count=ch)


def gated_delta_chunked(q, k, v, log_a, beta, s0):
    b, t, h, _ = q.shape
    n = t // DN_CHUNK

    def chunks(z):
        z = z.reshape((b, n, DN_CHUNK, h) + z.shape[3:])
        return jnp.moveaxis(z, (1, 3), (0, 2))

    qc, kc, vc, bc = chunks(q), chunks(k), chunks(v), chunks(beta)
    g = jnp.cumsum(chunks(log_a), axis=-1)
    tri = jnp.tril(jnp.ones((DN_CHUNK, DN_CHUNK), bool))
    strict = jnp.tril(jnp.ones((DN_CHUNK, DN_CHUNK), bool), -1)
    decay = jnp.exp(jnp.where(tri, g[..., :, None] - g[..., None, :], -jnp.inf))
    kb = kc * bc[..., None]
    lmat = jnp.where(strict, jnp.einsum('nbhid,nbhjd->nbhij', kb, kc) * decay, 0.0)
    eye = jnp.eye(DN_CHUNK, dtype=F32)
    tinv = lax.linalg.triangular_solve(eye + lmat, jnp.broadcast_to(eye, lmat.shape),
                                       left_side=True, lower=True)
    u = jnp.einsum('nbhij,nbhjd->nbhid', tinv, vc * bc[..., None])
    w = jnp.einsum('nbhij,nbhjd->nbhid', tinv, kb * jnp.exp(g)[..., None])
    attn = jnp.einsum('nbhid,nbhjd->nbhij', qc, kc) * decay
    qg = qc * jnp.exp(g)[..., None]
    kg = kc * jnp.exp(g[..., -1:] - g)[..., None]
    g_last = jnp.exp(g[..., -1])

    def step(s, inp):
        u_n, w_n, qg_n, kg_n, attn_n, gl_n = inp
        v_new = u_n - jnp.einsum('bhcd,bhde->bhce', w_n, s)
        o = jnp.einsum('bhcd,bhde->bhce', qg_n, s) + jnp.einsum('bhij,bhje->bhie', attn_n, v_new)
        s = s * gl_n[..., None, None] + jnp.einsum('bhcd,bhce->bhde', kg_n, v_new)
        return s, o

    s_fin, o = lax.scan(step, s0, (u, w, qg, kg, attn, g_last))
    o = jnp.moveaxis(o, (0, 2), (1, 3)).reshape(b, t, h, DN_DV)
    return o, s_fin


def delta_mixer(proj, s0, conv_w, a_log, dt_bias, g_norm):
    b, t, _ = proj.shape
    qkv = jax.nn.silu(dwconv_centred(proj[..., OFF_DN_QKV:OFF_DN_Z], conv_w)).astype(F32)
    q, k, v = jnp.split(qkv, 3, axis=-1)
    q = l2norm(q.reshape(b, t, DN_HEADS, DN_DK)) * (DN_DK ** -0.5)
    k = l2norm(k.reshape(b, t, DN_HEADS, DN_DK))
    v = v.reshape(b, t, DN_HEADS, DN_DV)
    beta = jax.nn.sigmoid(proj[..., OFF_DN_BETA:OFF_DN_A].astype(F32)).reshape(b, t, 2, DN_HEADS)
    a_in = proj[..., OFF_DN_A:OFF_NA_Q].astype(F32).reshape(b, t, 2, DN_HEADS)
    log_a = -jnp.exp(a_log.astype(F32)) * jax.nn.softplus(a_in + dt_bias.astype(F32))
    s0 = s0.astype(F32)
    o_f, s_f = gated_delta_chunked(q, k, v, log_a[:, :, 0], beta[:, :, 0], s0[:, 0])
    flip = lambda a: jnp.flip(a, axis=1)
    o_b, s_b = gated_delta_chunked(flip(q), flip(k), flip(v), flip(log_a[:, :, 1]),
                                   flip(beta[:, :, 1]), s0[:, 1])
    o = rmsnorm(o_f + flip(o_b), g_norm).reshape(b, t, DN_WIDTH)
    y = o * jax.nn.silu(proj[..., OFF_DN_Z:OFF_DN_BETA].astype(F32))
    return y.astype(proj.dtype), jnp.stack([s_f, s_b], axis=1)


def context_attention(q, k, v):
    b, l, h, d = q.shape
    q_blocks = jnp.moveaxis(q.reshape(b, l // CTX_QBLK, CTX_QBLK, h, d), 1, 0)

    def block(q_i):
        s = jnp.einsum('bqhd,bkhd->bhqk', q_i, k).astype(F32) * (d ** -0.5)
        p = jax.nn.softmax(s, axis=-1).astype(v.dtype)
        return jnp.einsum('bhqk,bkhd->bqhd', p, v)

    o = lax.map(block, q_blocks)
    return jnp.moveaxis(o, 0, 1).reshape(b, l, h, d)


def neighbourhood_attention(q, k, v, k_ctx, v_ctx, bias_tab):
    b, t, h, d = q.shape
    rows = t // GRID_W
    wr = min(NA_WIN_R, rows)
    ncb = GRID_W // NA_QBLK_C
    scale = d ** -0.5
    col = np.arange(GRID_W)
    q_col = col.reshape(ncb, NA_QBLK_C)
    q_cs = np.clip(col - NA_WIN_C // 2, 0, GRID_W - NA_WIN_C).reshape(ncb, NA_QBLK_C)
    k_col = (np.clip(np.arange(ncb) * NA_QBLK_C - NA_WIN_C // 2, 0, GRID_W - NA_KBLK_C)[:, None]
             + np.arange(NA_KBLK_C))
    col_ok = (k_col[:, None, :] >= q_cs[:, :, None]) & (k_col[:, None, :] < q_cs[:, :, None] + NA_WIN_C)
    dc_idx = np.clip(k_col[:, None, :] - q_col[:, :, None] + NA_WIN_C - 1, 0, 2 * NA_WIN_C - 2)
    col_bias = bias_tab.astype(F32)[:, :, dc_idx]
    mask = col_ok[:, :, None, :]
    q_g = q.reshape(b, rows, ncb, NA_QBLK_C, h, d)
    k_g = k.reshape(b, rows, GRID_W, h, d)[:, :, k_col]
    v_g = v.reshape(b, rows, GRID_W, h, d)[:, :, k_col]
    n_lat = wr * NA_KBLK_C

    def row_block(r):
        rs = jnp.clip(r - wr // 2, 0, rows - wr)
        k_r = lax.dynamic_slice_in_dim(k_g, rs, wr, axis=1)
        v_r = lax.dynamic_slice_in_dim(v_g, rs, wr, axis=1)
        q_r = lax.dynamic_index_in_dim(q_g, r, axis=1, keepdims=False)
        bias_r = jnp.take(col_bias, rs + jnp.arange(wr) - r + NA_WIN_R - 1, axis=1)
        bias_r = jnp.transpose(bias_r, (0, 2, 3, 1, 4))
        s_lat = jnp.einsum('bnqhd,bwnkhd->bhnqwk', q_r, k_r).astype(F32) * scale + bias_r
        s_lat = jnp.where(mask, s_lat, NEG_INF).reshape(b, h, ncb, NA_QBLK_C, n_lat)
        s_ctx = jnp.einsum('bnqhd,blhd->bhnql', q_r, k_ctx).astype(F32) * scale
        p = jax.nn.softmax(jnp.concatenate([s_lat, s_ctx], axis=-1), axis=-1).astype(v.dtype)
        p_lat = p[..., :n_lat].reshape(b, h, ncb, NA_QBLK_C, wr, NA_KBLK_C)
        return (jnp.einsum('bhnqwk,bwnkhd->bnqhd', p_lat, v_r)
                + jnp.einsum('bhnql,blhd->bnqhd', p[..., n_lat:], v_ctx))

    o = lax.map(row_block, jnp.arange(rows))
    return jnp.moveaxis(o, 0, 1).reshape(b, t, h, d)


def pool_mixer(u, pool_w, pool_scale):
    b, t, _ = u.shape
    uf = u.astype(F32).reshape(b, t, POOL_GROUPS, POOL_GC)
    csum = jnp.concatenate([jnp.zeros_like(uf[:, :1]), jnp.cumsum(uf, axis=1)], axis=1)
    pos = jnp.arange(t)
    means = []
    for gi, win in enumerate(POOL_WINDOWS):
        lo = jnp.maximum(pos - win // 2, 0)
        hi = jnp.minimum(pos + win // 2 - 1, t - 1)
        cg = csum[:, :, gi]
        total = jnp.take(cg, hi + 1, axis=1) - jnp.take(cg, lo, axis=1)
        means.append(total / (hi - lo + 1).astype(F32)[None, :, None])
    pooled = jnp.stack(means, axis=2) - uf
    y = jnp.einsum('btgc,gce->btge', pooled, pool_w.astype(F32)).reshape(b, t, POOL_WIDTH)
    return y * pool_scale.astype(F32)


def trunk_layer(x, mod, s0, k_ctx, v_ctx, g_pre, g_post, w_in, conv_w, a_log, dt_bias, g_norm,
                na_bias, pool_w, pool_scale, w_br_dn, w_br_na, w_br_pl, w_out):
    b, t, _ = x.shape
    shift, scale, gate = jnp.split(mod, 3, axis=-1)
    h = rmsnorm(x, g_pre) * (1.0 + scale[:, None]) + shift[:, None]
    proj = jnp.einsum('btd,dn->btn', h, w_in)
    heads = lambda a: a.reshape(b, t, NA_HEADS, NA_HD)
    q_na = heads(proj[..., OFF_NA_Q:OFF_NA_K])
    k_na = heads(proj[..., OFF_NA_K:OFF_NA_V])
    v_na = heads(proj[..., OFF_NA_V:OFF_NA_Z])
    if k_ctx is None:
        s0 = jnp.zeros((b, 2, DN_HEADS, DN_DK, DN_DV), F32)
        o_na = context_attention(q_na, k_na, v_na)
    else:
        o_na = neighbourhood_attention(q_na, k_na, v_na, k_ctx, v_ctx, na_bias)
    y_dn, s_fin = delta_mixer(proj, s0, conv_w, a_log, dt_bias, g_norm)
    y_na = (o_na.reshape(b, t, NA_WIDTH) * jax.nn.silu(proj[..., OFF_NA_Z:OFF_PL_U])).astype(x.dtype)
    y_pl = (pool_mixer(proj[..., OFF_PL_U:OFF_PL_Z], pool_w, pool_scale)
            * jax.nn.silu(proj[..., OFF_PL_Z:OFF_GATE].astype(F32))).astype(x.dtype)
    g_dn, g_na, g_pl = jnp.split(jax.nn.sigmoid(proj[..., OFF_GATE:]), 3, axis=-1)
    merged = (g_dn * jnp.einsum('btw,wd->btd', y_dn, w_br_dn)
              + g_na * jnp.einsum('btw,wd->btd', y_na, w_br_na)
              + g_pl * jnp.einsum('btw,wd->btd', y_pl, w_br_pl))
    out = rmsnorm(jnp.einsum('btd,de->bte', merged, w_out), g_post)
    return x + gate[:, None] * out, k_na, v_na, s_fin


def setup_inputs(seed: int = 0) -> dict:
    key = jax.random.key(seed)
    ks = jax.random.split(key, 24)
    nrm = lambda kk, shape, s: jax.random.normal(kk, shape, F32) * s
    x_prompt = nrm(ks[0], (BATCH, SEQ, D_MODEL), 1.0)
    x_sample = nrm(ks[1], (DEC_BATCH, DEC_SEQ, D_MODEL), 1.0)
    c = nrm(ks[2], (DEC_BATCH, D_MODEL), 1.0)
    cache_k_na = nrm(ks[3], (DEC_BATCH, DEPTH, PAST_LEN, NA_HEADS, NA_HD), 1.0)
    cache_v_na = nrm(ks[4], (DEC_BATCH, DEPTH, PAST_LEN, NA_HEADS, NA_HD), 1.0)
    state_dn = nrm(ks[5], (DEC_BATCH, DEPTH, 2, DN_HEADS, DN_DK, DN_DV), 0.5)
    c_ctx = nrm(ks[6], (D_MODEL,), 1.0)
    w_ada = nrm(ks[7], (DEPTH, D_MODEL, 3 * D_MODEL), 0.5 * D_MODEL ** -0.5)
    b_ada = nrm(ks[8], (DEPTH, 3 * D_MODEL), 0.02)
    g_pre = 1.0 + nrm(ks[9], (DEPTH, D_MODEL), 0.05)
    g_post = 1.0 + nrm(ks[10], (DEPTH, D_MODEL), 0.05)
    w_in = nrm(ks[11], (DEPTH, D_MODEL, N_IN), D_MODEL ** -0.5)
    conv_dn = nrm(ks[12], (DEPTH, DN_CONV, 3 * DN_WIDTH), DN_CONV ** -0.5)
    a_log_dn = jnp.log(jax.random.uniform(ks[13], (DEPTH, 2, DN_HEADS), F32, 1.0, 16.0))
    dt = jnp.exp(jax.random.uniform(ks[14], (DEPTH, 2, DN_HEADS), F32, math.log(1e-3), math.log(1e-1)))
    dt_bias_dn = dt + jnp.log(-jnp.expm1(-dt))
    g_norm_dn = 1.0 + nrm(ks[15], (DEPTH, DN_DV), 0.05)
    na_bias = nrm(ks[16], (DEPTH, NA_HEADS, 2 * NA_WIN_R - 1, 2 * NA_WIN_C - 1), 0.02)
    pool_w = nrm(ks[17], (DEPTH, POOL_GROUPS, POOL_GC, POOL_GC), POOL_GC ** -0.5)
    pool_scale = 1.0 + nrm(ks[18], (DEPTH, POOL_WIDTH), 0.05)
    w_br_dn = nrm(ks[19], (DEPTH, DN_WIDTH, D_MODEL), DN_WIDTH ** -0.5)
    w_br_na = nrm(ks[20], (DEPTH, NA_WIDTH, D_MODEL), NA_WIDTH ** -0.5)
    w_br_pl = nrm(ks[21], (DEPTH, POOL_WIDTH, D_MODEL), POOL_WIDTH ** -0.5)
    w_out = nrm(ks[22], (DEPTH, D_MODEL, D_MODEL), D_MODEL ** -0.5)
    return {'x_prompt': x_prompt, 'x_sample': x_sample, 'c': c,
            'cache_k_na': cache_k_na, 'cache_v_na': cache_v_na, 'state_dn': state_dn,
            'c_ctx': c_ctx, 'w_ada': w_ada, 'b_ada': b_ada, 'g_pre': g_pre, 'g_post': g_post,
            'w_in': w_in, 'conv_dn': conv_dn, 'a_log_dn': a_log_dn, 'dt_bias_dn': dt_bias_dn,
            'g_norm_dn': g_norm_dn, 'na_bias': na_bias, 'pool_w': pool_w, 'pool_scale': pool_scale,
            'w_br_dn': w_br_dn, 'w_br_na': w_br_na, 'w_br_pl': w_br_pl, 'w_out': w_out}


def reference(x_prompt, x_sample, c, cache_k_na, cache_v_na, state_dn, c_ctx, w_ada, b_ada, g_pre,
              g_post, w_in, conv_dn, a_log_dn, dt_bias_dn, g_norm_dn, na_bias, pool_w, pool_scale,
              w_br_dn, w_br_na, w_br_pl, w_out):
    xp = x_prompt
    xs = x_sample
    k_list, v_list, s_list = [], [], []
    for l in range(DEPTH):
        lw = (g_pre[l], g_post[l], w_in[l], conv_dn[l], a_log_dn[l], dt_bias_dn[l], g_norm_dn[l],
              na_bias[l], pool_w[l], pool_scale[l], w_br_dn[l], w_br_na[l], w_br_pl[l], w_out[l])
        mod_ctx = (jnp.einsum('d,dn->n', jax.nn.silu(c_ctx), w_ada[l]) + b_ada[l])[None]
        mod_lat = jnp.einsum('bd,dn->bn', jax.nn.silu(c), w_ada[l]) + b_ada[l]
        xp, k_l, v_l, s_l = trunk_layer(xp, mod_ctx, None, None, None, *lw)
        k_list.append(k_l)
        v_list.append(v_l)
        s_list.append(s_l)
        xs, _, _, _ = trunk_layer(xs, mod_lat, state_dn[:, l], cache_k_na[:, l], cache_v_na[:, l], *lw)
    new_k_na = jnp.stack(k_list, axis=1)
    new_v_na = jnp.stack(v_list, axis=1)
    new_state_dn = jnp.stack(s_list, axis=1)
    return (xp, xs, new_k_na, new_v_na, new_state_dn)
```

```python
import contextlib
import numpy as np
import concourse.bass as bass
import concourse.mybir as mybir
from concourse.ap import AP
from concourse.bass_utils import run_bass_kernel_spmd

F32 = mybir.dt.float32
F32R = mybir.dt.float32r
ALU = mybir.AluOpType
AF = mybir.ActivationFunctionType

ENGS = ("pe", "act", "dve", "pool", "sp")
NDSEM = 12

D = 1024
DEPTH = 4
SEQ = 256
DSEQ = 2048
NIN = 8208
OFF_DN_Z = 1536
OFF_DN_BETA = 2048
OFF_DN_A = 2056
OFF_NA_Q = 2064
OFF_NA_K = 2576
OFF_NA_V = 3088
OFF_NA_Z = 3600
OFF_PL_U = 4112
OFF_PL_Z = 4624
OFF_GATE = 5136
EPS = 1e-6
POOL_WINDOWS = (2, 4, 8, 16)
NPS = 4
NCST = 582 + 6 * 64 + 64
BIG = 30000.0


class Dep:
    __slots__ = ("w", "r", "excl")

    def __init__(self, w=None, excl=False):
        self.w = w
        self.r = []
        self.excl = excl


class Op:
    __slots__ = ("eng", "fn", "waits", "signal", "count", "is_dma", "dsem", "dval", "dprev")

    def __init__(self, eng, fn, is_dma=False):
        self.eng = eng
        self.fn = fn
        self.waits = []
        self.signal = False
        self.count = None
        self.is_dma = is_dma
        self.dsem = None
        self.dval = None
        self.dprev = None


class Prog:
    def __init__(self, nc):
        self.nc = nc
        self.ops = {e: [] for e in ENGS}
        self.ndma = {e: 0 for e in ENGS}
        self.dtot = {e: [0] * NDSEM for e in ENGS}
        self.nops = 0

    def _mk(self, eng, fn, reads, writes, is_dma):
        o = Op(eng, fn, is_dma)
        ex = [t for t in reads if t.excl]
        if ex:
            reads = [t for t in reads if not t.excl]
            writes = list(writes) + [t for t in ex if t not in writes]
        deps = []
        seen = set()

        def add(d):
            if d is None or id(d) in seen:
                return
            if (not d.is_dma) and d.eng == "pe" and eng == "pe" and not is_dma:
                return
            seen.add(id(d))
            deps.append(d)

        for t in reads:
            add(t.w)
        for t in writes:
            add(t.w)
            for r in t.r:
                add(r)
        o.waits = deps
        for d in deps:
            d.signal = True
        for t in reads:
            if not is_dma:
                t.r = [x for x in t.r if x.is_dma or x.eng != eng]
            t.r.append(o)
        for t in writes:
            t.w = o
            t.r = []
        self.ops[eng].append(o)
        self.nops += 1
        return o

    def op(self, eng, fn, reads=(), writes=()):
        return self._mk(eng, fn, reads, writes, False)

    def dma(self, eng, out, in_, reads=(), writes=(), r32=False, slow=False):
        nc = self.nc

        def fn(e):
            kw = {}
            if slow:
                kw["allow_slow_non_contiguous"] = True
            if r32:
                nc.dge_precook = False
            ins = e.dma_start(out=out, in_=in_, **kw)
            if r32:
                nc.dge_precook = True
            return ins

        o = self._mk(eng, fn, reads, writes, True)
        i = self.ndma[eng] % NDSEM
        self.ndma[eng] += 1
        o.dsem = i
        o.dprev = self.dtot[eng][i]
        self.dtot[eng][i] += 16
        o.dval = self.dtot[eng][i]
        o.signal = True
        return o

    def emit(self):
        nc = self.nc
        for e in ENGS:
            c = 0
            for o in self.ops[e]:
                if not o.is_dma and o.signal:
                    c += 1
                    o.count = c
        nsig = {e: sum(1 for o in self.ops[e] if (not o.is_dma and o.signal)) for e in ENGS}
        with contextlib.ExitStack() as st:
            esem = {e: st.enter_context(nc.semaphore("s_" + e)) for e in ENGS}
            dsem = {
                e: [st.enter_context(nc.semaphore("d_%s_%d" % (e, i))) for i in range(NDSEM)]
                for e in ("sp", "act", "pool")
            }
            block = st.enter_context(nc.Block())
            ops = self.ops
            dtot = self.dtot

            def run(e, engobj, final=False):
                seen_e = {x: 0 for x in ENGS}
                seen_d = {}
                for o in ops[e]:
                    for d in o.waits:
                        if d.is_dma:
                            key = (d.eng, d.dsem)
                            if seen_d.get(key, 0) >= d.dval:
                                continue
                            engobj.wait_ge(dsem[d.eng][d.dsem], d.dval)
                            seen_d[key] = d.dval
                        else:
                            if seen_e[d.eng] >= d.count:
                                continue
                            engobj.wait_ge(esem[d.eng], d.count)
                            seen_e[d.eng] = d.count
                    if o.is_dma:
                        key = (e, o.dsem)
                        if o.dprev > 0 and seen_d.get(key, 0) < o.dprev:
                            engobj.wait_ge(dsem[e][o.dsem], o.dprev)
                            seen_d[key] = o.dprev
                        ins = o.fn(engobj)
                        ins.then_inc(dsem[e][o.dsem], 16)
                    else:
                        ins = o.fn(engobj)
                        if o.signal:
                            ins.then_inc(esem[e], 1)
                if final:
                    for x in ENGS:
                        if x != e and nsig[x] > 0:
                            engobj.wait_ge(esem[x], nsig[x])
                    for q in ("sp", "act", "pool"):
                        for i in range(NDSEM):
                            if dtot[q][i] > 0:
                                engobj.wait_ge(dsem[q][i], dtot[q][i])

            @block.tensor
            def _(eng):
                run("pe", eng)

            @block.vector
            def _(eng):
                run("dve", eng)

            @block.scalar
            def _(eng):
                run("act", eng)

            @block.gpsimd
            def _(eng):
                run("pool", eng)

            @block.sync
            def _(eng):
                run("sp", eng, final=True)


class K:
    def __init__(self, nc, st):
        self.nc = nc
        self.st = st
        self.P = Prog(nc)
        self.din = {}
        self.dout = {}
        self.psb = [st.enter_context(nc.psum_tensor("psb%d" % i, [128, 512], F32)) for i in range(8)]
        self.psd = [Dep(excl=True) for _ in range(8)]
        self.psi = 0
        self.dq = 0

    def inp(self, name, shape, dt=F32):
        t = self.nc.dram_tensor(name, list(shape), dt, kind="ExternalInput")
        self.din[name] = t
        return t

    def outp(self, name, shape):
        t = self.nc.dram_tensor(name, list(shape), F32, kind="ExternalOutput")
        self.dout[name] = t
        return t

    def scr(self, name, shape, dt=F32):
        return self.nc.dram_tensor(name, list(shape), dt, kind="Internal")

    def sb(self, name, shape, dt=F32):
        return self.st.enter_context(self.nc.sbuf_tensor(name, list(shape), dt))

    def ps(self):
        i = self.psi
        self.psi = (i + 1) % 8
        return self.psb[i], self.psd[i]

    def q(self):
        self.dq ^= 1
        return "sp" if self.dq else "act"

    def mm(self, out, lhsT, rhs, start, stop, reads, writes):
        self.P.op("pe", lambda e: e.matmul(out, lhsT=lhsT, rhs=rhs, start=start, stop=stop), reads, writes)

    def tr(self, out, in_, ident, reads, writes):
        self.P.op("pe", lambda e: e.transpose(out, in_, ident), reads, writes)

    def act(self, out, in_, func, reads, writes, bias=None, scale=1.0, accum=None):
        def fn(e):
            kw = {}
            if bias is not None:
                kw["bias"] = bias
            if accum is not None:
                kw["accum_out"] = accum
            return e.activation(out=out, in_=in_, func=func, scale=scale, **kw)

        self.P.op("act", fn, reads, writes)

    def tt(self, out, in0, in1, op, reads, writes, eng="dve"):
        self.P.op(eng, lambda e: e.tensor_tensor(out=out, in0=in0, in1=in1, op=op), reads, writes)

    def ts(self, out, in0, s1, s2, op0, op1, reads, writes, eng="dve"):
        if s2 is None:
            self.P.op(eng, lambda e: e.tensor_scalar(out=out, in0=in0, scalar1=s1, scalar2=None, op0=op0), reads, writes)
        else:
            self.P.op(eng, lambda e: e.tensor_scalar(out=out, in0=in0, scalar1=s1, scalar2=s2, op0=op0, op1=op1), reads, writes)

    def stt(self, out, in0, scalar, in1, op0, op1, reads, writes, eng="dve"):
        self.P.op(eng, lambda e: e.scalar_tensor_tensor(out=out, in0=in0, scalar=scalar, in1=in1, op0=op0, op1=op1),
                  reads, writes)

    def rcp(self, out, in_, reads, writes):
        self.P.op("dve", lambda e: e.reciprocal(out=out, in_=in_), reads, writes)

    def cp(self, out, in_, reads, writes, eng="dve"):
        self.P.op(eng, lambda e: e.tensor_copy(out=out, in_=in_), reads, writes)

    def ms(self, ap, val, writes, eng="pool"):
        self.P.op(eng, lambda e: e.memset(ap, val), (), writes)


def build_program(depth=DEPTH, do_dn=True, do_na=True):
    nc = bass.Bass("TRN2", target_bir_lowering=False)
    st = contextlib.ExitStack()
    with st:
        k = K(nc, st)
        P = k.P
        x_p = k.inp("x_p", [NPS, SEQ, D])
        x_s = k.inp("x_s", [DSEQ, D])
        cvec = k.inp("cvec", [2, D])
        w_ada = k.inp("w_ada", [depth, D, 3 * D], F32R)
        b_ada = k.inp("b_ada", [depth, 3 * D])
        g_pre = k.inp("g_pre", [depth, D])
        g_post = k.inp("g_post", [depth, D])
        w_in = k.inp("w_in", [depth, D, NIN], F32R)
        pool_w = k.inp("pool_w", [depth, 4, 128, 128], F32R)
        pool_scale = k.inp("pool_scale", [depth, 512])
        w_br = [k.inp(n, [depth, 512, D], F32R) for n in ("w_br_dn", "w_br_na", "w_br_pl")]
        w_out = k.inp("w_out", [depth, D, D], F32R)
        invcnt_p = k.inp("invcnt_p", [4, SEQ])
        invcnt_s = k.inp("invcnt_s", [4, DSEQ])
        ident_in = k.inp("ident", [128, 128])
        conv_dn = k.inp("conv_dn", [depth, 3, 1536])
        a_log = k.inp("a_log", [depth, 8])
        dt_bias = k.inp("dt_bias", [depth, 8])
        g_norm = k.inp("g_norm", [depth, 128])
        sdn = k.inp("sdn", [depth, 2, 4, 128, 128])
        dncst_in = k.inp("dncst", [128, NCST])
        ck_in = k.inp("ck", [depth, 256, 512])
        cv_in = k.inp("cvv", [depth, 256, 512], F32R)
        rpad_in = k.inp("rpad", [depth, 8, 15, 127])

        y_p = k.outp("y_p", [NPS, SEQ, D])
        y_s = k.outp("y_s", [DSEQ, D])
        nk = k.outp("nk", [NPS, depth, SEQ, 512])
        nv = k.outp("nv", [NPS, depth, SEQ, 512])
        d_nkv = Dep()
        nst = k.outp("nst", [NPS, depth, 2, 4, 128, 128])
        d_nst = Dep()

        xs_p = [k.scr("xs_p%d" % i, [NPS, SEQ, D]) for i in range(2)]
        xs_s = [k.scr("xs_s%d" % i, [DSEQ, D]) for i in range(2)]
        yscr = k.scr("yscr", [12, 128, DSEQ], F32R)
        d_xs_p = [[Dep() for _ in range(NPS)] for _ in range(2)]
        d_xs_s = [Dep() for _ in range(2)]
        d_yscr = [Dep() for _ in range(12)]

        ident = k.sb("ident_sb", [128, 128])
        d_ident = Dep()
        P.dma("sp", ident[:], ident_in.ap(), writes=[d_ident])
        cst = k.sb("dncst_sb", [128, NCST])
        d_cst = Dep()
        P.dma("act", cst[:], dncst_in.ap(), writes=[d_cst])
        TRI = [cst[:, 0:128], cst[:, 128:256]]
        BLK = cst[:, 256:384]
        ONES = cst[:, 384:512]
        I2 = cst[:, 512:576]
        SEL2 = cst[:, 576:578]
        SELLAST = cst[:, 578:582]
        MASK = [[cst[:, 582 + (ty * 2 + d_) * 64: 582 + (ty * 2 + d_ + 1) * 64] for d_ in range(2)] for ty in range(3)]
        CM = cst[:, 582 + 384:582 + 384 + 64]
        hT = k.sb("hT", [128, 8, DSEQ], F32R)
        d_hT = Dep()
        small = k.sb("small", [128, 16])
        d_small = Dep()
        silucT = k.sb("silucT", [128, 8, 2], F32R)
        d_siluc = Dep()
        modcol = k.sb("modcol", [128, 16, 2])
        s1col = k.sb("s1col", [128, 8, 2])
        d_modcol = Dep()
        gg = k.sb("gg", [128, 2, D])
        d_gg = Dep()
        RSZ = 15 * 1024
        FSZ = 17 * 1024 + 512
        arenaR = k.sb("arenaR", [128, RSZ], F32R)
        arenaF = k.sb("arenaF", [128, FSZ])
        arena_deps = []

        def fence():
            f = P.op("dve", lambda e: e.memset(small[:, 15:16], 0.0), reads=(), writes=list(arena_deps))
            arena_deps.clear()
            return f

        class Carve:
            def __init__(self, seed):
                self.offR = 0
                self.offF = 0
                self.seed = seed

            def get(self, cols, dt=F32):
                if dt == F32R:
                    a = arenaR[:, self.offR:self.offR + cols]
                    self.offR += cols
                    assert self.offR <= RSZ, self.offR
                else:
                    a = arenaF[:, self.offF:self.offF + cols]
                    self.offF += cols
                    assert self.offF <= FSZ, self.offF
                d = Dep(self.seed)
                arena_deps.append(d)
                return a, d

        craw = k.sb("craw", [128, 8, 2])
        for j in range(2):
            P.dma("sp", craw[:, :, j], cvec.ap()[j].rearrange("(c p) -> p c", p=128), writes=[d_siluc], slow=True)
        k.act(silucT[:], craw[:], AF.Silu, [d_siluc], [d_siluc])

        seqs = [("p", i, SEQ) for i in range(NPS)] + [("s", 0, DSEQ)]
        zscr = k.scr("zscr", [depth, 120, 64, 127])
        d_zscr = [Dep() for _ in range(depth)]
        if do_na:
            for l_ in range(depth):
                P.dma("sp", zscr.ap()[l_], AP(rpad_in, l_ * 120 * 127, [[127, 120], [0, 64], [1, 127]]), writes=[d_zscr[l_]])

        def dn_unit(l, kind, si, T, h):
            NT = T // 128
            TT = min(T, 512)
            NTL = T // TT
            fz = fence()
            cv = Carve(fz)
            ws = []
            for off in (0, 512, 1024, OFF_DN_Z):
                wa, dw = cv.get(8 * 128, F32R)
                w3 = wa.rearrange("p (c n) -> p c n", c=8)
                c0 = off + h * 128
                P.dma(k.q(), w3, w_in.ap()[l, :, c0:c0 + 128].rearrange("(c p) n -> p c n", p=128), writes=[dw], r32=True)
                ws.append((w3, dw))
            (wq, d_wq), (wk, d_wk), (wv, d_wv), (wz, d_wz) = ws
            wba_f, d_wba = cv.get(8 * 4, F32R)
            wba = wba_f.rearrange("p (c n) -> p c n", c=8)
            for j4 in range(4):
                cj4 = OFF_DN_BETA + 4 * j4 + h
                P.dma("sp", wba[:, :, j4], w_in.ap()[l, :, cj4].rearrange("(c p) -> p c", p=128), writes=[d_wba], r32=True, slow=True)
            raw, d_raw = cv.get(T + 2)
            qf, d_qf = cv.get(T)
            kf, d_kf = cv.get(T)
            vf, d_vf = cv.get(T)
            oacc, _ = cv.get(T)
            d_oacc = [Dep(fz) for _ in range(NT)]
            arena_deps.extend(d_oacc)
            tmpb, d_tmpb = cv.get(512)
            cw, d_cw = cv.get(9)
            for idx in range(3):
                c0 = idx * 512 + h * 128
                P.dma("act", cw[:, idx * 3:(idx + 1) * 3], conv_dn.ap()[l, :, c0:c0 + 128].rearrange("t p -> p t"),
                      writes=[d_cw], slow=True)
            k.ms(raw[:, 0:1], 0.0, [d_raw])
            k.ms(raw[:, T + 1:T + 2], 0.0, [d_raw])
            for i in range(NT):
                k.ms(oacc[:, i * 128:(i + 1) * 128], 0.0, [d_oacc[i]])
            dsts = [(qf, d_qf), (kf, d_kf), (vf, d_vf)]
            for idx in range(3):
                w3, dw = ws[idx]
                dst, dd = dsts[idx]
                for t in range(NTL):
                    pb, pd = k.ps()
                    for kc in range(8):
                        k.mm(pb[:, 0:TT], w3[:, kc, :], hT[:, kc, t * TT:(t + 1) * TT], kc == 0, kc == 7, [dw, d_hT], [pd])
                    k.cp(raw[:, 1 + t * TT:1 + (t + 1) * TT], pb[:, 0:TT], [pd], [d_raw])
                for t in range(NTL):
                    a = t * TT
                    k.ts(dst[:, a:a + TT], raw[:, a:a + TT], cw[:, idx * 3:idx * 3 + 1], None, ALU.mult, None, [d_raw, d_cw], [dd])
                    k.stt(tmpb[:, 0:TT], raw[:, a + 1:a + 1 + TT], cw[:, idx * 3 + 1:idx * 3 + 2], dst[:, a:a + TT],
                          ALU.mult, ALU.add, [d_raw, d_cw, dd], [d_tmpb])
                    k.stt(dst[:, a:a + TT], raw[:, a + 2:a + 2 + TT], cw[:, idx * 3 + 2:idx * 3 + 3], tmpb[:, 0:TT],
                          ALU.mult, ALU.add, [d_raw, d_cw, d_tmpb], [dd])
                    k.act(dst[:, a:a + TT], dst[:, a:a + TT], AF.Silu, [dd], [dd])
            for idx in range(2):
                dst, dd = dsts[idx]
                for t in range(NTL):
                    a = t * TT
                    k.act(tmpb[:, 0:TT], dst[:, a:a + TT], AF.Square, [dd], [d_tmpb])
                    pb, pd = k.ps()
                    k.mm(pb[:, 0:TT], ONES, tmpb[:, 0:TT], True, True, [d_cst, d_tmpb], [pd])
                    k.act(tmpb[:, 0:TT], pb[:, 0:TT], AF.Sqrt, [pd], [d_tmpb], bias=EPS)
                    k.rcp(tmpb[:, 0:TT], tmpb[:, 0:TT], [d_tmpb], [d_tmpb])
                    if idx == 0:
                        k.stt(dst[:, a:a + TT], dst[:, a:a + TT], 128.0 ** -0.5, tmpb[:, 0:TT], ALU.mult, ALU.mult, [dd, d_tmpb], [dd])
                    else:
                        k.tt(dst[:, a:a + TT], dst[:, a:a + TT], tmpb[:, 0:TT], ALU.mult, [dd, d_tmpb], [dd])
            zs = raw
            d_zs = d_raw
            for t in range(NTL):
                pb, pd = k.ps()
                for kc in range(8):
                    k.mm(pb[:, 0:TT], wz[:, kc, :], hT[:, kc, t * TT:(t + 1) * TT], kc == 0, kc == 7, [d_wz, d_hT], [pd])
                k.act(zs[:, t * TT:(t + 1) * TT], pb[:, 0:TT], AF.Silu, [pd], [d_zs])
            d_g = Dep(fz)
            arena_deps.append(d_g)
            ba, _ = cv.get(NT * 4)
            ba3 = ba.rearrange("p (t n) -> p t n", n=4)
            pb, pd = k.ps()
            for i in range(NT):
                for kc in range(8):
                    k.mm(pb[:, i * 4:(i + 1) * 4], hT[:, kc, i * 128:(i + 1) * 128], wba[:, kc, :], kc == 0, kc == 7, [d_wba, d_hT], [pd])
            k.cp(ba, pb[:, 0:NT * 4], [pd], [d_g])

            def g2():
                a_, _ = cv.get(NT * 2)
                return a_, a_.rearrange("p (t n) -> p t n", n=2)

            beta, beta3 = g2()
            negb, negb3 = g2()
            lnb, lnb3 = g2()
            la, la3 = g2()
            G, G3 = g2()
            Gb, Gb3 = g2()
            eG, eG3 = g2()
            bG, bG3 = g2()
            eGL, eGL3 = g2()
            gt, gt3 = g2()
            gsel2, _ = cv.get(NT * 4)
            glall, _ = cv.get(NT * 4)
            glall4 = glall.rearrange("p (t d c) -> p t d c", d=2, c=2)
            rowc, _ = cv.get(4)
            gn_row, d_gn = cv.get(128)
            P.dma("sp", rowc[:, 0:1], dt_bias.ap()[l, h:h + 1].partition_broadcast(128), writes=[d_g])
            P.dma("sp", rowc[:, 1:2], dt_bias.ap()[l, 4 + h:5 + h].partition_broadcast(128), writes=[d_g])
            P.dma("act", rowc[:, 2:3], a_log.ap()[l, h:h + 1].partition_broadcast(128), writes=[d_g])
            P.dma("act", rowc[:, 3:4], a_log.ap()[l, 4 + h:5 + h].partition_broadcast(128), writes=[d_g])
            P.dma("sp", gn_row, g_norm.ap()[l].partition_broadcast(128), writes=[d_gn])
            G_ = [d_g]
            k.act(rowc[:, 2:4], rowc[:, 2:4], AF.Exp, G_, G_)
            k.ts(rowc[:, 2:4], rowc[:, 2:4], -1.0, None, ALU.mult, None, G_, G_)
            k.act(beta3, ba3[:, :, 0:2], AF.Sigmoid, G_, G_)
            k.ts(negb, beta, -1.0, None, ALU.mult, None, G_, G_)
            k.act(gt3, ba3[:, :, 0:2], AF.Exp, G_, G_, scale=-1.0)
            k.act(gt, gt, AF.Ln, G_, G_, bias=1.0)
            k.ts(lnb, gt, -1.0, None, ALU.mult, None, G_, G_)
            k.tt(gt3, ba3[:, :, 2:4], rowc[:, 0:2].unsqueeze(1).to_broadcast([128, NT, 2]), ALU.add, G_, G_)
            k.act(gt, gt, AF.Exp, G_, G_)
            k.act(gt, gt, AF.Ln, G_, G_, bias=1.0)
            k.tt(la3, gt3, rowc[:, 2:4].unsqueeze(1).to_broadcast([128, NT, 2]), ALU.mult, G_, G_)
            pb, pd = k.ps()
            for i in range(NT):
                k.mm(pb[:, i * 2:(i + 1) * 2], TRI[0], la3[:, i, :], True, True, [d_cst, d_g], [pd])
                k.mm(pb[:, 64 + i * 2:64 + (i + 1) * 2], TRI[1], la3[:, i, :], True, True, [d_cst, d_g], [pd])
            k.cp(G3[:, :, 0:1], pb[:, 0:NT * 2].rearrange("p (t n) -> p t n", n=2)[:, :, 0:1], [pd], G_)
            k.cp(G3[:, :, 1:2], pb[:, 64:64 + NT * 2].rearrange("p (t n) -> p t n", n=2)[:, :, 1:2], [pd], G_)
            k.tt(Gb, G, lnb, ALU.add, G_, G_)
            k.act(eG, G, AF.Exp, G_, G_)
            k.tt(bG, beta, eG, ALU.mult, G_, G_)
            k.tt(gt3, G3, SEL2.unsqueeze(1).to_broadcast([128, NT, 2]), ALU.mult, G_ + [d_cst], G_)
            pb, pd = k.ps()
            k.mm(pb[:, 0:NT * 2], BLK, gt, True, True, [d_cst, d_g], [pd])
            k.tt(gt, pb[:, 0:NT * 2], G, ALU.subtract, [pd, d_g], G_)
            k.act(eGL, gt, AF.Exp, G_, G_)
            k.tt(gsel2.rearrange("p (t d c) -> p t d c", d=2, c=2), G3.unsqueeze(3).to_broadcast([128, NT, 2, 2]),
                 SELLAST.rearrange("p (d c) -> p d c", d=2).unsqueeze(1).to_broadcast([128, NT, 2, 2]), ALU.mult,
                 G_ + [d_cst], G_)
            pb, pd = k.ps()
            k.mm(pb[:, 0:NT * 4], ONES, gsel2, True, True, [d_cst, d_g], [pd])
            k.act(glall, pb[:, 0:NT * 4], AF.Exp, [pd], G_)
            S = []
            for d_ in range(2):
                s_, ds_ = cv.get(128)
                if kind == "p":
                    k.ms(s_, 0.0, [ds_])
                else:
                    P.dma(k.q(), s_, sdn.ap()[l, d_, h], writes=[ds_])
                S.append((s_, ds_))
            TB = []
            for d_ in range(2):
                nm = {}
                for name, cols in (("Gd", 64), ("Gbd", 64), ("E1", 64), ("E2", 64), ("E3", 64), ("kg", 128), ("X", 128),
                                   ("vb", 128), ("attnT", 64), ("MM0", 128), ("MM1", 128), ("PT", 64), ("uw", 256),
                                   ("vnew", 128), ("oasb", 128), ("otmp", 128)):
                    nm[name] = cv.get(cols)
                TB.append(nm)

            def prep(i, d_):
                B = TB[d_]
                tsl = slice(i * 128, (i + 1) * 128)
                Gc = G3[:, i, d_:d_ + 1]
                (Gd, d_Gd), (Gbd, d_Gbd) = B["Gd"], B["Gbd"]
                (E1, d_E1), (E2, d_E2), (E3, d_E3) = B["E1"], B["E2"], B["E3"]
                (kg, d_kg), (X, d_X), (vb, d_vb) = B["kg"], B["X"], B["vb"]
                (attnT, d_at), (PT, d_PT), (uw, d_uw) = B["attnT"], B["PT"], B["uw"]
                pk, pkd = k.ps()
                k.tr(pk[:, 0:128], kf[:, tsl], ident[:], [d_kf, d_ident], [pkd])
                k.tr(pk[:, 128:256], vf[:, tsl], ident[:], [d_vf, d_ident], [pkd])
                k.ts(kg, pk[:, 0:128], eGL3[:, i, d_:d_ + 1], None, ALU.mult, None, [pkd, d_g], [d_kg])
                k.act(X, pk[:, 0:128], AF.Copy, [pkd, d_g], [d_X], scale=bG3[:, i, d_:d_ + 1])
                k.ts(vb, pk[:, 128:256], beta3[:, i, d_:d_ + 1], None, ALU.mult, None, [pkd, d_g], [d_vb])
                pa, pad = k.ps()
                k.mm(pa[:, 0:128], kf[:, tsl], kf[:, tsl], True, True, [d_kf], [pad])
                k.mm(pa[:, 128:256], kf[:, tsl], qf[:, tsl], True, True, [d_kf, d_qf], [pad])
                k.ts(Gd, I2, Gc, None, ALU.mult, None, [d_cst, d_g], [d_Gd], eng="pool")
                k.ts(Gbd, I2, Gb3[:, i, d_:d_ + 1], None, ALU.mult, None, [d_cst, d_g], [d_Gbd], eng="pool")
                pg, pgd = k.ps()
                k.mm(pg[:, 0:64], BLK, Gd, True, True, [d_cst, d_Gd], [pgd])
                k.mm(pg[:, 64:128], BLK, Gbd, True, True, [d_cst, d_Gbd], [pgd])
                k.stt(E1, pg[:, 0:64], Gc, MASK[0][d_], ALU.subtract, ALU.add, [pgd, d_g, d_cst], [d_E1])
                k.act(E1, E1, AF.Exp, [d_E1], [d_E1])
                k.stt(E2, pg[:, 64:128], Gc, MASK[1][d_], ALU.subtract, ALU.add, [pgd, d_g, d_cst], [d_E2])
                k.act(E2, E2, AF.Exp, [d_E2], [d_E2])
                k.stt(E3, pg[:, 0:64], Gc, MASK[2][d_], ALU.subtract, ALU.add, [pgd, d_g, d_cst], [d_E3])
                k.act(E3, E3, AF.Exp, [d_E3], [d_E3], scale=-1.0)
                cur, dcur = B["MM0"]
                nxt, dnxt = B["MM1"]
                for cp in range(2):
                    bs = slice(cp * 64, (cp + 1) * 64)
                    k.tt(attnT[bs, :], pa[bs, 128 + cp * 64:128 + (cp + 1) * 64], E1[bs, :], ALU.mult, [pad, d_E1], [d_at])
                    k.stt(cur[bs, 64:128], pa[bs, cp * 64:(cp + 1) * 64], -1.0, E2[bs, :], ALU.mult, ALU.mult, [pad, d_E2], [dcur])
                    k.stt(cur[bs, 0:64], pa[bs, cp * 64:(cp + 1) * 64], negb3[bs, i, d_:d_ + 1], E3[bs, :], ALU.mult, ALU.mult,
                          [pad, d_E3, d_g], [dcur])
                k.tt(PT, I2, cur[:, 64:128], ALU.add, [d_cst, dcur], [d_PT])
                for kk in range(5):
                    lastk = (kk == 4)
                    pm, pmd = k.ps()
                    for cp in range(2):
                        bs = slice(cp * 64, (cp + 1) * 64)
                        k.mm(pm[bs, 0:64], cur[bs, 64:128], cur[bs, 0:64], True, True, [dcur], [pmd])
                        if not lastk:
                            k.mm(pm[bs, 64:128], cur[bs, 0:64], cur[bs, 64:128], True, True, [dcur], [pmd])
                    ncols = 64 if lastk else 128
                    k.act(nxt[:, 0:ncols], pm[:, 0:ncols], AF.Copy, [pmd], [dnxt])
                    pp, ppd = k.ps()
                    for cp in range(2):
                        bs = slice(cp * 64, (cp + 1) * 64)
                        k.mm(pp[bs, 0:64], nxt[bs, 0:64], PT[bs, :], True, True, [dnxt, d_PT], [ppd])
                    k.tt(PT, PT, pp[:, 0:64], ALU.add, [d_PT, ppd], [d_PT])
                    cur, dcur, nxt, dnxt = nxt, dnxt, cur, dcur
                pu, pud = k.ps()
                for cp in range(2):
                    bs = slice(cp * 64, (cp + 1) * 64)
                    k.mm(pu[bs, 0:128], PT[bs, :], vb[bs, :], True, True, [d_PT, d_vb], [pud])
                    k.mm(pu[:, 128 + cp * 64:128 + (cp + 1) * 64], X[bs, :], PT[bs, :], True, True, [d_X, d_PT], [pud])
                k.cp(uw, pu[:, 0:256], [pud], [d_uw])

            def scan(i, d_):
                B = TB[d_]
                (kg, d_kg), (attnT, d_at), (uw, d_uw) = B["kg"], B["attnT"], B["uw"]
                (vnew, d_vn), (oasb, d_oa), (otmp, d_ot) = B["vnew"], B["oasb"], B["otmp"]
                s_, ds_ = S[d_]
                for cp in ((0, 1) if d_ == 0 else (1, 0)):
                    bs = slice(cp * 64, (cp + 1) * 64)
                    cs = slice(i * 128 + cp * 64, i * 128 + (cp + 1) * 64)
                    p1, p1d = k.ps()
                    k.mm(p1[bs, 0:128], uw[:, 128 + cp * 64:128 + (cp + 1) * 64], s_, True, True, [d_uw, ds_], [p1d])
                    k.mm(p1[bs, 128:256], qf[:, cs], s_, True, True, [d_qf, ds_], [p1d])
                    k.tt(vnew[bs, :], uw[bs, 0:128], p1[bs, 0:128], ALU.subtract, [d_uw, p1d], [d_vn])
                    p2, p2d = k.ps()
                    k.mm(p2[bs, 0:128], attnT[bs, :], vnew[bs, :], True, True, [d_at, d_vn], [p2d])
                    k.mm(p2[:, 128:256], kg[bs, :], vnew[bs, :], True, True, [d_kg, d_vn], [p2d])
                    k.act(oasb[bs, :], p2[bs, 0:128], AF.Copy, [p2d], [d_oa])
                    k.stt(otmp[bs, :], p1[bs, 128:256], eG3[bs, i, d_:d_ + 1], oasb[bs, :], ALU.mult, ALU.add,
                          [p1d, d_g, d_oa], [d_ot])
                    k.tt(oacc[bs, i * 128:(i + 1) * 128], oacc[bs, i * 128:(i + 1) * 128], otmp[bs, :], ALU.add,
                         [d_oacc[i], d_ot], [d_oacc[i]], eng="pool")
                    k.stt(s_, s_, glall4[:, i, d_, cp:cp + 1], p2[:, 128:256], ALU.mult, ALU.add, [ds_, d_g, p2d], [ds_])

            for step in range(NT):
                prep(step, 0)
                prep(NT - 1 - step, 1)
                scan(step, 0)
                scan(NT - 1 - step, 1)
            if kind == "p":
                for d_ in range(2):
                    P.dma(k.q(), nst.ap()[si, l, d_, h], S[d_][0], reads=[S[d_][1]], writes=[d_nst])
            rs, d_rs = cv.get(4)
            on, d_on = cv.get(128)
            yT, d_yT = cv.get(128, F32R)
            for i in range(NT):
                tsl = slice(i * 128, (i + 1) * 128)
                k.act(on, oacc[:, tsl], AF.Square, [d_oacc[i]], [d_on, d_rs], accum=rs[:, 0:1])
                k.act(rs[:, 1:2], rs[:, 0:1], AF.Sqrt, [d_rs], [d_rs], bias=EPS, scale=1.0 / 128)
                k.rcp(rs[:, 2:3], rs[:, 1:2], [d_rs], [d_rs])
                k.stt(on, oacc[:, tsl], rs[:, 2:3], gn_row, ALU.mult, ALU.mult, [d_oacc[i], d_rs, d_gn, d_on], [d_on])
                pt, ptd = k.ps()
                k.tr(pt[:, 0:128], on, ident[:], [d_on, d_ident], [ptd])
                k.tt(yT, pt[:, 0:128], zs[:, tsl], ALU.mult, [ptd, d_zs], [d_yT])
                P.dma(k.q(), yscr.ap()[h, :, tsl], yT, reads=[d_yT], writes=[d_yscr[h]], r32=True)

        def na_unit(l, kind, si, T, hp):
            NT = T // 128
            TT = min(T, 512)
            fz = fence()
            cv = Carve(fz)
            ws = []
            for off in (OFF_NA_Q, OFF_NA_K, OFF_NA_V, OFF_NA_Z):
                wa, dw = cv.get(8 * 128, F32R)
                w3 = wa.rearrange("p (c n) -> p c n", c=8)
                c0 = off + hp * 128
                P.dma(k.q(), w3, w_in.ap()[l, :, c0:c0 + 128].rearrange("(c p) n -> p c n", p=128), writes=[dw], r32=True)
                ws.append((w3, dw))
            (wq, d_wq), (wk, d_wk), (wv, d_wv), (wz, d_wz) = ws
            qT, d_qT = cv.get(T, F32R)
            kT, d_kT = cv.get(T, F32R)
            vt_f, d_vt = cv.get(NT * 132, F32R)
            vtok = vt_f.rearrange("p (t h e) -> p t h e", t=NT, h=2)
            zs, d_zs = cv.get(T)
            ones, d_ones = cv.get(2)
            stg, d_stg = cv.get(256)
            k.ms(ones, 1.0, [d_ones])
            for t in range(T // TT):
                ts_ = slice(t * TT, (t + 1) * TT)
                pb, pd = k.ps()
                for kc in range(8):
                    k.mm(pb[:, 0:TT], wq[:, kc, :], hT[:, kc, ts_], kc == 0, kc == 7, [d_wq, d_hT], [pd])
                k.ts(qT[:, ts_], pb[:, 0:TT], 0.125, None, ALU.mult, None, [pd], [d_qT])
                pb, pd = k.ps()
                for kc in range(8):
                    k.mm(pb[:, 0:TT], wk[:, kc, :], hT[:, kc, ts_], kc == 0, kc == 7, [d_wk, d_hT], [pd])
                k.cp(kT[:, ts_], pb[:, 0:TT], [pd], [d_kT])
                pb, pd = k.ps()
                for kc in range(8):
                    k.mm(pb[:, 0:TT], wz[:, kc, :], hT[:, kc, ts_], kc == 0, kc == 7, [d_wz, d_hT], [pd])
                k.act(zs[:, ts_], pb[:, 0:TT], AF.Silu, [pd], [d_zs])
            for i in range(NT):
                is_ = slice(i * 128, (i + 1) * 128)
                pb, pd = k.ps()
                for kc in range(8):
                    k.mm(pb[:, 0:128], hT[:, kc, is_], wv[:, kc, :], kc == 0, kc == 7, [d_wv, d_hT], [pd])
                k.cp(vtok[:, i, :, 0:64], pb[:, 0:128].rearrange("p (h e) -> p h e", h=2), [pd], [d_vt])
                k.cp(vtok[:, i, :, 64:66], ones[:, 0:2].unsqueeze(1).to_broadcast([128, 2, 2]), [d_ones], [d_vt], eng="pool")
                if kind == "p":
                    k.cp(stg[:, 0:128], pb[:, 0:128], [pd], [d_stg])
                    P.dma("act", nv.ap()[si, l, is_, hp * 128:(hp + 1) * 128], stg[:, 0:128], reads=[d_stg], writes=[d_nkv])
                    pb, pd = k.ps()
                    for kc in range(8):
                        k.mm(pb[:, 0:128], hT[:, kc, is_], wk[:, kc, :], kc == 0, kc == 7, [d_wk, d_hT], [pd])
                    k.cp(stg[:, 128:256], pb[:, 0:128], [pd], [d_stg])
                    P.dma("act", nk.ap()[si, l, is_, hp * 128:(hp + 1) * 128], stg[:, 128:256], reads=[d_stg], writes=[d_nkv])
            otok, d_otok = cv.get(128)
            rden, d_rden = cv.get(2)
            yT, d_yT = cv.get(128, F32R)
            if kind == "p":
                PT = [cv.get(256, F32R) for _ in range(4)]
                po, pod = k.ps()
                for h2 in range(2):
                    hs = slice(h2 * 64, (h2 + 1) * 64)
                    for kt in range(2):
                        pt, d_pt = PT[h2 * 2 + kt]
                        pb, pd = k.ps()
                        k.mm(pb[:, 0:256], kT[hs, kt * 128:(kt + 1) * 128], qT[hs, 0:256], True, True, [d_kT, d_qT], [pd])
                        k.act(pt, pb[:, 0:256], AF.Exp, [pd], [d_pt])
                    for qt in range(2):
                        for kt in range(2):
                            pt, d_pt = PT[h2 * 2 + kt]
                            c0 = qt * 132 + h2 * 66
                            k.mm(po[:, c0:c0 + 66], pt[:, qt * 128:(qt + 1) * 128], vtok[:, kt, h2, :], kt == 0, kt == 1,
                                 [d_pt, d_vt], [pod])
                for qt in range(2):
                    qs = slice(qt * 128, (qt + 1) * 128)
                    for h2 in range(2):
                        c0 = qt * 132 + h2 * 66
                        k.rcp(rden[:, h2:h2 + 1], po[:, c0 + 64:c0 + 65], [pod], [d_rden])
                        k.ts(otok[:, h2 * 64:(h2 + 1) * 64], po[:, c0:c0 + 64], rden[:, h2:h2 + 1], None, ALU.mult, None,
                             [pod, d_rden], [d_otok])
                    pb, pd = k.ps()
                    k.tr(pb[:, 0:128], otok, ident[:], [d_otok, d_ident], [pd])
                    k.tt(yT, pb[:, 0:128], zs[:, qs], ALU.mult, [pd, d_zs], [d_yT])
                    P.dma(k.q(), yscr.ap()[4 + hp, :, qs], yT, reads=[d_yT], writes=[d_yscr[4 + hp]], r32=True)
            else:
                kcT, d_kcT = cv.get(256, F32R)
                vc_f, d_vc = cv.get(2 * 132, F32R)
                vctx = vc_f.rearrange("p (t h e) -> p t h e", t=2, h=2)
                cst_k, d_cstk = cv.get(256)
                P.dma("sp", cst_k.rearrange("p (t n) -> p t n", t=2),
                      ck_in.ap()[l, :, hp * 128:(hp + 1) * 128].rearrange("(t p) n -> p t n", p=128), writes=[d_cstk])
                pb, pd = k.ps()
                for t in range(2):
                    k.tr(pb[:, t * 128:(t + 1) * 128], cst_k[:, t * 128:(t + 1) * 128], ident[:], [d_cstk, d_ident], [pd])
                k.cp(kcT, pb[:, 0:256], [pd], [d_kcT])
                for t in range(2):
                    P.dma("act", vctx[:, t, :, 0:64],
                          cv_in.ap()[l, t * 128:(t + 1) * 128, hp * 128:(hp + 1) * 128].rearrange("p (h e) -> p h e", h=2),
                          writes=[d_vc], r32=True)
                    k.cp(vctx[:, t, :, 64:66], ones[:, 0:2].unsqueeze(1).to_broadcast([128, 2, 2]), [d_ones], [d_vc], eng="pool")
                E2f, d_E2 = cv.get(2 * 15 * 64)
                E2 = E2f.rearrange("p (h r c) -> p h r c", h=2, r=15)
                for a in range(2):
                    for h2 in range(2):
                        head = hp * 2 + h2
                        src = AP(zscr, ((l * 8 + head) * 15) * 8128 + 63, [[126, 64], [8128, 15], [1, 64]])
                        P.dma("sp" if a == 0 else "act", E2[a * 64:(a + 1) * 64, h2, :, :], src, reads=[d_zscr[l]], writes=[d_E2])
                k.act(E2f, E2f, AF.Exp, [d_E2], [d_E2])
                k.tt(E2f.rearrange("p (g c) -> p g c", c=64), E2f.rearrange("p (g c) -> p g c", c=64),
                     CM.unsqueeze(1).to_broadcast([128, 30, 64]), ALU.mult, [d_E2, d_cst], [d_E2])
                TABf, d_TAB = cv.get(2 * 21 * 128)
                TAB = TABf.rearrange("p (h t q) -> p h t q", h=2, t=21)
                k.ms(TABf, 0.0, [d_TAB])
                plans = {}
                tid = 0
                plans["int"] = []
                for j in range(5):
                    for a in range(2):
                        for b in range(2):
                            dr = 2 * j - 4 + a - b
                            if -4 <= dr <= 3:
                                plans["int"].append((tid, a, b, dr))
                    tid += 1
                tid0 = {"int": 0}
                for m_ in (0, 1, 14, 15):
                    tid0[m_] = tid
                    kt0 = 0 if m_ < 2 else 12
                    plans[m_] = []
                    for j in range(4):
                        for a in range(2):
                            for b in range(2):
                                kr = 2 * (kt0 + j) + a
                                r = 2 * m_ + b
                                rs_ = min(max(r - 4, 0), 24)
                                if rs_ <= kr <= rs_ + 7:
                                    plans[m_].append((tid, a, b, kr - r))
                        tid += 1
                assert tid == 21
                for key_, pl in plans.items():
                    for (tid_, a, b, dr) in pl:
                        k.cp(TAB[a * 64:(a + 1) * 64, :, tid_, b * 64:(b + 1) * 64], E2[a * 64:(a + 1) * 64, :, dr + 7, :],
                             [d_E2], [d_TAB], eng="pool")
                PTf, d_PT = cv.get(7 * 128, F32R)
                tmpE, d_tmpE = cv.get(512)
                tmpE2, d_tmpE2 = cv.get(128)
                for m_ in range(16):
                    qs = slice(m_ * 128, (m_ + 1) * 128)
                    if 2 <= m_ <= 13:
                        kts = [m_ - 2 + j for j in range(5)]
                        t0 = 0
                    else:
                        kts = [(0 if m_ < 2 else 12) + j for j in range(4)]
                        t0 = tid0[m_]
                    nl = len(kts)
                    po, pod = k.ps()
                    for h2 in range(2):
                        hs = slice(h2 * 64, (h2 + 1) * 64)
                        pA, pAd = k.ps()
                        for j in range(4):
                            k.mm(pA[:, j * 128:(j + 1) * 128], kT[hs, kts[j] * 128:(kts[j] + 1) * 128], qT[hs, qs], True, True,
                                 [d_kT, d_qT], [pAd])
                        pB, pBd = k.ps()
                        c_ = 0
                        if nl == 5:
                            k.mm(pB[:, 0:128], kT[hs, kts[4] * 128:(kts[4] + 1) * 128], qT[hs, qs], True, True, [d_kT, d_qT], [pBd])
                            c_ = 128
                        for t in range(2):
                            k.mm(pB[:, c_ + t * 128:c_ + (t + 1) * 128], kcT[hs, t * 128:(t + 1) * 128], qT[hs, qs], True, True,
                                 [d_kcT, d_qT], [pBd])
                        k.act(tmpE, pA[:, 0:512], AF.Exp, [pAd], [d_tmpE])
                        k.tt(PTf[:, 0:512], tmpE, TABf[:, (h2 * 21 + t0) * 128:(h2 * 21 + t0 + 4) * 128], ALU.mult,
                             [d_tmpE, d_TAB], [d_PT])
                        if nl == 5:
                            k.act(tmpE2, pB[:, 0:128], AF.Exp, [pBd], [d_tmpE2])
                            k.tt(PTf[:, 512:640], tmpE2, TABf[:, (h2 * 21 + 4) * 128:(h2 * 21 + 5) * 128], ALU.mult,
                                 [d_tmpE2, d_TAB], [d_PT])
                        k.act(PTf[:, nl * 128:(nl + 2) * 128], pB[:, c_:c_ + 256], AF.Exp, [pBd], [d_PT])
                        c0 = h2 * 66
                        ntile = nl + 2
                        for j in range(ntile):
                            if j < nl:
                                rhs_ = vtok[:, kts[j], h2, :]
                                rd = d_vt
                            else:
                                rhs_ = vctx[:, j - nl, h2, :]
                                rd = d_vc
                            k.mm(po[:, c0:c0 + 66], PTf[:, j * 128:(j + 1) * 128], rhs_, j == 0, j == ntile - 1, [d_PT, rd], [pod])
                    for h2 in range(2):
                        c0 = h2 * 66
                        k.rcp(rden[:, h2:h2 + 1], po[:, c0 + 64:c0 + 65], [pod], [d_rden])
                        k.ts(otok[:, h2 * 64:(h2 + 1) * 64], po[:, c0:c0 + 64], rden[:, h2:h2 + 1], None, ALU.mult, None,
                             [pod, d_rden], [d_otok])
                    pb, pd = k.ps()
                    k.tr(pb[:, 0:128], otok, ident[:], [d_otok, d_ident], [pd])
                    k.tt(yT, pb[:, 0:128], zs[:, qs], ALU.mult, [pd, d_zs], [d_yT])
                    P.dma(k.q(), yscr.ap()[4 + hp, :, qs], yT, reads=[d_yT], writes=[d_yscr[4 + hp]], r32=True)

        for l in range(depth):
            last = (l == depth - 1)
            fz = fence()
            cv = Carve(fz)
            bcol, d_bcol = cv.get(16)
            gpre_col, _d = cv.get(8)
            brow, _d = cv.get(D)
            gpost_row, _d = cv.get(D)
            wada_f, d_wada = cv.get(8 * 512, F32R)
            wada = wada_f.rearrange("p (c n) -> p c n", c=8)
            sbc_f, d_sbc = cv.get(8 * 2 * 128, F32R)
            siluc_bc = sbc_f.rearrange("p (c j n) -> p c j n", c=8, j=2)
            for kc in range(8):
                for j in range(2):
                    k.cp(siluc_bc[:, kc, j, :], silucT[:, kc, j:j + 1].bitcast(F32).to_broadcast([128, 128]), [d_siluc], [d_sbc])
            P.dma("sp", bcol, b_ada.ap()[l, 0:2048].rearrange("(c p) -> p c", p=128), writes=[d_bcol], slow=True)
            P.dma("act", gpre_col, g_pre.ap()[l].rearrange("(c p) -> p c", p=128), writes=[d_bcol], slow=True)
            P.dma("sp", brow, b_ada.ap()[l, 2048:3072].partition_broadcast(128), writes=[d_bcol])
            P.dma("act", gpost_row, g_post.ap()[l].partition_broadcast(128), writes=[d_bcol])
            for blk in range(6):
                P.dma(k.q(), wada, w_ada.ap()[l, :, blk * 512:(blk + 1) * 512].rearrange("(c p) n -> p c n", p=128),
                      writes=[d_wada], r32=True)
                if blk < 4:
                    pb, pd = k.ps()
                    for cc in range(4):
                        for kc in range(8):
                            k.mm(pb[:, cc * 2:cc * 2 + 2], wada[:, kc, cc * 128:(cc + 1) * 128], silucT[:, kc, :],
                                 kc == 0, kc == 7, [d_wada, d_siluc], [pd])
                    k.tt(modcol[:, blk * 4:blk * 4 + 4, :], pb[:, 0:8].rearrange("p (c j) -> p c j", j=2),
                         bcol[:, blk * 4:blk * 4 + 4].unsqueeze(2).to_broadcast([128, 4, 2]), ALU.add,
                         [pd, d_bcol], [d_modcol])
                else:
                    for j in range(2):
                        pb, pd = k.ps()
                        for kc in range(8):
                            k.mm(pb[:], siluc_bc[:, kc, j, :], wada[:, kc, :], kc == 0, kc == 7, [d_wada, d_sbc], [pd])
                        c0 = (blk - 4) * 512
                        k.tt(gg[:, j, c0:c0 + 512], pb[:], brow[:, c0:c0 + 512], ALU.add, [pd, d_bcol], [d_gg])
                        k.tt(gg[:, j, c0:c0 + 512], gg[:, j, c0:c0 + 512], gpost_row[:, c0:c0 + 512], ALU.mult,
                             [d_gg, d_bcol], [d_gg])
            for j in range(2):
                k.stt(s1col[:, :, j], modcol[:, 8:16, j], 1.0, gpre_col, ALU.add, ALU.mult, [d_modcol, d_bcol], [d_modcol])

            for (kind, si, T) in seqs:
                cj = 0 if kind == "p" else 1
                TT = min(T, 512)
                NTL = T // TT
                if l == 0:
                    xin = x_p.ap()[si] if kind == "p" else x_s.ap()
                    d_xin = Dep()
                else:
                    xin = xs_p[(l - 1) % 2].ap()[si] if kind == "p" else xs_s[(l - 1) % 2].ap()
                    d_xin = d_xs_p[(l - 1) % 2][si] if kind == "p" else d_xs_s[(l - 1) % 2]
                if last:
                    xout = y_p.ap()[si] if kind == "p" else y_s.ap()
                    d_xout = Dep()
                else:
                    xout = xs_p[l % 2].ap()[si] if kind == "p" else xs_s[l % 2].ap()
                    d_xout = d_xs_p[l % 2][si] if kind == "p" else d_xs_s[l % 2]

                fz = fence()
                cv = Carve(fz)
                _xt, _dxt = cv.get(D)
                xt = [_xt, _xt]
                d_xt = [_dxt, _dxt]
                xn, d_xn = cv.get(D)
                for i in range(T // 128):
                    b = i % 2
                    P.dma(k.q(), xt[b], xin[i * 128:(i + 1) * 128, :], reads=[d_xin], writes=[d_xt[b]])
                    k.act(xn, xt[b], AF.Square, [d_xt[b]], [d_xn, d_small], accum=small[:, 0:1])
                    k.act(small[:, 1:2], small[:, 0:1], AF.Sqrt, [d_small], [d_small], bias=EPS, scale=1.0 / D)
                    k.rcp(small[:, 2:3], small[:, 1:2], [d_small], [d_small])
                    k.ts(xn, xt[b], small[:, 2:3], None, ALU.mult, None, [d_xt[b], d_small], [d_xn])
                    for half in range(2):
                        pb, pd = k.ps()
                        for c4 in range(4):
                            kc = half * 4 + c4
                            k.tr(pb[:, c4 * 128:(c4 + 1) * 128], xn[:, kc * 128:(kc + 1) * 128], ident[:], [d_xn, d_ident], [pd])
                        for c4 in range(4):
                            kc = half * 4 + c4
                            k.ts(hT[:, kc, i * 128:(i + 1) * 128], pb[:, c4 * 128:(c4 + 1) * 128],
                                 s1col[:, kc, cj:cj + 1], modcol[:, kc, cj:cj + 1], ALU.mult, ALU.add,
                                 [pd, d_modcol], [d_hT], eng=("dve" if c4 % 2 == 0 else "pool") if False else "dve")

                fz = fence()
                cv = Carve(fz)
                zbuf, d_z = cv.get(TT, F32R)
                zsrc, d_zs0 = cv.get(TT)
                k.ms(zsrc, 0.0, [d_zs0])
                k.cp(zbuf, zsrc, [d_zs0], [d_z])
                for ch in range(8):
                    if (ch < 4 and not do_dn) or (ch >= 4 and not do_na):
                        for t in range(NTL):
                            P.dma(k.q(), yscr.ap()[ch, :, t * TT:(t + 1) * TT], zbuf, reads=[d_z], writes=[d_yscr[ch]], r32=True)

                if do_dn:
                    for h in range(4):
                        dn_unit(l, kind, si, T, h)

                if do_na:
                    for hp in range(4):
                        na_unit(l, kind, si, T, hp)

                for g in range(4):
                    win = POOL_WINDOWS[g]
                    fz = fence()
                    cv = Carve(fz)
                    wu, d_wu = cv.get(8 * 128, F32R)
                    wz, d_wz = cv.get(8 * 128, F32R)
                    pw, d_pw = cv.get(128, F32R)
                    psc, d_psc = cv.get(1)
                    U, d_U = cv.get(T + 16)
                    zs, d_zs = cv.get(T)
                    s_a, d_sa = cv.get(T + 16)
                    s_b, d_sb = cv.get(T + 16)
                    icnt, d_icnt = cv.get(T)
                    pooled, d_pooled = cv.get(T, F32R)
                    yT, d_yT = cv.get(TT, F32R)
                    wu3 = wu.rearrange("p (c n) -> p c n", c=8)
                    wz3 = wz.rearrange("p (c n) -> p c n", c=8)
                    cu = OFF_PL_U + g * 128
                    cz = OFF_PL_Z + g * 128
                    P.dma("sp", wu3, w_in.ap()[l, :, cu:cu + 128].rearrange("(c p) n -> p c n", p=128), writes=[d_wu], r32=True)
                    P.dma("act", wz3, w_in.ap()[l, :, cz:cz + 128].rearrange("(c p) n -> p c n", p=128), writes=[d_wz], r32=True)
                    P.dma("sp", pw, pool_w.ap()[l, g], writes=[d_pw], r32=True)
                    P.dma("act", psc, pool_scale.ap()[l, g * 128:(g + 1) * 128].rearrange("(p o) -> p o", o=1), writes=[d_psc], slow=True)
                    ic_src = (invcnt_p if kind == "p" else invcnt_s).ap()[g]
                    P.dma("sp", icnt, ic_src.partition_broadcast(128), writes=[d_icnt])
                    k.ms(U[:, 0:8], 0.0, [d_U])
                    k.ms(U[:, T + 8:T + 16], 0.0, [d_U])
                    for t in range(NTL):
                        pb, pd = k.ps()
                        for kc in range(8):
                            k.mm(pb[:, 0:TT], wu3[:, kc, :], hT[:, kc, t * TT:(t + 1) * TT], kc == 0, kc == 7, [d_wu, d_hT], [pd])
                        k.cp(U[:, 8 + t * TT:8 + (t + 1) * TT], pb[:, 0:TT], [pd], [d_U])
                        pb, pd = k.ps()
                        for kc in range(8):
                            k.mm(pb[:, 0:TT], wz3[:, kc, :], hT[:, kc, t * TT:(t + 1) * TT], kc == 0, kc == 7, [d_wz, d_hT], [pd])
                        k.act(zs[:, t * TT:(t + 1) * TT], pb[:, 0:TT], AF.Silu, [pd], [d_zs])
                    cur, dcur, curlen = U, d_U, T + 16
                    step = 1
                    bufs = [(s_a, d_sa), (s_b, d_sb)]
                    bi = 0
                    while step < win:
                        nb, dnb = bufs[bi]
                        bi ^= 1
                        nlen = curlen - step
                        k.tt(nb[:, 0:nlen], cur[:, 0:nlen], cur[:, step:step + nlen], ALU.add, [dcur], [dnb])
                        cur, dcur, curlen = nb, dnb, nlen
                        step *= 2
                    o0 = 8 - win // 2
                    nb, dnb = bufs[bi]
                    k.tt(nb[:, 0:T], cur[:, o0:o0 + T], icnt, ALU.mult, [dcur, d_icnt], [dnb])
                    k.tt(pooled, nb[:, 0:T], U[:, 8:8 + T], ALU.subtract, [dnb, d_U], [d_pooled])
                    for t in range(NTL):
                        pb, pd = k.ps()
                        k.mm(pb[:, 0:TT], pw, pooled[:, t * TT:(t + 1) * TT], True, True, [d_pw, d_pooled], [pd])
                        k.stt(yT, pb[:, 0:TT], psc[:, 0:1], zs[:, t * TT:(t + 1) * TT], ALU.mult, ALU.mult, [pd, d_psc, d_zs], [d_yT])
                        P.dma(k.q(), yscr.ap()[8 + g, :, t * TT:(t + 1) * TT], yT, reads=[d_yT], writes=[d_yscr[8 + g]], r32=True)

                TF = 256
                NTF = T // TF
                fz = fence()
                cv = Carve(fz)
                ysb_l = []
                for _i in range(2):
                    _a, _d = cv.get(4 * TF, F32R)
                    ysb_l.append((_a.rearrange("p (c n) -> p c n", c=4), _d))
                wg_l = []
                wbr_l = []
                for _i in range(2):
                    _a, _d = cv.get(8 * 128, F32R)
                    wg_l.append((_a.rearrange("p (c n) -> p c n", c=8), _d))
                    _a, _d = cv.get(4 * 128, F32R)
                    wbr_l.append((_a.rearrange("p (c n) -> p c n", c=4), _d))
                mrgT, d_mrg = cv.get(8 * TF, F32R)
                mrg3 = mrgT.rearrange("p (c n) -> p c n", c=8)
                accf, d_acc = cv.get(8 * TF)
                acc3 = accf.rearrange("p (c n) -> p c n", c=8)
                sg, d_sg = cv.get(TF)
                tmp, d_tmp = cv.get(TF)
                wo, d_wo = cv.get(8 * D, F32R)
                wo3 = wo.rearrange("p (c n) -> p c n", c=8)
                xr, d_xr = cv.get(D)
                xo, d_xo = cv.get(D)
                xn, d_xn = cv.get(512)
                P.dma("sp", wo3, w_out.ap()[l].rearrange("(c p) n -> p c n", p=128), writes=[d_wo], r32=True)
                wi = 0
                for t in range(NTF):
                    for br in range(3):
                        ysb3, d_ysb = ysb_l[br % 2]
                        P.dma(k.q(), ysb3, yscr.ap()[br * 4:(br + 1) * 4, :, t * TF:(t + 1) * TF].rearrange("c p n -> p c n"),
                              reads=list(d_yscr[br * 4:(br + 1) * 4]), writes=[d_ysb], r32=True)
                        for dc in range(8):
                            wg3, d_wg = wg_l[wi % 2]
                            wbr3, d_wbr = wbr_l[wi % 2]
                            wi += 1
                            c0 = OFF_GATE + br * D + dc * 128
                            P.dma("sp", wg3, w_in.ap()[l, :, c0:c0 + 128].rearrange("(c p) n -> p c n", p=128),
                                  writes=[d_wg], r32=True)
                            P.dma("sp", wbr3, w_br[br].ap()[l, :, dc * 128:(dc + 1) * 128].rearrange("(c p) n -> p c n", p=128),
                                  writes=[d_wbr], r32=True)
                            pg, pgd = k.ps()
                            for kc in range(8):
                                k.mm(pg[:, 0:TF], wg3[:, kc, :], hT[:, kc, t * TF:(t + 1) * TF], kc == 0, kc == 7, [d_wg, d_hT], [pgd])
                            k.act(sg, pg[:, 0:TF], AF.Sigmoid, [pgd], [d_sg])
                            pa, pad = k.ps()
                            for wc in range(4):
                                k.mm(pa[:, 0:TF], wbr3[:, wc, :], ysb3[:, wc, :], wc == 0, wc == 3, [d_wbr, d_ysb], [pad])
                            if br == 0:
                                k.tt(acc3[:, dc, :], sg, pa[:, 0:TF], ALU.mult, [d_sg, pad], [d_acc])
                            elif br == 1:
                                k.tt(tmp, sg, pa[:, 0:TF], ALU.mult, [d_sg, pad], [d_tmp])
                                k.tt(acc3[:, dc, :], acc3[:, dc, :], tmp, ALU.add, [d_acc, d_tmp], [d_acc], eng="pool")
                            else:
                                k.tt(tmp, sg, pa[:, 0:TF], ALU.mult, [d_sg, pad], [d_tmp])
                                k.tt(mrg3[:, dc, :], acc3[:, dc, :], tmp, ALU.add, [d_acc, d_tmp], [d_mrg], eng="pool")
                    for sub in range(TF // 128):
                        r0 = t * TF + sub * 128
                        P.dma("act", xr, xin[r0:r0 + 128, :], reads=[d_xin], writes=[d_xr])
                        pos = []
                        for half in range(2):
                            po, pod = k.ps()
                            for kc in range(8):
                                k.mm(po[:], mrg3[:, kc, sub * 128:(sub + 1) * 128], wo3[:, kc, half * 512:(half + 1) * 512],
                                     kc == 0, kc == 7, [d_mrg, d_wo], [pod])
                            k.act(xn[:, 0:512], po[:], AF.Square, [pod], [d_xn, d_small], accum=small[:, 4 + half:5 + half])
                            pos.append((po, pod))
                        k.tt(small[:, 6:7], small[:, 4:5], small[:, 5:6], ALU.add, [d_small], [d_small])
                        k.act(small[:, 7:8], small[:, 6:7], AF.Sqrt, [d_small], [d_small], bias=EPS, scale=1.0 / D)
                        k.rcp(small[:, 8:9], small[:, 7:8], [d_small], [d_small])
                        for half in range(2):
                            po, pod = pos[half]
                            hs = slice(half * 512, (half + 1) * 512)
                            k.stt(xo[:, hs], po[:], small[:, 8:9], gg[:, cj, hs], ALU.mult, ALU.mult, [pod, d_small, d_gg], [d_xo])
                            k.tt(xo[:, hs], xo[:, hs], xr[:, hs], ALU.add, [d_xo, d_xr], [d_xo], eng="pool")
                        P.dma("act", xout[r0:r0 + 128, :], xo, reads=[d_xo], writes=[d_xout])
        P.emit()
    return nc, k


_CACHE = {}


def _consts():
    def invcnt(T):
        out = np.zeros((4, T), np.float32)
        pos = np.arange(T)
        for gi, win in enumerate(POOL_WINDOWS):
            lo = np.maximum(pos - win // 2, 0)
            hi = np.minimum(pos + win // 2 - 1, T - 1)
            out[gi] = 1.0 / (hi - lo + 1).astype(np.float32)
        return out

    t = np.arange(128)
    same = (t[:, None] // 64) == (t[None, :] // 64)
    cst = np.zeros((128, NCST), np.float32)
    cst[:, 0:128] = same & (t[:, None] <= t[None, :])
    cst[:, 128:256] = same & (t[:, None] >= t[None, :])
    cst[:, 256:384] = same
    cst[:, 384:512] = 1.0
    f = np.arange(64)
    pm = t % 64
    cst[:, 512:576] = (pm[:, None] == f[None, :])
    cst[:, 576] = (pm == 63)
    cst[:, 577] = (pm == 0)
    cst[:, 578] = (t == 63)
    cst[:, 579] = (t == 127)
    cst[:, 580] = (t == 0)
    cst[:, 581] = (t == 64)
    P_ = pm[:, None]
    F_ = f[None, :]
    valid = [[F_ >= P_, F_ <= P_], [F_ > P_, F_ < P_], [F_ < P_, F_ > P_]]
    for ty in range(3):
        for d_ in range(2):
            sign = 1.0 if ty == 2 else -1.0
            cst[:, 582 + (ty * 2 + d_) * 64: 582 + (ty * 2 + d_ + 1) * 64] = np.where(valid[ty][d_], 0.0, sign * BIG)
    cq = np.arange(64)
    csq = np.clip(cq - 8, 0, 48)
    cm = (f[:, None] >= csq[None, :]) & (f[:, None] < csq[None, :] + 16)
    cst[:, 582 + 384:582 + 384 + 64] = np.concatenate([cm, cm], axis=0)
    return {"invcnt_p": invcnt(SEQ), "invcnt_s": invcnt(DSEQ), "ident": np.eye(128, dtype=np.float32), "dncst": cst}


def kernel(x_prompt, x_sample, c, cache_k_na, cache_v_na, state_dn, c_ctx, w_ada, b_ada, g_pre, g_post,
           w_in, conv_dn, a_log_dn, dt_bias_dn, g_norm_dn, na_bias, pool_w, pool_scale,
           w_br_dn, w_br_na, w_br_pl, w_out, _depth=DEPTH, _dn=True, _na=True):
    f = lambda a: np.ascontiguousarray(np.asarray(a, dtype=np.float32))
    key = (_depth, _dn, _na)
    if key not in _CACHE:
        _CACHE[key] = build_program(_depth, _dn, _na)
    nc, k = _CACHE[key]
    cs = _consts()
    dd = _depth
    shared = {"w_ada": f(w_ada[:dd]), "b_ada": f(b_ada[:dd]), "g_pre": f(g_pre[:dd]), "g_post": f(g_post[:dd]), "w_in": f(w_in[:dd]),
              "pool_w": f(pool_w[:dd]), "pool_scale": f(pool_scale[:dd]), "w_br_dn": f(w_br_dn[:dd]), "w_br_na": f(w_br_na[:dd]),
              "w_br_pl": f(w_br_pl[:dd]), "w_out": f(w_out[:dd]), "conv_dn": f(conv_dn[:dd]),
              "a_log": f(a_log_dn[:dd]).reshape(dd, 8), "dt_bias": f(dt_bias_dn[:dd]).reshape(dd, 8), "g_norm": f(g_norm_dn[:dd])}
    shared.update(cs)
    rpad = np.zeros((dd, 8, 15, 127), np.float32)
    rpad[..., 48:79] = f(na_bias[:dd])[..., ::-1]
    x_prompt = f(x_prompt)
    x_sample = f(x_sample)
    in_maps = []
    for core in range(8):
        b = core // 4
        m = dict(shared)
        m["x_p"] = x_prompt[core * NPS:(core + 1) * NPS]
        m["x_s"] = x_sample[b]
        m["cvec"] = np.stack([f(c_ctx), f(c)[b]])
        m["sdn"] = f(state_dn[b, :dd])
        m["ck"] = f(cache_k_na[b, :dd]).reshape(dd, 256, 512)
        m["cvv"] = f(cache_v_na[b, :dd]).reshape(dd, 256, 512)
        m["rpad"] = rpad
        in_maps.append({n: m[n] for n in k.din})
    res = run_bass_kernel_spmd(nc, in_maps, core_ids=list(range(8)))
    r = res.results
    y_p = np.concatenate([r[i]["y_p"] for i in range(8)], axis=0)
    y_s = np.stack([r[0]["y_s"], r[4]["y_s"]])
    n_k = np.concatenate([r[i]["nk"] for i in range(8)], axis=0).reshape(32, dd, SEQ, 8, 64)
    n_v = np.concatenate([r[i]["nv"] for i in range(8)], axis=0).reshape(32, dd, SEQ, 8, 64)
    n_s = np.concatenate([r[i]["nst"] for i in range(8)], axis=0)
    return y_p, y_s, n_k, n_v, n_s
```

```python
import contextlib
import numpy as np
import concourse.bass as bass
import concourse.mybir as mybir
from concourse.ap import AP
from concourse.bass_utils import run_bass_kernel_spmd

F32 = mybir.dt.float32
F32R = mybir.dt.float32r
ALU = mybir.AluOpType
AF = mybir.ActivationFunctionType

ENGS = ("pe", "act", "dve", "pool", "sp")
NDSEM = 12

D = 1024
DEPTH = 4
SEQ = 256
DSEQ = 2048
NIN = 8208
OFF_DN_Z = 1536
OFF_DN_BETA = 2048
OFF_DN_A = 2056
OFF_NA_Q = 2064
OFF_NA_K = 2576
OFF_NA_V = 3088
OFF_NA_Z = 3600
OFF_PL_U = 4112
OFF_PL_Z = 4624
OFF_GATE = 5136
EPS = 1e-6
POOL_WINDOWS = (2, 4, 8, 16)
NPS = 4
NCST = 582 + 6 * 64 + 64
BIG = 30000.0


class Dep:
    __slots__ = ("w", "r", "excl")

    def __init__(self, w=None, excl=False):
        self.w = w
        self.r = []
        self.excl = excl


class Op:
    __slots__ = ("eng", "fn", "waits", "signal", "count", "is_dma", "dsem", "dval", "dprev")

    def __init__(self, eng, fn, is_dma=False):
        self.eng = eng
        self.fn = fn
        self.waits = []
        self.signal = False
        self.count = None
        self.is_dma = is_dma
        self.dsem = None
        self.dval = None
        self.dprev = None


class Prog:
    def __init__(self, nc):
        self.nc = nc
        self.ops = {e: [] for e in ENGS}
        self.ndma = {e: 0 for e in ENGS}
        self.dtot = {e: [0] * NDSEM for e in ENGS}
        self.nops = 0

    def _mk(self, eng, fn, reads, writes, is_dma):
        o = Op(eng, fn, is_dma)
        ex = [t for t in reads if t.excl]
        if ex:
            reads = [t for t in reads if not t.excl]
            writes = list(writes) + [t for t in ex if t not in writes]
        deps = []
        seen = set()

        def add(d):
            if d is None or id(d) in seen:
                return
            if (not d.is_dma) and d.eng == "pe" and eng == "pe" and not is_dma:
                return
            seen.add(id(d))
            deps.append(d)

        for t in reads:
            add(t.w)
        for t in writes:
            add(t.w)
            for r in t.r:
                add(r)
        o.waits = deps
        for d in deps:
            d.signal = True
        for t in reads:
            if not is_dma:
                t.r = [x for x in t.r if x.is_dma or x.eng != eng]
            t.r.append(o)
        for t in writes:
            t.w = o
            t.r = []
        self.ops[eng].append(o)
        self.nops += 1
        return o

    def op(self, eng, fn, reads=(), writes=()):
        return self._mk(eng, fn, reads, writes, False)

    def dma(self, eng, out, in_, reads=(), writes=(), r32=False, slow=False):
        nc = self.nc

        def fn(e):
            kw = {}
            if slow:
                kw["allow_slow_non_contiguous"] = True
            if r32:
                nc.dge_precook = False
            ins = e.dma_start(out=out, in_=in_, **kw)
            if r32:
                nc.dge_precook = True
            return ins

        o = self._mk(eng, fn, reads, writes, True)
        i = self.ndma[eng] % NDSEM
        self.ndma[eng] += 1
        o.dsem = i
        o.dprev = self.dtot[eng][i]
        self.dtot[eng][i] += 16
        o.dval = self.dtot[eng][i]
        o.signal = True
        return o

    def emit(self):
        nc = self.nc
        for e in ENGS:
            c = 0
            for o in self.ops[e]:
                if not o.is_dma and o.signal:
                    c += 1
                    o.count = c
        nsig = {e: sum(1 for o in self.ops[e] if (not o.is_dma and o.signal)) for e in ENGS}
        with contextlib.ExitStack() as st:
            esem = {e: st.enter_context(nc.semaphore("s_" + e)) for e in ENGS}
            dsem = {
                e: [st.enter_context(nc.semaphore("d_%s_%d" % (e, i))) for i in range(NDSEM)]
                for e in ("sp", "act", "pool")
            }
            block = st.enter_context(nc.Block())
            ops = self.ops
            dtot = self.dtot

            def run(e, engobj, final=False):
                seen_e = {x: 0 for x in ENGS}
                seen_d = {}
                for o in ops[e]:
                    for d in o.waits:
                        if d.is_dma:
                            key = (d.eng, d.dsem)
                            if seen_d.get(key, 0) >= d.dval:
                                continue
                            engobj.wait_ge(dsem[d.eng][d.dsem], d.dval)
                            seen_d[key] = d.dval
                        else:
                            if seen_e[d.eng] >= d.count:
                                continue
                            engobj.wait_ge(esem[d.eng], d.count)
                            seen_e[d.eng] = d.count
                    if o.is_dma:
                        key = (e, o.dsem)
                        if o.dprev > 0 and seen_d.get(key, 0) < o.dprev:
                            engobj.wait_ge(dsem[e][o.dsem], o.dprev)
                            seen_d[key] = o.dprev
                        ins = o.fn(engobj)
                        ins.then_inc(dsem[e][o.dsem], 16)
                    else:
                        ins = o.fn(engobj)
                        if o.signal:
                            ins.then_inc(esem[e], 1)
                if final:
                    for x in ENGS:
                        if x != e and nsig[x] > 0:
                            engobj.wait_ge(esem[x], nsig[x])
                    for q in ("sp", "act", "pool"):
                        for i in range(NDSEM):
                            if dtot[q][i] > 0:
                                engobj.wait_ge(dsem[q][i], dtot[q][i])

            @block.tensor
            def _(eng):
                run("pe", eng)

            @block.vector
            def _(eng):
                run("dve", eng)

            @block.scalar
            def _(eng):
                run("act", eng)

            @block.gpsimd
            def _(eng):
                run("pool", eng)

            @block.sync
            def _(eng):
                run("sp", eng, final=True)


class K:
    def __init__(self, nc, st):
        self.nc = nc
        self.st = st
        self.P = Prog(nc)
        self.din = {}
        self.dout = {}
        self.psb = [st.enter_context(nc.psum_tensor("psb%d" % i, [128, 512], F32)) for i in range(8)]
        self.psd = [Dep(excl=True) for _ in range(8)]
        self.psi = 0
        self.dq = 0

    def inp(self, name, shape, dt=F32):
        t = self.nc.dram_tensor(name, list(shape), dt, kind="ExternalInput")
        self.din[name] = t
        return t

    def outp(self, name, shape):
        t = self.nc.dram_tensor(name, list(shape), F32, kind="ExternalOutput")
        self.dout[name] = t
        return t

    def scr(self, name, shape, dt=F32):
        return self.nc.dram_tensor(name, list(shape), dt, kind="Internal")

    def sb(self, name, shape, dt=F32):
        return self.st.enter_context(self.nc.sbuf_tensor(name, list(shape), dt))

    def ps(self):
        i = self.psi
        self.psi = (i + 1) % 8
        return self.psb[i], self.psd[i]

    def q(self):
        self.dq ^= 1
        return "sp" if self.dq else "act"

    def mm(self, out, lhsT, rhs, start, stop, reads, writes):
        self.P.op("pe", lambda e: e.matmul(out, lhsT=lhsT, rhs=rhs, start=start, stop=stop), reads, writes)

    def tr(self, out, in_, ident, reads, writes):
        self.P.op("pe", lambda e: e.transpose(out, in_, ident), reads, writes)

    def act(self, out, in_, func, reads, writes, bias=None, scale=1.0, accum=None):
        def fn(e):
            kw = {}
            if bias is not None:
                kw["bias"] = bias
            if accum is not None:
                kw["accum_out"] = accum
            return e.activation(out=out, in_=in_, func=func, scale=scale, **kw)

        self.P.op("act", fn, reads, writes)

    def tt(self, out, in0, in1, op, reads, writes, eng="dve"):
        self.P.op(eng, lambda e: e.tensor_tensor(out=out, in0=in0, in1=in1, op=op), reads, writes)

    def ts(self, out, in0, s1, s2, op0, op1, reads, writes, eng="dve"):
        if s2 is None:
            self.P.op(eng, lambda e: e.tensor_scalar(out=out, in0=in0, scalar1=s1, scalar2=None, op0=op0), reads, writes)
        else:
            self.P.op(eng, lambda e: e.tensor_scalar(out=out, in0=in0, scalar1=s1, scalar2=s2, op0=op0, op1=op1), reads, writes)

    def stt(self, out, in0, scalar, in1, op0, op1, reads, writes, eng="dve"):
        self.P.op(eng, lambda e: e.scalar_tensor_tensor(out=out, in0=in0, scalar=scalar, in1=in1, op0=op0, op1=op1),
                  reads, writes)

    def rcp(self, out, in_, reads, writes):
        self.P.op("dve", lambda e: e.reciprocal(out=out, in_=in_), reads, writes)

    def cp(self, out, in_, reads, writes, eng="dve"):
        self.P.op(eng, lambda e: e.tensor_copy(out=out, in_=in_), reads, writes)

    def ms(self, ap, val, writes, eng="pool"):
        self.P.op(eng, lambda e: e.memset(ap, val), (), writes)


def build_program(depth=DEPTH, do_dn=True, do_na=True):
    nc = bass.Bass("TRN2", target_bir_lowering=False)
    st = contextlib.ExitStack()
    with st:
        k = K(nc, st)
        P = k.P
        x_p = k.inp("x_p", [NPS, SEQ, D])
        x_s = k.inp("x_s", [DSEQ, D])
        cvec = k.inp("cvec", [2, D])
        w_ada = k.inp("w_ada", [depth, D, 3 * D], F32R)
        b_ada = k.inp("b_ada", [depth, 3 * D])
        g_pre = k.inp("g_pre", [depth, D])
        g_post = k.inp("g_post", [depth, D])
        w_in = k.inp("w_in", [depth, D, NIN], F32R)
        pool_w = k.inp("pool_w", [depth, 4, 128, 128], F32R)
        pool_scale = k.inp("pool_scale", [depth, 512])
        w_br = [k.inp(n, [depth, 512, D], F32R) for n in ("w_br_dn", "w_br_na", "w_br_pl")]
        w_out = k.inp("w_out", [depth, D, D], F32R)
        invcnt_p = k.inp("invcnt_p", [4, SEQ])
        invcnt_s = k.inp("invcnt_s", [4, DSEQ])
        ident_in = k.inp("ident", [128, 128])
        conv_dn = k.inp("conv_dn", [depth, 3, 1536])
        a_log = k.inp("a_log", [depth, 8])
        dt_bias = k.inp("dt_bias", [depth, 8])
        g_norm = k.inp("g_norm", [depth, 128])
        sdn = k.inp("sdn", [depth, 2, 4, 128, 128])
        dncst_in = k.inp("dncst", [128, NCST])
        ck_in = k.inp("ck", [depth, 256, 512])
        cv_in = k.inp("cvv", [depth, 256, 512], F32R)
        rpad_in = k.inp("rpad", [depth, 8, 15, 127])

        y_p = k.outp("y_p", [NPS, SEQ, D])
        y_s = k.outp("y_s", [DSEQ, D])
        nk = k.outp("nk", [NPS, depth, SEQ, 512])
        nv = k.outp("nv", [NPS, depth, SEQ, 512])
        d_nkv = Dep()
        nst = k.outp("nst", [NPS, depth, 2, 4, 128, 128])
        d_nst = Dep()

        xs_p = [k.scr("xs_p%d" % i, [NPS, SEQ, D]) for i in range(2)]
        xs_s = [k.scr("xs_s%d" % i, [DSEQ, D]) for i in range(2)]
        yscr = k.scr("yscr", [12, 128, DSEQ], F32R)
        d_xs_p = [[Dep() for _ in range(NPS)] for _ in range(2)]
        d_xs_s = [Dep() for _ in range(2)]
        d_yscr = [Dep() for _ in range(12)]

        ident = k.sb("ident_sb", [128, 128])
        d_ident = Dep()
        P.dma("sp", ident[:], ident_in.ap(), writes=[d_ident])
        cst = k.sb("dncst_sb", [128, NCST])
        d_cst = Dep()
        P.dma("act", cst[:], dncst_in.ap(), writes=[d_cst])
        TRI = [cst[:, 0:128], cst[:, 128:256]]
        BLK = cst[:, 256:384]
        ONES = cst[:, 384:512]
        I2 = cst[:, 512:576]
        SEL2 = cst[:, 576:578]
        SELLAST = cst[:, 578:582]
        MASK = [[cst[:, 582 + (ty * 2 + d_) * 64: 582 + (ty * 2 + d_ + 1) * 64] for d_ in range(2)] for ty in range(3)]
        CM = cst[:, 582 + 384:582 + 384 + 64]
        hT = k.sb("hT", [128, 8, DSEQ], F32R)
        d_hT = Dep()
        small = k.sb("small", [128, 16])
        d_small = Dep()
        silucT = k.sb("silucT", [128, 8, 2], F32R)
        d_siluc = Dep()
        modcol = k.sb("modcol", [128, 16, 2])
        s1col = k.sb("s1col", [128, 8, 2])
        d_modcol = Dep()
        ggscr = k.scr("ggscr", [2, 128, D])
        d_gg = Dep()
        RSZ = 15 * 1024
        FSZ = 19 * 1024 + 256
        arenaR = k.sb("arenaR", [128, RSZ], F32R)
        arenaF = k.sb("arenaF", [128, FSZ])
        arena_deps = []

        def fence():
            f = P.op("dve", lambda e: e.memset(small[:, 15:16], 0.0), reads=(), writes=list(arena_deps))
            arena_deps.clear()
            return f

        class Carve:
            def __init__(self, seed):
                self.offR = 0
                self.offF = 0
                self.seed = seed

            def get(self, cols, dt=F32):
                if dt == F32R:
                    a = arenaR[:, self.offR:self.offR + cols]
                    self.offR += cols
                    assert self.offR <= RSZ, self.offR
                else:
                    a = arenaF[:, self.offF:self.offF + cols]
                    self.offF += cols
                    assert self.offF <= FSZ, self.offF
                d = Dep(self.seed)
                arena_deps.append(d)
                return a, d

        craw = k.sb("craw", [128, 8, 2])
        for j in range(2):
            P.dma("sp", craw[:, :, j], cvec.ap()[j].rearrange("(c p) -> p c", p=128), writes=[d_siluc], slow=True)
        k.act(silucT[:], craw[:], AF.Silu, [d_siluc], [d_siluc])

        seqs = [("p", i, SEQ) for i in range(NPS)] + [("s", 0, DSEQ)]
        zscr = k.scr("zscr", [depth, 120, 64, 127])
        d_zscr = [Dep() for _ in range(depth)]
        if do_na:
            for l_ in range(depth):
                P.dma("sp", zscr.ap()[l_], AP(rpad_in, l_ * 120 * 127, [[127, 120], [0, 64], [1, 127]]), writes=[d_zscr[l_]])

        def dn_unit(l, kind, si, T, h):
            NT = T // 128
            TT = min(T, 512)
            NTL = T // TT
            fz = fence()
            cv = Carve(fz)
            ws = []
            for off in (0, 512, 1024, OFF_DN_Z):
                wa, dw = cv.get(8 * 128, F32R)
                w3 = wa.rearrange("p (c n) -> p c n", c=8)
                c0 = off + h * 128
                P.dma(k.q(), w3, w_in.ap()[l, :, c0:c0 + 128].rearrange("(c p) n -> p c n", p=128), writes=[dw], r32=True)
                ws.append((w3, dw))
            (wq, d_wq), (wk, d_wk), (wv, d_wv), (wz, d_wz) = ws
            wba_f, d_wba = cv.get(8 * 4, F32R)
            wba = wba_f.rearrange("p (c n) -> p c n", c=8)
            for j4 in range(4):
                cj4 = OFF_DN_BETA + 4 * j4 + h
                P.dma("sp", wba[:, :, j4], w_in.ap()[l, :, cj4].rearrange("(c p) -> p c", p=128), writes=[d_wba], r32=True, slow=True)
            raw, d_raw = cv.get(T + 2)
            qf, d_qf = cv.get(T)
            kf, d_kf = cv.get(T)
            vf, d_vf = cv.get(T)
            oacc, _ = cv.get(T)
            d_oacc = [Dep(fz) for _ in range(NT)]
            arena_deps.extend(d_oacc)
            tmpb, d_tmpb = cv.get(512)
            cw, d_cw = cv.get(9)
            for idx in range(3):
                c0 = idx * 512 + h * 128
                P.dma("act", cw[:, idx * 3:(idx + 1) * 3], conv_dn.ap()[l, :, c0:c0 + 128].rearrange("t p -> p t"),
                      writes=[d_cw], slow=True)
            k.ms(raw[:, 0:1], 0.0, [d_raw])
            k.ms(raw[:, T + 1:T + 2], 0.0, [d_raw])
            for i in range(NT):
                k.ms(oacc[:, i * 128:(i + 1) * 128], 0.0, [d_oacc[i]])
            dsts = [(qf, d_qf), (kf, d_kf), (vf, d_vf)]
            for idx in range(3):
                w3, dw = ws[idx]
                dst, dd = dsts[idx]
                for t in range(NTL):
                    pb, pd = k.ps()
                    for kc in range(8):
                        k.mm(pb[:, 0:TT], w3[:, kc, :], hT[:, kc, t * TT:(t + 1) * TT], kc == 0, kc == 7, [dw, d_hT], [pd])
                    k.cp(raw[:, 1 + t * TT:1 + (t + 1) * TT], pb[:, 0:TT], [pd], [d_raw])
                for t in range(NTL):
                    a = t * TT
                    k.ts(dst[:, a:a + TT], raw[:, a:a + TT], cw[:, idx * 3:idx * 3 + 1], None, ALU.mult, None, [d_raw, d_cw], [dd])
                    k.stt(tmpb[:, 0:TT], raw[:, a + 1:a + 1 + TT], cw[:, idx * 3 + 1:idx * 3 + 2], dst[:, a:a + TT],
                          ALU.mult, ALU.add, [d_raw, d_cw, dd], [d_tmpb])
                    k.stt(dst[:, a:a + TT], raw[:, a + 2:a + 2 + TT], cw[:, idx * 3 + 2:idx * 3 + 3], tmpb[:, 0:TT],
                          ALU.mult, ALU.add, [d_raw, d_cw, d_tmpb], [dd])
                    k.act(dst[:, a:a + TT], dst[:, a:a + TT], AF.Silu, [dd], [dd])
            for idx in range(2):
                dst, dd = dsts[idx]
                for t in range(NTL):
                    a = t * TT
                    k.act(tmpb[:, 0:TT], dst[:, a:a + TT], AF.Square, [dd], [d_tmpb])
                    pb, pd = k.ps()
                    k.mm(pb[:, 0:TT], ONES, tmpb[:, 0:TT], True, True, [d_cst, d_tmpb], [pd])
                    k.act(tmpb[:, 0:TT], pb[:, 0:TT], AF.Sqrt, [pd], [d_tmpb], bias=EPS)
                    k.rcp(tmpb[:, 0:TT], tmpb[:, 0:TT], [d_tmpb], [d_tmpb])
                    if idx == 0:
                        k.stt(dst[:, a:a + TT], dst[:, a:a + TT], 128.0 ** -0.5, tmpb[:, 0:TT], ALU.mult, ALU.mult, [dd, d_tmpb], [dd])
                    else:
                        k.tt(dst[:, a:a + TT], dst[:, a:a + TT], tmpb[:, 0:TT], ALU.mult, [dd, d_tmpb], [dd])
            zs = raw
            d_zs = d_raw
            for t in range(NTL):
                pb, pd = k.ps()
                for kc in range(8):
                    k.mm(pb[:, 0:TT], wz[:, kc, :], hT[:, kc, t * TT:(t + 1) * TT], kc == 0, kc == 7, [d_wz, d_hT], [pd])
                k.act(zs[:, t * TT:(t + 1) * TT], pb[:, 0:TT], AF.Silu, [pd], [d_zs])
            d_g = Dep(fz)
            arena_deps.append(d_g)
            G_ = [d_g]
            ba, _ = cv.get(NT * 4)
            ba3 = ba.rearrange("p (t n) -> p t n", n=4)
            pb, pd = k.ps()
            for i in range(NT):
                for kc in range(8):
                    k.mm(pb[:, i * 4:(i + 1) * 4], hT[:, kc, i * 128:(i + 1) * 128], wba[:, kc, :], kc == 0, kc == 7, [d_wba, d_hT], [pd])
            k.cp(ba, pb[:, 0:NT * 4], [pd], G_)
            NK = 10
            GA, _ = cv.get(NT * NK * 2)
            GA4 = GA.rearrange("p (t k d) -> p t k d", k=NK, d=2)
            K_BETA, K_NEGB, K_G, K_GB, K_EG, K_BG, K_EGL, K_GL = range(8)
            gk = lambda kk: GA4[:, :, kk, :]

            def g2():
                a_, _ = cv.get(NT * 2)
                return a_, a_.rearrange("p (t n) -> p t n", n=2)

            lnb, lnb3 = g2()
            la, la3 = g2()
            gt, gt3 = g2()
            rowc, _ = cv.get(4)
            gn_row, d_gn = cv.get(128)
            P.dma("sp", rowc[:, 0:1], dt_bias.ap()[l, h:h + 1].partition_broadcast(128), writes=G_)
            P.dma("sp", rowc[:, 1:2], dt_bias.ap()[l, 4 + h:5 + h].partition_broadcast(128), writes=G_)
            P.dma("act", rowc[:, 2:3], a_log.ap()[l, h:h + 1].partition_broadcast(128), writes=G_)
            P.dma("act", rowc[:, 3:4], a_log.ap()[l, 4 + h:5 + h].partition_broadcast(128), writes=G_)
            P.dma("sp", gn_row, g_norm.ap()[l].partition_broadcast(128), writes=[d_gn])
            k.act(rowc[:, 2:4], rowc[:, 2:4], AF.Exp, G_, G_)
            k.ts(rowc[:, 2:4], rowc[:, 2:4], -1.0, None, ALU.mult, None, G_, G_)
            k.act(gk(K_BETA), ba3[:, :, 0:2], AF.Sigmoid, G_, G_)
            k.ts(gk(K_NEGB), gk(K_BETA), -1.0, None, ALU.mult, None, G_, G_)
            k.act(gt3, ba3[:, :, 0:2], AF.Exp, G_, G_, scale=-1.0)
            k.act(gt, gt, AF.Ln, G_, G_, bias=1.0)
            k.ts(lnb, gt, -1.0, None, ALU.mult, None, G_, G_)
            k.tt(gt3, ba3[:, :, 2:4], rowc[:, 0:2].unsqueeze(1).to_broadcast([128, NT, 2]), ALU.add, G_, G_)
            k.act(gt, gt, AF.Exp, G_, G_)
            k.act(gt, gt, AF.Ln, G_, G_, bias=1.0)
            k.tt(la3, gt3, rowc[:, 2:4].unsqueeze(1).to_broadcast([128, NT, 2]), ALU.mult, G_, G_)
            pb, pd = k.ps()
            for i in range(NT):
                k.mm(pb[:, i * 2:(i + 1) * 2], TRI[0], la3[:, i, :], True, True, [d_cst, d_g], [pd])
                k.mm(pb[:, 64 + i * 2:64 + (i + 1) * 2], TRI[1], la3[:, i, :], True, True, [d_cst, d_g], [pd])
            k.cp(gk(K_G)[:, :, 0:1], pb[:, 0:NT * 2].rearrange("p (t n) -> p t n", n=2)[:, :, 0:1], [pd], G_)
            k.cp(gk(K_G)[:, :, 1:2], pb[:, 64:64 + NT * 2].rearrange("p (t n) -> p t n", n=2)[:, :, 1:2], [pd], G_)
            k.tt(gk(K_GB), gk(K_G), lnb3, ALU.add, G_, G_)
            k.act(gk(K_EG), gk(K_G), AF.Exp, G_, G_)
            k.tt(gk(K_BG), gk(K_BETA), gk(K_EG), ALU.mult, G_, G_)
            k.tt(gt3, gk(K_G), SEL2.unsqueeze(1).to_broadcast([128, NT, 2]), ALU.mult, G_ + [d_cst], G_)
            pb, pd = k.ps()
            k.mm(pb[:, 0:NT * 2], BLK, gt, True, True, [d_cst, d_g], [pd])
            k.tt(gt3, pb[:, 0:NT * 2].rearrange("p (t n) -> p t n", n=2), gk(K_G), ALU.subtract, [pd, d_g], G_)
            k.act(gk(K_EGL), gt3, AF.Exp, G_, G_)
            gsel2, _ = cv.get(NT * 4)
            k.tt(gsel2.rearrange("p (t d c) -> p t d c", d=2, c=2), gk(K_G).unsqueeze(3).to_broadcast([128, NT, 2, 2]),
                 SELLAST.rearrange("p (d c) -> p d c", d=2).unsqueeze(1).to_broadcast([128, NT, 2, 2]), ALU.mult,
                 G_ + [d_cst], G_)
            pb, pd = k.ps()
            k.mm(pb[:, 0:NT * 4], ONES, gsel2, True, True, [d_cst, d_g], [pd])
            for cp in range(2):
                k.act(gk(K_GL + cp), pb[:, 0:NT * 4].rearrange("p (t d c) -> p t d c", d=2, c=2)[:, :, :, cp], AF.Exp, [pd], G_)
            S = []
            for d_ in range(2):
                s_, ds_ = cv.get(128)
                if kind == "p":
                    k.ms(s_, 0.0, [ds_])
                else:
                    P.dma(k.q(), s_, sdn.ap()[l, d_, h], writes=[ds_])
                S.append((s_, ds_))
            X, d_X = cv.get(256)
            vb, d_vb = cv.get(256)
            Gd, d_Gd = cv.get(128)
            Gbd, d_Gbd = cv.get(128)
            E1, d_E1 = cv.get(128)
            E2, d_E2 = cv.get(128)
            E3, d_E3 = cv.get(128)
            MMs = [cv.get(256), cv.get(256)]
            PT, d_PT = cv.get(128)
            OB = []
            for _i in range(2):
                o_ = {}
                o_["attnT"], o_["d_at"] = cv.get(128)
                o_["kg"], o_["d_kg"] = cv.get(256)
                o_["uw"], o_["d_uw"] = cv.get(512)
                ls_, o_["d_LS"] = cv.get(NK * 2)
                o_["LS3"] = ls_.rearrange("p (k d) -> p k d", d=2)
                OB.append(o_)
            SC = []
            for d_ in range(2):
                SC.append((cv.get(128), cv.get(128), cv.get(128)))
            PB = k.psb
            PD = k.psd

            def lanes(s_):
                return (s_, NT - 1 - s_)

            def prep(s_):
                il = lanes(s_)
                ob = OB[s_ % 2]
                LS3, d_LS = ob["LS3"], ob["d_LS"]
                ls = lambda kk, d_: LS3[:, kk, d_:d_ + 1]
                tsl = [slice(il[d_] * 128, (il[d_] + 1) * 128) for d_ in range(2)]
                for d_ in range(2):
                    k.cp(LS3[:, :, d_], GA4[:, il[d_], :, d_], [d_g], [d_LS], eng="pool")
                pk, pkd = PB[0], PD[0]
                for d_ in range(2):
                    k.tr(pk[:, d_ * 256:d_ * 256 + 128], kf[:, tsl[d_]], ident[:], [d_kf, d_ident], [pkd])
                    k.tr(pk[:, d_ * 256 + 128:d_ * 256 + 256], vf[:, tsl[d_]], ident[:], [d_vf, d_ident], [pkd])
                for d_ in range(2):
                    ds = slice(d_ * 128, (d_ + 1) * 128)
                    k.ts(ob["kg"][:, ds], pk[:, d_ * 256:d_ * 256 + 128], ls(K_EGL, d_), None, ALU.mult, None, [pkd, d_LS], [ob["d_kg"]])
                    k.act(X[:, ds], pk[:, d_ * 256:d_ * 256 + 128], AF.Copy, [pkd, d_LS], [d_X], scale=ls(K_BG, d_))
                    k.ts(vb[:, ds], pk[:, d_ * 256 + 128:d_ * 256 + 256], ls(K_BETA, d_), None, ALU.mult, None, [pkd, d_LS], [d_vb])
                yield
                pa, pad = PB[1], PD[1]
                for d_ in range(2):
                    k.mm(pa[:, d_ * 256:d_ * 256 + 128], kf[:, tsl[d_]], kf[:, tsl[d_]], True, True, [d_kf], [pad])
                    k.mm(pa[:, d_ * 256 + 128:d_ * 256 + 256], kf[:, tsl[d_]], qf[:, tsl[d_]], True, True, [d_kf, d_qf], [pad])
                for d_ in range(2):
                    es = slice(d_ * 64, (d_ + 1) * 64)
                    k.ts(Gd[:, es], I2, ls(K_G, d_), None, ALU.mult, None, [d_cst, d_LS], [d_Gd], eng="pool")
                    k.ts(Gbd[:, es], I2, ls(K_GB, d_), None, ALU.mult, None, [d_cst, d_LS], [d_Gbd], eng="pool")
                pg, pgd = PB[2], PD[2]
                k.mm(pg[:, 0:128], BLK, Gd, True, True, [d_cst, d_Gd], [pgd])
                k.mm(pg[:, 128:256], BLK, Gbd, True, True, [d_cst, d_Gbd], [pgd])
                yield
                for d_ in range(2):
                    es = slice(d_ * 64, (d_ + 1) * 64)
                    k.stt(E1[:, es], pg[:, es], ls(K_G, d_), MASK[0][d_], ALU.subtract, ALU.add, [pgd, d_LS, d_cst], [d_E1])
                    k.stt(E2[:, es], pg[:, 128 + d_ * 64:128 + (d_ + 1) * 64], ls(K_G, d_), MASK[1][d_], ALU.subtract, ALU.add,
                          [pgd, d_LS, d_cst], [d_E2])
                    k.stt(E3[:, es], pg[:, es], ls(K_G, d_), MASK[2][d_], ALU.subtract, ALU.add, [pgd, d_LS, d_cst], [d_E3])
                k.act(E1, E1, AF.Exp, [d_E1], [d_E1])
                k.act(E2, E2, AF.Exp, [d_E2], [d_E2])
                k.act(E3, E3, AF.Exp, [d_E3], [d_E3], scale=-1.0)
                yield
                (cur, dcur), (nxt, dnxt) = MMs
                for d_ in range(2):
                    es = slice(d_ * 64, (d_ + 1) * 64)
                    for cp in range(2):
                        bs = slice(cp * 64, (cp + 1) * 64)
                        ab = slice(d_ * 256 + cp * 64, d_ * 256 + (cp + 1) * 64)
                        qb = slice(d_ * 256 + 128 + cp * 64, d_ * 256 + 128 + (cp + 1) * 64)
                        k.tt(ob["attnT"][bs, es], pa[bs, qb], E1[bs, es], ALU.mult, [pad, d_E1], [ob["d_at"]])
                        k.stt(cur[bs, d_ * 128 + 64:d_ * 128 + 128], pa[bs, ab], -1.0, E2[bs, es], ALU.mult, ALU.mult,
                              [pad, d_E2], [dcur])
                        k.stt(cur[bs, d_ * 128:d_ * 128 + 64], pa[bs, ab], LS3[bs, K_NEGB, d_:d_ + 1], E3[bs, es], ALU.mult, ALU.mult,
                              [pad, d_E3, d_LS], [dcur])
                k.tt(PT.rearrange("p (d n) -> p d n", d=2), I2.unsqueeze(1).to_broadcast([128, 2, 64]),
                     cur.rearrange("p (d n) -> p d n", d=2)[:, :, 64:128], ALU.add, [d_cst, dcur], [d_PT], eng="pool")
                yield
                for lev in range(5):
                    lastl = (lev == 4)
                    pm, pmd = PB[3], PD[3]
                    for d_ in range(2):
                        for cp in range(2):
                            bs = slice(cp * 64, (cp + 1) * 64)
                            k.mm(pm[bs, d_ * 128:d_ * 128 + 64], cur[bs, d_ * 128 + 64:d_ * 128 + 128], cur[bs, d_ * 128:d_ * 128 + 64],
                                 True, True, [dcur], [pmd])
                            if not lastl:
                                k.mm(pm[bs, d_ * 128 + 64:d_ * 128 + 128], cur[bs, d_ * 128:d_ * 128 + 64],
                                     cur[bs, d_ * 128 + 64:d_ * 128 + 128], True, True, [dcur], [pmd])
                    if lastl:
                        k.act(nxt.rearrange("p (d n) -> p d n", d=2)[:, :, 0:64], pm[:, 0:256].rearrange("p (d n) -> p d n", d=2)[:, :, 0:64],
                              AF.Copy, [pmd], [dnxt])
                    else:
                        k.act(nxt, pm[:, 0:256], AF.Copy, [pmd], [dnxt])
                    yield
                    pp, ppd = PB[0], PD[0]
                    for d_ in range(2):
                        es = slice(d_ * 64, (d_ + 1) * 64)
                        for cp in range(2):
                            bs = slice(cp * 64, (cp + 1) * 64)
                            k.mm(pp[bs, es], nxt[bs, d_ * 128:d_ * 128 + 64], PT[bs, es], True, True, [dnxt, d_PT], [ppd])
                    k.tt(PT, PT, pp[:, 0:128], ALU.add, [d_PT, ppd], [d_PT])
                    yield
                    cur, dcur, nxt, dnxt = nxt, dnxt, cur, dcur
                pu, pud = PB[2], PD[2]
                for d_ in range(2):
                    ds = slice(d_ * 128, (d_ + 1) * 128)
                    es = slice(d_ * 64, (d_ + 1) * 64)
                    for cp in range(2):
                        bs = slice(cp * 64, (cp + 1) * 64)
                        k.mm(pu[bs, d_ * 256:d_ * 256 + 128], PT[bs, es], vb[bs, ds], True, True, [d_PT, d_vb], [pud])
                        k.mm(pu[:, d_ * 256 + 128 + cp * 64:d_ * 256 + 128 + (cp + 1) * 64], X[bs, ds], PT[bs, es], True, True,
                             [d_X, d_PT], [pud])
                k.cp(ob["uw"], pu[:, 0:512], [pud], [ob["d_uw"]])
                yield

            def scan(s_, d_):
                i = lanes(s_)[d_]
                ob = OB[s_ % 2]
                LS3, d_LS = ob["LS3"], ob["d_LS"]
                ds = slice(d_ * 128, (d_ + 1) * 128)
                es = slice(d_ * 64, (d_ + 1) * 64)
                (vnew, d_vn), (oasb, d_oa), (otmp, d_ot) = SC[d_]
                s_t, ds_ = S[d_]
                p1, p1d = PB[4 + 2 * d_], PD[4 + 2 * d_]
                p2, p2d = PB[5 + 2 * d_], PD[5 + 2 * d_]
                for cp in ((0, 1) if d_ == 0 else (1, 0)):
                    bs = slice(cp * 64, (cp + 1) * 64)
                    cs = slice(i * 128 + cp * 64, i * 128 + (cp + 1) * 64)
                    k.mm(p1[bs, 0:128], ob["uw"][:, d_ * 256 + 128 + cp * 64:d_ * 256 + 128 + (cp + 1) * 64], s_t, True, True,
                         [ob["d_uw"], ds_], [p1d])
                    k.mm(p1[bs, 128:256], qf[:, cs], s_t, True, True, [d_qf, ds_], [p1d])
                    k.tt(vnew[bs, :], ob["uw"][bs, d_ * 256:d_ * 256 + 128], p1[bs, 0:128], ALU.subtract, [ob["d_uw"], p1d], [d_vn])
                    yield
                    k.mm(p2[bs, 0:128], ob["attnT"][bs, es], vnew[bs, :], True, True, [ob["d_at"], d_vn], [p2d])
                    k.mm(p2[:, 128:256], ob["kg"][bs, ds], vnew[bs, :], True, True, [ob["d_kg"], d_vn], [p2d])
                    k.act(oasb[bs, :], p2[bs, 0:128], AF.Copy, [p2d], [d_oa])
                    k.stt(s_t, s_t, LS3[:, K_GL + cp, d_:d_ + 1], p2[:, 128:256], ALU.mult, ALU.add, [ds_, d_LS, p2d], [ds_])
                    yield
                    k.stt(otmp[bs, :], p1[bs, 128:256], LS3[bs, K_EG, d_:d_ + 1], oasb[bs, :], ALU.mult, ALU.add,
                          [p1d, d_LS, d_oa], [d_ot])
                    k.tt(oacc[bs, i * 128:(i + 1) * 128], oacc[bs, i * 128:(i + 1) * 128], otmp[bs, :], ALU.add,
                         [d_oacc[i], d_ot], [d_oacc[i]], eng="pool")
                    yield

            def run_gens(gens):
                gens = list(gens)
                while gens:
                    for g_ in list(gens):
                        try:
                            next(g_)
                        except StopIteration:
                            gens.remove(g_)

            run_gens([prep(0)])
            for s_ in range(NT):
                gl_ = [scan(s_, 0), scan(s_, 1)]
                if s_ + 1 < NT:
                    gl_.insert(0, prep(s_ + 1))
                run_gens(gl_)
            if kind == "p":
                for d_ in range(2):
                    P.dma(k.q(), nst.ap()[si, l, d_, h], S[d_][0], reads=[S[d_][1]], writes=[d_nst])
            rs, d_rs = cv.get(4)
            on, d_on = cv.get(128)
            yT, d_yT = cv.get(128, F32R)
            for i in range(NT):
                tsl = slice(i * 128, (i + 1) * 128)
                k.act(on, oacc[:, tsl], AF.Square, [d_oacc[i]], [d_on, d_rs], accum=rs[:, 0:1])
                k.act(rs[:, 1:2], rs[:, 0:1], AF.Sqrt, [d_rs], [d_rs], bias=EPS, scale=1.0 / 128)
                k.rcp(rs[:, 2:3], rs[:, 1:2], [d_rs], [d_rs])
                k.stt(on, oacc[:, tsl], rs[:, 2:3], gn_row, ALU.mult, ALU.mult, [d_oacc[i], d_rs, d_gn, d_on], [d_on])
                pt, ptd = k.ps()
                k.tr(pt[:, 0:128], on, ident[:], [d_on, d_ident], [ptd])
                k.tt(yT, pt[:, 0:128], zs[:, tsl], ALU.mult, [ptd, d_zs], [d_yT])
                P.dma(k.q(), yscr.ap()[h, :, tsl], yT, reads=[d_yT], writes=[d_yscr[h]], r32=True)

        def na_unit(l, kind, si, T, hp):
            NT = T // 128
            TT = min(T, 512)
            fz = fence()
            cv = Carve(fz)
            ws = []
            for off in (OFF_NA_Q, OFF_NA_K, OFF_NA_V, OFF_NA_Z):
                wa, dw = cv.get(8 * 128, F32R)
                w3 = wa.rearrange("p (c n) -> p c n", c=8)
                c0 = off + hp * 128
                P.dma(k.q(), w3, w_in.ap()[l, :, c0:c0 + 128].rearrange("(c p) n -> p c n", p=128), writes=[dw], r32=True)
                ws.append((w3, dw))
            (wq, d_wq), (wk, d_wk), (wv, d_wv), (wz, d_wz) = ws
            qT, d_qT = cv.get(T, F32R)
            kT, d_kT = cv.get(T, F32R)
            vt_f, d_vt = cv.get(NT * 132, F32R)
            vtok = vt_f.rearrange("p (t h e) -> p t h e", t=NT, h=2)
            zs, d_zs = cv.get(T)
            ones, d_ones = cv.get(2)
            stg, d_stg = cv.get(256)
            k.ms(ones, 1.0, [d_ones])
            for t in range(T // TT):
                ts_ = slice(t * TT, (t + 1) * TT)
                pb, pd = k.ps()
                for kc in range(8):
                    k.mm(pb[:, 0:TT], wq[:, kc, :], hT[:, kc, ts_], kc == 0, kc == 7, [d_wq, d_hT], [pd])
                k.ts(qT[:, ts_], pb[:, 0:TT], 0.125, None, ALU.mult, None, [pd], [d_qT])
                pb, pd = k.ps()
                for kc in range(8):
                    k.mm(pb[:, 0:TT], wk[:, kc, :], hT[:, kc, ts_], kc == 0, kc == 7, [d_wk, d_hT], [pd])
                k.cp(kT[:, ts_], pb[:, 0:TT], [pd], [d_kT])
                pb, pd = k.ps()
                for kc in range(8):
                    k.mm(pb[:, 0:TT], wz[:, kc, :], hT[:, kc, ts_], kc == 0, kc == 7, [d_wz, d_hT], [pd])
                k.act(zs[:, ts_], pb[:, 0:TT], AF.Silu, [pd], [d_zs])
            for i in range(NT):
                is_ = slice(i * 128, (i + 1) * 128)
                pb, pd = k.ps()
                for kc in range(8):
                    k.mm(pb[:, 0:128], hT[:, kc, is_], wv[:, kc, :], kc == 0, kc == 7, [d_wv, d_hT], [pd])
                k.cp(vtok[:, i, :, 0:64], pb[:, 0:128].rearrange("p (h e) -> p h e", h=2), [pd], [d_vt])
                k.cp(vtok[:, i, :, 64:66], ones[:, 0:2].unsqueeze(1).to_broadcast([128, 2, 2]), [d_ones], [d_vt], eng="pool")
                if kind == "p":
                    k.cp(stg[:, 0:128], pb[:, 0:128], [pd], [d_stg])
                    P.dma("act", nv.ap()[si, l, is_, hp * 128:(hp + 1) * 128], stg[:, 0:128], reads=[d_stg], writes=[d_nkv])
                    pb, pd = k.ps()
                    for kc in range(8):
                        k.mm(pb[:, 0:128], hT[:, kc, is_], wk[:, kc, :], kc == 0, kc == 7, [d_wk, d_hT], [pd])
                    k.cp(stg[:, 128:256], pb[:, 0:128], [pd], [d_stg])
                    P.dma("act", nk.ap()[si, l, is_, hp * 128:(hp + 1) * 128], stg[:, 128:256], reads=[d_stg], writes=[d_nkv])
            otok, d_otok = cv.get(128)
            rden, d_rden = cv.get(2)
            yT, d_yT = cv.get(128, F32R)
            if kind == "p":
                PT = [cv.get(256, F32R) for _ in range(4)]
                po, pod = k.ps()
                for h2 in range(2):
                    hs = slice(h2 * 64, (h2 + 1) * 64)
                    for kt in range(2):
                        pt, d_pt = PT[h2 * 2 + kt]
                        pb, pd = k.ps()
                        k.mm(pb[:, 0:256], kT[hs, kt * 128:(kt + 1) * 128], qT[hs, 0:256], True, True, [d_kT, d_qT], [pd])
                        k.act(pt, pb[:, 0:256], AF.Exp, [pd], [d_pt])
                    for qt in range(2):
                        for kt in range(2):
                            pt, d_pt = PT[h2 * 2 + kt]
                            c0 = qt * 132 + h2 * 66
                            k.mm(po[:, c0:c0 + 66], pt[:, qt * 128:(qt + 1) * 128], vtok[:, kt, h2, :], kt == 0, kt == 1,
                                 [d_pt, d_vt], [pod])
                for qt in range(2):
                    qs = slice(qt * 128, (qt + 1) * 128)
                    for h2 in range(2):
                        c0 = qt * 132 + h2 * 66
                        k.rcp(rden[:, h2:h2 + 1], po[:, c0 + 64:c0 + 65], [pod], [d_rden])
                        k.ts(otok[:, h2 * 64:(h2 + 1) * 64], po[:, c0:c0 + 64], rden[:, h2:h2 + 1], None, ALU.mult, None,
                             [pod, d_rden], [d_otok])
                    pb, pd = k.ps()
                    k.tr(pb[:, 0:128], otok, ident[:], [d_otok, d_ident], [pd])
                    k.tt(yT, pb[:, 0:128], zs[:, qs], ALU.mult, [pd, d_zs], [d_yT])
                    P.dma(k.q(), yscr.ap()[4 + hp, :, qs], yT, reads=[d_yT], writes=[d_yscr[4 + hp]], r32=True)
            else:
                kcT, d_kcT = cv.get(256, F32R)
                vc_f, d_vc = cv.get(2 * 132, F32R)
                vctx = vc_f.rearrange("p (t h e) -> p t h e", t=2, h=2)
                cst_k, d_cstk = cv.get(256)
                P.dma("sp", cst_k.rearrange("p (t n) -> p t n", t=2),
                      ck_in.ap()[l, :, hp * 128:(hp + 1) * 128].rearrange("(t p) n -> p t n", p=128), writes=[d_cstk])
                pb, pd = k.ps()
                for t in range(2):
                    k.tr(pb[:, t * 128:(t + 1) * 128], cst_k[:, t * 128:(t + 1) * 128], ident[:], [d_cstk, d_ident], [pd])
                k.cp(kcT, pb[:, 0:256], [pd], [d_kcT])
                for t in range(2):
                    P.dma("act", vctx[:, t, :, 0:64],
                          cv_in.ap()[l, t * 128:(t + 1) * 128, hp * 128:(hp + 1) * 128].rearrange("p (h e) -> p h e", h=2),
                          writes=[d_vc], r32=True)
                    k.cp(vctx[:, t, :, 64:66], ones[:, 0:2].unsqueeze(1).to_broadcast([128, 2, 2]), [d_ones], [d_vc], eng="pool")
                E2f, d_E2 = cv.get(2 * 15 * 64)
                E2 = E2f.rearrange("p (h r c) -> p h r c", h=2, r=15)
                for a in range(2):
                    for h2 in range(2):
                        head = hp * 2 + h2
                        src = AP(zscr, ((l * 8 + head) * 15) * 8128 + 63, [[126, 64], [8128, 15], [1, 64]])
                        P.dma("sp" if a == 0 else "act", E2[a * 64:(a + 1) * 64, h2, :, :], src, reads=[d_zscr[l]], writes=[d_E2])
                k.act(E2f, E2f, AF.Exp, [d_E2], [d_E2])
                k.tt(E2f.rearrange("p (g c) -> p g c", c=64), E2f.rearrange("p (g c) -> p g c", c=64),
                     CM.unsqueeze(1).to_broadcast([128, 30, 64]), ALU.mult, [d_E2, d_cst], [d_E2])
                TABf, d_TAB = cv.get(2 * 21 * 128)
                TAB = TABf.rearrange("p (h t q) -> p h t q", h=2, t=21)
                k.ms(TABf, 0.0, [d_TAB])
                plans = {}
                tid = 0
                plans["int"] = []
                for j in range(5):
                    for a in range(2):
                        for b in range(2):
                            dr = 2 * j - 4 + a - b
                            if -4 <= dr <= 3:
                                plans["int"].append((tid, a, b, dr))
                    tid += 1
                tid0 = {"int": 0}
                for m_ in (0, 1, 14, 15):
                    tid0[m_] = tid
                    kt0 = 0 if m_ < 2 else 12
                    plans[m_] = []
                    for j in range(4):
                        for a in range(2):
                            for b in range(2):
                                kr = 2 * (kt0 + j) + a
                                r = 2 * m_ + b
                                rs_ = min(max(r - 4, 0), 24)
                                if rs_ <= kr <= rs_ + 7:
                                    plans[m_].append((tid, a, b, kr - r))
                        tid += 1
                assert tid == 21
                for key_, pl in plans.items():
                    for (tid_, a, b, dr) in pl:
                        k.cp(TAB[a * 64:(a + 1) * 64, :, tid_, b * 64:(b + 1) * 64], E2[a * 64:(a + 1) * 64, :, dr + 7, :],
                             [d_E2], [d_TAB], eng="pool")
                PTf, d_PT = cv.get(7 * 128, F32R)
                tmpE, d_tmpE = cv.get(512)
                tmpE2, d_tmpE2 = cv.get(128)
                for m_ in range(16):
                    qs = slice(m_ * 128, (m_ + 1) * 128)
                    if 2 <= m_ <= 13:
                        kts = [m_ - 2 + j for j in range(5)]
                        t0 = 0
                    else:
                        kts = [(0 if m_ < 2 else 12) + j for j in range(4)]
                        t0 = tid0[m_]
                    nl = len(kts)
                    po, pod = k.ps()
                    for h2 in range(2):
                        hs = slice(h2 * 64, (h2 + 1) * 64)
                        pA, pAd = k.ps()
                        for j in range(4):
                            k.mm(pA[:, j * 128:(j + 1) * 128], kT[hs, kts[j] * 128:(kts[j] + 1) * 128], qT[hs, qs], True, True,
                                 [d_kT, d_qT], [pAd])
                        pB, pBd = k.ps()
                        c_ = 0
                        if nl == 5:
                            k.mm(pB[:, 0:128], kT[hs, kts[4] * 128:(kts[4] + 1) * 128], qT[hs, qs], True, True, [d_kT, d_qT], [pBd])
                            c_ = 128
                        for t in range(2):
                            k.mm(pB[:, c_ + t * 128:c_ + (t + 1) * 128], kcT[hs, t * 128:(t + 1) * 128], qT[hs, qs], True, True,
                                 [d_kcT, d_qT], [pBd])
                        k.act(tmpE, pA[:, 0:512], AF.Exp, [pAd], [d_tmpE])
                        k.tt(PTf[:, 0:512], tmpE, TABf[:, (h2 * 21 + t0) * 128:(h2 * 21 + t0 + 4) * 128], ALU.mult,
                             [d_tmpE, d_TAB], [d_PT])
                        if nl == 5:
                            k.act(tmpE2, pB[:, 0:128], AF.Exp, [pBd], [d_tmpE2])
                            k.tt(PTf[:, 512:640], tmpE2, TABf[:, (h2 * 21 + 4) * 128:(h2 * 21 + 5) * 128], ALU.mult,
                                 [d_tmpE2, d_TAB], [d_PT])
                        k.act(PTf[:, nl * 128:(nl + 2) * 128], pB[:, c_:c_ + 256], AF.Exp, [pBd], [d_PT])
                        c0 = h2 * 66
                        ntile = nl + 2
                        for j in range(ntile):
                            if j < nl:
                                rhs_ = vtok[:, kts[j], h2, :]
                                rd = d_vt
                            else:
                                rhs_ = vctx[:, j - nl, h2, :]
                                rd = d_vc
                            k.mm(po[:, c0:c0 + 66], PTf[:, j * 128:(j + 1) * 128], rhs_, j == 0, j == ntile - 1, [d_PT, rd], [pod])
                    for h2 in range(2):
                        c0 = h2 * 66
                        k.rcp(rden[:, h2:h2 + 1], po[:, c0 + 64:c0 + 65], [pod], [d_rden])
                        k.ts(otok[:, h2 * 64:(h2 + 1) * 64], po[:, c0:c0 + 64], rden[:, h2:h2 + 1], None, ALU.mult, None,
                             [pod, d_rden], [d_otok])
                    pb, pd = k.ps()
                    k.tr(pb[:, 0:128], otok, ident[:], [d_otok, d_ident], [pd])
                    k.tt(yT, pb[:, 0:128], zs[:, qs], ALU.mult, [pd, d_zs], [d_yT])
                    P.dma(k.q(), yscr.ap()[4 + hp, :, qs], yT, reads=[d_yT], writes=[d_yscr[4 + hp]], r32=True)

        for l in range(depth):
            last = (l == depth - 1)
            fz = fence()
            cv = Carve(fz)
            bcol, d_bcol = cv.get(16)
            gpre_col, _d = cv.get(8)
            brow, _d = cv.get(D)
            gpost_row, _d = cv.get(D)
            ggt, d_ggt = cv.get(512)
            wada_f, d_wada = cv.get(8 * 512, F32R)
            wada = wada_f.rearrange("p (c n) -> p c n", c=8)
            sbc_f, d_sbc = cv.get(8 * 2 * 128, F32R)
            siluc_bc = sbc_f.rearrange("p (c j n) -> p c j n", c=8, j=2)
            for kc in range(8):
                for j in range(2):
                    k.cp(siluc_bc[:, kc, j, :], silucT[:, kc, j:j + 1].bitcast(F32).to_broadcast([128, 128]), [d_siluc], [d_sbc])
            P.dma("sp", bcol, b_ada.ap()[l, 0:2048].rearrange("(c p) -> p c", p=128), writes=[d_bcol], slow=True)
            P.dma("act", gpre_col, g_pre.ap()[l].rearrange("(c p) -> p c", p=128), writes=[d_bcol], slow=True)
            P.dma("sp", brow, b_ada.ap()[l, 2048:3072].partition_broadcast(128), writes=[d_bcol])
            P.dma("act", gpost_row, g_post.ap()[l].partition_broadcast(128), writes=[d_bcol])
            for blk in range(6):
                P.dma(k.q(), wada, w_ada.ap()[l, :, blk * 512:(blk + 1) * 512].rearrange("(c p) n -> p c n", p=128),
                      writes=[d_wada], r32=True)
                if blk < 4:
                    pb, pd = k.ps()
                    for cc in range(4):
                        for kc in range(8):
                            k.mm(pb[:, cc * 2:cc * 2 + 2], wada[:, kc, cc * 128:(cc + 1) * 128], silucT[:, kc, :],
                                 kc == 0, kc == 7, [d_wada, d_siluc], [pd])
                    k.tt(modcol[:, blk * 4:blk * 4 + 4, :], pb[:, 0:8].rearrange("p (c j) -> p c j", j=2),
                         bcol[:, blk * 4:blk * 4 + 4].unsqueeze(2).to_broadcast([128, 4, 2]), ALU.add,
                         [pd, d_bcol], [d_modcol])
                else:
                    for j in range(2):
                        pb, pd = k.ps()
                        for kc in range(8):
                            k.mm(pb[:], siluc_bc[:, kc, j, :], wada[:, kc, :], kc == 0, kc == 7, [d_wada, d_sbc], [pd])
                        c0 = (blk - 4) * 512
                        k.tt(ggt[:, 0:512], pb[:], brow[:, c0:c0 + 512], ALU.add, [pd, d_bcol], [d_ggt])
                        k.tt(ggt[:, 0:512], ggt[:, 0:512], gpost_row[:, c0:c0 + 512], ALU.mult, [d_ggt, d_bcol], [d_ggt])
                        P.dma("sp", ggscr.ap()[j, :, c0:c0 + 512], ggt[:, 0:512], reads=[d_ggt], writes=[d_gg])
            for j in range(2):
                k.stt(s1col[:, :, j], modcol[:, 8:16, j], 1.0, gpre_col, ALU.add, ALU.mult, [d_modcol, d_bcol], [d_modcol])

            for (kind, si, T) in seqs:
                cj = 0 if kind == "p" else 1
                TT = min(T, 512)
                NTL = T // TT
                if l == 0:
                    xin = x_p.ap()[si] if kind == "p" else x_s.ap()
                    d_xin = Dep()
                else:
                    xin = xs_p[(l - 1) % 2].ap()[si] if kind == "p" else xs_s[(l - 1) % 2].ap()
                    d_xin = d_xs_p[(l - 1) % 2][si] if kind == "p" else d_xs_s[(l - 1) % 2]
                if last:
                    xout = y_p.ap()[si] if kind == "p" else y_s.ap()
                    d_xout = Dep()
                else:
                    xout = xs_p[l % 2].ap()[si] if kind == "p" else xs_s[l % 2].ap()
                    d_xout = d_xs_p[l % 2][si] if kind == "p" else d_xs_s[l % 2]

                fz = fence()
                cv = Carve(fz)
                _xt, _dxt = cv.get(D)
                xt = [_xt, _xt]
                d_xt = [_dxt, _dxt]
                xn, d_xn = cv.get(D)
                for i in range(T // 128):
                    b = i % 2
                    P.dma(k.q(), xt[b], xin[i * 128:(i + 1) * 128, :], reads=[d_xin], writes=[d_xt[b]])
                    k.act(xn, xt[b], AF.Square, [d_xt[b]], [d_xn, d_small], accum=small[:, 0:1])
                    k.act(small[:, 1:2], small[:, 0:1], AF.Sqrt, [d_small], [d_small], bias=EPS, scale=1.0 / D)
                    k.rcp(small[:, 2:3], small[:, 1:2], [d_small], [d_small])
                    k.ts(xn, xt[b], small[:, 2:3], None, ALU.mult, None, [d_xt[b], d_small], [d_xn])
                    for half in range(2):
                        pb, pd = k.ps()
                        for c4 in range(4):
                            kc = half * 4 + c4
                            k.tr(pb[:, c4 * 128:(c4 + 1) * 128], xn[:, kc * 128:(kc + 1) * 128], ident[:], [d_xn, d_ident], [pd])
                        for c4 in range(4):
                            kc = half * 4 + c4
                            k.ts(hT[:, kc, i * 128:(i + 1) * 128], pb[:, c4 * 128:(c4 + 1) * 128],
                                 s1col[:, kc, cj:cj + 1], modcol[:, kc, cj:cj + 1], ALU.mult, ALU.add,
                                 [pd, d_modcol], [d_hT], eng=("dve" if c4 % 2 == 0 else "pool") if False else "dve")

                fz = fence()
                cv = Carve(fz)
                zbuf, d_z = cv.get(TT, F32R)
                zsrc, d_zs0 = cv.get(TT)
                k.ms(zsrc, 0.0, [d_zs0])
                k.cp(zbuf, zsrc, [d_zs0], [d_z])
                for ch in range(8):
                    if (ch < 4 and not do_dn) or (ch >= 4 and not do_na):
                        for t in range(NTL):
                            P.dma(k.q(), yscr.ap()[ch, :, t * TT:(t + 1) * TT], zbuf, reads=[d_z], writes=[d_yscr[ch]], r32=True)

                if do_dn:
                    for h in range(4):
                        dn_unit(l, kind, si, T, h)

                if do_na:
                    for hp in range(4):
                        na_unit(l, kind, si, T, hp)

                for g in range(4):
                    win = POOL_WINDOWS[g]
                    fz = fence()
                    cv = Carve(fz)
                    wu, d_wu = cv.get(8 * 128, F32R)
                    wz, d_wz = cv.get(8 * 128, F32R)
                    pw, d_pw = cv.get(128, F32R)
                    psc, d_psc = cv.get(1)
                    U, d_U = cv.get(T + 16)
                    zs, d_zs = cv.get(T)
                    s_a, d_sa = cv.get(T + 16)
                    s_b, d_sb = cv.get(T + 16)
                    icnt, d_icnt = cv.get(T)
                    pooled, d_pooled = cv.get(T, F32R)
                    yT, d_yT = cv.get(TT, F32R)
                    wu3 = wu.rearrange("p (c n) -> p c n", c=8)
                    wz3 = wz.rearrange("p (c n) -> p c n", c=8)
                    cu = OFF_PL_U + g * 128
                    cz = OFF_PL_Z + g * 128
                    P.dma("sp", wu3, w_in.ap()[l, :, cu:cu + 128].rearrange("(c p) n -> p c n", p=128), writes=[d_wu], r32=True)
                    P.dma("act", wz3, w_in.ap()[l, :, cz:cz + 128].rearrange("(c p) n -> p c n", p=128), writes=[d_wz], r32=True)
                    P.dma("sp", pw, pool_w.ap()[l, g], writes=[d_pw], r32=True)
                    P.dma("act", psc, pool_scale.ap()[l, g * 128:(g + 1) * 128].rearrange("(p o) -> p o", o=1), writes=[d_psc], slow=True)
                    ic_src = (invcnt_p if kind == "p" else invcnt_s).ap()[g]
                    P.dma("sp", icnt, ic_src.partition_broadcast(128), writes=[d_icnt])
                    k.ms(U[:, 0:8], 0.0, [d_U])
                    k.ms(U[:, T + 8:T + 16], 0.0, [d_U])
                    for t in range(NTL):
                        pb, pd = k.ps()
                        for kc in range(8):
                            k.mm(pb[:, 0:TT], wu3[:, kc, :], hT[:, kc, t * TT:(t + 1) * TT], kc == 0, kc == 7, [d_wu, d_hT], [pd])
                        k.cp(U[:, 8 + t * TT:8 + (t + 1) * TT], pb[:, 0:TT], [pd], [d_U])
                        pb, pd = k.ps()
                        for kc in range(8):
                            k.mm(pb[:, 0:TT], wz3[:, kc, :], hT[:, kc, t * TT:(t + 1) * TT], kc == 0, kc == 7, [d_wz, d_hT], [pd])
                        k.act(zs[:, t * TT:(t + 1) * TT], pb[:, 0:TT], AF.Silu, [pd], [d_zs])
                    cur, dcur, curlen = U, d_U, T + 16
                    step = 1
                    bufs = [(s_a, d_sa), (s_b, d_sb)]
                    bi = 0
                    while step < win:
                        nb, dnb = bufs[bi]
                        bi ^= 1
                        nlen = curlen - step
                        k.tt(nb[:, 0:nlen], cur[:, 0:nlen], cur[:, step:step + nlen], ALU.add, [dcur], [dnb])
                        cur, dcur, curlen = nb, dnb, nlen
                        step *= 2
                    o0 = 8 - win // 2
                    nb, dnb = bufs[bi]
                    k.tt(nb[:, 0:T], cur[:, o0:o0 + T], icnt, ALU.mult, [dcur, d_icnt], [dnb])
                    k.tt(pooled, nb[:, 0:T], U[:, 8:8 + T], ALU.subtract, [dnb, d_U], [d_pooled])
                    for t in range(NTL):
                        pb, pd = k.ps()
                        k.mm(pb[:, 0:TT], pw, pooled[:, t * TT:(t + 1) * TT], True, True, [d_pw, d_pooled], [pd])
                        k.stt(yT, pb[:, 0:TT], psc[:, 0:1], zs[:, t * TT:(t + 1) * TT], ALU.mult, ALU.mult, [pd, d_psc, d_zs], [d_yT])
                        P.dma(k.q(), yscr.ap()[8 + g, :, t * TT:(t + 1) * TT], yT, reads=[d_yT], writes=[d_yscr[8 + g]], r32=True)

                TF = 256
                NTF = T // TF
                fz = fence()
                cv = Carve(fz)
                ysb_l = []
                for _i in range(2):
                    _a, _d = cv.get(4 * TF, F32R)
                    ysb_l.append((_a.rearrange("p (c n) -> p c n", c=4), _d))
                wg_l = []
                wbr_l = []
                for _i in range(2):
                    _a, _d = cv.get(8 * 128, F32R)
                    wg_l.append((_a.rearrange("p (c n) -> p c n", c=8), _d))
                    _a, _d = cv.get(4 * 128, F32R)
                    wbr_l.append((_a.rearrange("p (c n) -> p c n", c=4), _d))
                mrgT, d_mrg = cv.get(8 * TF, F32R)
                mrg3 = mrgT.rearrange("p (c n) -> p c n", c=8)
                accf, d_acc = cv.get(8 * TF)
                acc3 = accf.rearrange("p (c n) -> p c n", c=8)
                sg, d_sg = cv.get(TF)
                tmp, d_tmp = cv.get(TF)
                wo, d_wo = cv.get(8 * D, F32R)
                wo3 = wo.rearrange("p (c n) -> p c n", c=8)
                xr, d_xr = cv.get(D)
                xo, d_xo = cv.get(D)
                xn, d_xn = cv.get(512)
                ggr, d_ggr = cv.get(D)
                P.dma("act", ggr, ggscr.ap()[cj], reads=[d_gg], writes=[d_ggr])
                P.dma("sp", wo3, w_out.ap()[l].rearrange("(c p) n -> p c n", p=128), writes=[d_wo], r32=True)
                wi = 0
                for t in range(NTF):
                    for br in range(3):
                        ysb3, d_ysb = ysb_l[br % 2]
                        P.dma(k.q(), ysb3, yscr.ap()[br * 4:(br + 1) * 4, :, t * TF:(t + 1) * TF].rearrange("c p n -> p c n"),
                              reads=list(d_yscr[br * 4:(br + 1) * 4]), writes=[d_ysb], r32=True)
                        for dc in range(8):
                            wg3, d_wg = wg_l[wi % 2]
                            wbr3, d_wbr = wbr_l[wi % 2]
                            wi += 1
                            c0 = OFF_GATE + br * D + dc * 128
                            P.dma("sp", wg3, w_in.ap()[l, :, c0:c0 + 128].rearrange("(c p) n -> p c n", p=128),
                                  writes=[d_wg], r32=True)
                            P.dma("sp", wbr3, w_br[br].ap()[l, :, dc * 128:(dc + 1) * 128].rearrange("(c p) n -> p c n", p=128),
                                  writes=[d_wbr], r32=True)
                            pg, pgd = k.ps()
                            for kc in range(8):
                                k.mm(pg[:, 0:TF], wg3[:, kc, :], hT[:, kc, t * TF:(t + 1) * TF], kc == 0, kc == 7, [d_wg, d_hT], [pgd])
                            k.act(sg, pg[:, 0:TF], AF.Sigmoid, [pgd], [d_sg])
                            pa, pad = k.ps()
                            for wc in range(4):
                                k.mm(pa[:, 0:TF], wbr3[:, wc, :], ysb3[:, wc, :], wc == 0, wc == 3, [d_wbr, d_ysb], [pad])
                            if br == 0:
                                k.tt(acc3[:, dc, :], sg, pa[:, 0:TF], ALU.mult, [d_sg, pad], [d_acc])
                            elif br == 1:
                                k.tt(tmp, sg, pa[:, 0:TF], ALU.mult, [d_sg, pad], [d_tmp])
                                k.tt(acc3[:, dc, :], acc3[:, dc, :], tmp, ALU.add, [d_acc, d_tmp], [d_acc], eng="pool")
                            else:
                                k.tt(tmp, sg, pa[:, 0:TF], ALU.mult, [d_sg, pad], [d_tmp])
                                k.tt(mrg3[:, dc, :], acc3[:, dc, :], tmp, ALU.add, [d_acc, d_tmp], [d_mrg], eng="pool")
                    for sub in range(TF // 128):
                        r0 = t * TF + sub * 128
                        P.dma("act", xr, xin[r0:r0 + 128, :], reads=[d_xin], writes=[d_xr])
                        pos = []
                        for half in range(2):
                            po, pod = k.ps()
                            for kc in range(8):
                                k.mm(po[:], mrg3[:, kc, sub * 128:(sub + 1) * 128], wo3[:, kc, half * 512:(half + 1) * 512],
                                     kc == 0, kc == 7, [d_mrg, d_wo], [pod])
                            k.act(xn[:, 0:512], po[:], AF.Square, [pod], [d_xn, d_small], accum=small[:, 4 + half:5 + half])
                            pos.append((po, pod))
                        k.tt(small[:, 6:7], small[:, 4:5], small[:, 5:6], ALU.add, [d_small], [d_small])
                        k.act(small[:, 7:8], small[:, 6:7], AF.Sqrt, [d_small], [d_small], bias=EPS, scale=1.0 / D)
                        k.rcp(small[:, 8:9], small[:, 7:8], [d_small], [d_small])
                        for half in range(2):
                            po, pod = pos[half]
                            hs = slice(half * 512, (half + 1) * 512)
                            k.stt(xo[:, hs], po[:], small[:, 8:9], ggr[:, hs], ALU.mult, ALU.mult, [pod, d_small, d_ggr], [d_xo])
                            k.tt(xo[:, hs], xo[:, hs], xr[:, hs], ALU.add, [d_xo, d_xr], [d_xo], eng="pool")
                        P.dma("act", xout[r0:r0 + 128, :], xo, reads=[d_xo], writes=[d_xout])
        P.emit()
    return nc, k


_CACHE = {}


def _consts():
    def invcnt(T):
        out = np.zeros((4, T), np.float32)
        pos = np.arange(T)
        for gi, win in enumerate(POOL_WINDOWS):
            lo = np.maximum(pos - win // 2, 0)
            hi = np.minimum(pos + win // 2 - 1, T - 1)
            out[gi] = 1.0 / (hi - lo + 1).astype(np.float32)
        return out

    t = np.arange(128)
    same = (t[:, None] // 64) == (t[None, :] // 64)
    cst = np.zeros((128, NCST), np.float32)
    cst[:, 0:128] = same & (t[:, None] <= t[None, :])
    cst[:, 128:256] = same & (t[:, None] >= t[None, :])
    cst[:, 256:384] = same
    cst[:, 384:512] = 1.0
    f = np.arange(64)
    pm = t % 64
    cst[:, 512:576] = (pm[:, None] == f[None, :])
    cst[:, 576] = (pm == 63)
    cst[:, 577] = (pm == 0)
    cst[:, 578] = (t == 63)
    cst[:, 579] = (t == 127)
    cst[:, 580] = (t == 0)
    cst[:, 581] = (t == 64)
    P_ = pm[:, None]
    F_ = f[None, :]
    valid = [[F_ >= P_, F_ <= P_], [F_ > P_, F_ < P_], [F_ < P_, F_ > P_]]
    for ty in range(3):
        for d_ in range(2):
            sign = 1.0 if ty == 2 else -1.0
            cst[:, 582 + (ty * 2 + d_) * 64: 582 + (ty * 2 + d_ + 1) * 64] = np.where(valid[ty][d_], 0.0, sign * BIG)
    cq = np.arange(64)
    csq = np.clip(cq - 8, 0, 48)
    cm = (f[:, None] >= csq[None, :]) & (f[:, None] < csq[None, :] + 16)
    cst[:, 582 + 384:582 + 384 + 64] = np.concatenate([cm, cm], axis=0)
    return {"invcnt_p": invcnt(SEQ), "invcnt_s": invcnt(DSEQ), "ident": np.eye(128, dtype=np.float32), "dncst": cst}


def kernel(x_prompt, x_sample, c, cache_k_na, cache_v_na, state_dn, c_ctx, w_ada, b_ada, g_pre, g_post,
           w_in, conv_dn, a_log_dn, dt_bias_dn, g_norm_dn, na_bias, pool_w, pool_scale,
           w_br_dn, w_br_na, w_br_pl, w_out, _depth=DEPTH, _dn=True, _na=True):
    f = lambda a: np.ascontiguousarray(np.asarray(a, dtype=np.float32))
    key = (_depth, _dn, _na)
    if key not in _CACHE:
        _CACHE[key] = build_program(_depth, _dn, _na)
    nc, k = _CACHE[key]
    cs = _consts()
    dd = _depth
    shared = {"w_ada": f(w_ada[:dd]), "b_ada": f(b_ada[:dd]), "g_pre": f(g_pre[:dd]), "g_post": f(g_post[:dd]), "w_in": f(w_in[:dd]),
              "pool_w": f(pool_w[:dd]), "pool_scale": f(pool_scale[:dd]), "w_br_dn": f(w_br_dn[:dd]), "w_br_na": f(w_br_na[:dd]),
              "w_br_pl": f(w_br_pl[:dd]), "w_out": f(w_out[:dd]), "conv_dn": f(conv_dn[:dd]),
              "a_log": f(a_log_dn[:dd]).reshape(dd, 8), "dt_bias": f(dt_bias_dn[:dd]).reshape(dd, 8), "g_norm": f(g_norm_dn[:dd])}
    shared.update(cs)
    rpad = np.zeros((dd, 8, 15, 127), np.float32)
    rpad[..., 48:79] = f(na_bias[:dd])[..., ::-1]
    x_prompt = f(x_prompt)
    x_sample = f(x_sample)
    in_maps = []
    for core in range(8):
        b = core // 4
        m = dict(shared)
        m["x_p"] = x_prompt[core * NPS:(core + 1) * NPS]
        m["x_s"] = x_sample[b]
        m["cvec"] = np.stack([f(c_ctx), f(c)[b]])
        m["sdn"] = f(state_dn[b, :dd])
        m["ck"] = f(cache_k_na[b, :dd]).reshape(dd, 256, 512)
        m["cvv"] = f(cache_v_na[b, :dd]).reshape(dd, 256, 512)
        m["rpad"] = rpad
        in_maps.append({n: m[n] for n in k.din})
    res = run_bass_kernel_spmd(nc, in_maps, core_ids=list(range(8)))
    r = res.results
    y_p = np.concatenate([r[i]["y_p"] for i in range(8)], axis=0)
    y_s = np.stack([r[0]["y_s"], r[4]["y_s"]])
    n_k = np.concatenate([r[i]["nk"] for i in range(8)], axis=0).reshape(32, dd, SEQ, 8, 64)
    n_v = np.concatenate([r[i]["nv"] for i in range(8)], axis=0).reshape(32, dd, SEQ, 8, 64)
    n_s = np.concatenate([r[i]["nst"] for i in range(8)], axis=0)
    return y_p, y_s, n_k, n_v, n_s
```

```python
import contextlib
import numpy as np
import concourse.bass as bass
import concourse.mybir as mybir
from concourse.ap import AP
from concourse.bass_utils import run_bass_kernel_spmd

F32 = mybir.dt.float32
F32R = mybir.dt.float32r
ALU = mybir.AluOpType
AF = mybir.ActivationFunctionType

ENGS = ("pe", "act", "dve", "pool", "sp")
NDSEM = 12

D = 1024
DEPTH = 4
SEQ = 256
DSEQ = 2048
NIN = 8208
OFF_DN_Z = 1536
OFF_DN_BETA = 2048
OFF_DN_A = 2056
OFF_NA_Q = 2064
OFF_NA_K = 2576
OFF_NA_V = 3088
OFF_NA_Z = 3600
OFF_PL_U = 4112
OFF_PL_Z = 4624
OFF_GATE = 5136
EPS = 1e-6
POOL_WINDOWS = (2, 4, 8, 16)
NPS = 4
NCST = 582 + 6 * 64 + 64 + 6 * 128
BIG = 30000.0


class Dep:
    __slots__ = ("w", "r", "excl")

    def __init__(self, w=None, excl=False):
        self.w = w
        self.r = []
        self.excl = excl


class Op:
    __slots__ = ("eng", "fn", "waits", "signal", "count", "is_dma", "dsem", "dval", "dprev")

    def __init__(self, eng, fn, is_dma=False):
        self.eng = eng
        self.fn = fn
        self.waits = []
        self.signal = False
        self.count = None
        self.is_dma = is_dma
        self.dsem = None
        self.dval = None
        self.dprev = None


class Prog:
    def __init__(self, nc):
        self.nc = nc
        self.ops = {e: [] for e in ENGS}
        self.ndma = {e: 0 for e in ENGS}
        self.dtot = {e: [0] * NDSEM for e in ENGS}
        self.nops = 0

    def _mk(self, eng, fn, reads, writes, is_dma):
        o = Op(eng, fn, is_dma)
        ex = [t for t in reads if t.excl]
        if ex:
            reads = [t for t in reads if not t.excl]
            writes = list(writes) + [t for t in ex if t not in writes]
        deps = []
        seen = set()

        def add(d):
            if d is None or id(d) in seen:
                return
            if (not d.is_dma) and d.eng == "pe" and eng == "pe" and not is_dma:
                return
            seen.add(id(d))
            deps.append(d)

        for t in reads:
            add(t.w)
        for t in writes:
            add(t.w)
            for r in t.r:
                add(r)
        o.waits = deps
        for d in deps:
            d.signal = True
        for t in reads:
            if not is_dma:
                t.r = [x for x in t.r if x.is_dma or x.eng != eng]
            t.r.append(o)
        for t in writes:
            t.w = o
            t.r = []
        self.ops[eng].append(o)
        self.nops += 1
        return o

    def op(self, eng, fn, reads=(), writes=()):
        return self._mk(eng, fn, reads, writes, False)

    def dma(self, eng, out, in_, reads=(), writes=(), r32=False, slow=False):
        nc = self.nc

        def fn(e):
            kw = {}
            if slow:
                kw["allow_slow_non_contiguous"] = True
            if r32:
                nc.dge_precook = False
            ins = e.dma_start(out=out, in_=in_, **kw)
            if r32:
                nc.dge_precook = True
            return ins

        o = self._mk(eng, fn, reads, writes, True)
        i = self.ndma[eng] % NDSEM
        self.ndma[eng] += 1
        o.dsem = i
        o.dprev = self.dtot[eng][i]
        self.dtot[eng][i] += 16
        o.dval = self.dtot[eng][i]
        o.signal = True
        return o

    def emit(self):
        nc = self.nc
        for e in ENGS:
            c = 0
            for o in self.ops[e]:
                if not o.is_dma and o.signal:
                    c += 1
                    o.count = c
        nsig = {e: sum(1 for o in self.ops[e] if (not o.is_dma and o.signal)) for e in ENGS}
        with contextlib.ExitStack() as st:
            esem = {e: st.enter_context(nc.semaphore("s_" + e)) for e in ENGS}
            dsem = {
                e: [st.enter_context(nc.semaphore("d_%s_%d" % (e, i))) for i in range(NDSEM)]
                for e in ("sp", "act", "pool")
            }
            block = st.enter_context(nc.Block())
            ops = self.ops
            dtot = self.dtot

            def run(e, engobj, final=False):
                seen_e = {x: 0 for x in ENGS}
                seen_d = {}
                for o in ops[e]:
                    for d in o.waits:
                        if d.is_dma:
                            key = (d.eng, d.dsem)
                            if seen_d.get(key, 0) >= d.dval:
                                continue
                            engobj.wait_ge(dsem[d.eng][d.dsem], d.dval)
                            seen_d[key] = d.dval
                        else:
                            if seen_e[d.eng] >= d.count:
                                continue
                            engobj.wait_ge(esem[d.eng], d.count)
                            seen_e[d.eng] = d.count
                    if o.is_dma:
                        key = (e, o.dsem)
                        if o.dprev > 0 and seen_d.get(key, 0) < o.dprev:
                            engobj.wait_ge(dsem[e][o.dsem], o.dprev)
                            seen_d[key] = o.dprev
                        ins = o.fn(engobj)
                        ins.then_inc(dsem[e][o.dsem], 16)
                    else:
                        ins = o.fn(engobj)
                        if o.signal:
                            ins.then_inc(esem[e], 1)
                if final:
                    for x in ENGS:
                        if x != e and nsig[x] > 0:
                            engobj.wait_ge(esem[x], nsig[x])
                    for q in ("sp", "act", "pool"):
                        for i in range(NDSEM):
                            if dtot[q][i] > 0:
                                engobj.wait_ge(dsem[q][i], dtot[q][i])

            @block.tensor
            def _(eng):
                run("pe", eng)

            @block.vector
            def _(eng):
                run("dve", eng)

            @block.scalar
            def _(eng):
                run("act", eng)

            @block.gpsimd
            def _(eng):
                run("pool", eng)

            @block.sync
            def _(eng):
                run("sp", eng, final=True)


class K:
    def __init__(self, nc, st):
        self.nc = nc
        self.st = st
        self.P = Prog(nc)
        self.din = {}
        self.dout = {}
        self.psb = [st.enter_context(nc.psum_tensor("psb%d" % i, [128, 512], F32)) for i in range(8)]
        self.psd = [Dep(excl=True) for _ in range(8)]
        self.psi = 0
        self.dq = 0

    def inp(self, name, shape, dt=F32):
        t = self.nc.dram_tensor(name, list(shape), dt, kind="ExternalInput")
        self.din[name] = t
        return t

    def outp(self, name, shape):
        t = self.nc.dram_tensor(name, list(shape), F32, kind="ExternalOutput")
        self.dout[name] = t
        return t

    def scr(self, name, shape, dt=F32):
        return self.nc.dram_tensor(name, list(shape), dt, kind="Internal")

    def sb(self, name, shape, dt=F32):
        return self.st.enter_context(self.nc.sbuf_tensor(name, list(shape), dt))

    def ps(self):
        i = self.psi
        self.psi = (i + 1) % 8
        return self.psb[i], self.psd[i]

    def q(self):
        self.dq ^= 1
        return "sp" if self.dq else "act"

    def mm(self, out, lhsT, rhs, start, stop, reads, writes):
        self.P.op("pe", lambda e: e.matmul(out, lhsT=lhsT, rhs=rhs, start=start, stop=stop), reads, writes)

    def tr(self, out, in_, ident, reads, writes):
        self.P.op("pe", lambda e: e.transpose(out, in_, ident), reads, writes)

    def act(self, out, in_, func, reads, writes, bias=None, scale=1.0, accum=None):
        def fn(e):
            kw = {}
            if bias is not None:
                kw["bias"] = bias
            if accum is not None:
                kw["accum_out"] = accum
            return e.activation(out=out, in_=in_, func=func, scale=scale, **kw)

        self.P.op("act", fn, reads, writes)

    def tt(self, out, in0, in1, op, reads, writes, eng="dve"):
        self.P.op(eng, lambda e: e.tensor_tensor(out=out, in0=in0, in1=in1, op=op), reads, writes)

    def ts(self, out, in0, s1, s2, op0, op1, reads, writes, eng="dve"):
        if s2 is None:
            self.P.op(eng, lambda e: e.tensor_scalar(out=out, in0=in0, scalar1=s1, scalar2=None, op0=op0), reads, writes)
        else:
            self.P.op(eng, lambda e: e.tensor_scalar(out=out, in0=in0, scalar1=s1, scalar2=s2, op0=op0, op1=op1), reads, writes)

    def stt(self, out, in0, scalar, in1, op0, op1, reads, writes, eng="dve"):
        self.P.op(eng, lambda e: e.scalar_tensor_tensor(out=out, in0=in0, scalar=scalar, in1=in1, op0=op0, op1=op1),
                  reads, writes)

    def rcp(self, out, in_, reads, writes):
        self.P.op("dve", lambda e: e.reciprocal(out=out, in_=in_), reads, writes)

    def cp(self, out, in_, reads, writes, eng="dve"):
        self.P.op(eng, lambda e: e.tensor_copy(out=out, in_=in_), reads, writes)

    def ms(self, ap, val, writes, eng="pool"):
        self.P.op(eng, lambda e: e.memset(ap, val), (), writes)


def build_program(depth=DEPTH, do_dn=True, do_na=True):
    nc = bass.Bass("TRN2", target_bir_lowering=False)
    st = contextlib.ExitStack()
    with st:
        k = K(nc, st)
        P = k.P
        x_p = k.inp("x_p", [NPS, SEQ, D])
        x_s = k.inp("x_s", [DSEQ, D])
        cvec = k.inp("cvec", [2, D])
        w_ada = k.inp("w_ada", [depth, D, 3 * D], F32R)
        b_ada = k.inp("b_ada", [depth, 3 * D])
        g_pre = k.inp("g_pre", [depth, D])
        g_post = k.inp("g_post", [depth, D])
        w_in = k.inp("w_in", [depth, D, NIN], F32R)
        pool_w = k.inp("pool_w", [depth, 4, 128, 128], F32R)
        pool_scale = k.inp("pool_scale", [depth, 512])
        w_br = [k.inp(n, [depth, 512, D], F32R) for n in ("w_br_dn", "w_br_na", "w_br_pl")]
        w_out = k.inp("w_out", [depth, D, D], F32R)
        invcnt_p = k.inp("invcnt_p", [4, SEQ])
        invcnt_s = k.inp("invcnt_s", [4, DSEQ])
        ident_in = k.inp("ident", [128, 128])
        conv_dn = k.inp("conv_dn", [depth, 3, 1536])
        a_log = k.inp("a_log", [depth, 8])
        dt_bias = k.inp("dt_bias", [depth, 8])
        g_norm = k.inp("g_norm", [depth, 128])
        sdn = k.inp("sdn", [depth, 2, 4, 128, 128])
        dncst_in = k.inp("dncst", [128, NCST])
        ck_in = k.inp("ck", [depth, 256, 512])
        cv_in = k.inp("cvv", [depth, 256, 512], F32R)
        rpad_in = k.inp("rpad", [depth, 8, 15, 127])

        y_p = k.outp("y_p", [NPS, SEQ, D])
        y_s = k.outp("y_s", [DSEQ, D])
        nk = k.outp("nk", [NPS, depth, SEQ, 512])
        nv = k.outp("nv", [NPS, depth, SEQ, 512])
        d_nkv = Dep()
        nst = k.outp("nst", [NPS, depth, 2, 4, 128, 128])
        d_nst = Dep()

        xs_p = [k.scr("xs_p%d" % i, [NPS, SEQ, D]) for i in range(2)]
        xs_s = [k.scr("xs_s%d" % i, [DSEQ, D]) for i in range(2)]
        yscr = k.scr("yscr", [12, 128, DSEQ], F32R)
        d_xs_p = [[Dep() for _ in range(NPS)] for _ in range(2)]
        d_xs_s = [Dep() for _ in range(2)]
        d_yscr = [Dep() for _ in range(12)]
        gscr = k.scr("gscr", [24, 128, DSEQ])
        d_gscr = [Dep() for _ in range(24)]
        mscr = k.scr("mscr", [8, 128, DSEQ], F32R)
        d_mscr = Dep()

        ident = k.sb("ident_sb", [128, 128])
        d_ident = Dep()
        P.dma("sp", ident[:], ident_in.ap(), writes=[d_ident])
        cst = k.sb("dncst_sb", [128, NCST])
        d_cst = Dep()
        P.dma("act", cst[:], dncst_in.ap(), writes=[d_cst])
        TRI = [cst[:, 0:128], cst[:, 128:256]]
        BLK = cst[:, 256:384]
        ONES = cst[:, 384:512]
        I2 = cst[:, 512:576]
        SEL2 = cst[:, 576:578]
        SELLAST = cst[:, 578:582]
        MASK = [[cst[:, 582 + (ty * 2 + d_) * 64: 582 + (ty * 2 + d_ + 1) * 64] for d_ in range(2)] for ty in range(3)]
        CM = cst[:, 582 + 384:582 + 384 + 64]
        MASKB = [[cst[:, 1030 + (ty * 2 + d_) * 128: 1030 + (ty * 2 + d_ + 1) * 128] for d_ in range(2)] for ty in range(3)]
        hT = k.sb("hT", [128, 8, DSEQ], F32R)
        d_hT = Dep()
        small = k.sb("small", [128, 16])
        d_small = Dep()
        silucT = k.sb("silucT", [128, 8, 2], F32R)
        d_siluc = Dep()
        modcol = k.sb("modcol", [128, 16, 2])
        s1col = k.sb("s1col", [128, 8, 2])
        d_modcol = Dep()
        ggscr = k.scr("ggscr", [2, 128, D])
        d_gg = Dep()
        RSZ = 15 * 1024
        FSZ = 18 * 1024 + 512
        arenaR = k.sb("arenaR", [128, RSZ], F32R)
        arenaF = k.sb("arenaF", [128, FSZ])
        arena_deps = []

        def fence():
            f = P.op("dve", lambda e: e.memset(small[:, 15:16], 0.0), reads=(), writes=list(arena_deps))
            arena_deps.clear()
            return f

        class Carve:
            def __init__(self, seed):
                self.offR = 0
                self.offF = 0
                self.seed = seed

            def get(self, cols, dt=F32):
                if dt == F32R:
                    a = arenaR[:, self.offR:self.offR + cols]
                    self.offR += cols
                    assert self.offR <= RSZ, self.offR
                else:
                    a = arenaF[:, self.offF:self.offF + cols]
                    self.offF += cols
                    assert self.offF <= FSZ, self.offF
                d = Dep(self.seed)
                arena_deps.append(d)
                return a, d

        craw = k.sb("craw", [128, 8, 2])
        for j in range(2):
            P.dma("sp", craw[:, :, j], cvec.ap()[j].rearrange("(c p) -> p c", p=128), writes=[d_siluc], slow=True)
        k.act(silucT[:], craw[:], AF.Silu, [d_siluc], [d_siluc])

        seqs = [("p", i, SEQ) for i in range(NPS)] + [("s", 0, DSEQ)]
        zscr = k.scr("zscr", [depth, 120, 64, 127])
        d_zscr = [Dep() for _ in range(depth)]
        if do_na:
            for l_ in range(depth):
                P.dma("sp", zscr.ap()[l_], AP(rpad_in, l_ * 120 * 127, [[127, 120], [0, 64], [1, 127]]), writes=[d_zscr[l_]])

        def dn_unit(l, kind, si, T, h):
            NT = T // 128
            TT = min(T, 512)
            NTL = T // TT
            fz = fence()
            cv = Carve(fz)
            ws = []
            for off in (0, 512, 1024, OFF_DN_Z):
                wa, dw = cv.get(8 * 128, F32R)
                w3 = wa.rearrange("p (c n) -> p c n", c=8)
                c0 = off + h * 128
                P.dma(k.q(), w3, w_in.ap()[l, :, c0:c0 + 128].rearrange("(c p) n -> p c n", p=128), writes=[dw], r32=True)
                ws.append((w3, dw))
            (wq, d_wq), (wk, d_wk), (wv, d_wv), (wz, d_wz) = ws
            wba_f, d_wba = cv.get(8 * 4, F32R)
            wba = wba_f.rearrange("p (c n) -> p c n", c=8)
            for j4 in range(4):
                cj4 = OFF_DN_BETA + 4 * j4 + h
                P.dma("sp", wba[:, :, j4], w_in.ap()[l, :, cj4].rearrange("(c p) -> p c", p=128), writes=[d_wba], r32=True, slow=True)
            raw, d_raw = cv.get(T + 2)
            qf, d_qf = cv.get(T)
            kf, d_kf = cv.get(T)
            vf, d_vf = cv.get(T)
            oacc, _ = cv.get(T)
            d_oacc = [Dep(fz) for _ in range(NT)]
            arena_deps.extend(d_oacc)
            tmpb, d_tmpb = cv.get(512)
            cw, d_cw = cv.get(9)
            for idx in range(3):
                c0 = idx * 512 + h * 128
                P.dma("act", cw[:, idx * 3:(idx + 1) * 3], conv_dn.ap()[l, :, c0:c0 + 128].rearrange("t p -> p t"),
                      writes=[d_cw], slow=True)
            k.ms(raw[:, 0:1], 0.0, [d_raw])
            k.ms(raw[:, T + 1:T + 2], 0.0, [d_raw])
            for i in range(NT):
                k.ms(oacc[:, i * 128:(i + 1) * 128], 0.0, [d_oacc[i]])
            dsts = [(qf, d_qf), (kf, d_kf), (vf, d_vf)]
            for idx in range(3):
                w3, dw = ws[idx]
                dst, dd = dsts[idx]
                for t in range(NTL):
                    pb, pd = k.ps()
                    for kc in range(8):
                        k.mm(pb[:, 0:TT], w3[:, kc, :], hT[:, kc, t * TT:(t + 1) * TT], kc == 0, kc == 7, [dw, d_hT], [pd])
                    k.cp(raw[:, 1 + t * TT:1 + (t + 1) * TT], pb[:, 0:TT], [pd], [d_raw])
                for t in range(NTL):
                    a = t * TT
                    k.ts(dst[:, a:a + TT], raw[:, a:a + TT], cw[:, idx * 3:idx * 3 + 1], None, ALU.mult, None, [d_raw, d_cw], [dd])
                    k.stt(tmpb[:, 0:TT], raw[:, a + 1:a + 1 + TT], cw[:, idx * 3 + 1:idx * 3 + 2], dst[:, a:a + TT],
                          ALU.mult, ALU.add, [d_raw, d_cw, dd], [d_tmpb])
                    k.stt(dst[:, a:a + TT], raw[:, a + 2:a + 2 + TT], cw[:, idx * 3 + 2:idx * 3 + 3], tmpb[:, 0:TT],
                          ALU.mult, ALU.add, [d_raw, d_cw, d_tmpb], [dd])
                    k.act(dst[:, a:a + TT], dst[:, a:a + TT], AF.Silu, [dd], [dd])
            for idx in range(2):
                dst, dd = dsts[idx]
                for t in range(NTL):
                    a = t * TT
                    k.act(tmpb[:, 0:TT], dst[:, a:a + TT], AF.Square, [dd], [d_tmpb])
                    pb, pd = k.ps()
                    k.mm(pb[:, 0:TT], ONES, tmpb[:, 0:TT], True, True, [d_cst, d_tmpb], [pd])
                    k.act(tmpb[:, 0:TT], pb[:, 0:TT], AF.Sqrt, [pd], [d_tmpb], bias=EPS)
                    k.rcp(tmpb[:, 0:TT], tmpb[:, 0:TT], [d_tmpb], [d_tmpb])
                    if idx == 0:
                        k.stt(dst[:, a:a + TT], dst[:, a:a + TT], 128.0 ** -0.5, tmpb[:, 0:TT], ALU.mult, ALU.mult, [dd, d_tmpb], [dd])
                    else:
                        k.tt(dst[:, a:a + TT], dst[:, a:a + TT], tmpb[:, 0:TT], ALU.mult, [dd, d_tmpb], [dd])
            zs = raw
            d_zs = d_raw
            for t in range(NTL):
                pb, pd = k.ps()
                for kc in range(8):
                    k.mm(pb[:, 0:TT], wz[:, kc, :], hT[:, kc, t * TT:(t + 1) * TT], kc == 0, kc == 7, [d_wz, d_hT], [pd])
                k.act(zs[:, t * TT:(t + 1) * TT], pb[:, 0:TT], AF.Silu, [pd], [d_zs])
            d_g = Dep(fz)
            arena_deps.append(d_g)
            G_ = [d_g]
            ba, _ = cv.get(NT * 4)
            ba3 = ba.rearrange("p (t n) -> p t n", n=4)
            pb, pd = k.ps()
            for i in range(NT):
                for kc in range(8):
                    k.mm(pb[:, i * 4:(i + 1) * 4], hT[:, kc, i * 128:(i + 1) * 128], wba[:, kc, :], kc == 0, kc == 7, [d_wba, d_hT], [pd])
            k.cp(ba, pb[:, 0:NT * 4], [pd], G_)
            NK = 10
            GA, _ = cv.get(NT * NK * 2)
            GA4 = GA.rearrange("p (t k d) -> p t k d", k=NK, d=2)
            K_BETA, K_NEGB, K_G, K_GB, K_EG, K_BG, K_EGL, K_GL = range(8)
            gk = lambda kk: GA4[:, :, kk, :]

            def g2():
                a_, _ = cv.get(NT * 2)
                return a_, a_.rearrange("p (t n) -> p t n", n=2)

            lnb, lnb3 = g2()
            la, la3 = g2()
            gt, gt3 = g2()
            rowc, _ = cv.get(4)
            gn_row, d_gn = cv.get(128)
            P.dma("sp", rowc[:, 0:1], dt_bias.ap()[l, h:h + 1].partition_broadcast(128), writes=G_)
            P.dma("sp", rowc[:, 1:2], dt_bias.ap()[l, 4 + h:5 + h].partition_broadcast(128), writes=G_)
            P.dma("act", rowc[:, 2:3], a_log.ap()[l, h:h + 1].partition_broadcast(128), writes=G_)
            P.dma("act", rowc[:, 3:4], a_log.ap()[l, 4 + h:5 + h].partition_broadcast(128), writes=G_)
            P.dma("sp", gn_row, g_norm.ap()[l].partition_broadcast(128), writes=[d_gn])
            k.act(rowc[:, 2:4], rowc[:, 2:4], AF.Exp, G_, G_)
            k.ts(rowc[:, 2:4], rowc[:, 2:4], -1.0, None, ALU.mult, None, G_, G_)
            k.act(gk(K_BETA), ba3[:, :, 0:2], AF.Sigmoid, G_, G_)
            k.ts(gk(K_NEGB), gk(K_BETA), -1.0, None, ALU.mult, None, G_, G_)
            k.act(gt3, ba3[:, :, 0:2], AF.Exp, G_, G_, scale=-1.0)
            k.act(gt, gt, AF.Ln, G_, G_, bias=1.0)
            k.ts(lnb, gt, -1.0, None, ALU.mult, None, G_, G_)
            k.tt(gt3, ba3[:, :, 2:4], rowc[:, 0:2].unsqueeze(1).to_broadcast([128, NT, 2]), ALU.add, G_, G_)
            k.act(gt, gt, AF.Exp, G_, G_)
            k.act(gt, gt, AF.Ln, G_, G_, bias=1.0)
            k.tt(la3, gt3, rowc[:, 2:4].unsqueeze(1).to_broadcast([128, NT, 2]), ALU.mult, G_, G_)
            pb, pd = k.ps()
            for i in range(NT):
                k.mm(pb[:, i * 2:(i + 1) * 2], TRI[0], la3[:, i, :], True, True, [d_cst, d_g], [pd])
                k.mm(pb[:, 64 + i * 2:64 + (i + 1) * 2], TRI[1], la3[:, i, :], True, True, [d_cst, d_g], [pd])
            k.cp(gk(K_G)[:, :, 0:1], pb[:, 0:NT * 2].rearrange("p (t n) -> p t n", n=2)[:, :, 0:1], [pd], G_)
            k.cp(gk(K_G)[:, :, 1:2], pb[:, 64:64 + NT * 2].rearrange("p (t n) -> p t n", n=2)[:, :, 1:2], [pd], G_)
            k.tt(gk(K_GB), gk(K_G), lnb3, ALU.add, G_, G_)
            k.act(gk(K_EG), gk(K_G), AF.Exp, G_, G_)
            k.tt(gk(K_BG), gk(K_BETA), gk(K_EG), ALU.mult, G_, G_)
            k.tt(gt3, gk(K_G), SEL2.unsqueeze(1).to_broadcast([128, NT, 2]), ALU.mult, G_ + [d_cst], G_)
            pb, pd = k.ps()
            k.mm(pb[:, 0:NT * 2], BLK, gt, True, True, [d_cst, d_g], [pd])
            k.tt(gt3, pb[:, 0:NT * 2].rearrange("p (t n) -> p t n", n=2), gk(K_G), ALU.subtract, [pd, d_g], G_)
            k.act(gk(K_EGL), gt3, AF.Exp, G_, G_)
            gsel2, _ = cv.get(NT * 4)
            k.tt(gsel2.rearrange("p (t d c) -> p t d c", d=2, c=2), gk(K_G).unsqueeze(3).to_broadcast([128, NT, 2, 2]),
                 SELLAST.rearrange("p (d c) -> p d c", d=2).unsqueeze(1).to_broadcast([128, NT, 2, 2]), ALU.mult,
                 G_ + [d_cst], G_)
            pb, pd = k.ps()
            k.mm(pb[:, 0:NT * 4], ONES, gsel2, True, True, [d_cst, d_g], [pd])
            for cp in range(2):
                k.act(gk(K_GL + cp), pb[:, 0:NT * 4].rearrange("p (t d c) -> p t d c", d=2, c=2)[:, :, :, cp], AF.Exp, [pd], G_)
            S = []
            for d_ in range(2):
                s_, ds_ = cv.get(128)
                if kind == "p":
                    k.ms(s_, 0.0, [ds_])
                else:
                    P.dma(k.q(), s_, sdn.ap()[l, d_, h], writes=[ds_])
                S.append((s_, ds_))
            X, d_X = cv.get(256)
            vb, d_vb = cv.get(256)
            Gd, d_Gd = cv.get(256)
            Gbd, d_Gbd = cv.get(256)
            E1, d_E1 = cv.get(256)
            E2, d_E2 = cv.get(256)
            E3, d_E3 = cv.get(256)
            MMs = [cv.get(512), cv.get(512)]
            PT, d_PT = cv.get(256)
            OB = []
            for _i in range(2):
                o_ = {}
                o_["attnT"], o_["d_at"] = cv.get(256)
                o_["kg"], o_["d_kg"] = cv.get(256)
                o_["uw"], o_["d_uw"] = cv.get(512)
                ls_, o_["d_LS"] = cv.get(NK * 2)
                o_["LS3"] = ls_.rearrange("p (k d) -> p k d", d=2)
                OB.append(o_)
            SC = []
            for d_ in range(2):
                SC.append((cv.get(128), cv.get(128), cv.get(128)))
            PB = k.psb
            PD = k.psd

            def lanes(s_):
                return (s_, NT - 1 - s_)

            def prep(s_):
                il = lanes(s_)
                ob = OB[s_ % 2]
                LS3, d_LS = ob["LS3"], ob["d_LS"]
                ls = lambda kk, d_: LS3[:, kk, d_:d_ + 1]
                tsl = [slice(il[d_] * 128, (il[d_] + 1) * 128) for d_ in range(2)]
                for d_ in range(2):
                    k.cp(LS3[:, :, d_], GA4[:, il[d_], :, d_], [d_g], [d_LS], eng="pool")
                pk, pkd = PB[0], PD[0]
                for d_ in range(2):
                    k.tr(pk[:, d_ * 256:d_ * 256 + 128], kf[:, tsl[d_]], ident[:], [d_kf, d_ident], [pkd])
                    k.tr(pk[:, d_ * 256 + 128:d_ * 256 + 256], vf[:, tsl[d_]], ident[:], [d_vf, d_ident], [pkd])
                for d_ in range(2):
                    ds = slice(d_ * 128, (d_ + 1) * 128)
                    k.ts(ob["kg"][:, ds], pk[:, d_ * 256:d_ * 256 + 128], ls(K_EGL, d_), None, ALU.mult, None, [pkd, d_LS], [ob["d_kg"]])
                    k.act(X[:, ds], pk[:, d_ * 256:d_ * 256 + 128], AF.Copy, [pkd, d_LS], [d_X], scale=ls(K_BG, d_))
                    k.ts(vb[:, ds], pk[:, d_ * 256 + 128:d_ * 256 + 256], ls(K_BETA, d_), None, ALU.mult, None, [pkd, d_LS], [d_vb])
                yield
                pa, pad = PB[1], PD[1]
                for d_ in range(2):
                    k.mm(pa[:, d_ * 256:d_ * 256 + 128], kf[:, tsl[d_]], kf[:, tsl[d_]], True, True, [d_kf], [pad])
                    k.mm(pa[:, d_ * 256 + 128:d_ * 256 + 256], kf[:, tsl[d_]], qf[:, tsl[d_]], True, True, [d_kf, d_qf], [pad])
                for d_ in range(2):
                    ds = slice(d_ * 128, (d_ + 1) * 128)
                    k.ts(Gd[:, ds], ident[:], ls(K_G, d_), None, ALU.mult, None, [d_ident, d_LS], [d_Gd], eng="pool")
                    k.ts(Gbd[:, ds], ident[:], ls(K_GB, d_), None, ALU.mult, None, [d_ident, d_LS], [d_Gbd], eng="pool")
                pg, pgd = PB[2], PD[2]
                k.mm(pg[:, 0:256], ONES, Gd, True, True, [d_cst, d_Gd], [pgd])
                k.mm(pg[:, 256:512], ONES, Gbd, True, True, [d_cst, d_Gbd], [pgd])
                yield
                for d_ in range(2):
                    ds = slice(d_ * 128, (d_ + 1) * 128)
                    k.stt(E1[:, ds], pg[:, ds], ls(K_G, d_), MASKB[0][d_], ALU.subtract, ALU.add, [pgd, d_LS, d_cst], [d_E1])
                    k.stt(E2[:, ds], pg[:, 256 + d_ * 128:256 + (d_ + 1) * 128], ls(K_G, d_), MASKB[1][d_], ALU.subtract, ALU.add,
                          [pgd, d_LS, d_cst], [d_E2])
                    k.stt(E3[:, ds], pg[:, ds], ls(K_G, d_), MASKB[2][d_], ALU.subtract, ALU.add, [pgd, d_LS, d_cst], [d_E3])
                k.act(E1, E1, AF.Exp, [d_E1], [d_E1])
                k.act(E2, E2, AF.Exp, [d_E2], [d_E2])
                k.act(E3, E3, AF.Exp, [d_E3], [d_E3], scale=-1.0)
                yield
                (cur, dcur), (nxt, dnxt) = MMs
                for d_ in range(2):
                    ds = slice(d_ * 128, (d_ + 1) * 128)
                    k.tt(ob["attnT"][:, ds], pa[:, d_ * 256 + 128:d_ * 256 + 256], E1[:, ds], ALU.mult, [pad, d_E1], [ob["d_at"]])
                    k.stt(cur[:, d_ * 256 + 128:d_ * 256 + 256], pa[:, d_ * 256:d_ * 256 + 128], -1.0, E2[:, ds], ALU.mult, ALU.mult,
                          [pad, d_E2], [dcur])
                    k.stt(cur[:, d_ * 256:d_ * 256 + 128], pa[:, d_ * 256:d_ * 256 + 128], ls(K_NEGB, d_), E3[:, ds], ALU.mult, ALU.mult,
                          [pad, d_E3, d_LS], [dcur])
                k.tt(PT.rearrange("p (d n) -> p d n", d=2), ident[:].unsqueeze(1).to_broadcast([128, 2, 128]),
                     cur.rearrange("p (d n) -> p d n", d=2)[:, :, 128:256], ALU.add, [d_ident, dcur], [d_PT], eng="pool")
                yield
                for lev in range(5):
                    lastl = (lev == 4)
                    pm, pmd = PB[3], PD[3]
                    for d_ in range(2):
                        k.mm(pm[:, d_ * 256:d_ * 256 + 128], cur[:, d_ * 256 + 128:d_ * 256 + 256], cur[:, d_ * 256:d_ * 256 + 128],
                             True, True, [dcur], [pmd])
                        if not lastl:
                            k.mm(pm[:, d_ * 256 + 128:d_ * 256 + 256], cur[:, d_ * 256:d_ * 256 + 128],
                                 cur[:, d_ * 256 + 128:d_ * 256 + 256], True, True, [dcur], [pmd])
                    if lastl:
                        k.act(nxt.rearrange("p (d n) -> p d n", d=2)[:, :, 0:128], pm.rearrange("p (d n) -> p d n", d=2)[:, :, 0:128],
                              AF.Copy, [pmd], [dnxt])
                    else:
                        k.act(nxt, pm[:, 0:512], AF.Copy, [pmd], [dnxt])
                    yield
                    pp, ppd = PB[0], PD[0]
                    for d_ in range(2):
                        ds = slice(d_ * 128, (d_ + 1) * 128)
                        k.mm(pp[:, ds], nxt[:, d_ * 256:d_ * 256 + 128], PT[:, ds], True, True, [dnxt, d_PT], [ppd])
                    k.tt(PT, PT, pp[:, 0:256], ALU.add, [d_PT, ppd], [d_PT])
                    yield
                    cur, dcur, nxt, dnxt = nxt, dnxt, cur, dcur
                pu, pud = PB[2], PD[2]
                for d_ in range(2):
                    ds = slice(d_ * 128, (d_ + 1) * 128)
                    k.mm(pu[:, d_ * 256:d_ * 256 + 128], PT[:, ds], vb[:, ds], True, True, [d_PT, d_vb], [pud])
                    k.mm(pu[:, d_ * 256 + 128:d_ * 256 + 256], X[:, ds], PT[:, ds], True, True, [d_X, d_PT], [pud])
                k.cp(ob["uw"], pu[:, 0:512], [pud], [ob["d_uw"]])
                yield

            def scan(s_, d_):
                i = lanes(s_)[d_]
                ob = OB[s_ % 2]
                LS3, d_LS = ob["LS3"], ob["d_LS"]
                ds = slice(d_ * 128, (d_ + 1) * 128)
                es = slice(d_ * 64, (d_ + 1) * 64)
                (vnew, d_vn), (oasb, d_oa), (otmp, d_ot) = SC[d_]
                s_t, ds_ = S[d_]
                p1, p1d = PB[4 + 2 * d_], PD[4 + 2 * d_]
                p2, p2d = PB[5 + 2 * d_], PD[5 + 2 * d_]
                for cp in ((0, 1) if d_ == 0 else (1, 0)):
                    bs = slice(cp * 64, (cp + 1) * 64)
                    cs = slice(i * 128 + cp * 64, i * 128 + (cp + 1) * 64)
                    k.mm(p1[bs, 0:128], ob["uw"][:, d_ * 256 + 128 + cp * 64:d_ * 256 + 128 + (cp + 1) * 64], s_t, True, True,
                         [ob["d_uw"], ds_], [p1d])
                    k.mm(p1[bs, 128:256], qf[:, cs], s_t, True, True, [d_qf, ds_], [p1d])
                    k.tt(vnew[bs, :], ob["uw"][bs, d_ * 256:d_ * 256 + 128], p1[bs, 0:128], ALU.subtract, [ob["d_uw"], p1d], [d_vn])
                    yield
                    k.mm(p2[bs, 0:128], ob["attnT"][bs, d_ * 128 + cp * 64:d_ * 128 + (cp + 1) * 64], vnew[bs, :], True, True, [ob["d_at"], d_vn], [p2d])
                    k.mm(p2[:, 128:256], ob["kg"][bs, ds], vnew[bs, :], True, True, [ob["d_kg"], d_vn], [p2d])
                    k.act(oasb[bs, :], p2[bs, 0:128], AF.Copy, [p2d], [d_oa])
                    k.stt(s_t, s_t, LS3[:, K_GL + cp, d_:d_ + 1], p2[:, 128:256], ALU.mult, ALU.add, [ds_, d_LS, p2d], [ds_])
                    yield
                    k.stt(otmp[bs, :], p1[bs, 128:256], LS3[bs, K_EG, d_:d_ + 1], oasb[bs, :], ALU.mult, ALU.add,
                          [p1d, d_LS, d_oa], [d_ot])
                    k.tt(oacc[bs, i * 128:(i + 1) * 128], oacc[bs, i * 128:(i + 1) * 128], otmp[bs, :], ALU.add,
                         [d_oacc[i], d_ot], [d_oacc[i]], eng="pool")
                    yield

            def run_gens(gens):
                gens = list(gens)
                while gens:
                    for g_ in list(gens):
                        try:
                            next(g_)
                        except StopIteration:
                            gens.remove(g_)

            run_gens([prep(0)])
            for s_ in range(NT):
                gl_ = [scan(s_, 0), scan(s_, 1)]
                if s_ + 1 < NT:
                    gl_.insert(0, prep(s_ + 1))
                run_gens(gl_)
            if kind == "p":
                for d_ in range(2):
                    P.dma(k.q(), nst.ap()[si, l, d_, h], S[d_][0], reads=[S[d_][1]], writes=[d_nst])
            rs, d_rs = cv.get(4)
            on, d_on = cv.get(128)
            yT, d_yT = cv.get(128, F32R)
            for i in range(NT):
                tsl = slice(i * 128, (i + 1) * 128)
                k.act(on, oacc[:, tsl], AF.Square, [d_oacc[i]], [d_on, d_rs], accum=rs[:, 0:1])
                k.act(rs[:, 1:2], rs[:, 0:1], AF.Sqrt, [d_rs], [d_rs], bias=EPS, scale=1.0 / 128)
                k.rcp(rs[:, 2:3], rs[:, 1:2], [d_rs], [d_rs])
                k.stt(on, oacc[:, tsl], rs[:, 2:3], gn_row, ALU.mult, ALU.mult, [d_oacc[i], d_rs, d_gn, d_on], [d_on])
                pt, ptd = k.ps()
                k.tr(pt[:, 0:128], on, ident[:], [d_on, d_ident], [ptd])
                k.tt(yT, pt[:, 0:128], zs[:, tsl], ALU.mult, [ptd, d_zs], [d_yT])
                P.dma(k.q(), yscr.ap()[h, :, tsl], yT, reads=[d_yT], writes=[d_yscr[h]], r32=True)

        def na_unit(l, kind, si, T, hp):
            NT = T // 128
            TT = min(T, 512)
            fz = fence()
            cv = Carve(fz)
            ws = []
            for off in (OFF_NA_Q, OFF_NA_K, OFF_NA_V, OFF_NA_Z):
                wa, dw = cv.get(8 * 128, F32R)
                w3 = wa.rearrange("p (c n) -> p c n", c=8)
                c0 = off + hp * 128
                P.dma(k.q(), w3, w_in.ap()[l, :, c0:c0 + 128].rearrange("(c p) n -> p c n", p=128), writes=[dw], r32=True)
                ws.append((w3, dw))
            (wq, d_wq), (wk, d_wk), (wv, d_wv), (wz, d_wz) = ws
            qT, d_qT = cv.get(T, F32R)
            kT, d_kT = cv.get(T, F32R)
            vt_f, d_vt = cv.get(NT * 132, F32R)
            vtok = vt_f.rearrange("p (t h e) -> p t h e", t=NT, h=2)
            zs, d_zs = cv.get(T)
            ones, d_ones = cv.get(2)
            stg, d_stg = cv.get(256)
            k.ms(ones, 1.0, [d_ones])
            for t in range(T // TT):
                ts_ = slice(t * TT, (t + 1) * TT)
                pb, pd = k.ps()
                for kc in range(8):
                    k.mm(pb[:, 0:TT], wq[:, kc, :], hT[:, kc, ts_], kc == 0, kc == 7, [d_wq, d_hT], [pd])
                k.ts(qT[:, ts_], pb[:, 0:TT], 0.125, None, ALU.mult, None, [pd], [d_qT])
                pb, pd = k.ps()
                for kc in range(8):
                    k.mm(pb[:, 0:TT], wk[:, kc, :], hT[:, kc, ts_], kc == 0, kc == 7, [d_wk, d_hT], [pd])
                k.cp(kT[:, ts_], pb[:, 0:TT], [pd], [d_kT])
                pb, pd = k.ps()
                for kc in range(8):
                    k.mm(pb[:, 0:TT], wz[:, kc, :], hT[:, kc, ts_], kc == 0, kc == 7, [d_wz, d_hT], [pd])
                k.act(zs[:, ts_], pb[:, 0:TT], AF.Silu, [pd], [d_zs])
            for i in range(NT):
                is_ = slice(i * 128, (i + 1) * 128)
                pb, pd = k.ps()
                for kc in range(8):
                    k.mm(pb[:, 0:128], hT[:, kc, is_], wv[:, kc, :], kc == 0, kc == 7, [d_wv, d_hT], [pd])
                k.cp(vtok[:, i, :, 0:64], pb[:, 0:128].rearrange("p (h e) -> p h e", h=2), [pd], [d_vt])
                k.cp(vtok[:, i, :, 64:66], ones[:, 0:2].unsqueeze(1).to_broadcast([128, 2, 2]), [d_ones], [d_vt], eng="pool")
                if kind == "p":
                    k.cp(stg[:, 0:128], pb[:, 0:128], [pd], [d_stg])
                    P.dma("act", nv.ap()[si, l, is_, hp * 128:(hp + 1) * 128], stg[:, 0:128], reads=[d_stg], writes=[d_nkv])
                    pb, pd = k.ps()
                    for kc in range(8):
                        k.mm(pb[:, 0:128], hT[:, kc, is_], wk[:, kc, :], kc == 0, kc == 7, [d_wk, d_hT], [pd])
                    k.cp(stg[:, 128:256], pb[:, 0:128], [pd], [d_stg])
                    P.dma("act", nk.ap()[si, l, is_, hp * 128:(hp + 1) * 128], stg[:, 128:256], reads=[d_stg], writes=[d_nkv])
            otok, d_otok = cv.get(128)
            rden, d_rden = cv.get(2)
            yT, d_yT = cv.get(128, F32R)
            if kind == "p":
                PT = [cv.get(256, F32R) for _ in range(4)]
                po, pod = k.ps()
                for h2 in range(2):
                    hs = slice(h2 * 64, (h2 + 1) * 64)
                    for kt in range(2):
                        pt, d_pt = PT[h2 * 2 + kt]
                        pb, pd = k.ps()
                        k.mm(pb[:, 0:256], kT[hs, kt * 128:(kt + 1) * 128], qT[hs, 0:256], True, True, [d_kT, d_qT], [pd])
                        k.act(pt, pb[:, 0:256], AF.Exp, [pd], [d_pt])
                    for qt in range(2):
                        for kt in range(2):
                            pt, d_pt = PT[h2 * 2 + kt]
                            c0 = qt * 132 + h2 * 66
                            k.mm(po[:, c0:c0 + 66], pt[:, qt * 128:(qt + 1) * 128], vtok[:, kt, h2, :], kt == 0, kt == 1,
                                 [d_pt, d_vt], [pod])
                for qt in range(2):
                    qs = slice(qt * 128, (qt + 1) * 128)
                    for h2 in range(2):
                        c0 = qt * 132 + h2 * 66
                        k.rcp(rden[:, h2:h2 + 1], po[:, c0 + 64:c0 + 65], [pod], [d_rden])
                        k.ts(otok[:, h2 * 64:(h2 + 1) * 64], po[:, c0:c0 + 64], rden[:, h2:h2 + 1], None, ALU.mult, None,
                             [pod, d_rden], [d_otok])
                    pb, pd = k.ps()
                    k.tr(pb[:, 0:128], otok, ident[:], [d_otok, d_ident], [pd])
                    k.tt(yT, pb[:, 0:128], zs[:, qs], ALU.mult, [pd, d_zs], [d_yT])
                    P.dma(k.q(), yscr.ap()[4 + hp, :, qs], yT, reads=[d_yT], writes=[d_yscr[4 + hp]], r32=True)
            else:
                kcT, d_kcT = cv.get(256, F32R)
                vc_f, d_vc = cv.get(2 * 132, F32R)
                vctx = vc_f.rearrange("p (t h e) -> p t h e", t=2, h=2)
                cst_k, d_cstk = cv.get(256)
                P.dma("sp", cst_k.rearrange("p (t n) -> p t n", t=2),
                      ck_in.ap()[l, :, hp * 128:(hp + 1) * 128].rearrange("(t p) n -> p t n", p=128), writes=[d_cstk])
                pb, pd = k.ps()
                for t in range(2):
                    k.tr(pb[:, t * 128:(t + 1) * 128], cst_k[:, t * 128:(t + 1) * 128], ident[:], [d_cstk, d_ident], [pd])
                k.cp(kcT, pb[:, 0:256], [pd], [d_kcT])
                for t in range(2):
                    P.dma("act", vctx[:, t, :, 0:64],
                          cv_in.ap()[l, t * 128:(t + 1) * 128, hp * 128:(hp + 1) * 128].rearrange("p (h e) -> p h e", h=2),
                          writes=[d_vc], r32=True)
                    k.cp(vctx[:, t, :, 64:66], ones[:, 0:2].unsqueeze(1).to_broadcast([128, 2, 2]), [d_ones], [d_vc], eng="pool")
                E2f, d_E2 = cv.get(2 * 15 * 64)
                E2 = E2f.rearrange("p (h r c) -> p h r c", h=2, r=15)
                for a in range(2):
                    for h2 in range(2):
                        head = hp * 2 + h2
                        src = AP(zscr, ((l * 8 + head) * 15) * 8128 + 63, [[126, 64], [8128, 15], [1, 64]])
                        P.dma("sp" if a == 0 else "act", E2[a * 64:(a + 1) * 64, h2, :, :], src, reads=[d_zscr[l]], writes=[d_E2])
                k.act(E2f, E2f, AF.Exp, [d_E2], [d_E2])
                k.tt(E2f.rearrange("p (g c) -> p g c", c=64), E2f.rearrange("p (g c) -> p g c", c=64),
                     CM.unsqueeze(1).to_broadcast([128, 30, 64]), ALU.mult, [d_E2, d_cst], [d_E2])
                TABf, d_TAB = cv.get(2 * 21 * 128)
                TAB = TABf.rearrange("p (h t q) -> p h t q", h=2, t=21)
                k.ms(TABf, 0.0, [d_TAB])
                plans = {}
                tid = 0
                plans["int"] = []
                for j in range(5):
                    for a in range(2):
                        for b in range(2):
                            dr = 2 * j - 4 + a - b
                            if -4 <= dr <= 3:
                                plans["int"].append((tid, a, b, dr))
                    tid += 1
                tid0 = {"int": 0}
                for m_ in (0, 1, 14, 15):
                    tid0[m_] = tid
                    kt0 = 0 if m_ < 2 else 12
                    plans[m_] = []
                    for j in range(4):
                        for a in range(2):
                            for b in range(2):
                                kr = 2 * (kt0 + j) + a
                                r = 2 * m_ + b
                                rs_ = min(max(r - 4, 0), 24)
                                if rs_ <= kr <= rs_ + 7:
                                    plans[m_].append((tid, a, b, kr - r))
                        tid += 1
                assert tid == 21
                for key_, pl in plans.items():
                    for (tid_, a, b, dr) in pl:
                        k.cp(TAB[a * 64:(a + 1) * 64, :, tid_, b * 64:(b + 1) * 64], E2[a * 64:(a + 1) * 64, :, dr + 7, :],
                             [d_E2], [d_TAB], eng="pool")
                PTf, d_PT = cv.get(7 * 128, F32R)
                tmpE, d_tmpE = cv.get(512)
                tmpE2, d_tmpE2 = cv.get(128)
                for m_ in range(16):
                    qs = slice(m_ * 128, (m_ + 1) * 128)
                    if 2 <= m_ <= 13:
                        kts = [m_ - 2 + j for j in range(5)]
                        t0 = 0
                    else:
                        kts = [(0 if m_ < 2 else 12) + j for j in range(4)]
                        t0 = tid0[m_]
                    nl = len(kts)
                    po, pod = k.ps()
                    for h2 in range(2):
                        hs = slice(h2 * 64, (h2 + 1) * 64)
                        pA, pAd = k.ps()
                        for j in range(4):
                            k.mm(pA[:, j * 128:(j + 1) * 128], kT[hs, kts[j] * 128:(kts[j] + 1) * 128], qT[hs, qs], True, True,
                                 [d_kT, d_qT], [pAd])
                        pB, pBd = k.ps()
                        c_ = 0
                        if nl == 5:
                            k.mm(pB[:, 0:128], kT[hs, kts[4] * 128:(kts[4] + 1) * 128], qT[hs, qs], True, True, [d_kT, d_qT], [pBd])
                            c_ = 128
                        for t in range(2):
                            k.mm(pB[:, c_ + t * 128:c_ + (t + 1) * 128], kcT[hs, t * 128:(t + 1) * 128], qT[hs, qs], True, True,
                                 [d_kcT, d_qT], [pBd])
                        k.act(tmpE, pA[:, 0:512], AF.Exp, [pAd], [d_tmpE])
                        k.tt(PTf[:, 0:512], tmpE, TABf[:, (h2 * 21 + t0) * 128:(h2 * 21 + t0 + 4) * 128], ALU.mult,
                             [d_tmpE, d_TAB], [d_PT])
                        if nl == 5:
                            k.act(tmpE2, pB[:, 0:128], AF.Exp, [pBd], [d_tmpE2])
                            k.tt(PTf[:, 512:640], tmpE2, TABf[:, (h2 * 21 + 4) * 128:(h2 * 21 + 5) * 128], ALU.mult,
                                 [d_tmpE2, d_TAB], [d_PT])
                        k.act(PTf[:, nl * 128:(nl + 2) * 128], pB[:, c_:c_ + 256], AF.Exp, [pBd], [d_PT])
                        c0 = h2 * 66
                        ntile = nl + 2
                        for j in range(ntile):
                            if j < nl:
                                rhs_ = vtok[:, kts[j], h2, :]
                                rd = d_vt
                            else:
                                rhs_ = vctx[:, j - nl, h2, :]
                                rd = d_vc
                            k.mm(po[:, c0:c0 + 66], PTf[:, j * 128:(j + 1) * 128], rhs_, j == 0, j == ntile - 1, [d_PT, rd], [pod])
                    for h2 in range(2):
                        c0 = h2 * 66
                        k.rcp(rden[:, h2:h2 + 1], po[:, c0 + 64:c0 + 65], [pod], [d_rden])
                        k.ts(otok[:, h2 * 64:(h2 + 1) * 64], po[:, c0:c0 + 64], rden[:, h2:h2 + 1], None, ALU.mult, None,
                             [pod, d_rden], [d_otok])
                    pb, pd = k.ps()
                    k.tr(pb[:, 0:128], otok, ident[:], [d_otok, d_ident], [pd])
                    k.tt(yT, pb[:, 0:128], zs[:, qs], ALU.mult, [pd, d_zs], [d_yT])
                    P.dma(k.q(), yscr.ap()[4 + hp, :, qs], yT, reads=[d_yT], writes=[d_yscr[4 + hp]], r32=True)

        for l in range(depth):
            last = (l == depth - 1)
            fz = fence()
            cv = Carve(fz)
            bcol, d_bcol = cv.get(16)
            gpre_col, _d = cv.get(8)
            brow, _d = cv.get(D)
            gpost_row, _d = cv.get(D)
            ggt, d_ggt = cv.get(512)
            wada_f, d_wada = cv.get(8 * 512, F32R)
            wada = wada_f.rearrange("p (c n) -> p c n", c=8)
            sbc_f, d_sbc = cv.get(8 * 2 * 128, F32R)
            siluc_bc = sbc_f.rearrange("p (c j n) -> p c j n", c=8, j=2)
            for kc in range(8):
                for j in range(2):
                    k.cp(siluc_bc[:, kc, j, :], silucT[:, kc, j:j + 1].bitcast(F32).to_broadcast([128, 128]), [d_siluc], [d_sbc])
            P.dma("sp", bcol, b_ada.ap()[l, 0:2048].rearrange("(c p) -> p c", p=128), writes=[d_bcol], slow=True)
            P.dma("act", gpre_col, g_pre.ap()[l].rearrange("(c p) -> p c", p=128), writes=[d_bcol], slow=True)
            P.dma("sp", brow, b_ada.ap()[l, 2048:3072].partition_broadcast(128), writes=[d_bcol])
            P.dma("act", gpost_row, g_post.ap()[l].partition_broadcast(128), writes=[d_bcol])
            for blk in range(6):
                P.dma(k.q(), wada, w_ada.ap()[l, :, blk * 512:(blk + 1) * 512].rearrange("(c p) n -> p c n", p=128),
                      writes=[d_wada], r32=True)
                if blk < 4:
                    pb, pd = k.ps()
                    for cc in range(4):
                        for kc in range(8):
                            k.mm(pb[:, cc * 2:cc * 2 + 2], wada[:, kc, cc * 128:(cc + 1) * 128], silucT[:, kc, :],
                                 kc == 0, kc == 7, [d_wada, d_siluc], [pd])
                    k.tt(modcol[:, blk * 4:blk * 4 + 4, :], pb[:, 0:8].rearrange("p (c j) -> p c j", j=2),
                         bcol[:, blk * 4:blk * 4 + 4].unsqueeze(2).to_broadcast([128, 4, 2]), ALU.add,
                         [pd, d_bcol], [d_modcol])
                else:
                    for j in range(2):
                        pb, pd = k.ps()
                        for kc in range(8):
                            k.mm(pb[:], siluc_bc[:, kc, j, :], wada[:, kc, :], kc == 0, kc == 7, [d_wada, d_sbc], [pd])
                        c0 = (blk - 4) * 512
                        k.tt(ggt[:, 0:512], pb[:], brow[:, c0:c0 + 512], ALU.add, [pd, d_bcol], [d_ggt])
                        k.tt(ggt[:, 0:512], ggt[:, 0:512], gpost_row[:, c0:c0 + 512], ALU.mult, [d_ggt, d_bcol], [d_ggt])
                        P.dma("sp", ggscr.ap()[j, :, c0:c0 + 512], ggt[:, 0:512], reads=[d_ggt], writes=[d_gg])
            for j in range(2):
                k.stt(s1col[:, :, j], modcol[:, 8:16, j], 1.0, gpre_col, ALU.add, ALU.mult, [d_modcol, d_bcol], [d_modcol])

            for (kind, si, T) in seqs:
                cj = 0 if kind == "p" else 1
                TT = min(T, 512)
                NTL = T // TT
                if l == 0:
                    xin = x_p.ap()[si] if kind == "p" else x_s.ap()
                    d_xin = Dep()
                else:
                    xin = xs_p[(l - 1) % 2].ap()[si] if kind == "p" else xs_s[(l - 1) % 2].ap()
                    d_xin = d_xs_p[(l - 1) % 2][si] if kind == "p" else d_xs_s[(l - 1) % 2]
                if last:
                    xout = y_p.ap()[si] if kind == "p" else y_s.ap()
                    d_xout = Dep()
                else:
                    xout = xs_p[l % 2].ap()[si] if kind == "p" else xs_s[l % 2].ap()
                    d_xout = d_xs_p[l % 2][si] if kind == "p" else d_xs_s[l % 2]

                fz = fence()
                cv = Carve(fz)
                _xt, _dxt = cv.get(D)
                xt = [_xt, _xt]
                d_xt = [_dxt, _dxt]
                xn, d_xn = cv.get(D)
                for i in range(T // 128):
                    b = i % 2
                    P.dma(k.q(), xt[b], xin[i * 128:(i + 1) * 128, :], reads=[d_xin], writes=[d_xt[b]])
                    k.act(xn, xt[b], AF.Square, [d_xt[b]], [d_xn, d_small], accum=small[:, 0:1])
                    k.act(small[:, 1:2], small[:, 0:1], AF.Sqrt, [d_small], [d_small], bias=EPS, scale=1.0 / D)
                    k.rcp(small[:, 2:3], small[:, 1:2], [d_small], [d_small])
                    k.ts(xn, xt[b], small[:, 2:3], None, ALU.mult, None, [d_xt[b], d_small], [d_xn])
                    for half in range(2):
                        pb, pd = k.ps()
                        for c4 in range(4):
                            kc = half * 4 + c4
                            k.tr(pb[:, c4 * 128:(c4 + 1) * 128], xn[:, kc * 128:(kc + 1) * 128], ident[:], [d_xn, d_ident], [pd])
                        for c4 in range(4):
                            kc = half * 4 + c4
                            k.ts(hT[:, kc, i * 128:(i + 1) * 128], pb[:, c4 * 128:(c4 + 1) * 128],
                                 s1col[:, kc, cj:cj + 1], modcol[:, kc, cj:cj + 1], ALU.mult, ALU.add,
                                 [pd, d_modcol], [d_hT], eng=("dve" if c4 % 2 == 0 else "pool") if False else "dve")

                fz = fence()
                cv = Carve(fz)
                zbuf, d_z = cv.get(TT, F32R)
                zsrc, d_zs0 = cv.get(TT)
                k.ms(zsrc, 0.0, [d_zs0])
                k.cp(zbuf, zsrc, [d_zs0], [d_z])
                for ch in range(8):
                    if (ch < 4 and not do_dn) or (ch >= 4 and not do_na):
                        for t in range(NTL):
                            P.dma(k.q(), yscr.ap()[ch, :, t * TT:(t + 1) * TT], zbuf, reads=[d_z], writes=[d_yscr[ch]], r32=True)

                if do_dn:
                    for h in range(4):
                        dn_unit(l, kind, si, T, h)

                if do_na:
                    for hp in range(4):
                        na_unit(l, kind, si, T, hp)

                for g in range(4):
                    win = POOL_WINDOWS[g]
                    fz = fence()
                    cv = Carve(fz)
                    wu, d_wu = cv.get(8 * 128, F32R)
                    wz, d_wz = cv.get(8 * 128, F32R)
                    pw, d_pw = cv.get(128, F32R)
                    psc, d_psc = cv.get(1)
                    U, d_U = cv.get(T + 16)
                    zs, d_zs = cv.get(T)
                    s_a, d_sa = cv.get(T + 16)
                    s_b, d_sb = cv.get(T + 16)
                    icnt, d_icnt = cv.get(T)
                    pooled, d_pooled = cv.get(T, F32R)
                    yT, d_yT = cv.get(TT, F32R)
                    wu3 = wu.rearrange("p (c n) -> p c n", c=8)
                    wz3 = wz.rearrange("p (c n) -> p c n", c=8)
                    cu = OFF_PL_U + g * 128
                    cz = OFF_PL_Z + g * 128
                    P.dma("sp", wu3, w_in.ap()[l, :, cu:cu + 128].rearrange("(c p) n -> p c n", p=128), writes=[d_wu], r32=True)
                    P.dma("act", wz3, w_in.ap()[l, :, cz:cz + 128].rearrange("(c p) n -> p c n", p=128), writes=[d_wz], r32=True)
                    P.dma("sp", pw, pool_w.ap()[l, g], writes=[d_pw], r32=True)
                    P.dma("act", psc, pool_scale.ap()[l, g * 128:(g + 1) * 128].rearrange("(p o) -> p o", o=1), writes=[d_psc], slow=True)
                    ic_src = (invcnt_p if kind == "p" else invcnt_s).ap()[g]
                    P.dma("sp", icnt, ic_src.partition_broadcast(128), writes=[d_icnt])
                    k.ms(U[:, 0:8], 0.0, [d_U])
                    k.ms(U[:, T + 8:T + 16], 0.0, [d_U])
                    for t in range(NTL):
                        pb, pd = k.ps()
                        for kc in range(8):
                            k.mm(pb[:, 0:TT], wu3[:, kc, :], hT[:, kc, t * TT:(t + 1) * TT], kc == 0, kc == 7, [d_wu, d_hT], [pd])
                        k.cp(U[:, 8 + t * TT:8 + (t + 1) * TT], pb[:, 0:TT], [pd], [d_U])
                        pb, pd = k.ps()
                        for kc in range(8):
                            k.mm(pb[:, 0:TT], wz3[:, kc, :], hT[:, kc, t * TT:(t + 1) * TT], kc == 0, kc == 7, [d_wz, d_hT], [pd])
                        k.act(zs[:, t * TT:(t + 1) * TT], pb[:, 0:TT], AF.Silu, [pd], [d_zs])
                    cur, dcur, curlen = U, d_U, T + 16
                    step = 1
                    bufs = [(s_a, d_sa), (s_b, d_sb)]
                    bi = 0
                    while step < win:
                        nb, dnb = bufs[bi]
                        bi ^= 1
                        nlen = curlen - step
                        k.tt(nb[:, 0:nlen], cur[:, 0:nlen], cur[:, step:step + nlen], ALU.add, [dcur], [dnb])
                        cur, dcur, curlen = nb, dnb, nlen
                        step *= 2
                    o0 = 8 - win // 2
                    nb, dnb = bufs[bi]
                    k.tt(nb[:, 0:T], cur[:, o0:o0 + T], icnt, ALU.mult, [dcur, d_icnt], [dnb])
                    k.tt(pooled, nb[:, 0:T], U[:, 8:8 + T], ALU.subtract, [dnb, d_U], [d_pooled])
                    for t in range(NTL):
                        pb, pd = k.ps()
                        k.mm(pb[:, 0:TT], pw, pooled[:, t * TT:(t + 1) * TT], True, True, [d_pw, d_pooled], [pd])
                        k.stt(yT, pb[:, 0:TT], psc[:, 0:1], zs[:, t * TT:(t + 1) * TT], ALU.mult, ALU.mult, [pd, d_psc, d_zs], [d_yT])
                        P.dma(k.q(), yscr.ap()[8 + g, :, t * TT:(t + 1) * TT], yT, reads=[d_yT], writes=[d_yscr[8 + g]], r32=True)

                for u in range(6):
                    fz = fence()
                    cv = Carve(fz)
                    wgu_f, d_wgu = cv.get(8 * 512, F32R)
                    wgu = wgu_f.rearrange("p (c n) -> p c n", c=8)
                    c0 = OFF_GATE + u * 512
                    P.dma(k.q(), wgu, w_in.ap()[l, :, c0:c0 + 512].rearrange("(c p) n -> p c n", p=128), writes=[d_wgu], r32=True)
                    gsbs = [cv.get(TT) for _ in range(2)]
                    gi = 0
                    for t in range(NTL):
                        for c4 in range(4):
                            gsb, d_gsb = gsbs[gi % 2]
                            gi += 1
                            pg_, pgd_ = k.ps()
                            for kc in range(8):
                                k.mm(pg_[:, 0:TT], wgu[:, kc, c4 * 128:(c4 + 1) * 128], hT[:, kc, t * TT:(t + 1) * TT], kc == 0, kc == 7,
                                     [d_wgu, d_hT], [pgd_])
                            k.act(gsb, pg_[:, 0:TT], AF.Sigmoid, [pgd_], [d_gsb])
                            P.dma(k.q(), gscr.ap()[u * 4 + c4, :, t * TT:(t + 1) * TT], gsb, reads=[d_gsb], writes=[d_gscr[u * 4 + c4]])

                fz = fence()
                cv = Carve(fz)
                ysb_l = []
                wbr_l = []
                gin_l = []
                mo_l = []
                for _i in range(2):
                    _a, _d = cv.get(4 * TT, F32R)
                    ysb_l.append((_a.rearrange("p (c n) -> p c n", c=4), _d))
                    _a, _d = cv.get(4 * D, F32R)
                    wbr_l.append((_a.rearrange("p (c n) -> p c n", c=4), _d))
                    _a, _d = cv.get(8 * TT)
                    gin_l.append((_a.rearrange("p (c n) -> p c n", c=8), _d))
                    mo_l.append(cv.get(TT, F32R))
                accf, d_acc = cv.get(8 * TT)
                acc3 = accf.rearrange("p (c n) -> p c n", c=8)
                tmp, d_tmp = cv.get(TT)
                bi = 0
                mi = 0
                for t in range(NTL):
                    tsl_ = slice(t * TT, (t + 1) * TT)
                    for br in range(3):
                        ysb3, d_ysb = ysb_l[bi % 2]
                        wbr3, d_wbr = wbr_l[bi % 2]
                        gin3, d_gin = gin_l[bi % 2]
                        bi += 1
                        P.dma("sp", ysb3, yscr.ap()[br * 4:(br + 1) * 4, :, tsl_].rearrange("c p n -> p c n"),
                              reads=list(d_yscr[br * 4:(br + 1) * 4]), writes=[d_ysb], r32=True)
                        P.dma("act", gin3, gscr.ap()[br * 8:(br + 1) * 8, :, tsl_].rearrange("c p n -> p c n"),
                              reads=list(d_gscr[br * 8:(br + 1) * 8]), writes=[d_gin])
                        P.dma("sp", wbr3, w_br[br].ap()[l].rearrange("(c p) n -> p c n", p=128), writes=[d_wbr], r32=True)
                        for dc in range(8):
                            pa, pad = k.ps()
                            for wc in range(4):
                                k.mm(pa[:, 0:TT], wbr3[:, wc, dc * 128:(dc + 1) * 128], ysb3[:, wc, :], wc == 0, wc == 3, [d_wbr, d_ysb], [pad])
                            if br == 0:
                                k.tt(acc3[:, dc, :], gin3[:, dc, :], pa[:, 0:TT], ALU.mult, [d_gin, pad], [d_acc])
                            elif br == 1:
                                k.tt(tmp, gin3[:, dc, :], pa[:, 0:TT], ALU.mult, [d_gin, pad], [d_tmp])
                                k.tt(acc3[:, dc, :], acc3[:, dc, :], tmp, ALU.add, [d_acc, d_tmp], [d_acc], eng="pool")
                            else:
                                mo, d_mo = mo_l[mi % 2]
                                mi += 1
                                k.tt(tmp, gin3[:, dc, :], pa[:, 0:TT], ALU.mult, [d_gin, pad], [d_tmp])
                                k.tt(mo, acc3[:, dc, :], tmp, ALU.add, [d_acc, d_tmp], [d_mo], eng="pool")
                                P.dma("act", mscr.ap()[dc, :, tsl_], mo, reads=[d_mo], writes=[d_mscr], r32=True)

                fz = fence()
                cv = Carve(fz)
                wo, d_wo = cv.get(8 * D, F32R)
                wo3 = wo.rearrange("p (c n) -> p c n", c=8)
                mt_l = []
                for _i in range(2):
                    _a, _d = cv.get(8 * 128, F32R)
                    mt_l.append((_a.rearrange("p (c n) -> p c n", c=8), _d))
                xr_l = [cv.get(D) for _ in range(2)]
                xo_l = [cv.get(D) for _ in range(2)]
                xn, d_xn = cv.get(512)
                ggr, d_ggr = cv.get(D)
                P.dma("act", ggr, ggscr.ap()[cj], reads=[d_gg], writes=[d_ggr])
                P.dma("sp", wo3, w_out.ap()[l].rearrange("(c p) n -> p c n", p=128), writes=[d_wo], r32=True)
                for sub in range(T // 128):
                    r0 = sub * 128
                    mt3, d_mt = mt_l[sub % 2]
                    xr, d_xr = xr_l[sub % 2]
                    xo, d_xo = xo_l[sub % 2]
                    P.dma("sp", mt3, mscr.ap()[:, :, r0:r0 + 128].rearrange("c p n -> p c n"), reads=[d_mscr], writes=[d_mt], r32=True)
                    P.dma("act", xr, xin[r0:r0 + 128, :], reads=[d_xin], writes=[d_xr])
                    pos = []
                    for half in range(2):
                        po, pod = k.ps()
                        for kc in range(8):
                            k.mm(po[:], mt3[:, kc, :], wo3[:, kc, half * 512:(half + 1) * 512], kc == 0, kc == 7, [d_mt, d_wo], [pod])
                        k.act(xn[:, 0:512], po[:], AF.Square, [pod], [d_xn, d_small], accum=small[:, 4 + half:5 + half])
                        pos.append((po, pod))
                    k.tt(small[:, 6:7], small[:, 4:5], small[:, 5:6], ALU.add, [d_small], [d_small])
                    k.act(small[:, 7:8], small[:, 6:7], AF.Sqrt, [d_small], [d_small], bias=EPS, scale=1.0 / D)
                    k.rcp(small[:, 8:9], small[:, 7:8], [d_small], [d_small])
                    for half in range(2):
                        po, pod = pos[half]
                        hs = slice(half * 512, (half + 1) * 512)
                        k.stt(xo[:, hs], po[:], small[:, 8:9], ggr[:, hs], ALU.mult, ALU.mult, [pod, d_small, d_ggr], [d_xo])
                        k.tt(xo[:, hs], xo[:, hs], xr[:, hs], ALU.add, [d_xo, d_xr], [d_xo], eng="pool")
                    P.dma("sp", xout[r0:r0 + 128, :], xo, reads=[d_xo], writes=[d_xout])
        P.emit()
    return nc, k


_CACHE = {}


def _consts():
    def invcnt(T):
        out = np.zeros((4, T), np.float32)
        pos = np.arange(T)
        for gi, win in enumerate(POOL_WINDOWS):
            lo = np.maximum(pos - win // 2, 0)
            hi = np.minimum(pos + win // 2 - 1, T - 1)
            out[gi] = 1.0 / (hi - lo + 1).astype(np.float32)
        return out

    t = np.arange(128)
    same = (t[:, None] // 64) == (t[None, :] // 64)
    cst = np.zeros((128, NCST), np.float32)
    cst[:, 0:128] = same & (t[:, None] <= t[None, :])
    cst[:, 128:256] = same & (t[:, None] >= t[None, :])
    cst[:, 256:384] = same
    cst[:, 384:512] = 1.0
    f = np.arange(64)
    pm = t % 64
    cst[:, 512:576] = (pm[:, None] == f[None, :])
    cst[:, 576] = (pm == 63)
    cst[:, 577] = (pm == 0)
    cst[:, 578] = (t == 63)
    cst[:, 579] = (t == 127)
    cst[:, 580] = (t == 0)
    cst[:, 581] = (t == 64)
    P_ = pm[:, None]
    F_ = f[None, :]
    valid = [[F_ >= P_, F_ <= P_], [F_ > P_, F_ < P_], [F_ < P_, F_ > P_]]
    for ty in range(3):
        for d_ in range(2):
            sign = 1.0 if ty == 2 else -1.0
            cst[:, 582 + (ty * 2 + d_) * 64: 582 + (ty * 2 + d_ + 1) * 64] = np.where(valid[ty][d_], 0.0, sign * BIG)
    cq = np.arange(64)
    csq = np.clip(cq - 8, 0, 48)
    cm = (f[:, None] >= csq[None, :]) & (f[:, None] < csq[None, :] + 16)
    cst[:, 582 + 384:582 + 384 + 64] = np.concatenate([cm, cm], axis=0)
    Pf = t[:, None]
    Ff = t[None, :]
    validb = [[Ff >= Pf, Ff <= Pf], [Ff > Pf, Ff < Pf], [Ff < Pf, Ff > Pf]]
    for ty in range(3):
        for d_ in range(2):
            sign = 1.0 if ty == 2 else -1.0
            c0 = 1030 + (ty * 2 + d_) * 128
            cst[:, c0:c0 + 128] = np.where(validb[ty][d_] & same, 0.0, sign * BIG)
    return {"invcnt_p": invcnt(SEQ), "invcnt_s": invcnt(DSEQ), "ident": np.eye(128, dtype=np.float32), "dncst": cst}


def kernel(x_prompt, x_sample, c, cache_k_na, cache_v_na, state_dn, c_ctx, w_ada, b_ada, g_pre, g_post,
           w_in, conv_dn, a_log_dn, dt_bias_dn, g_norm_dn, na_bias, pool_w, pool_scale,
           w_br_dn, w_br_na, w_br_pl, w_out, _depth=DEPTH, _dn=True, _na=True):
    f = lambda a: np.ascontiguousarray(np.asarray(a, dtype=np.float32))
    key = (_depth, _dn, _na)
    if key not in _CACHE:
        _CACHE[key] = build_program(_depth, _dn, _na)
    nc, k = _CACHE[key]
    cs = _consts()
    dd = _depth
    shared = {"w_ada": f(w_ada[:dd]), "b_ada": f(b_ada[:dd]), "g_pre": f(g_pre[:dd]), "g_post": f(g_post[:dd]), "w_in": f(w_in[:dd]),
              "pool_w": f(pool_w[:dd]), "pool_scale": f(pool_scale[:dd]), "w_br_dn": f(w_br_dn[:dd]), "w_br_na": f(w_br_na[:dd]),
              "w_br_pl": f(w_br_pl[:dd]), "w_out": f(w_out[:dd]), "conv_dn": f(conv_dn[:dd]),
              "a_log": f(a_log_dn[:dd]).reshape(dd, 8), "dt_bias": f(dt_bias_dn[:dd]).reshape(dd, 8), "g_norm": f(g_norm_dn[:dd])}
    shared.update(cs)
    rpad = np.zeros((dd, 8, 15, 127), np.float32)
    rpad[..., 48:79] = f(na_bias[:dd])[..., ::-1]
    x_prompt = f(x_prompt)
    x_sample = f(x_sample)
    in_maps = []
    for core in range(8):
        b = core // 4
        m = dict(shared)
        m["x_p"] = x_prompt[core * NPS:(core + 1) * NPS]
        m["x_s"] = x_sample[b]
        m["cvec"] = np.stack([f(c_ctx), f(c)[b]])
        m["sdn"] = f(state_dn[b, :dd])
        m["ck"] = f(cache_k_na[b, :dd]).reshape(dd, 256, 512)
        m["cvv"] = f(cache_v_na[b, :dd]).reshape(dd, 256, 512)
        m["rpad"] = rpad
        in_maps.append({n: m[n] for n in k.din})
    res = run_bass_kernel_spmd(nc, in_maps, core_ids=list(range(8)))
    r = res.results
    y_p = np.concatenate([r[i]["y_p"] for i in range(8)], axis=0)
    y_s = np.stack([r[0]["y_s"], r[4]["y_s"]])
    n_k = np.concatenate([r[i]["nk"] for i in range(8)], axis=0).reshape(32, dd, SEQ, 8, 64)
    n_v = np.concatenate([r[i]["nv"] for i in range(8)], axis=0).reshape(32, dd, SEQ, 8, 64)
    n_s = np.concatenate([r[i]["nst"] for i in range(8)], axis=0)
    return y_p, y_s, n_k, n_v, n_s
```

```python
import contextlib
import numpy as np
import concourse.bass as bass
import concourse.mybir as mybir
from concourse.ap import AP
from concourse.bass_utils import run_bass_kernel_spmd

F32 = mybir.dt.float32
F32R = mybir.dt.float32r
ALU = mybir.AluOpType
AF = mybir.ActivationFunctionType

ENGS = ("pe", "act", "dve", "pool", "sp")
NDSEM = 12

D = 1024
DEPTH = 4
SEQ = 256
DSEQ = 2048
NIN = 8208
OFF_DN_Z = 1536
OFF_DN_BETA = 2048
OFF_DN_A = 2056
OFF_NA_Q = 2064
OFF_NA_K = 2576
OFF_NA_V = 3088
OFF_NA_Z = 3600
OFF_PL_U = 4112
OFF_PL_Z = 4624
OFF_GATE = 5136
EPS = 1e-6
POOL_WINDOWS = (2, 4, 8, 16)
NPS = 4
NCST = 582 + 6 * 64 + 64 + 6 * 128
BIG = 30000.0


class Dep:
    __slots__ = ("w", "r", "excl")

    def __init__(self, w=None, excl=False):
        self.w = w
        self.r = []
        self.excl = excl


class Op:
    __slots__ = ("eng", "fn", "waits", "signal", "count", "is_dma", "dsem", "dval", "dprev")

    def __init__(self, eng, fn, is_dma=False):
        self.eng = eng
        self.fn = fn
        self.waits = []
        self.signal = False
        self.count = None
        self.is_dma = is_dma
        self.dsem = None
        self.dval = None
        self.dprev = None


class Prog:
    def __init__(self, nc):
        self.nc = nc
        self.ops = {e: [] for e in ENGS}
        self.ndma = {e: 0 for e in ENGS}
        self.dtot = {e: [0] * NDSEM for e in ENGS}
        self.nops = 0

    def _mk(self, eng, fn, reads, writes, is_dma):
        o = Op(eng, fn, is_dma)
        ex = [t for t in reads if t.excl]
        if ex:
            reads = [t for t in reads if not t.excl]
            writes = list(writes) + [t for t in ex if t not in writes]
        deps = []
        seen = set()

        def add(d):
            if d is None or id(d) in seen:
                return
            if (not d.is_dma) and d.eng == "pe" and eng == "pe" and not is_dma:
                return
            seen.add(id(d))
            deps.append(d)

        for t in reads:
            add(t.w)
        for t in writes:
            add(t.w)
            for r in t.r:
                add(r)
        o.waits = deps
        for d in deps:
            d.signal = True
        for t in reads:
            if not is_dma:
                t.r = [x for x in t.r if x.is_dma or x.eng != eng]
            t.r.append(o)
        for t in writes:
            t.w = o
            t.r = []
        self.ops[eng].append(o)
        self.nops += 1
        return o

    def op(self, eng, fn, reads=(), writes=()):
        return self._mk(eng, fn, reads, writes, False)

    def dma(self, eng, out, in_, reads=(), writes=(), r32=False, slow=False):
        nc = self.nc
        eng = "pool" if type(out.tensor).__name__.startswith("DRam") else "sp"

        def fn(e):
            kw = {}
            if slow:
                kw["allow_slow_non_contiguous"] = True
            if r32:
                nc.dge_precook = False
            ins = e.dma_start(out=out, in_=in_, **kw)
            if r32:
                nc.dge_precook = True
            return ins

        o = self._mk(eng, fn, reads, writes, True)
        i = self.ndma[eng] % NDSEM
        self.ndma[eng] += 1
        o.dsem = i
        o.dprev = self.dtot[eng][i]
        self.dtot[eng][i] += 16
        o.dval = self.dtot[eng][i]
        o.signal = True
        return o

    def emit(self):
        nc = self.nc
        for e in ENGS:
            c = 0
            for o in self.ops[e]:
                if not o.is_dma and o.signal:
                    c += 1
                    o.count = c
        nsig = {e: sum(1 for o in self.ops[e] if (not o.is_dma and o.signal)) for e in ENGS}
        with contextlib.ExitStack() as st:
            esem = {e: st.enter_context(nc.semaphore("s_" + e)) for e in ENGS}
            dsem = {
                e: [st.enter_context(nc.semaphore("d_%s_%d" % (e, i))) for i in range(NDSEM)]
                for e in ("sp", "act", "pool")
            }
            block = st.enter_context(nc.Block())
            ops = self.ops
            dtot = self.dtot

            def run(e, engobj, final=False):
                seen_e = {x: 0 for x in ENGS}
                seen_d = {}
                for o in ops[e]:
                    for d in o.waits:
                        if d.is_dma:
                            key = (d.eng, d.dsem)
                            if seen_d.get(key, 0) >= d.dval:
                                continue
                            engobj.wait_ge(dsem[d.eng][d.dsem], d.dval)
                            seen_d[key] = d.dval
                        else:
                            if seen_e[d.eng] >= d.count:
                                continue
                            engobj.wait_ge(esem[d.eng], d.count)
                            seen_e[d.eng] = d.count
                    if o.is_dma:
                        key = (e, o.dsem)
                        if o.dprev > 0 and seen_d.get(key, 0) < o.dprev:
                            engobj.wait_ge(dsem[e][o.dsem], o.dprev)
                            seen_d[key] = o.dprev
                        ins = o.fn(engobj)
                        ins.then_inc(dsem[e][o.dsem], 16)
                    else:
                        ins = o.fn(engobj)
                        if o.signal:
                            ins.then_inc(esem[e], 1)
                if final:
                    for x in ENGS:
                        if x != e and nsig[x] > 0:
                            engobj.wait_ge(esem[x], nsig[x])
                    for q in ("sp", "act", "pool"):
                        for i in range(NDSEM):
                            if dtot[q][i] > 0:
                                engobj.wait_ge(dsem[q][i], dtot[q][i])

            @block.tensor
            def _(eng):
                run("pe", eng)

            @block.vector
            def _(eng):
                run("dve", eng)

            @block.scalar
            def _(eng):
                run("act", eng)

            @block.gpsimd
            def _(eng):
                run("pool", eng)

            @block.sync
            def _(eng):
                run("sp", eng, final=True)


class K:
    def __init__(self, nc, st):
        self.nc = nc
        self.st = st
        self.P = Prog(nc)
        self.din = {}
        self.dout = {}
        self.psb = [st.enter_context(nc.psum_tensor("psb%d" % i, [128, 512], F32)) for i in range(8)]
        self.psd = [Dep(excl=True) for _ in range(8)]
        self.psi = 0
        self.dq = 0

    def inp(self, name, shape, dt=F32):
        t = self.nc.dram_tensor(name, list(shape), dt, kind="ExternalInput")
        self.din[name] = t
        return t

    def outp(self, name, shape):
        t = self.nc.dram_tensor(name, list(shape), F32, kind="ExternalOutput")
        self.dout[name] = t
        return t

    def scr(self, name, shape, dt=F32):
        return self.nc.dram_tensor(name, list(shape), dt, kind="Internal")

    def sb(self, name, shape, dt=F32):
        return self.st.enter_context(self.nc.sbuf_tensor(name, list(shape), dt))

    def ps(self):
        i = self.psi
        self.psi = (i + 1) % 8
        return self.psb[i], self.psd[i]

    def q(self):
        self.dq ^= 1
        return "sp" if self.dq else "act"

    def mm(self, out, lhsT, rhs, start, stop, reads, writes):
        self.P.op("pe", lambda e: e.matmul(out, lhsT=lhsT, rhs=rhs, start=start, stop=stop), reads, writes)

    def tr(self, out, in_, ident, reads, writes):
        self.P.op("pe", lambda e: e.transpose(out, in_, ident), reads, writes)

    def act(self, out, in_, func, reads, writes, bias=None, scale=1.0, accum=None):
        def fn(e):
            kw = {}
            if bias is not None:
                kw["bias"] = bias
            if accum is not None:
                kw["accum_out"] = accum
            return e.activation(out=out, in_=in_, func=func, scale=scale, **kw)

        self.P.op("act", fn, reads, writes)

    def tt(self, out, in0, in1, op, reads, writes, eng="dve"):
        self.P.op(eng, lambda e: e.tensor_tensor(out=out, in0=in0, in1=in1, op=op), reads, writes)

    def ts(self, out, in0, s1, s2, op0, op1, reads, writes, eng="dve"):
        if s2 is None:
            self.P.op(eng, lambda e: e.tensor_scalar(out=out, in0=in0, scalar1=s1, scalar2=None, op0=op0), reads, writes)
        else:
            self.P.op(eng, lambda e: e.tensor_scalar(out=out, in0=in0, scalar1=s1, scalar2=s2, op0=op0, op1=op1), reads, writes)

    def stt(self, out, in0, scalar, in1, op0, op1, reads, writes, eng="dve"):
        self.P.op(eng, lambda e: e.scalar_tensor_tensor(out=out, in0=in0, scalar=scalar, in1=in1, op0=op0, op1=op1),
                  reads, writes)

    def rcp(self, out, in_, reads, writes):
        self.P.op("dve", lambda e: e.reciprocal(out=out, in_=in_), reads, writes)

    def cp(self, out, in_, reads, writes, eng="dve"):
        self.P.op(eng, lambda e: e.tensor_copy(out=out, in_=in_), reads, writes)

    def ms(self, ap, val, writes, eng="pool"):
        self.P.op(eng, lambda e: e.memset(ap, val), (), writes)


def build_program(depth=DEPTH, do_dn=True, do_na=True):
    nc = bass.Bass("TRN2", target_bir_lowering=False)
    st = contextlib.ExitStack()
    with st:
        k = K(nc, st)
        P = k.P
        x_p = k.inp("x_p", [NPS, SEQ, D])
        x_s = k.inp("x_s", [DSEQ, D])
        cvec = k.inp("cvec", [2, D])
        w_ada = k.inp("w_ada", [depth, D, 3 * D], F32R)
        b_ada = k.inp("b_ada", [depth, 3 * D])
        g_pre = k.inp("g_pre", [depth, D])
        g_post = k.inp("g_post", [depth, D])
        w_in = k.inp("w_in", [depth, D, NIN], F32R)
        pool_w = k.inp("pool_w", [depth, 4, 128, 128], F32R)
        pool_scale = k.inp("pool_scale", [depth, 512])
        w_br = [k.inp(n, [depth, 512, D], F32R) for n in ("w_br_dn", "w_br_na", "w_br_pl")]
        w_out = k.inp("w_out", [depth, D, D], F32R)
        invcnt_p = k.inp("invcnt_p", [4, SEQ])
        invcnt_s = k.inp("invcnt_s", [4, DSEQ])
        ident_in = k.inp("ident", [128, 128])
        conv_dn = k.inp("conv_dn", [depth, 3, 1536])
        a_log = k.inp("a_log", [depth, 8])
        dt_bias = k.inp("dt_bias", [depth, 8])
        g_norm = k.inp("g_norm", [depth, 128])
        sdn = k.inp("sdn", [depth, 2, 4, 128, 128])
        dncst_in = k.inp("dncst", [128, NCST])
        ck_in = k.inp("ck", [depth, 256, 512])
        cv_in = k.inp("cvv", [depth, 256, 512], F32R)
        rpad_in = k.inp("rpad", [depth, 8, 15, 127])

        y_p = k.outp("y_p", [NPS, SEQ, D])
        y_s = k.outp("y_s", [DSEQ, D])
        nk = k.outp("nk", [NPS, depth, SEQ, 512])
        nv = k.outp("nv", [NPS, depth, SEQ, 512])
        d_nkv = Dep()
        nst = k.outp("nst", [NPS, depth, 2, 4, 128, 128])
        d_nst = Dep()

        xs_p = [k.scr("xs_p%d" % i, [NPS, SEQ, D]) for i in range(2)]
        xs_s = [k.scr("xs_s%d" % i, [DSEQ, D]) for i in range(2)]
        yscr = k.scr("yscr", [12, 128, DSEQ], F32R)
        d_xs_p = [[Dep() for _ in range(NPS)] for _ in range(2)]
        d_xs_s = [Dep() for _ in range(2)]
        d_yscr = [Dep() for _ in range(12)]
        gscr = k.scr("gscr", [24, 128, DSEQ])
        d_gscr = [Dep() for _ in range(24)]
        mscr = k.scr("mscr", [8, 128, DSEQ], F32R)
        d_mscr = Dep()

        ident = k.sb("ident_sb", [128, 128])
        d_ident = Dep()
        P.dma("sp", ident[:], ident_in.ap(), writes=[d_ident])
        cst = k.sb("dncst_sb", [128, NCST])
        d_cst = Dep()
        P.dma("act", cst[:], dncst_in.ap(), writes=[d_cst])
        TRI = [cst[:, 0:128], cst[:, 128:256]]
        BLK = cst[:, 256:384]
        ONES = cst[:, 384:512]
        I2 = cst[:, 512:576]
        SEL2 = cst[:, 576:578]
        SELLAST = cst[:, 578:582]
        MASK = [[cst[:, 582 + (ty * 2 + d_) * 64: 582 + (ty * 2 + d_ + 1) * 64] for d_ in range(2)] for ty in range(3)]
        CM = cst[:, 582 + 384:582 + 384 + 64]
        MASKB = [[cst[:, 1030 + (ty * 2 + d_) * 128: 1030 + (ty * 2 + d_ + 1) * 128] for d_ in range(2)] for ty in range(3)]
        hT = k.sb("hT", [128, 8, DSEQ], F32R)
        d_hT = Dep()
        small = k.sb("small", [128, 16])
        d_small = Dep()
        silucT = k.sb("silucT", [128, 8, 2], F32R)
        d_siluc = Dep()
        modcol = k.sb("modcol", [128, 16, 2])
        s1col = k.sb("s1col", [128, 8, 2])
        d_modcol = Dep()
        ggscr = k.scr("ggscr", [2, 128, D])
        d_gg = Dep()
        RSZ = 15 * 1024
        FSZ = 18 * 1024 + 512
        arenaR = k.sb("arenaR", [128, RSZ], F32R)
        arenaF = k.sb("arenaF", [128, FSZ])
        arena_deps = []

        def fence():
            f = P.op("dve", lambda e: e.memset(small[:, 15:16], 0.0), reads=(), writes=list(arena_deps))
            arena_deps.clear()
            return f

        class Carve:
            def __init__(self, seed):
                self.offR = 0
                self.offF = 0
                self.seed = seed

            def get(self, cols, dt=F32):
                if dt == F32R:
                    a = arenaR[:, self.offR:self.offR + cols]
                    self.offR += cols
                    assert self.offR <= RSZ, self.offR
                else:
                    a = arenaF[:, self.offF:self.offF + cols]
                    self.offF += cols
                    assert self.offF <= FSZ, self.offF
                d = Dep(self.seed)
                arena_deps.append(d)
                return a, d

        craw = k.sb("craw", [128, 8, 2])
        for j in range(2):
            P.dma("sp", craw[:, :, j], cvec.ap()[j].rearrange("(c p) -> p c", p=128), writes=[d_siluc], slow=True)
        k.act(silucT[:], craw[:], AF.Silu, [d_siluc], [d_siluc])

        seqs = [("p", i, SEQ) for i in range(NPS)] + [("s", 0, DSEQ)]
        zscr = k.scr("zscr", [depth, 120, 64, 127])
        d_zscr = [Dep() for _ in range(depth)]
        if do_na:
            for l_ in range(depth):
                P.dma("sp", zscr.ap()[l_], AP(rpad_in, l_ * 120 * 127, [[127, 120], [0, 64], [1, 127]]), writes=[d_zscr[l_]])

        def dn_unit(l, kind, si, T, h):
            NT = T // 128
            TT = min(T, 512)
            NTL = T // TT
            fz = fence()
            cv = Carve(fz)
            ws = []
            for off in (0, 512, 1024, OFF_DN_Z):
                wa, dw = cv.get(8 * 128, F32R)
                w3 = wa.rearrange("p (c n) -> p c n", c=8)
                c0 = off + h * 128
                P.dma(k.q(), w3, w_in.ap()[l, :, c0:c0 + 128].rearrange("(c p) n -> p c n", p=128), writes=[dw], r32=True)
                ws.append((w3, dw))
            (wq, d_wq), (wk, d_wk), (wv, d_wv), (wz, d_wz) = ws
            wba_f, d_wba = cv.get(8 * 4, F32R)
            wba = wba_f.rearrange("p (c n) -> p c n", c=8)
            for j4 in range(4):
                cj4 = OFF_DN_BETA + 4 * j4 + h
                P.dma("sp", wba[:, :, j4], w_in.ap()[l, :, cj4].rearrange("(c p) -> p c", p=128), writes=[d_wba], r32=True, slow=True)
            raw, d_raw = cv.get(T + 2)
            qf, d_qf = cv.get(T)
            kf, d_kf = cv.get(T)
            vf, d_vf = cv.get(T)
            oacc, _ = cv.get(T)
            d_oacc = [Dep(fz) for _ in range(NT)]
            arena_deps.extend(d_oacc)
            tmpb, d_tmpb = cv.get(512)
            cw, d_cw = cv.get(9)
            for idx in range(3):
                c0 = idx * 512 + h * 128
                P.dma("act", cw[:, idx * 3:(idx + 1) * 3], conv_dn.ap()[l, :, c0:c0 + 128].rearrange("t p -> p t"),
                      writes=[d_cw], slow=True)
            k.ms(raw[:, 0:1], 0.0, [d_raw])
            k.ms(raw[:, T + 1:T + 2], 0.0, [d_raw])
            for i in range(NT):
                k.ms(oacc[:, i * 128:(i + 1) * 128], 0.0, [d_oacc[i]])
            dsts = [(qf, d_qf), (kf, d_kf), (vf, d_vf)]
            for idx in range(3):
                w3, dw = ws[idx]
                dst, dd = dsts[idx]
                for t in range(NTL):
                    pb, pd = k.ps()
                    for kc in range(8):
                        k.mm(pb[:, 0:TT], w3[:, kc, :], hT[:, kc, t * TT:(t + 1) * TT], kc == 0, kc == 7, [dw, d_hT], [pd])
                    k.cp(raw[:, 1 + t * TT:1 + (t + 1) * TT], pb[:, 0:TT], [pd], [d_raw])
                for t in range(NTL):
                    a = t * TT
                    k.ts(dst[:, a:a + TT], raw[:, a:a + TT], cw[:, idx * 3:idx * 3 + 1], None, ALU.mult, None, [d_raw, d_cw], [dd])
                    k.stt(tmpb[:, 0:TT], raw[:, a + 1:a + 1 + TT], cw[:, idx * 3 + 1:idx * 3 + 2], dst[:, a:a + TT],
                          ALU.mult, ALU.add, [d_raw, d_cw, dd], [d_tmpb])
                    k.stt(dst[:, a:a + TT], raw[:, a + 2:a + 2 + TT], cw[:, idx * 3 + 2:idx * 3 + 3], tmpb[:, 0:TT],
                          ALU.mult, ALU.add, [d_raw, d_cw, d_tmpb], [dd])
                    k.act(dst[:, a:a + TT], dst[:, a:a + TT], AF.Silu, [dd], [dd])
            for idx in range(2):
                dst, dd = dsts[idx]
                for t in range(NTL):
                    a = t * TT
                    k.act(tmpb[:, 0:TT], dst[:, a:a + TT], AF.Square, [dd], [d_tmpb])
                    pb, pd = k.ps()
                    k.mm(pb[:, 0:TT], ONES, tmpb[:, 0:TT], True, True, [d_cst, d_tmpb], [pd])
                    k.act(tmpb[:, 0:TT], pb[:, 0:TT], AF.Sqrt, [pd], [d_tmpb], bias=EPS)
                    k.rcp(tmpb[:, 0:TT], tmpb[:, 0:TT], [d_tmpb], [d_tmpb])
                    if idx == 0:
                        k.stt(dst[:, a:a + TT], dst[:, a:a + TT], 128.0 ** -0.5, tmpb[:, 0:TT], ALU.mult, ALU.mult, [dd, d_tmpb], [dd])
                    else:
                        k.tt(dst[:, a:a + TT], dst[:, a:a + TT], tmpb[:, 0:TT], ALU.mult, [dd, d_tmpb], [dd])
            zs = raw
            d_zs = d_raw
            for t in range(NTL):
                pb, pd = k.ps()
                for kc in range(8):
                    k.mm(pb[:, 0:TT], wz[:, kc, :], hT[:, kc, t * TT:(t + 1) * TT], kc == 0, kc == 7, [d_wz, d_hT], [pd])
                k.act(zs[:, t * TT:(t + 1) * TT], pb[:, 0:TT], AF.Silu, [pd], [d_zs])
            d_g = Dep(fz)
            arena_deps.append(d_g)
            G_ = [d_g]
            ba, _ = cv.get(NT * 4)
            ba3 = ba.rearrange("p (t n) -> p t n", n=4)
            pb, pd = k.ps()
            for i in range(NT):
                for kc in range(8):
                    k.mm(pb[:, i * 4:(i + 1) * 4], hT[:, kc, i * 128:(i + 1) * 128], wba[:, kc, :], kc == 0, kc == 7, [d_wba, d_hT], [pd])
            k.cp(ba, pb[:, 0:NT * 4], [pd], G_)
            NK = 10
            GA, _ = cv.get(NT * NK * 2)
            GA4 = GA.rearrange("p (t k d) -> p t k d", k=NK, d=2)
            K_BETA, K_NEGB, K_G, K_GB, K_EG, K_BG, K_EGL, K_GL = range(8)
            gk = lambda kk: GA4[:, :, kk, :]

            def g2():
                a_, _ = cv.get(NT * 2)
                return a_, a_.rearrange("p (t n) -> p t n", n=2)

            lnb, lnb3 = g2()
            la, la3 = g2()
            gt, gt3 = g2()
            rowc, _ = cv.get(4)
            gn_row, d_gn = cv.get(128)
            P.dma("sp", rowc[:, 0:1], dt_bias.ap()[l, h:h + 1].partition_broadcast(128), writes=G_)
            P.dma("sp", rowc[:, 1:2], dt_bias.ap()[l, 4 + h:5 + h].partition_broadcast(128), writes=G_)
            P.dma("act", rowc[:, 2:3], a_log.ap()[l, h:h + 1].partition_broadcast(128), writes=G_)
            P.dma("act", rowc[:, 3:4], a_log.ap()[l, 4 + h:5 + h].partition_broadcast(128), writes=G_)
            P.dma("sp", gn_row, g_norm.ap()[l].partition_broadcast(128), writes=[d_gn])
            k.act(rowc[:, 2:4], rowc[:, 2:4], AF.Exp, G_, G_)
            k.ts(rowc[:, 2:4], rowc[:, 2:4], -1.0, None, ALU.mult, None, G_, G_)
            k.act(gk(K_BETA), ba3[:, :, 0:2], AF.Sigmoid, G_, G_)
            k.ts(gk(K_NEGB), gk(K_BETA), -1.0, None, ALU.mult, None, G_, G_)
            k.act(gt3, ba3[:, :, 0:2], AF.Exp, G_, G_, scale=-1.0)
            k.act(gt, gt, AF.Ln, G_, G_, bias=1.0)
            k.ts(lnb, gt, -1.0, None, ALU.mult, None, G_, G_)
            k.tt(gt3, ba3[:, :, 2:4], rowc[:, 0:2].unsqueeze(1).to_broadcast([128, NT, 2]), ALU.add, G_, G_)
            k.act(gt, gt, AF.Exp, G_, G_)
            k.act(gt, gt, AF.Ln, G_, G_, bias=1.0)
            k.tt(la3, gt3, rowc[:, 2:4].unsqueeze(1).to_broadcast([128, NT, 2]), ALU.mult, G_, G_)
            pb, pd = k.ps()
            for i in range(NT):
                k.mm(pb[:, i * 2:(i + 1) * 2], TRI[0], la3[:, i, :], True, True, [d_cst, d_g], [pd])
                k.mm(pb[:, 64 + i * 2:64 + (i + 1) * 2], TRI[1], la3[:, i, :], True, True, [d_cst, d_g], [pd])
            k.cp(gk(K_G)[:, :, 0:1], pb[:, 0:NT * 2].rearrange("p (t n) -> p t n", n=2)[:, :, 0:1], [pd], G_)
            k.cp(gk(K_G)[:, :, 1:2], pb[:, 64:64 + NT * 2].rearrange("p (t n) -> p t n", n=2)[:, :, 1:2], [pd], G_)
            k.tt(gk(K_GB), gk(K_G), lnb3, ALU.add, G_, G_)
            k.act(gk(K_EG), gk(K_G), AF.Exp, G_, G_)
            k.tt(gk(K_BG), gk(K_BETA), gk(K_EG), ALU.mult, G_, G_)
            k.tt(gt3, gk(K_G), SEL2.unsqueeze(1).to_broadcast([128, NT, 2]), ALU.mult, G_ + [d_cst], G_)
            pb, pd = k.ps()
            k.mm(pb[:, 0:NT * 2], BLK, gt, True, True, [d_cst, d_g], [pd])
            k.tt(gt3, pb[:, 0:NT * 2].rearrange("p (t n) -> p t n", n=2), gk(K_G), ALU.subtract, [pd, d_g], G_)
            k.act(gk(K_EGL), gt3, AF.Exp, G_, G_)
            gsel2, _ = cv.get(NT * 4)
            k.tt(gsel2.rearrange("p (t d c) -> p t d c", d=2, c=2), gk(K_G).unsqueeze(3).to_broadcast([128, NT, 2, 2]),
                 SELLAST.rearrange("p (d c) -> p d c", d=2).unsqueeze(1).to_broadcast([128, NT, 2, 2]), ALU.mult,
                 G_ + [d_cst], G_)
            pb, pd = k.ps()
            k.mm(pb[:, 0:NT * 4], ONES, gsel2, True, True, [d_cst, d_g], [pd])
            for cp in range(2):
                k.act(gk(K_GL + cp), pb[:, 0:NT * 4].rearrange("p (t d c) -> p t d c", d=2, c=2)[:, :, :, cp], AF.Exp, [pd], G_)
            S = []
            for d_ in range(2):
                s_, ds_ = cv.get(128)
                if kind == "p":
                    k.ms(s_, 0.0, [ds_])
                else:
                    P.dma(k.q(), s_, sdn.ap()[l, d_, h], writes=[ds_])
                S.append((s_, ds_))
            X, d_X = cv.get(256)
            vb, d_vb = cv.get(256)
            Gd, d_Gd = cv.get(256)
            Gbd, d_Gbd = cv.get(256)
            E1, d_E1 = cv.get(256)
            E2, d_E2 = cv.get(256)
            E3, d_E3 = cv.get(256)
            MML = [[cv.get(256), cv.get(256)] for _ in range(2)]
            PTL = [cv.get(128) for _ in range(2)]
            OB = []
            for _i in range(2):
                o_ = {}
                o_["attnT"], o_["d_at"] = cv.get(256)
                o_["kg"], o_["d_kg"] = cv.get(256)
                o_["uw"], o_["d_uw"] = cv.get(512)
                ls_, o_["d_LS"] = cv.get(NK * 2)
                o_["LS3"] = ls_.rearrange("p (k d) -> p k d", d=2)
                OB.append(o_)
            SC = []
            for d_ in range(2):
                SC.append((cv.get(128), cv.get(128), cv.get(128)))
            PB = k.psb
            PD = k.psd

            def lanes(s_):
                return (s_, NT - 1 - s_)

            def prep(s_):
                il = lanes(s_)
                ob = OB[s_ % 2]
                LS3, d_LS = ob["LS3"], ob["d_LS"]
                ls = lambda kk, d_: LS3[:, kk, d_:d_ + 1]
                tsl = [slice(il[d_] * 128, (il[d_] + 1) * 128) for d_ in range(2)]
                for d_ in range(2):
                    k.cp(LS3[:, :, d_], GA4[:, il[d_], :, d_], [d_g], [d_LS], eng="pool")
                pk, pkd = PB[0], PD[0]
                for d_ in range(2):
                    k.tr(pk[:, d_ * 256:d_ * 256 + 128], kf[:, tsl[d_]], ident[:], [d_kf, d_ident], [pkd])
                    k.tr(pk[:, d_ * 256 + 128:d_ * 256 + 256], vf[:, tsl[d_]], ident[:], [d_vf, d_ident], [pkd])
                for d_ in range(2):
                    ds = slice(d_ * 128, (d_ + 1) * 128)
                    k.ts(ob["kg"][:, ds], pk[:, d_ * 256:d_ * 256 + 128], ls(K_EGL, d_), None, ALU.mult, None, [pkd, d_LS], [ob["d_kg"]])
                    k.act(X[:, ds], pk[:, d_ * 256:d_ * 256 + 128], AF.Copy, [pkd, d_LS], [d_X], scale=ls(K_BG, d_))
                    k.ts(vb[:, ds], pk[:, d_ * 256 + 128:d_ * 256 + 256], ls(K_BETA, d_), None, ALU.mult, None, [pkd, d_LS], [d_vb])
                yield
                pa, pad = PB[1], PD[1]
                for d_ in range(2):
                    k.mm(pa[:, d_ * 256:d_ * 256 + 128], kf[:, tsl[d_]], kf[:, tsl[d_]], True, True, [d_kf], [pad])
                    k.mm(pa[:, d_ * 256 + 128:d_ * 256 + 256], kf[:, tsl[d_]], qf[:, tsl[d_]], True, True, [d_kf, d_qf], [pad])
                for d_ in range(2):
                    ds = slice(d_ * 128, (d_ + 1) * 128)
                    k.ts(Gd[:, ds], ident[:], ls(K_G, d_), None, ALU.mult, None, [d_ident, d_LS], [d_Gd])
                    k.act(Gbd[:, ds], ident[:], AF.Copy, [d_ident, d_LS], [d_Gbd], scale=ls(K_GB, d_))
                pg, pgd = PB[2], PD[2]
                k.mm(pg[:, 0:256], ONES, Gd, True, True, [d_cst, d_Gd], [pgd])
                k.mm(pg[:, 256:512], ONES, Gbd, True, True, [d_cst, d_Gbd], [pgd])
                yield
                for d_ in range(2):
                    ds = slice(d_ * 128, (d_ + 1) * 128)
                    k.stt(E1[:, ds], pg[:, ds], ls(K_G, d_), MASKB[0][d_], ALU.subtract, ALU.add, [pgd, d_LS, d_cst], [d_E1])
                    k.stt(E2[:, ds], pg[:, 256 + d_ * 128:256 + (d_ + 1) * 128], ls(K_G, d_), MASKB[1][d_], ALU.subtract, ALU.add,
                          [pgd, d_LS, d_cst], [d_E2])
                    k.stt(E3[:, ds], pg[:, ds], ls(K_G, d_), MASKB[2][d_], ALU.subtract, ALU.add, [pgd, d_LS, d_cst], [d_E3])
                k.act(E1, E1, AF.Exp, [d_E1], [d_E1])
                k.act(E2, E2, AF.Exp, [d_E2], [d_E2])
                k.act(E3, E3, AF.Exp, [d_E3], [d_E3], scale=-1.0)
                yield
                cur = [MML[d_][0] for d_ in range(2)]
                nxt = [MML[d_][1] for d_ in range(2)]
                for d_ in range(2):
                    ds = slice(d_ * 128, (d_ + 1) * 128)
                    c_, dc_ = cur[d_]
                    k.tt(ob["attnT"][:, ds], pa[:, d_ * 256 + 128:d_ * 256 + 256], E1[:, ds], ALU.mult, [pad, d_E1], [ob["d_at"]])
                    k.stt(c_[:, 128:256], pa[:, d_ * 256:d_ * 256 + 128], -1.0, E2[:, ds], ALU.mult, ALU.mult, [pad, d_E2], [dc_])
                    k.stt(c_[:, 0:128], pa[:, d_ * 256:d_ * 256 + 128], ls(K_NEGB, d_), E3[:, ds], ALU.mult, ALU.mult,
                          [pad, d_E3, d_LS], [dc_])
                    k.tt(PTL[d_][0], ident[:], c_[:, 128:256], ALU.add, [d_ident, dc_], [PTL[d_][1]])
                yield
                pmb = (3, 2)
                ppb = (0, 1)
                for lev in range(5):
                    lastl = (lev == 4)
                    for d_ in range(2):
                        c_, dc_ = cur[d_]
                        n_, dn_ = nxt[d_]
                        pm, pmd = PB[pmb[d_]], PD[pmb[d_]]
                        k.mm(pm[:, 0:128], c_[:, 128:256], c_[:, 0:128], True, True, [dc_], [pmd])
                        if not lastl:
                            k.mm(pm[:, 128:256], c_[:, 0:128], c_[:, 128:256], True, True, [dc_], [pmd])
                        ncols = 128 if lastl else 256
                        if d_ == 0:
                            k.act(n_[:, 0:ncols], pm[:, 0:ncols], AF.Copy, [pmd], [dn_])
                        else:
                            k.cp(n_[:, 0:ncols], pm[:, 0:ncols], [pmd], [dn_])
                    yield
                    for d_ in range(2):
                        n_, dn_ = nxt[d_]
                        pt_, dpt_ = PTL[d_]
                        pp, ppd = PB[ppb[d_]], PD[ppb[d_]]
                        k.mm(pp[:, 0:128], n_[:, 0:128], pt_, True, True, [dn_, dpt_], [ppd])
                        k.tt(pt_, pt_, pp[:, 0:128], ALU.add, [dpt_, ppd], [dpt_])
                    yield
                    cur, nxt = nxt, cur
                pu, pud = PB[2], PD[2]
                for d_ in range(2):
                    ds = slice(d_ * 128, (d_ + 1) * 128)
                    pt_, dpt_ = PTL[d_]
                    k.mm(pu[:, d_ * 256:d_ * 256 + 128], pt_, vb[:, ds], True, True, [dpt_, d_vb], [pud])
                    k.mm(pu[:, d_ * 256 + 128:d_ * 256 + 256], X[:, ds], pt_, True, True, [d_X, dpt_], [pud])
                k.cp(ob["uw"], pu[:, 0:512], [pud], [ob["d_uw"]])
                yield

            def scan(s_, d_):
                i = lanes(s_)[d_]
                ob = OB[s_ % 2]
                LS3, d_LS = ob["LS3"], ob["d_LS"]
                ds = slice(d_ * 128, (d_ + 1) * 128)
                es = slice(d_ * 64, (d_ + 1) * 64)
                (vnew, d_vn), (oasb, d_oa), (otmp, d_ot) = SC[d_]
                s_t, ds_ = S[d_]
                p1, p1d = PB[4 + 2 * d_], PD[4 + 2 * d_]
                p2, p2d = PB[5 + 2 * d_], PD[5 + 2 * d_]
                for cp in ((0, 1) if d_ == 0 else (1, 0)):
                    bs = slice(cp * 64, (cp + 1) * 64)
                    cs = slice(i * 128 + cp * 64, i * 128 + (cp + 1) * 64)
                    k.mm(p1[bs, 0:128], ob["uw"][:, d_ * 256 + 128 + cp * 64:d_ * 256 + 128 + (cp + 1) * 64], s_t, True, True,
                         [ob["d_uw"], ds_], [p1d])
                    k.mm(p1[bs, 128:256], qf[:, cs], s_t, True, True, [d_qf, ds_], [p1d])
                    k.tt(vnew[bs, :], ob["uw"][bs, d_ * 256:d_ * 256 + 128], p1[bs, 0:128], ALU.subtract, [ob["d_uw"], p1d], [d_vn])
                    yield
                    k.mm(p2[bs, 0:128], ob["attnT"][bs, d_ * 128 + cp * 64:d_ * 128 + (cp + 1) * 64], vnew[bs, :], True, True, [ob["d_at"], d_vn], [p2d])
                    k.mm(p2[:, 128:256], ob["kg"][bs, ds], vnew[bs, :], True, True, [ob["d_kg"], d_vn], [p2d])
                    k.act(oasb[bs, :], p2[bs, 0:128], AF.Copy, [p2d], [d_oa])
                    k.stt(s_t, s_t, LS3[:, K_GL + cp, d_:d_ + 1], p2[:, 128:256], ALU.mult, ALU.add, [ds_, d_LS, p2d], [ds_])
                    yield
                    k.stt(otmp[bs, :], p1[bs, 128:256], LS3[bs, K_EG, d_:d_ + 1], oasb[bs, :], ALU.mult, ALU.add,
                          [p1d, d_LS, d_oa], [d_ot])
                    k.tt(oacc[bs, i * 128:(i + 1) * 128], oacc[bs, i * 128:(i + 1) * 128], otmp[bs, :], ALU.add,
                         [d_oacc[i], d_ot], [d_oacc[i]], eng="pool")
                    yield

            def run_gens(gens):
                gens = list(gens)
                while gens:
                    for g_ in list(gens):
                        try:
                            next(g_)
                        except StopIteration:
                            gens.remove(g_)

            run_gens([prep(0)])
            for s_ in range(NT):
                gl_ = [scan(s_, 0), scan(s_, 1)]
                if s_ + 1 < NT:
                    gl_.insert(0, prep(s_ + 1))
                run_gens(gl_)
            if kind == "p":
                for d_ in range(2):
                    P.dma(k.q(), nst.ap()[si, l, d_, h], S[d_][0], reads=[S[d_][1]], writes=[d_nst])
            rs, d_rs = cv.get(4)
            on, d_on = cv.get(128)
            yT, d_yT = cv.get(128, F32R)
            for i in range(NT):
                tsl = slice(i * 128, (i + 1) * 128)
                k.act(on, oacc[:, tsl], AF.Square, [d_oacc[i]], [d_on, d_rs], accum=rs[:, 0:1])
                k.act(rs[:, 1:2], rs[:, 0:1], AF.Sqrt, [d_rs], [d_rs], bias=EPS, scale=1.0 / 128)
                k.rcp(rs[:, 2:3], rs[:, 1:2], [d_rs], [d_rs])
                k.stt(on, oacc[:, tsl], rs[:, 2:3], gn_row, ALU.mult, ALU.mult, [d_oacc[i], d_rs, d_gn, d_on], [d_on])
                pt, ptd = k.ps()
                k.tr(pt[:, 0:128], on, ident[:], [d_on, d_ident], [ptd])
                k.tt(yT, pt[:, 0:128], zs[:, tsl], ALU.mult, [ptd, d_zs], [d_yT])
                P.dma(k.q(), yscr.ap()[h, :, tsl], yT, reads=[d_yT], writes=[d_yscr[h]], r32=True)

        def na_unit(l, kind, si, T, hp):
            NT = T // 128
            TT = min(T, 512)
            fz = fence()
            cv = Carve(fz)
            ws = []
            for off in (OFF_NA_Q, OFF_NA_K, OFF_NA_V, OFF_NA_Z):
                wa, dw = cv.get(8 * 128, F32R)
                w3 = wa.rearrange("p (c n) -> p c n", c=8)
                c0 = off + hp * 128
                P.dma(k.q(), w3, w_in.ap()[l, :, c0:c0 + 128].rearrange("(c p) n -> p c n", p=128), writes=[dw], r32=True)
                ws.append((w3, dw))
            (wq, d_wq), (wk, d_wk), (wv, d_wv), (wz, d_wz) = ws
            qT, d_qT = cv.get(T, F32R)
            kT, d_kT = cv.get(T, F32R)
            vt_f, d_vt = cv.get(NT * 132, F32R)
            vtok = vt_f.rearrange("p (t h e) -> p t h e", t=NT, h=2)
            zs, d_zs = cv.get(T)
            ones, d_ones = cv.get(2)
            stg, d_stg = cv.get(256)
            k.ms(ones, 1.0, [d_ones])
            for t in range(T // TT):
                ts_ = slice(t * TT, (t + 1) * TT)
                pb, pd = k.ps()
                for kc in range(8):
                    k.mm(pb[:, 0:TT], wq[:, kc, :], hT[:, kc, ts_], kc == 0, kc == 7, [d_wq, d_hT], [pd])
                k.ts(qT[:, ts_], pb[:, 0:TT], 0.125, None, ALU.mult, None, [pd], [d_qT])
                pb, pd = k.ps()
                for kc in range(8):
                    k.mm(pb[:, 0:TT], wk[:, kc, :], hT[:, kc, ts_], kc == 0, kc == 7, [d_wk, d_hT], [pd])
                k.cp(kT[:, ts_], pb[:, 0:TT], [pd], [d_kT])
                pb, pd = k.ps()
                for kc in range(8):
                    k.mm(pb[:, 0:TT], wz[:, kc, :], hT[:, kc, ts_], kc == 0, kc == 7, [d_wz, d_hT], [pd])
                k.act(zs[:, ts_], pb[:, 0:TT], AF.Silu, [pd], [d_zs])
            for i in range(NT):
                is_ = slice(i * 128, (i + 1) * 128)
                pb, pd = k.ps()
                for kc in range(8):
                    k.mm(pb[:, 0:128], hT[:, kc, is_], wv[:, kc, :], kc == 0, kc == 7, [d_wv, d_hT], [pd])
                k.cp(vtok[:, i, :, 0:64], pb[:, 0:128].rearrange("p (h e) -> p h e", h=2), [pd], [d_vt])
                k.cp(vtok[:, i, :, 64:66], ones[:, 0:2].unsqueeze(1).to_broadcast([128, 2, 2]), [d_ones], [d_vt], eng="pool")
                if kind == "p":
                    k.cp(stg[:, 0:128], pb[:, 0:128], [pd], [d_stg])
                    P.dma("act", nv.ap()[si, l, is_, hp * 128:(hp + 1) * 128], stg[:, 0:128], reads=[d_stg], writes=[d_nkv])
                    pb, pd = k.ps()
                    for kc in range(8):
                        k.mm(pb[:, 0:128], hT[:, kc, is_], wk[:, kc, :], kc == 0, kc == 7, [d_wk, d_hT], [pd])
                    k.cp(stg[:, 128:256], pb[:, 0:128], [pd], [d_stg])
                    P.dma("act", nk.ap()[si, l, is_, hp * 128:(hp + 1) * 128], stg[:, 128:256], reads=[d_stg], writes=[d_nkv])
            otok, d_otok = cv.get(128)
            rden, d_rden = cv.get(2)
            yT, d_yT = cv.get(128, F32R)
            if kind == "p":
                PT = [cv.get(256, F32R) for _ in range(4)]
                po, pod = k.ps()
                for h2 in range(2):
                    hs = slice(h2 * 64, (h2 + 1) * 64)
                    for kt in range(2):
                        pt, d_pt = PT[h2 * 2 + kt]
                        pb, pd = k.ps()
                        k.mm(pb[:, 0:256], kT[hs, kt * 128:(kt + 1) * 128], qT[hs, 0:256], True, True, [d_kT, d_qT], [pd])
                        k.act(pt, pb[:, 0:256], AF.Exp, [pd], [d_pt])
                    for qt in range(2):
                        for kt in range(2):
                            pt, d_pt = PT[h2 * 2 + kt]
                            c0 = qt * 132 + h2 * 66
                            k.mm(po[:, c0:c0 + 66], pt[:, qt * 128:(qt + 1) * 128], vtok[:, kt, h2, :], kt == 0, kt == 1,
                                 [d_pt, d_vt], [pod])
                for qt in range(2):
                    qs = slice(qt * 128, (qt + 1) * 128)
                    for h2 in range(2):
                        c0 = qt * 132 + h2 * 66
                        k.rcp(rden[:, h2:h2 + 1], po[:, c0 + 64:c0 + 65], [pod], [d_rden])
                        k.ts(otok[:, h2 * 64:(h2 + 1) * 64], po[:, c0:c0 + 64], rden[:, h2:h2 + 1], None, ALU.mult, None,
                             [pod, d_rden], [d_otok])
                    pb, pd = k.ps()
                    k.tr(pb[:, 0:128], otok, ident[:], [d_otok, d_ident], [pd])
                    k.tt(yT, pb[:, 0:128], zs[:, qs], ALU.mult, [pd, d_zs], [d_yT])
                    P.dma(k.q(), yscr.ap()[4 + hp, :, qs], yT, reads=[d_yT], writes=[d_yscr[4 + hp]], r32=True)
            else:
                kcT, d_kcT = cv.get(256, F32R)
                vc_f, d_vc = cv.get(2 * 132, F32R)
                vctx = vc_f.rearrange("p (t h e) -> p t h e", t=2, h=2)
                cst_k, d_cstk = cv.get(256)
                P.dma("sp", cst_k.rearrange("p (t n) -> p t n", t=2),
                      ck_in.ap()[l, :, hp * 128:(hp + 1) * 128].rearrange("(t p) n -> p t n", p=128), writes=[d_cstk])
                pb, pd = k.ps()
                for t in range(2):
                    k.tr(pb[:, t * 128:(t + 1) * 128], cst_k[:, t * 128:(t + 1) * 128], ident[:], [d_cstk, d_ident], [pd])
                k.cp(kcT, pb[:, 0:256], [pd], [d_kcT])
                for t in range(2):
                    P.dma("act", vctx[:, t, :, 0:64],
                          cv_in.ap()[l, t * 128:(t + 1) * 128, hp * 128:(hp + 1) * 128].rearrange("p (h e) -> p h e", h=2),
                          writes=[d_vc], r32=True)
                    k.cp(vctx[:, t, :, 64:66], ones[:, 0:2].unsqueeze(1).to_broadcast([128, 2, 2]), [d_ones], [d_vc], eng="pool")
                E2f, d_E2 = cv.get(2 * 15 * 64)
                E2 = E2f.rearrange("p (h r c) -> p h r c", h=2, r=15)
                for a in range(2):
                    for h2 in range(2):
                        head = hp * 2 + h2
                        src = AP(zscr, ((l * 8 + head) * 15) * 8128 + 63, [[126, 64], [8128, 15], [1, 64]])
                        P.dma("sp" if a == 0 else "act", E2[a * 64:(a + 1) * 64, h2, :, :], src, reads=[d_zscr[l]], writes=[d_E2])
                k.act(E2f, E2f, AF.Exp, [d_E2], [d_E2])
                k.tt(E2f.rearrange("p (g c) -> p g c", c=64), E2f.rearrange("p (g c) -> p g c", c=64),
                     CM.unsqueeze(1).to_broadcast([128, 30, 64]), ALU.mult, [d_E2, d_cst], [d_E2])
                TABf, d_TAB = cv.get(2 * 21 * 128)
                TAB = TABf.rearrange("p (h t q) -> p h t q", h=2, t=21)
                k.ms(TABf, 0.0, [d_TAB])
                plans = {}
                tid = 0
                plans["int"] = []
                for j in range(5):
                    for a in range(2):
                        for b in range(2):
                            dr = 2 * j - 4 + a - b
                            if -4 <= dr <= 3:
                                plans["int"].append((tid, a, b, dr))
                    tid += 1
                tid0 = {"int": 0}
                for m_ in (0, 1, 14, 15):
                    tid0[m_] = tid
                    kt0 = 0 if m_ < 2 else 12
                    plans[m_] = []
                    for j in range(4):
                        for a in range(2):
                            for b in range(2):
                                kr = 2 * (kt0 + j) + a
                                r = 2 * m_ + b
                                rs_ = min(max(r - 4, 0), 24)
                                if rs_ <= kr <= rs_ + 7:
                                    plans[m_].append((tid, a, b, kr - r))
                        tid += 1
                assert tid == 21
                _ci = 0
                for key_, pl in plans.items():
                    for (tid_, a, b, dr) in pl:
                        k.cp(TAB[a * 64:(a + 1) * 64, :, tid_, b * 64:(b + 1) * 64], E2[a * 64:(a + 1) * 64, :, dr + 7, :],
                             [d_E2], [d_TAB], eng=("dve" if _ci % 2 == 0 else "pool"))
                        _ci += 1
                PTb = [cv.get(7 * 128, F32R) for _ in range(2)]
                tEb = [cv.get(512) for _ in range(2)]
                tE2b = [cv.get(128) for _ in range(2)]
                its = [(m_, h2) for m_ in range(16) for h2 in range(2)]

                def info(m_):
                    if 2 <= m_ <= 13:
                        return [m_ - 2 + j for j in range(5)], 0
                    return [(0 if m_ < 2 else 12) + j for j in range(4)], tid0[m_]

                def emit_scores(m_, h2):
                    kts, t0 = info(m_)
                    nl = len(kts)
                    qs = slice(m_ * 128, (m_ + 1) * 128)
                    hs = slice(h2 * 64, (h2 + 1) * 64)
                    pA, pAd = k.ps()
                    for j in range(4):
                        k.mm(pA[:, j * 128:(j + 1) * 128], kT[hs, kts[j] * 128:(kts[j] + 1) * 128], qT[hs, qs], True, True,
                             [d_kT, d_qT], [pAd])
                    pB, pBd = k.ps()
                    c_ = 0
                    if nl == 5:
                        k.mm(pB[:, 0:128], kT[hs, kts[4] * 128:(kts[4] + 1) * 128], qT[hs, qs], True, True, [d_kT, d_qT], [pBd])
                        c_ = 128
                    for t in range(2):
                        k.mm(pB[:, c_ + t * 128:c_ + (t + 1) * 128], kcT[hs, t * 128:(t + 1) * 128], qT[hs, qs], True, True,
                             [d_kcT, d_qT], [pBd])
                    return (pA, pAd, pB, pBd, c_)

                pend = emit_scores(*its[0])
                po, pod = None, None
                for ii, (m_, h2) in enumerate(its):
                    cur_sc = pend
                    if ii + 1 < len(its):
                        pend = emit_scores(*its[ii + 1])
                    pA, pAd, pB, pBd, c_ = cur_sc
                    kts, t0 = info(m_)
                    nl = len(kts)
                    qs = slice(m_ * 128, (m_ + 1) * 128)
                    PTf, d_PT = PTb[ii % 2]
                    tmpE, d_tmpE = tEb[ii % 2]
                    tmpE2, d_tmpE2 = tE2b[ii % 2]
                    k.act(tmpE, pA[:, 0:512], AF.Exp, [pAd], [d_tmpE])
                    k.tt(PTf[:, 0:512], tmpE, TABf[:, (h2 * 21 + t0) * 128:(h2 * 21 + t0 + 4) * 128], ALU.mult,
                         [d_tmpE, d_TAB], [d_PT])
                    if nl == 5:
                        k.act(tmpE2, pB[:, 0:128], AF.Exp, [pBd], [d_tmpE2])
                        k.tt(PTf[:, 512:640], tmpE2, TABf[:, (h2 * 21 + 4) * 128:(h2 * 21 + 5) * 128], ALU.mult,
                             [d_tmpE2, d_TAB], [d_PT])
                    k.act(PTf[:, nl * 128:(nl + 2) * 128], pB[:, c_:c_ + 256], AF.Exp, [pBd], [d_PT])
                    if h2 == 0:
                        po, pod = k.ps()
                    c0 = h2 * 66
                    ntile = nl + 2
                    for j in range(ntile):
                        if j < nl:
                            rhs_ = vtok[:, kts[j], h2, :]
                            rd = d_vt
                        else:
                            rhs_ = vctx[:, j - nl, h2, :]
                            rd = d_vc
                        k.mm(po[:, c0:c0 + 66], PTf[:, j * 128:(j + 1) * 128], rhs_, j == 0, j == ntile - 1, [d_PT, rd], [pod])
                    if h2 == 1:
                        for hh in range(2):
                            cc0 = hh * 66
                            k.rcp(rden[:, hh:hh + 1], po[:, cc0 + 64:cc0 + 65], [pod], [d_rden])
                            k.ts(otok[:, hh * 64:(hh + 1) * 64], po[:, cc0:cc0 + 64], rden[:, hh:hh + 1], None, ALU.mult, None,
                                 [pod, d_rden], [d_otok])
                        pb, pd = k.ps()
                        k.tr(pb[:, 0:128], otok, ident[:], [d_otok, d_ident], [pd])
                        k.tt(yT, pb[:, 0:128], zs[:, qs], ALU.mult, [pd, d_zs], [d_yT])
                        P.dma(k.q(), yscr.ap()[4 + hp, :, qs], yT, reads=[d_yT], writes=[d_yscr[4 + hp]], r32=True)

        for l in range(depth):
            last = (l == depth - 1)
            fz = fence()
            cv = Carve(fz)
            bcol, d_bcol = cv.get(16)
            gpre_col, _d = cv.get(8)
            brow, _d = cv.get(D)
            gpost_row, _d = cv.get(D)
            ggt, d_ggt = cv.get(512)
            wada_f, d_wada = cv.get(8 * 512, F32R)
            wada = wada_f.rearrange("p (c n) -> p c n", c=8)
            sbc_f, d_sbc = cv.get(8 * 2 * 128, F32R)
            siluc_bc = sbc_f.rearrange("p (c j n) -> p c j n", c=8, j=2)
            for kc in range(8):
                for j in range(2):
                    k.cp(siluc_bc[:, kc, j, :], silucT[:, kc, j:j + 1].bitcast(F32).to_broadcast([128, 128]), [d_siluc], [d_sbc])
            P.dma("sp", bcol, b_ada.ap()[l, 0:2048].rearrange("(c p) -> p c", p=128), writes=[d_bcol], slow=True)
            P.dma("act", gpre_col, g_pre.ap()[l].rearrange("(c p) -> p c", p=128), writes=[d_bcol], slow=True)
            P.dma("sp", brow, b_ada.ap()[l, 2048:3072].partition_broadcast(128), writes=[d_bcol])
            P.dma("act", gpost_row, g_post.ap()[l].partition_broadcast(128), writes=[d_bcol])
            for blk in range(6):
                P.dma(k.q(), wada, w_ada.ap()[l, :, blk * 512:(blk + 1) * 512].rearrange("(c p) n -> p c n", p=128),
                      writes=[d_wada], r32=True)
                if blk < 4:
                    pb, pd = k.ps()
                    for cc in range(4):
                        for kc in range(8):
                            k.mm(pb[:, cc * 2:cc * 2 + 2], wada[:, kc, cc * 128:(cc + 1) * 128], silucT[:, kc, :],
                                 kc == 0, kc == 7, [d_wada, d_siluc], [pd])
                    k.tt(modcol[:, blk * 4:blk * 4 + 4, :], pb[:, 0:8].rearrange("p (c j) -> p c j", j=2),
                         bcol[:, blk * 4:blk * 4 + 4].unsqueeze(2).to_broadcast([128, 4, 2]), ALU.add,
                         [pd, d_bcol], [d_modcol])
                else:
                    for j in range(2):
                        pb, pd = k.ps()
                        for kc in range(8):
                            k.mm(pb[:], siluc_bc[:, kc, j, :], wada[:, kc, :], kc == 0, kc == 7, [d_wada, d_sbc], [pd])
                        c0 = (blk - 4) * 512
                        k.tt(ggt[:, 0:512], pb[:], brow[:, c0:c0 + 512], ALU.add, [pd, d_bcol], [d_ggt])
                        k.tt(ggt[:, 0:512], ggt[:, 0:512], gpost_row[:, c0:c0 + 512], ALU.mult, [d_ggt, d_bcol], [d_ggt])
                        P.dma("sp", ggscr.ap()[j, :, c0:c0 + 512], ggt[:, 0:512], reads=[d_ggt], writes=[d_gg])
            for j in range(2):
                k.stt(s1col[:, :, j], modcol[:, 8:16, j], 1.0, gpre_col, ALU.add, ALU.mult, [d_modcol, d_bcol], [d_modcol])

            for (kind, si, T) in seqs:
                cj = 0 if kind == "p" else 1
                TT = min(T, 512)
                NTL = T // TT
                if l == 0:
                    xin = x_p.ap()[si] if kind == "p" else x_s.ap()
                    d_xin = Dep()
                else:
                    xin = xs_p[(l - 1) % 2].ap()[si] if kind == "p" else xs_s[(l - 1) % 2].ap()
                    d_xin = d_xs_p[(l - 1) % 2][si] if kind == "p" else d_xs_s[(l - 1) % 2]
                if last:
                    xout = y_p.ap()[si] if kind == "p" else y_s.ap()
                    d_xout = Dep()
                else:
                    xout = xs_p[l % 2].ap()[si] if kind == "p" else xs_s[l % 2].ap()
                    d_xout = d_xs_p[l % 2][si] if kind == "p" else d_xs_s[l % 2]

                fz = fence()
                cv = Carve(fz)
                _x0, _d0 = cv.get(D)
                _x1, _d1 = cv.get(D)
                xt = [_x0, _x1]
                d_xt = [_d0, _d1]
                xn, d_xn = cv.get(D)
                for i in range(T // 128):
                    b = i % 2
                    P.dma(k.q(), xt[b], xin[i * 128:(i + 1) * 128, :], reads=[d_xin], writes=[d_xt[b]])
                    k.act(xn, xt[b], AF.Square, [d_xt[b]], [d_xn, d_small], accum=small[:, 0:1])
                    k.act(small[:, 1:2], small[:, 0:1], AF.Sqrt, [d_small], [d_small], bias=EPS, scale=1.0 / D)
                    k.rcp(small[:, 2:3], small[:, 1:2], [d_small], [d_small])
                    k.ts(xn, xt[b], small[:, 2:3], None, ALU.mult, None, [d_xt[b], d_small], [d_xn])
                    for half in range(2):
                        pb, pd = k.ps()
                        for c4 in range(4):
                            kc = half * 4 + c4
                            k.tr(pb[:, c4 * 128:(c4 + 1) * 128], xn[:, kc * 128:(kc + 1) * 128], ident[:], [d_xn, d_ident], [pd])
                        for c4 in range(4):
                            kc = half * 4 + c4
                            k.ts(hT[:, kc, i * 128:(i + 1) * 128], pb[:, c4 * 128:(c4 + 1) * 128],
                                 s1col[:, kc, cj:cj + 1], modcol[:, kc, cj:cj + 1], ALU.mult, ALU.add,
                                 [pd, d_modcol], [d_hT], eng=("dve" if c4 % 2 == 0 else "pool") if False else "dve")

                fz = fence()
                cv = Carve(fz)
                zbuf, d_z = cv.get(TT, F32R)
                zsrc, d_zs0 = cv.get(TT)
                k.ms(zsrc, 0.0, [d_zs0])
                k.cp(zbuf, zsrc, [d_zs0], [d_z])
                for ch in range(8):
                    if (ch < 4 and not do_dn) or (ch >= 4 and not do_na):
                        for t in range(NTL):
                            P.dma(k.q(), yscr.ap()[ch, :, t * TT:(t + 1) * TT], zbuf, reads=[d_z], writes=[d_yscr[ch]], r32=True)

                if do_dn:
                    for h in range(4):
                        dn_unit(l, kind, si, T, h)

                if do_na:
                    for hp in range(4):
                        na_unit(l, kind, si, T, hp)

                for g in range(4):
                    win = POOL_WINDOWS[g]
                    fz = fence()
                    cv = Carve(fz)
                    wu, d_wu = cv.get(8 * 128, F32R)
                    wz, d_wz = cv.get(8 * 128, F32R)
                    pw, d_pw = cv.get(128, F32R)
                    psc, d_psc = cv.get(1)
                    U, d_U = cv.get(T + 16)
                    zs, d_zs = cv.get(T)
                    s_a, d_sa = cv.get(T + 16)
                    s_b, d_sb = cv.get(T + 16)
                    icnt, d_icnt = cv.get(T)
                    pooled, d_pooled = cv.get(T, F32R)
                    yT, d_yT = cv.get(TT, F32R)
                    wu3 = wu.rearrange("p (c n) -> p c n", c=8)
                    wz3 = wz.rearrange("p (c n) -> p c n", c=8)
                    cu = OFF_PL_U + g * 128
                    cz = OFF_PL_Z + g * 128
                    P.dma("sp", wu3, w_in.ap()[l, :, cu:cu + 128].rearrange("(c p) n -> p c n", p=128), writes=[d_wu], r32=True)
                    P.dma("act", wz3, w_in.ap()[l, :, cz:cz + 128].rearrange("(c p) n -> p c n", p=128), writes=[d_wz], r32=True)
                    P.dma("sp", pw, pool_w.ap()[l, g], writes=[d_pw], r32=True)
                    P.dma("act", psc, pool_scale.ap()[l, g * 128:(g + 1) * 128].rearrange("(p o) -> p o", o=1), writes=[d_psc], slow=True)
                    ic_src = (invcnt_p if kind == "p" else invcnt_s).ap()[g]
                    P.dma("sp", icnt, ic_src.partition_broadcast(128), writes=[d_icnt])
                    k.ms(U[:, 0:8], 0.0, [d_U])
                    k.ms(U[:, T + 8:T + 16], 0.0, [d_U])
                    for t in range(NTL):
                        pb, pd = k.ps()
                        for kc in range(8):
                            k.mm(pb[:, 0:TT], wu3[:, kc, :], hT[:, kc, t * TT:(t + 1) * TT], kc == 0, kc == 7, [d_wu, d_hT], [pd])
                        k.cp(U[:, 8 + t * TT:8 + (t + 1) * TT], pb[:, 0:TT], [pd], [d_U])
                        pb, pd = k.ps()
                        for kc in range(8):
                            k.mm(pb[:, 0:TT], wz3[:, kc, :], hT[:, kc, t * TT:(t + 1) * TT], kc == 0, kc == 7, [d_wz, d_hT], [pd])
                        k.act(zs[:, t * TT:(t + 1) * TT], pb[:, 0:TT], AF.Silu, [pd], [d_zs])
                    cur, dcur, curlen = U, d_U, T + 16
                    step = 1
                    bufs = [(s_a, d_sa), (s_b, d_sb)]
                    bi = 0
                    while step < win:
                        nb, dnb = bufs[bi]
                        bi ^= 1
                        nlen = curlen - step
                        k.tt(nb[:, 0:nlen], cur[:, 0:nlen], cur[:, step:step + nlen], ALU.add, [dcur], [dnb])
                        cur, dcur, curlen = nb, dnb, nlen
                        step *= 2
                    o0 = 8 - win // 2
                    nb, dnb = bufs[bi]
                    k.tt(nb[:, 0:T], cur[:, o0:o0 + T], icnt, ALU.mult, [dcur, d_icnt], [dnb])
                    k.tt(pooled, nb[:, 0:T], U[:, 8:8 + T], ALU.subtract, [dnb, d_U], [d_pooled])
                    for t in range(NTL):
                        pb, pd = k.ps()
                        k.mm(pb[:, 0:TT], pw, pooled[:, t * TT:(t + 1) * TT], True, True, [d_pw, d_pooled], [pd])
                        k.stt(yT, pb[:, 0:TT], psc[:, 0:1], zs[:, t * TT:(t + 1) * TT], ALU.mult, ALU.mult, [pd, d_psc, d_zs], [d_yT])
                        P.dma(k.q(), yscr.ap()[8 + g, :, t * TT:(t + 1) * TT], yT, reads=[d_yT], writes=[d_yscr[8 + g]], r32=True)

                for u in range(6):
                    fz = fence()
                    cv = Carve(fz)
                    wgu_f, d_wgu = cv.get(8 * 512, F32R)
                    wgu = wgu_f.rearrange("p (c n) -> p c n", c=8)
                    c0 = OFF_GATE + u * 512
                    P.dma(k.q(), wgu, w_in.ap()[l, :, c0:c0 + 512].rearrange("(c p) n -> p c n", p=128), writes=[d_wgu], r32=True)
                    gsbs = [cv.get(TT) for _ in range(2)]
                    gi = 0
                    for t in range(NTL):
                        for c4 in range(4):
                            gsb, d_gsb = gsbs[gi % 2]
                            gi += 1
                            pg_, pgd_ = k.ps()
                            for kc in range(8):
                                k.mm(pg_[:, 0:TT], wgu[:, kc, c4 * 128:(c4 + 1) * 128], hT[:, kc, t * TT:(t + 1) * TT], kc == 0, kc == 7,
                                     [d_wgu, d_hT], [pgd_])
                            k.act(gsb, pg_[:, 0:TT], AF.Sigmoid, [pgd_], [d_gsb])
                            P.dma(k.q(), gscr.ap()[u * 4 + c4, :, t * TT:(t + 1) * TT], gsb, reads=[d_gsb], writes=[d_gscr[u * 4 + c4]])

                fz = fence()
                cv = Carve(fz)
                ysb_l = []
                wbr_l = []
                gin_l = []
                mo_l = []
                for _i in range(2):
                    _a, _d = cv.get(4 * TT, F32R)
                    ysb_l.append((_a.rearrange("p (c n) -> p c n", c=4), _d))
                    _a, _d = cv.get(4 * D, F32R)
                    wbr_l.append((_a.rearrange("p (c n) -> p c n", c=4), _d))
                    _a, _d = cv.get(8 * TT)
                    gin_l.append((_a.rearrange("p (c n) -> p c n", c=8), _d))
                    mo_l.append(cv.get(TT, F32R))
                accf, d_acc = cv.get(8 * TT)
                acc3 = accf.rearrange("p (c n) -> p c n", c=8)
                tmp, d_tmp = cv.get(TT)
                bi = 0
                mi = 0
                for t in range(NTL):
                    tsl_ = slice(t * TT, (t + 1) * TT)
                    for br in range(3):
                        ysb3, d_ysb = ysb_l[bi % 2]
                        wbr3, d_wbr = wbr_l[bi % 2]
                        gin3, d_gin = gin_l[bi % 2]
                        bi += 1
                        P.dma("sp", ysb3, yscr.ap()[br * 4:(br + 1) * 4, :, tsl_].rearrange("c p n -> p c n"),
                              reads=list(d_yscr[br * 4:(br + 1) * 4]), writes=[d_ysb], r32=True)
                        P.dma("act", gin3, gscr.ap()[br * 8:(br + 1) * 8, :, tsl_].rearrange("c p n -> p c n"),
                              reads=list(d_gscr[br * 8:(br + 1) * 8]), writes=[d_gin])
                        P.dma("sp", wbr3, w_br[br].ap()[l].rearrange("(c p) n -> p c n", p=128), writes=[d_wbr], r32=True)
                        for dc in range(8):
                            pa, pad = k.ps()
                            for wc in range(4):
                                k.mm(pa[:, 0:TT], wbr3[:, wc, dc * 128:(dc + 1) * 128], ysb3[:, wc, :], wc == 0, wc == 3, [d_wbr, d_ysb], [pad])
                            if br == 0:
                                k.tt(acc3[:, dc, :], gin3[:, dc, :], pa[:, 0:TT], ALU.mult, [d_gin, pad], [d_acc])
                            elif br == 1:
                                k.tt(tmp, gin3[:, dc, :], pa[:, 0:TT], ALU.mult, [d_gin, pad], [d_tmp])
                                k.tt(acc3[:, dc, :], acc3[:, dc, :], tmp, ALU.add, [d_acc, d_tmp], [d_acc], eng="pool")
                            else:
                                mo, d_mo = mo_l[mi % 2]
                                mi += 1
                                k.tt(tmp, gin3[:, dc, :], pa[:, 0:TT], ALU.mult, [d_gin, pad], [d_tmp])
                                k.tt(mo, acc3[:, dc, :], tmp, ALU.add, [d_acc, d_tmp], [d_mo], eng="pool")
                                P.dma("act", mscr.ap()[dc, :, tsl_], mo, reads=[d_mo], writes=[d_mscr], r32=True)

                fz = fence()
                cv = Carve(fz)
                wo, d_wo = cv.get(8 * D, F32R)
                wo3 = wo.rearrange("p (c n) -> p c n", c=8)
                mt_l = []
                for _i in range(2):
                    _a, _d = cv.get(8 * 128, F32R)
                    mt_l.append((_a.rearrange("p (c n) -> p c n", c=8), _d))
                xr_l = [cv.get(D) for _ in range(2)]
                xo_l = [cv.get(D) for _ in range(2)]
                xn, d_xn = cv.get(512)
                ggr, d_ggr = cv.get(D)
                P.dma("act", ggr, ggscr.ap()[cj], reads=[d_gg], writes=[d_ggr])
                P.dma("sp", wo3, w_out.ap()[l].rearrange("(c p) n -> p c n", p=128), writes=[d_wo], r32=True)
                for sub in range(T // 128):
                    r0 = sub * 128
                    mt3, d_mt = mt_l[sub % 2]
                    xr, d_xr = xr_l[sub % 2]
                    xo, d_xo = xo_l[sub % 2]
                    P.dma("sp", mt3, mscr.ap()[:, :, r0:r0 + 128].rearrange("c p n -> p c n"), reads=[d_mscr], writes=[d_mt], r32=True)
                    P.dma("act", xr, xin[r0:r0 + 128, :], reads=[d_xin], writes=[d_xr])
                    pos = []
                    for half in range(2):
                        po, pod = k.ps()
                        for kc in range(8):
                            k.mm(po[:], mt3[:, kc, :], wo3[:, kc, half * 512:(half + 1) * 512], kc == 0, kc == 7, [d_mt, d_wo], [pod])
                        k.act(xn[:, 0:512], po[:], AF.Square, [pod], [d_xn, d_small], accum=small[:, 4 + half:5 + half])
                        pos.append((po, pod))
                    k.tt(small[:, 6:7], small[:, 4:5], small[:, 5:6], ALU.add, [d_small], [d_small])
                    k.act(small[:, 7:8], small[:, 6:7], AF.Sqrt, [d_small], [d_small], bias=EPS, scale=1.0 / D)
                    k.rcp(small[:, 8:9], small[:, 7:8], [d_small], [d_small])
                    for half in range(2):
                        po, pod = pos[half]
                        hs = slice(half * 512, (half + 1) * 512)
                        k.stt(xo[:, hs], po[:], small[:, 8:9], ggr[:, hs], ALU.mult, ALU.mult, [pod, d_small, d_ggr], [d_xo])
                        k.tt(xo[:, hs], xo[:, hs], xr[:, hs], ALU.add, [d_xo, d_xr], [d_xo], eng="pool")
                    P.dma("sp", xout[r0:r0 + 128, :], xo, reads=[d_xo], writes=[d_xout])
        P.emit()
    return nc, k


_CACHE = {}


def _consts():
    def invcnt(T):
        out = np.zeros((4, T), np.float32)
        pos = np.arange(T)
        for gi, win in enumerate(POOL_WINDOWS):
            lo = np.maximum(pos - win // 2, 0)
            hi = np.minimum(pos + win // 2 - 1, T - 1)
            out[gi] = 1.0 / (hi - lo + 1).astype(np.float32)
        return out

    t = np.arange(128)
    same = (t[:, None] // 64) == (t[None, :] // 64)
    cst = np.zeros((128, NCST), np.float32)
    cst[:, 0:128] = same & (t[:, None] <= t[None, :])
    cst[:, 128:256] = same & (t[:, None] >= t[None, :])
    cst[:, 256:384] = same
    cst[:, 384:512] = 1.0
    f = np.arange(64)
    pm = t % 64
    cst[:, 512:576] = (pm[:, None] == f[None, :])
    cst[:, 576] = (pm == 63)
    cst[:, 577] = (pm == 0)
    cst[:, 578] = (t == 63)
    cst[:, 579] = (t == 127)
    cst[:, 580] = (t == 0)
    cst[:, 581] = (t == 64)
    P_ = pm[:, None]
    F_ = f[None, :]
    valid = [[F_ >= P_, F_ <= P_], [F_ > P_, F_ < P_], [F_ < P_, F_ > P_]]
    for ty in range(3):
        for d_ in range(2):
            sign = 1.0 if ty == 2 else -1.0
            cst[:, 582 + (ty * 2 + d_) * 64: 582 + (ty * 2 + d_ + 1) * 64] = np.where(valid[ty][d_], 0.0, sign * BIG)
    cq = np.arange(64)
    csq = np.clip(cq - 8, 0, 48)
    cm = (f[:, None] >= csq[None, :]) & (f[:, None] < csq[None, :] + 16)
    cst[:, 582 + 384:582 + 384 + 64] = np.concatenate([cm, cm], axis=0)
    Pf = t[:, None]
    Ff = t[None, :]
    validb = [[Ff >= Pf, Ff <= Pf], [Ff > Pf, Ff < Pf], [Ff < Pf, Ff > Pf]]
    for ty in range(3):
        for d_ in range(2):
            sign = 1.0 if ty == 2 else -1.0
            c0 = 1030 + (ty * 2 + d_) * 128
            cst[:, c0:c0 + 128] = np.where(validb[ty][d_] & same, 0.0, sign * BIG)
    return {"invcnt_p": invcnt(SEQ), "invcnt_s": invcnt(DSEQ), "ident": np.eye(128, dtype=np.float32), "dncst": cst}


def kernel(x_prompt, x_sample, c, cache_k_na, cache_v_na, state_dn, c_ctx, w_ada, b_ada, g_pre, g_post,
           w_in, conv_dn, a_log_dn, dt_bias_dn, g_norm_dn, na_bias, pool_w, pool_scale,
           w_br_dn, w_br_na, w_br_pl, w_out, _depth=DEPTH, _dn=True, _na=True):
    f = lambda a: np.ascontiguousarray(np.asarray(a, dtype=np.float32))
    key = (_depth, _dn, _na)
    if key not in _CACHE:
        _CACHE[key] = build_program(_depth, _dn, _na)
    nc, k = _CACHE[key]
    cs = _consts()
    dd = _depth
    shared = {"w_ada": f(w_ada[:dd]), "b_ada": f(b_ada[:dd]), "g_pre": f(g_pre[:dd]), "g_post": f(g_post[:dd]), "w_in": f(w_in[:dd]),
              "pool_w": f(pool_w[:dd]), "pool_scale": f(pool_scale[:dd]), "w_br_dn": f(w_br_dn[:dd]), "w_br_na": f(w_br_na[:dd]),
              "w_br_pl": f(w_br_pl[:dd]), "w_out": f(w_out[:dd]), "conv_dn": f(conv_dn[:dd]),
              "a_log": f(a_log_dn[:dd]).reshape(dd, 8), "dt_bias": f(dt_bias_dn[:dd]).reshape(dd, 8), "g_norm": f(g_norm_dn[:dd])}
    shared.update(cs)
    rpad = np.zeros((dd, 8, 15, 127), np.float32)
    rpad[..., 48:79] = f(na_bias[:dd])[..., ::-1]
    x_prompt = f(x_prompt)
    x_sample = f(x_sample)
    in_maps = []
    for core in range(8):
        b = core // 4
        m = dict(shared)
        m["x_p"] = x_prompt[core * NPS:(core + 1) * NPS]
        m["x_s"] = x_sample[b]
        m["cvec"] = np.stack([f(c_ctx), f(c)[b]])
        m["sdn"] = f(state_dn[b, :dd])
        m["ck"] = f(cache_k_na[b, :dd]).reshape(dd, 256, 512)
        m["cvv"] = f(cache_v_na[b, :dd]).reshape(dd, 256, 512)
        m["rpad"] = rpad
        in_maps.append({n: m[n] for n in k.din})
    res = run_bass_kernel_spmd(nc, in_maps, core_ids=list(range(8)))
    r = res.results
    y_p = np.concatenate([r[i]["y_p"] for i in range(8)], axis=0)
    y_s = np.stack([r[0]["y_s"], r[4]["y_s"]])
    n_k = np.concatenate([r[i]["nk"] for i in range(8)], axis=0).reshape(32, dd, SEQ, 8, 64)
    n_v = np.concatenate([r[i]["nv"] for i in range(8)], axis=0).reshape(32, dd, SEQ, 8, 64)
    n_s = np.concatenate([r[i]["nst"] for i in range(8)], axis=0)
    return y_p, y_s, n_k, n_v, n_s
```

```python
import contextlib
import numpy as np
import concourse.bass as bass
import concourse.mybir as mybir
from concourse.ap import AP
from concourse.bass_utils import run_bass_kernel_spmd

F32 = mybir.dt.float32
F32R = mybir.dt.float32r
ALU = mybir.AluOpType
AF = mybir.ActivationFunctionType

ENGS = ("pe", "act", "dve", "pool", "sp")
NDSEM = 12

D = 1024
DEPTH = 4
SEQ = 256
DSEQ = 2048
NIN = 8208
OFF_DN_Z = 1536
OFF_DN_BETA = 2048
OFF_DN_A = 2056
OFF_NA_Q = 2064
OFF_NA_K = 2576
OFF_NA_V = 3088
OFF_NA_Z = 3600
OFF_PL_U = 4112
OFF_PL_Z = 4624
OFF_GATE = 5136
EPS = 1e-6
POOL_WINDOWS = (2, 4, 8, 16)
NPS = 4
NCST = 582 + 6 * 64 + 64 + 6 * 128
BIG = 30000.0


class Dep:
    __slots__ = ("w", "r", "excl")

    def __init__(self, w=None, excl=False):
        self.w = w
        self.r = []
        self.excl = excl


class Op:
    __slots__ = ("eng", "fn", "waits", "signal", "count", "is_dma", "dsem", "dval", "dprev")

    def __init__(self, eng, fn, is_dma=False):
        self.eng = eng
        self.fn = fn
        self.waits = []
        self.signal = False
        self.count = None
        self.is_dma = is_dma
        self.dsem = None
        self.dval = None
        self.dprev = None


class Prog:
    def __init__(self, nc):
        self.nc = nc
        self.ops = {e: [] for e in ENGS}
        self.ndma = {e: 0 for e in ENGS}
        self.dtot = {e: [0] * NDSEM for e in ENGS}
        self.nops = 0

    def _mk(self, eng, fn, reads, writes, is_dma):
        o = Op(eng, fn, is_dma)
        ex = [t for t in reads if t.excl]
        if ex:
            reads = [t for t in reads if not t.excl]
            writes = list(writes) + [t for t in ex if t not in writes]
        deps = []
        seen = set()

        def add(d):
            if d is None or id(d) in seen:
                return
            if (not d.is_dma) and d.eng == "pe" and eng == "pe" and not is_dma:
                return
            seen.add(id(d))
            deps.append(d)

        for t in reads:
            add(t.w)
        for t in writes:
            add(t.w)
            for r in t.r:
                add(r)
        o.waits = deps
        for d in deps:
            d.signal = True
        for t in reads:
            if not is_dma:
                t.r = [x for x in t.r if x.is_dma or x.eng != eng]
            t.r.append(o)
        for t in writes:
            t.w = o
            t.r = []
        self.ops[eng].append(o)
        self.nops += 1
        return o

    def op(self, eng, fn, reads=(), writes=()):
        return self._mk(eng, fn, reads, writes, False)

    def dma(self, eng, out, in_, reads=(), writes=(), r32=False, slow=False):
        nc = self.nc
        eng = "pool" if type(out.tensor).__name__.startswith("DRam") else "sp"

        def fn(e):
            kw = {}
            if slow:
                kw["allow_slow_non_contiguous"] = True
            if r32:
                nc.dge_precook = False
            ins = e.dma_start(out=out, in_=in_, **kw)
            if r32:
                nc.dge_precook = True
            return ins

        o = self._mk(eng, fn, reads, writes, True)
        i = self.ndma[eng] % NDSEM
        self.ndma[eng] += 1
        o.dsem = i
        o.dprev = self.dtot[eng][i]
        self.dtot[eng][i] += 16
        o.dval = self.dtot[eng][i]
        o.signal = True
        return o

    def emit(self):
        nc = self.nc
        for e in ENGS:
            c = 0
            for o in self.ops[e]:
                if not o.is_dma and o.signal:
                    c += 1
                    o.count = c
        nsig = {e: sum(1 for o in self.ops[e] if (not o.is_dma and o.signal)) for e in ENGS}
        with contextlib.ExitStack() as st:
            esem = {e: st.enter_context(nc.semaphore("s_" + e)) for e in ENGS}
            dsem = {
                e: [st.enter_context(nc.semaphore("d_%s_%d" % (e, i))) for i in range(NDSEM)]
                for e in ("sp", "act", "pool")
            }
            block = st.enter_context(nc.Block())
            ops = self.ops
            dtot = self.dtot

            def run(e, engobj, final=False):
                seen_e = {x: 0 for x in ENGS}
                seen_d = {}
                for o in ops[e]:
                    for d in o.waits:
                        if d.is_dma:
                            key = (d.eng, d.dsem)
                            if seen_d.get(key, 0) >= d.dval:
                                continue
                            engobj.wait_ge(dsem[d.eng][d.dsem], d.dval)
                            seen_d[key] = d.dval
                        else:
                            if seen_e[d.eng] >= d.count:
                                continue
                            engobj.wait_ge(esem[d.eng], d.count)
                            seen_e[d.eng] = d.count
                    if o.is_dma:
                        key = (e, o.dsem)
                        if o.dprev > 0 and seen_d.get(key, 0) < o.dprev:
                            engobj.wait_ge(dsem[e][o.dsem], o.dprev)
                            seen_d[key] = o.dprev
                        ins = o.fn(engobj)
                        ins.then_inc(dsem[e][o.dsem], 16)
                    else:
                        ins = o.fn(engobj)
                        if o.signal:
                            ins.then_inc(esem[e], 1)
                if final:
                    for x in ENGS:
                        if x != e and nsig[x] > 0:
                            engobj.wait_ge(esem[x], nsig[x])
                    for q in ("sp", "act", "pool"):
                        for i in range(NDSEM):
                            if dtot[q][i] > 0:
                                engobj.wait_ge(dsem[q][i], dtot[q][i])

            @block.tensor
            def _(eng):
                run("pe", eng)

            @block.vector
            def _(eng):
                run("dve", eng)

            @block.scalar
            def _(eng):
                run("act", eng)

            @block.gpsimd
            def _(eng):
                run("pool", eng)

            @block.sync
            def _(eng):
                run("sp", eng, final=True)


class K:
    def __init__(self, nc, st):
        self.nc = nc
        self.st = st
        self.P = Prog(nc)
        self.din = {}
        self.dout = {}
        self.psb = [st.enter_context(nc.psum_tensor("psb%d" % i, [128, 512], F32)) for i in range(8)]
        self.psd = [Dep(excl=True) for _ in range(8)]
        self.psi = 0
        self.dq = 0

    def inp(self, name, shape, dt=F32):
        t = self.nc.dram_tensor(name, list(shape), dt, kind="ExternalInput")
        self.din[name] = t
        return t

    def outp(self, name, shape):
        t = self.nc.dram_tensor(name, list(shape), F32, kind="ExternalOutput")
        self.dout[name] = t
        return t

    def scr(self, name, shape, dt=F32):
        return self.nc.dram_tensor(name, list(shape), dt, kind="Internal")

    def sb(self, name, shape, dt=F32):
        return self.st.enter_context(self.nc.sbuf_tensor(name, list(shape), dt))

    def ps(self):
        i = self.psi
        self.psi = (i + 1) % 8
        return self.psb[i], self.psd[i]

    def q(self):
        self.dq ^= 1
        return "sp" if self.dq else "act"

    def mm(self, out, lhsT, rhs, start, stop, reads, writes):
        self.P.op("pe", lambda e: e.matmul(out, lhsT=lhsT, rhs=rhs, start=start, stop=stop), reads, writes)

    def tr(self, out, in_, ident, reads, writes):
        self.P.op("pe", lambda e: e.transpose(out, in_, ident), reads, writes)

    def act(self, out, in_, func, reads, writes, bias=None, scale=1.0, accum=None):
        def fn(e):
            kw = {}
            if bias is not None:
                kw["bias"] = bias
            if accum is not None:
                kw["accum_out"] = accum
            return e.activation(out=out, in_=in_, func=func, scale=scale, **kw)

        self.P.op("act", fn, reads, writes)

    def tt(self, out, in0, in1, op, reads, writes, eng="dve"):
        self.P.op(eng, lambda e: e.tensor_tensor(out=out, in0=in0, in1=in1, op=op), reads, writes)

    def ts(self, out, in0, s1, s2, op0, op1, reads, writes, eng="dve"):
        if s2 is None:
            self.P.op(eng, lambda e: e.tensor_scalar(out=out, in0=in0, scalar1=s1, scalar2=None, op0=op0), reads, writes)
        else:
            self.P.op(eng, lambda e: e.tensor_scalar(out=out, in0=in0, scalar1=s1, scalar2=s2, op0=op0, op1=op1), reads, writes)

    def stt(self, out, in0, scalar, in1, op0, op1, reads, writes, eng="dve"):
        self.P.op(eng, lambda e: e.scalar_tensor_tensor(out=out, in0=in0, scalar=scalar, in1=in1, op0=op0, op1=op1),
                  reads, writes)

    def rcp(self, out, in_, reads, writes):
        self.P.op("dve", lambda e: e.reciprocal(out=out, in_=in_), reads, writes)

    def cp(self, out, in_, reads, writes, eng="dve"):
        self.P.op(eng, lambda e: e.tensor_copy(out=out, in_=in_), reads, writes)

    def ms(self, ap, val, writes, eng="pool"):
        self.P.op(eng, lambda e: e.memset(ap, val), (), writes)


def build_program(depth=DEPTH, do_dn=True, do_na=True):
    nc = bass.Bass("TRN2", target_bir_lowering=False)
    st = contextlib.ExitStack()
    with st:
        k = K(nc, st)
        P = k.P
        x_p = k.inp("x_p", [NPS, SEQ, D])
        x_s = k.inp("x_s", [DSEQ, D])
        cvec = k.inp("cvec", [2, D])
        w_ada = k.inp("w_ada", [depth, D, 3 * D], F32R)
        b_ada = k.inp("b_ada", [depth, 3 * D])
        g_pre = k.inp("g_pre", [depth, D])
        g_post = k.inp("g_post", [depth, D])
        w_in = k.inp("w_in", [depth, D, NIN], F32R)
        pool_w = k.inp("pool_w", [depth, 4, 128, 128], F32R)
        pool_scale = k.inp("pool_scale", [depth, 512])
        w_br = [k.inp(n, [depth, 512, D], F32R) for n in ("w_br_dn", "w_br_na", "w_br_pl")]
        w_out = k.inp("w_out", [depth, D, D], F32R)
        invcnt_p = k.inp("invcnt_p", [4, SEQ])
        invcnt_s = k.inp("invcnt_s", [4, DSEQ])
        ident_in = k.inp("ident", [128, 128])
        conv_dn = k.inp("conv_dn", [depth, 3, 1536])
        a_log = k.inp("a_log", [depth, 8])
        dt_bias = k.inp("dt_bias", [depth, 8])
        g_norm = k.inp("g_norm", [depth, 128])
        sdn = k.inp("sdn", [depth, 2, 4, 128, 128])
        dncst_in = k.inp("dncst", [128, NCST])
        ck_in = k.inp("ck", [depth, 256, 512])
        cv_in = k.inp("cvv", [depth, 256, 512], F32R)
        rpad_in = k.inp("rpad", [depth, 8, 15, 127])

        y_p = k.outp("y_p", [NPS, SEQ, D])
        y_s = k.outp("y_s", [DSEQ, D])
        nk = k.outp("nk", [NPS, depth, SEQ, 512])
        nv = k.outp("nv", [NPS, depth, SEQ, 512])
        d_nkv = Dep()
        nst = k.outp("nst", [NPS, depth, 2, 4, 128, 128])
        d_nst = Dep()

        xs_p = [k.scr("xs_p%d" % i, [NPS, SEQ, D]) for i in range(2)]
        xs_s = [k.scr("xs_s%d" % i, [DSEQ, D]) for i in range(2)]
        yscr = k.scr("yscr", [12, 128, DSEQ], F32R)
        d_xs_p = [[Dep() for _ in range(NPS)] for _ in range(2)]
        d_xs_s = [Dep() for _ in range(2)]
        d_yscr = [Dep() for _ in range(12)]
        gscr = k.scr("gscr", [24, 128, DSEQ])
        d_gscr = [Dep() for _ in range(24)]
        mscr = k.scr("mscr", [8, 128, DSEQ], F32R)
        d_mscr = Dep()

        ident = k.sb("ident_sb", [128, 128])
        d_ident = Dep()
        P.dma("sp", ident[:], ident_in.ap(), writes=[d_ident])
        cst = k.sb("dncst_sb", [128, NCST])
        d_cst = Dep()
        P.dma("act", cst[:], dncst_in.ap(), writes=[d_cst])
        TRI = [cst[:, 0:128], cst[:, 128:256]]
        BLK = cst[:, 256:384]
        ONES = cst[:, 384:512]
        I2 = cst[:, 512:576]
        SEL2 = cst[:, 576:578]
        SELLAST = cst[:, 578:582]
        MASK = [[cst[:, 582 + (ty * 2 + d_) * 64: 582 + (ty * 2 + d_ + 1) * 64] for d_ in range(2)] for ty in range(3)]
        CM = cst[:, 582 + 384:582 + 384 + 64]
        MASKB = [[cst[:, 1030 + (ty * 2 + d_) * 128: 1030 + (ty * 2 + d_ + 1) * 128] for d_ in range(2)] for ty in range(3)]
        hT = k.sb("hT", [128, 8, DSEQ], F32R)
        d_hT = Dep()
        small = k.sb("small", [128, 16])
        d_small = Dep()
        silucT = k.sb("silucT", [128, 8, 2], F32R)
        d_siluc = Dep()
        modcol = k.sb("modcol", [128, 16, 2])
        s1col = k.sb("s1col", [128, 8, 2])
        d_modcol = Dep()
        ggscr = k.scr("ggscr", [2, 128, D])
        d_gg = Dep()
        RSZ = 15 * 1024
        FSZ = 18 * 1024 + 512
        arenaR = k.sb("arenaR", [128, RSZ], F32R)
        arenaF = k.sb("arenaF", [128, FSZ])
        arena_deps = []
        WOFF = RSZ - 4096
        wdeps = [Dep() for _ in range(4)]

        def fence():
            f = P.op("dve", lambda e: e.memset(small[:, 15:16], 0.0), reads=(), writes=list(arena_deps))
            arena_deps.clear()
            return f

        class Carve:
            def __init__(self, seed):
                self.offR = 0
                self.offF = 0
                self.seed = seed

            def getw(self, j, n=1):
                a = arenaR[:, WOFF + j * 1024:WOFF + (j + n) * 1024]
                return a, wdeps[j:j + n]

            def get(self, cols, dt=F32):
                if dt == F32R:
                    a = arenaR[:, self.offR:self.offR + cols]
                    self.offR += cols
                    assert self.offR <= WOFF, self.offR
                else:
                    a = arenaF[:, self.offF:self.offF + cols]
                    self.offF += cols
                    assert self.offF <= FSZ, self.offF
                d = Dep(self.seed)
                arena_deps.append(d)
                return a, d

        craw = k.sb("craw", [128, 8, 2])
        for j in range(2):
            P.dma("sp", craw[:, :, j], cvec.ap()[j].rearrange("(c p) -> p c", p=128), writes=[d_siluc], slow=True)
        k.act(silucT[:], craw[:], AF.Silu, [d_siluc], [d_siluc])

        seqs = [("p", i, SEQ) for i in range(NPS)] + [("s", 0, DSEQ)]
        zscr = k.scr("zscr", [depth, 120, 64, 127])
        d_zscr = [Dep() for _ in range(depth)]
        if do_na:
            for l_ in range(depth):
                P.dma("sp", zscr.ap()[l_], AP(rpad_in, l_ * 120 * 127, [[127, 120], [0, 64], [1, 127]]), writes=[d_zscr[l_]])

        def dn_unit(l, kind, si, T, h):
            NT = T // 128
            TT = min(T, 512)
            NTL = T // TT
            fz = fence()
            cv = Carve(fz)
            ws = []
            for off in (0, 512, 1024, OFF_DN_Z):
                wa, dwl = cv.getw(len(ws))
                dw = dwl[0]
                w3 = wa.rearrange("p (c n) -> p c n", c=8)
                c0 = off + h * 128
                P.dma(k.q(), w3, w_in.ap()[l, :, c0:c0 + 128].rearrange("(c p) n -> p c n", p=128), writes=[dw], r32=True)
                ws.append((w3, dw))
            (wq, d_wq), (wk, d_wk), (wv, d_wv), (wz, d_wz) = ws
            wba_f, d_wba = cv.get(8 * 4, F32R)
            wba = wba_f.rearrange("p (c n) -> p c n", c=8)
            for j4 in range(4):
                cj4 = OFF_DN_BETA + 4 * j4 + h
                P.dma("sp", wba[:, :, j4], w_in.ap()[l, :, cj4].rearrange("(c p) -> p c", p=128), writes=[d_wba], r32=True, slow=True)
            raw, d_raw = cv.get(T + 2)
            qf, d_qf = cv.get(T)
            kf, d_kf = cv.get(T)
            vf, d_vf = cv.get(T)
            oacc, _ = cv.get(T)
            d_oacc = [Dep(fz) for _ in range(NT)]
            arena_deps.extend(d_oacc)
            tmpb, d_tmpb = cv.get(512)
            cw, d_cw = cv.get(9)
            for idx in range(3):
                c0 = idx * 512 + h * 128
                P.dma("act", cw[:, idx * 3:(idx + 1) * 3], conv_dn.ap()[l, :, c0:c0 + 128].rearrange("t p -> p t"),
                      writes=[d_cw], slow=True)
            k.ms(raw[:, 0:1], 0.0, [d_raw])
            k.ms(raw[:, T + 1:T + 2], 0.0, [d_raw])
            for i in range(NT):
                k.ms(oacc[:, i * 128:(i + 1) * 128], 0.0, [d_oacc[i]])
            dsts = [(qf, d_qf), (kf, d_kf), (vf, d_vf)]
            for idx in range(3):
                w3, dw = ws[idx]
                dst, dd = dsts[idx]
                for t in range(NTL):
                    pb, pd = k.ps()
                    for kc in range(8):
                        k.mm(pb[:, 0:TT], w3[:, kc, :], hTv[:, kc, t * TT:(t + 1) * TT], kc == 0, kc == 7, [dw, d_hT], [pd])
                    k.cp(raw[:, 1 + t * TT:1 + (t + 1) * TT], pb[:, 0:TT], [pd], [d_raw])
                for t in range(NTL):
                    a = t * TT
                    k.ts(dst[:, a:a + TT], raw[:, a:a + TT], cw[:, idx * 3:idx * 3 + 1], None, ALU.mult, None, [d_raw, d_cw], [dd])
                    k.stt(tmpb[:, 0:TT], raw[:, a + 1:a + 1 + TT], cw[:, idx * 3 + 1:idx * 3 + 2], dst[:, a:a + TT],
                          ALU.mult, ALU.add, [d_raw, d_cw, dd], [d_tmpb])
                    k.stt(dst[:, a:a + TT], raw[:, a + 2:a + 2 + TT], cw[:, idx * 3 + 2:idx * 3 + 3], tmpb[:, 0:TT],
                          ALU.mult, ALU.add, [d_raw, d_cw, d_tmpb], [dd])
                    k.act(dst[:, a:a + TT], dst[:, a:a + TT], AF.Silu, [dd], [dd])
            for idx in range(2):
                dst, dd = dsts[idx]
                for t in range(NTL):
                    a = t * TT
                    k.act(tmpb[:, 0:TT], dst[:, a:a + TT], AF.Square, [dd], [d_tmpb])
                    pb, pd = k.ps()
                    k.mm(pb[:, 0:TT], ONES, tmpb[:, 0:TT], True, True, [d_cst, d_tmpb], [pd])
                    k.act(tmpb[:, 0:TT], pb[:, 0:TT], AF.Sqrt, [pd], [d_tmpb], bias=EPS)
                    k.rcp(tmpb[:, 0:TT], tmpb[:, 0:TT], [d_tmpb], [d_tmpb])
                    if idx == 0:
                        k.stt(dst[:, a:a + TT], dst[:, a:a + TT], 128.0 ** -0.5, tmpb[:, 0:TT], ALU.mult, ALU.mult, [dd, d_tmpb], [dd])
                    else:
                        k.tt(dst[:, a:a + TT], dst[:, a:a + TT], tmpb[:, 0:TT], ALU.mult, [dd, d_tmpb], [dd])
            zs = raw
            d_zs = d_raw
            for t in range(NTL):
                pb, pd = k.ps()
                for kc in range(8):
                    k.mm(pb[:, 0:TT], wz[:, kc, :], hTv[:, kc, t * TT:(t + 1) * TT], kc == 0, kc == 7, [d_wz, d_hT], [pd])
                k.act(zs[:, t * TT:(t + 1) * TT], pb[:, 0:TT], AF.Silu, [pd], [d_zs])
            d_g = Dep(fz)
            arena_deps.append(d_g)
            G_ = [d_g]
            ba, _ = cv.get(NT * 4)
            ba3 = ba.rearrange("p (t n) -> p t n", n=4)
            pb, pd = k.ps()
            for i in range(NT):
                for kc in range(8):
                    k.mm(pb[:, i * 4:(i + 1) * 4], hTv[:, kc, i * 128:(i + 1) * 128], wba[:, kc, :], kc == 0, kc == 7, [d_wba, d_hT], [pd])
            k.cp(ba, pb[:, 0:NT * 4], [pd], G_)
            NK = 10
            GA, _ = cv.get(NT * NK * 2)
            GA4 = GA.rearrange("p (t k d) -> p t k d", k=NK, d=2)
            K_BETA, K_NEGB, K_G, K_GB, K_EG, K_BG, K_EGL, K_GL = range(8)
            gk = lambda kk: GA4[:, :, kk, :]

            def g2():
                a_, _ = cv.get(NT * 2)
                return a_, a_.rearrange("p (t n) -> p t n", n=2)

            lnb, lnb3 = g2()
            la, la3 = g2()
            gt, gt3 = g2()
            rowc, _ = cv.get(4)
            gn_row, d_gn = cv.get(128)
            P.dma("sp", rowc[:, 0:1], dt_bias.ap()[l, h:h + 1].partition_broadcast(128), writes=G_)
            P.dma("sp", rowc[:, 1:2], dt_bias.ap()[l, 4 + h:5 + h].partition_broadcast(128), writes=G_)
            P.dma("act", rowc[:, 2:3], a_log.ap()[l, h:h + 1].partition_broadcast(128), writes=G_)
            P.dma("act", rowc[:, 3:4], a_log.ap()[l, 4 + h:5 + h].partition_broadcast(128), writes=G_)
            P.dma("sp", gn_row, g_norm.ap()[l].partition_broadcast(128), writes=[d_gn])
            k.act(rowc[:, 2:4], rowc[:, 2:4], AF.Exp, G_, G_)
            k.ts(rowc[:, 2:4], rowc[:, 2:4], -1.0, None, ALU.mult, None, G_, G_)
            k.act(gk(K_BETA), ba3[:, :, 0:2], AF.Sigmoid, G_, G_)
            k.ts(gk(K_NEGB), gk(K_BETA), -1.0, None, ALU.mult, None, G_, G_)
            k.act(gt3, ba3[:, :, 0:2], AF.Exp, G_, G_, scale=-1.0)
            k.act(gt, gt, AF.Ln, G_, G_, bias=1.0)
            k.ts(lnb, gt, -1.0, None, ALU.mult, None, G_, G_)
            k.tt(gt3, ba3[:, :, 2:4], rowc[:, 0:2].unsqueeze(1).to_broadcast([128, NT, 2]), ALU.add, G_, G_)
            k.act(gt, gt, AF.Exp, G_, G_)
            k.act(gt, gt, AF.Ln, G_, G_, bias=1.0)
            k.tt(la3, gt3, rowc[:, 2:4].unsqueeze(1).to_broadcast([128, NT, 2]), ALU.mult, G_, G_)
            pb, pd = k.ps()
            for i in range(NT):
                k.mm(pb[:, i * 2:(i + 1) * 2], TRI[0], la3[:, i, :], True, True, [d_cst, d_g], [pd])
                k.mm(pb[:, 64 + i * 2:64 + (i + 1) * 2], TRI[1], la3[:, i, :], True, True, [d_cst, d_g], [pd])
            k.cp(gk(K_G)[:, :, 0:1], pb[:, 0:NT * 2].rearrange("p (t n) -> p t n", n=2)[:, :, 0:1], [pd], G_)
            k.cp(gk(K_G)[:, :, 1:2], pb[:, 64:64 + NT * 2].rearrange("p (t n) -> p t n", n=2)[:, :, 1:2], [pd], G_)
            k.tt(gk(K_GB), gk(K_G), lnb3, ALU.add, G_, G_)
            k.act(gk(K_EG), gk(K_G), AF.Exp, G_, G_)
            k.tt(gk(K_BG), gk(K_BETA), gk(K_EG), ALU.mult, G_, G_)
            k.tt(gt3, gk(K_G), SEL2.unsqueeze(1).to_broadcast([128, NT, 2]), ALU.mult, G_ + [d_cst], G_)
            pb, pd = k.ps()
            k.mm(pb[:, 0:NT * 2], BLK, gt, True, True, [d_cst, d_g], [pd])
            k.tt(gt3, pb[:, 0:NT * 2].rearrange("p (t n) -> p t n", n=2), gk(K_G), ALU.subtract, [pd, d_g], G_)
            k.act(gk(K_EGL), gt3, AF.Exp, G_, G_)
            gsel2, _ = cv.get(NT * 4)
            k.tt(gsel2.rearrange("p (t d c) -> p t d c", d=2, c=2), gk(K_G).unsqueeze(3).to_broadcast([128, NT, 2, 2]),
                 SELLAST.rearrange("p (d c) -> p d c", d=2).unsqueeze(1).to_broadcast([128, NT, 2, 2]), ALU.mult,
                 G_ + [d_cst], G_)
            pb, pd = k.ps()
            k.mm(pb[:, 0:NT * 4], ONES, gsel2, True, True, [d_cst, d_g], [pd])
            for cp in range(2):
                k.act(gk(K_GL + cp), pb[:, 0:NT * 4].rearrange("p (t d c) -> p t d c", d=2, c=2)[:, :, :, cp], AF.Exp, [pd], G_)
            S = []
            for d_ in range(2):
                s_, ds_ = cv.get(128)
                if kind == "p":
                    k.ms(s_, 0.0, [ds_])
                else:
                    P.dma(k.q(), s_, sdn.ap()[l, d_, h], writes=[ds_])
                S.append((s_, ds_))
            X, d_X = cv.get(256)
            vb, d_vb = cv.get(256)
            Gd, d_Gd = cv.get(256)
            Gbd, d_Gbd = cv.get(256)
            E1, d_E1 = cv.get(256)
            E2, d_E2 = cv.get(256)
            E3, d_E3 = cv.get(256)
            MML = [[cv.get(256), cv.get(256)] for _ in range(2)]
            PTL = [cv.get(128) for _ in range(2)]
            OB = []
            for _i in range(2):
                o_ = {}
                o_["attnT"], o_["d_at"] = cv.get(256)
                o_["kg"], o_["d_kg"] = cv.get(256)
                o_["uw"], o_["d_uw"] = cv.get(512)
                ls_, o_["d_LS"] = cv.get(NK * 2)
                o_["LS3"] = ls_.rearrange("p (k d) -> p k d", d=2)
                OB.append(o_)
            SC = []
            for d_ in range(2):
                SC.append((cv.get(128), cv.get(128), cv.get(128)))
            PB = k.psb
            PD = k.psd

            def lanes(s_):
                return (s_, NT - 1 - s_)

            def prep(s_):
                il = lanes(s_)
                ob = OB[s_ % 2]
                LS3, d_LS = ob["LS3"], ob["d_LS"]
                ls = lambda kk, d_: LS3[:, kk, d_:d_ + 1]
                tsl = [slice(il[d_] * 128, (il[d_] + 1) * 128) for d_ in range(2)]
                for d_ in range(2):
                    k.cp(LS3[:, :, d_], GA4[:, il[d_], :, d_], [d_g], [d_LS], eng="pool")
                pk, pkd = PB[0], PD[0]
                for d_ in range(2):
                    k.tr(pk[:, d_ * 256:d_ * 256 + 128], kf[:, tsl[d_]], ident[:], [d_kf, d_ident], [pkd])
                    k.tr(pk[:, d_ * 256 + 128:d_ * 256 + 256], vf[:, tsl[d_]], ident[:], [d_vf, d_ident], [pkd])
                for d_ in range(2):
                    ds = slice(d_ * 128, (d_ + 1) * 128)
                    k.ts(ob["kg"][:, ds], pk[:, d_ * 256:d_ * 256 + 128], ls(K_EGL, d_), None, ALU.mult, None, [pkd, d_LS], [ob["d_kg"]])
                    k.act(X[:, ds], pk[:, d_ * 256:d_ * 256 + 128], AF.Copy, [pkd, d_LS], [d_X], scale=ls(K_BG, d_))
                    k.ts(vb[:, ds], pk[:, d_ * 256 + 128:d_ * 256 + 256], ls(K_BETA, d_), None, ALU.mult, None, [pkd, d_LS], [d_vb])
                yield
                pa, pad = PB[1], PD[1]
                for d_ in range(2):
                    k.mm(pa[:, d_ * 256:d_ * 256 + 128], kf[:, tsl[d_]], kf[:, tsl[d_]], True, True, [d_kf], [pad])
                    k.mm(pa[:, d_ * 256 + 128:d_ * 256 + 256], kf[:, tsl[d_]], qf[:, tsl[d_]], True, True, [d_kf, d_qf], [pad])
                for d_ in range(2):
                    ds = slice(d_ * 128, (d_ + 1) * 128)
                    k.ts(Gd[:, ds], ident[:], ls(K_G, d_), None, ALU.mult, None, [d_ident, d_LS], [d_Gd])
                    k.act(Gbd[:, ds], ident[:], AF.Copy, [d_ident, d_LS], [d_Gbd], scale=ls(K_GB, d_))
                pg, pgd = PB[2], PD[2]
                k.mm(pg[:, 0:256], ONES, Gd, True, True, [d_cst, d_Gd], [pgd])
                k.mm(pg[:, 256:512], ONES, Gbd, True, True, [d_cst, d_Gbd], [pgd])
                yield
                for d_ in range(2):
                    ds = slice(d_ * 128, (d_ + 1) * 128)
                    k.stt(E1[:, ds], pg[:, ds], ls(K_G, d_), MASKB[0][d_], ALU.subtract, ALU.add, [pgd, d_LS, d_cst], [d_E1])
                    k.stt(E2[:, ds], pg[:, 256 + d_ * 128:256 + (d_ + 1) * 128], ls(K_G, d_), MASKB[1][d_], ALU.subtract, ALU.add,
                          [pgd, d_LS, d_cst], [d_E2])
                    k.stt(E3[:, ds], pg[:, ds], ls(K_G, d_), MASKB[2][d_], ALU.subtract, ALU.add, [pgd, d_LS, d_cst], [d_E3])
                k.act(E1, E1, AF.Exp, [d_E1], [d_E1])
                k.act(E2, E2, AF.Exp, [d_E2], [d_E2])
                k.act(E3, E3, AF.Exp, [d_E3], [d_E3], scale=-1.0)
                yield
                cur = [MML[d_][0] for d_ in range(2)]
                nxt = [MML[d_][1] for d_ in range(2)]
                for d_ in range(2):
                    ds = slice(d_ * 128, (d_ + 1) * 128)
                    c_, dc_ = cur[d_]
                    k.tt(ob["attnT"][:, ds], pa[:, d_ * 256 + 128:d_ * 256 + 256], E1[:, ds], ALU.mult, [pad, d_E1], [ob["d_at"]])
                    k.stt(c_[:, 128:256], pa[:, d_ * 256:d_ * 256 + 128], -1.0, E2[:, ds], ALU.mult, ALU.mult, [pad, d_E2], [dc_])
                    k.stt(c_[:, 0:128], pa[:, d_ * 256:d_ * 256 + 128], ls(K_NEGB, d_), E3[:, ds], ALU.mult, ALU.mult,
                          [pad, d_E3, d_LS], [dc_])
                    k.tt(PTL[d_][0], ident[:], c_[:, 128:256], ALU.add, [d_ident, dc_], [PTL[d_][1]])
                yield
                pmb = (3, 2)
                ppb = (0, 1)
                for lev in range(5):
                    lastl = (lev == 4)
                    for d_ in range(2):
                        c_, dc_ = cur[d_]
                        n_, dn_ = nxt[d_]
                        pm, pmd = PB[pmb[d_]], PD[pmb[d_]]
                        k.mm(pm[:, 0:128], c_[:, 128:256], c_[:, 0:128], True, True, [dc_], [pmd])
                        if not lastl:
                            k.mm(pm[:, 128:256], c_[:, 0:128], c_[:, 128:256], True, True, [dc_], [pmd])
                        ncols = 128 if lastl else 256
                        if d_ == 0:
                            k.act(n_[:, 0:ncols], pm[:, 0:ncols], AF.Copy, [pmd], [dn_])
                        else:
                            k.cp(n_[:, 0:ncols], pm[:, 0:ncols], [pmd], [dn_])
                    yield
                    for d_ in range(2):
                        n_, dn_ = nxt[d_]
                        pt_, dpt_ = PTL[d_]
                        pp, ppd = PB[ppb[d_]], PD[ppb[d_]]
                        k.mm(pp[:, 0:128], n_[:, 0:128], pt_, True, True, [dn_, dpt_], [ppd])
                        k.tt(pt_, pt_, pp[:, 0:128], ALU.add, [dpt_, ppd], [dpt_])
                    yield
                    cur, nxt = nxt, cur
                pu, pud = PB[2], PD[2]
                for d_ in range(2):
                    ds = slice(d_ * 128, (d_ + 1) * 128)
                    pt_, dpt_ = PTL[d_]
                    k.mm(pu[:, d_ * 256:d_ * 256 + 128], pt_, vb[:, ds], True, True, [dpt_, d_vb], [pud])
                    k.mm(pu[:, d_ * 256 + 128:d_ * 256 + 256], X[:, ds], pt_, True, True, [d_X, dpt_], [pud])
                k.cp(ob["uw"], pu[:, 0:512], [pud], [ob["d_uw"]])
                yield

            def scan(s_, d_):
                i = lanes(s_)[d_]
                ob = OB[s_ % 2]
                LS3, d_LS = ob["LS3"], ob["d_LS"]
                ds = slice(d_ * 128, (d_ + 1) * 128)
                es = slice(d_ * 64, (d_ + 1) * 64)
                (vnew, d_vn), (oasb, d_oa), (otmp, d_ot) = SC[d_]
                s_t, ds_ = S[d_]
                p1, p1d = PB[4 + 2 * d_], PD[4 + 2 * d_]
                p2, p2d = PB[5 + 2 * d_], PD[5 + 2 * d_]
                for cp in ((0, 1) if d_ == 0 else (1, 0)):
                    bs = slice(cp * 64, (cp + 1) * 64)
                    cs = slice(i * 128 + cp * 64, i * 128 + (cp + 1) * 64)
                    k.mm(p1[bs, 0:128], ob["uw"][:, d_ * 256 + 128 + cp * 64:d_ * 256 + 128 + (cp + 1) * 64], s_t, True, True,
                         [ob["d_uw"], ds_], [p1d])
                    k.mm(p1[bs, 128:256], qf[:, cs], s_t, True, True, [d_qf, ds_], [p1d])
                    k.tt(vnew[bs, :], ob["uw"][bs, d_ * 256:d_ * 256 + 128], p1[bs, 0:128], ALU.subtract, [ob["d_uw"], p1d], [d_vn])
                    yield
                    k.mm(p2[bs, 0:128], ob["attnT"][bs, d_ * 128 + cp * 64:d_ * 128 + (cp + 1) * 64], vnew[bs, :], True, True, [ob["d_at"], d_vn], [p2d])
                    k.mm(p2[:, 128:256], ob["kg"][bs, ds], vnew[bs, :], True, True, [ob["d_kg"], d_vn], [p2d])
                    k.act(oasb[bs, :], p2[bs, 0:128], AF.Copy, [p2d], [d_oa])
                    k.stt(s_t, s_t, LS3[:, K_GL + cp, d_:d_ + 1], p2[:, 128:256], ALU.mult, ALU.add, [ds_, d_LS, p2d], [ds_])
                    yield
                    k.stt(otmp[bs, :], p1[bs, 128:256], LS3[bs, K_EG, d_:d_ + 1], oasb[bs, :], ALU.mult, ALU.add,
                          [p1d, d_LS, d_oa], [d_ot])
                    k.tt(oacc[bs, i * 128:(i + 1) * 128], oacc[bs, i * 128:(i + 1) * 128], otmp[bs, :], ALU.add,
                         [d_oacc[i], d_ot], [d_oacc[i]], eng="pool")
                    yield

            def run_gens(gens):
                gens = list(gens)
                while gens:
                    for g_ in list(gens):
                        try:
                            next(g_)
                        except StopIteration:
                            gens.remove(g_)

            run_gens([prep(0)])
            for s_ in range(NT):
                gl_ = [scan(s_, 0), scan(s_, 1)]
                if s_ + 1 < NT:
                    gl_.insert(0, prep(s_ + 1))
                run_gens(gl_)
            if kind == "p":
                for d_ in range(2):
                    P.dma(k.q(), nst.ap()[si, l, d_, h], S[d_][0], reads=[S[d_][1]], writes=[d_nst])
            rs, d_rs = cv.get(4)
            on, d_on = cv.get(128)
            yT, d_yT = cv.get(128, F32R)
            for i in range(NT):
                tsl = slice(i * 128, (i + 1) * 128)
                k.act(on, oacc[:, tsl], AF.Square, [d_oacc[i]], [d_on, d_rs], accum=rs[:, 0:1])
                k.act(rs[:, 1:2], rs[:, 0:1], AF.Sqrt, [d_rs], [d_rs], bias=EPS, scale=1.0 / 128)
                k.rcp(rs[:, 2:3], rs[:, 1:2], [d_rs], [d_rs])
                k.stt(on, oacc[:, tsl], rs[:, 2:3], gn_row, ALU.mult, ALU.mult, [d_oacc[i], d_rs, d_gn, d_on], [d_on])
                pt, ptd = k.ps()
                k.tr(pt[:, 0:128], on, ident[:], [d_on, d_ident], [ptd])
                k.tt(yT, pt[:, 0:128], zs[:, tsl], ALU.mult, [ptd, d_zs], [d_yT])
                P.dma(k.q(), yscr_v[h, :, tsl], yT, reads=[d_yT], writes=[d_yscr[h]], r32=True)

        def na_unit(l, kind, si, T, hp):
            NT = T // 128
            TT = min(T, 512)
            fz = fence()
            cv = Carve(fz)
            ws = []
            for off in (OFF_NA_Q, OFF_NA_K, OFF_NA_V, OFF_NA_Z):
                wa, dwl = cv.getw(len(ws))
                dw = dwl[0]
                w3 = wa.rearrange("p (c n) -> p c n", c=8)
                c0 = off + hp * 128
                P.dma(k.q(), w3, w_in.ap()[l, :, c0:c0 + 128].rearrange("(c p) n -> p c n", p=128), writes=[dw], r32=True)
                ws.append((w3, dw))
            (wq, d_wq), (wk, d_wk), (wv, d_wv), (wz, d_wz) = ws
            qT, d_qT = cv.get(T, F32R)
            kT, d_kT = cv.get(T, F32R)
            vt_f, d_vt = cv.get(NT * 132, F32R)
            vtok = vt_f.rearrange("p (t h e) -> p t h e", t=NT, h=2)
            zs, d_zs = cv.get(T)
            ones, d_ones = cv.get(2)
            stg, d_stg = cv.get(256)
            k.ms(ones, 1.0, [d_ones])
            for t in range(T // TT):
                ts_ = slice(t * TT, (t + 1) * TT)
                pb, pd = k.ps()
                for kc in range(8):
                    k.mm(pb[:, 0:TT], wq[:, kc, :], hTv[:, kc, ts_], kc == 0, kc == 7, [d_wq, d_hT], [pd])
                k.ts(qT[:, ts_], pb[:, 0:TT], 0.125, None, ALU.mult, None, [pd], [d_qT])
                pb, pd = k.ps()
                for kc in range(8):
                    k.mm(pb[:, 0:TT], wk[:, kc, :], hTv[:, kc, ts_], kc == 0, kc == 7, [d_wk, d_hT], [pd])
                k.cp(kT[:, ts_], pb[:, 0:TT], [pd], [d_kT])
                pb, pd = k.ps()
                for kc in range(8):
                    k.mm(pb[:, 0:TT], wz[:, kc, :], hTv[:, kc, ts_], kc == 0, kc == 7, [d_wz, d_hT], [pd])
                k.act(zs[:, ts_], pb[:, 0:TT], AF.Silu, [pd], [d_zs])
            for i in range(NT):
                is_ = slice(i * 128, (i + 1) * 128)
                pb, pd = k.ps()
                for kc in range(8):
                    k.mm(pb[:, 0:128], hTv[:, kc, is_], wv[:, kc, :], kc == 0, kc == 7, [d_wv, d_hT], [pd])
                k.cp(vtok[:, i, :, 0:64], pb[:, 0:128].rearrange("p (h e) -> p h e", h=2), [pd], [d_vt])
                k.cp(vtok[:, i, :, 64:66], ones[:, 0:2].unsqueeze(1).to_broadcast([128, 2, 2]), [d_ones], [d_vt], eng="pool")
                if kind == "p":
                    k.cp(stg[:, 0:128], pb[:, 0:128], [pd], [d_stg])
                    P.dma("act", nv.ap()[si, l, is_, hp * 128:(hp + 1) * 128], stg[:, 0:128], reads=[d_stg], writes=[d_nkv])
                    pb, pd = k.ps()
                    for kc in range(8):
                        k.mm(pb[:, 0:128], hTv[:, kc, is_], wk[:, kc, :], kc == 0, kc == 7, [d_wk, d_hT], [pd])
                    k.cp(stg[:, 128:256], pb[:, 0:128], [pd], [d_stg])
                    P.dma("act", nk.ap()[si, l, is_, hp * 128:(hp + 1) * 128], stg[:, 128:256], reads=[d_stg], writes=[d_nkv])
            otok, d_otok = cv.get(128)
            rden, d_rden = cv.get(2)
            yT, d_yT = cv.get(128, F32R)
            if kind == "p":
                PT = [cv.get(256, F32R) for _ in range(4)]
                po, pod = k.ps()
                for h2 in range(2):
                    hs = slice(h2 * 64, (h2 + 1) * 64)
                    for kt in range(2):
                        pt, d_pt = PT[h2 * 2 + kt]
                        pb, pd = k.ps()
                        k.mm(pb[:, 0:256], kT[hs, kt * 128:(kt + 1) * 128], qT[hs, 0:256], True, True, [d_kT, d_qT], [pd])
                        k.act(pt, pb[:, 0:256], AF.Exp, [pd], [d_pt])
                    for qt in range(2):
                        for kt in range(2):
                            pt, d_pt = PT[h2 * 2 + kt]
                            c0 = qt * 132 + h2 * 66
                            k.mm(po[:, c0:c0 + 66], pt[:, qt * 128:(qt + 1) * 128], vtok[:, kt, h2, :], kt == 0, kt == 1,
                                 [d_pt, d_vt], [pod])
                for qt in range(2):
                    qs = slice(qt * 128, (qt + 1) * 128)
                    for h2 in range(2):
                        c0 = qt * 132 + h2 * 66
                        k.rcp(rden[:, h2:h2 + 1], po[:, c0 + 64:c0 + 65], [pod], [d_rden])
                        k.ts(otok[:, h2 * 64:(h2 + 1) * 64], po[:, c0:c0 + 64], rden[:, h2:h2 + 1], None, ALU.mult, None,
                             [pod, d_rden], [d_otok])
                    pb, pd = k.ps()
                    k.tr(pb[:, 0:128], otok, ident[:], [d_otok, d_ident], [pd])
                    k.tt(yT, pb[:, 0:128], zs[:, qs], ALU.mult, [pd, d_zs], [d_yT])
                    P.dma(k.q(), yscr_v[4 + hp, :, qs], yT, reads=[d_yT], writes=[d_yscr[4 + hp]], r32=True)
            else:
                kcT, d_kcT = cv.get(256, F32R)
                vc_f, d_vc = cv.get(2 * 132, F32R)
                vctx = vc_f.rearrange("p (t h e) -> p t h e", t=2, h=2)
                cst_k, d_cstk = cv.get(256)
                P.dma("sp", cst_k.rearrange("p (t n) -> p t n", t=2),
                      ck_in.ap()[l, :, hp * 128:(hp + 1) * 128].rearrange("(t p) n -> p t n", p=128), writes=[d_cstk])
                pb, pd = k.ps()
                for t in range(2):
                    k.tr(pb[:, t * 128:(t + 1) * 128], cst_k[:, t * 128:(t + 1) * 128], ident[:], [d_cstk, d_ident], [pd])
                k.cp(kcT, pb[:, 0:256], [pd], [d_kcT])
                for t in range(2):
                    P.dma("act", vctx[:, t, :, 0:64],
                          cv_in.ap()[l, t * 128:(t + 1) * 128, hp * 128:(hp + 1) * 128].rearrange("p (h e) -> p h e", h=2),
                          writes=[d_vc], r32=True)
                    k.cp(vctx[:, t, :, 64:66], ones[:, 0:2].unsqueeze(1).to_broadcast([128, 2, 2]), [d_ones], [d_vc], eng="pool")
                E2f, d_E2 = cv.get(2 * 15 * 64)
                E2 = E2f.rearrange("p (h r c) -> p h r c", h=2, r=15)
                for a in range(2):
                    for h2 in range(2):
                        head = hp * 2 + h2
                        src = AP(zscr, ((l * 8 + head) * 15) * 8128 + 63, [[126, 64], [8128, 15], [1, 64]])
                        P.dma("sp" if a == 0 else "act", E2[a * 64:(a + 1) * 64, h2, :, :], src, reads=[d_zscr[l]], writes=[d_E2])
                k.act(E2f, E2f, AF.Exp, [d_E2], [d_E2])
                k.tt(E2f.rearrange("p (g c) -> p g c", c=64), E2f.rearrange("p (g c) -> p g c", c=64),
                     CM.unsqueeze(1).to_broadcast([128, 30, 64]), ALU.mult, [d_E2, d_cst], [d_E2])
                TABf, d_TAB = cv.get(2 * 21 * 128)
                TAB = TABf.rearrange("p (h t q) -> p h t q", h=2, t=21)
                k.ms(TABf, 0.0, [d_TAB])
                plans = {}
                tid = 0
                plans["int"] = []
                for j in range(5):
                    for a in range(2):
                        for b in range(2):
                            dr = 2 * j - 4 + a - b
                            if -4 <= dr <= 3:
                                plans["int"].append((tid, a, b, dr))
                    tid += 1
                tid0 = {"int": 0}
                for m_ in (0, 1, 14, 15):
                    tid0[m_] = tid
                    kt0 = 0 if m_ < 2 else 12
                    plans[m_] = []
                    for j in range(4):
                        for a in range(2):
                            for b in range(2):
                                kr = 2 * (kt0 + j) + a
                                r = 2 * m_ + b
                                rs_ = min(max(r - 4, 0), 24)
                                if rs_ <= kr <= rs_ + 7:
                                    plans[m_].append((tid, a, b, kr - r))
                        tid += 1
                assert tid == 21
                _ci = 0
                for key_, pl in plans.items():
                    for (tid_, a, b, dr) in pl:
                        k.cp(TAB[a * 64:(a + 1) * 64, :, tid_, b * 64:(b + 1) * 64], E2[a * 64:(a + 1) * 64, :, dr + 7, :],
                             [d_E2], [d_TAB], eng=("dve" if _ci % 2 == 0 else "pool"))
                        _ci += 1
                PTb = [cv.get(7 * 128, F32R) for _ in range(2)]
                tEb = [cv.get(512) for _ in range(2)]
                tE2b = [cv.get(128) for _ in range(2)]
                its = [(m_, h2) for m_ in range(16) for h2 in range(2)]

                def info(m_):
                    if 2 <= m_ <= 13:
                        return [m_ - 2 + j for j in range(5)], 0
                    return [(0 if m_ < 2 else 12) + j for j in range(4)], tid0[m_]

                def emit_scores(m_, h2):
                    kts, t0 = info(m_)
                    nl = len(kts)
                    qs = slice(m_ * 128, (m_ + 1) * 128)
                    hs = slice(h2 * 64, (h2 + 1) * 64)
                    pA, pAd = k.ps()
                    for j in range(4):
                        k.mm(pA[:, j * 128:(j + 1) * 128], kT[hs, kts[j] * 128:(kts[j] + 1) * 128], qT[hs, qs], True, True,
                             [d_kT, d_qT], [pAd])
                    pB, pBd = k.ps()
                    c_ = 0
                    if nl == 5:
                        k.mm(pB[:, 0:128], kT[hs, kts[4] * 128:(kts[4] + 1) * 128], qT[hs, qs], True, True, [d_kT, d_qT], [pBd])
                        c_ = 128
                    for t in range(2):
                        k.mm(pB[:, c_ + t * 128:c_ + (t + 1) * 128], kcT[hs, t * 128:(t + 1) * 128], qT[hs, qs], True, True,
                             [d_kcT, d_qT], [pBd])
                    return (pA, pAd, pB, pBd, c_)

                pend = emit_scores(*its[0])
                po, pod = None, None
                for ii, (m_, h2) in enumerate(its):
                    cur_sc = pend
                    if ii + 1 < len(its):
                        pend = emit_scores(*its[ii + 1])
                    pA, pAd, pB, pBd, c_ = cur_sc
                    kts, t0 = info(m_)
                    nl = len(kts)
                    qs = slice(m_ * 128, (m_ + 1) * 128)
                    PTf, d_PT = PTb[ii % 2]
                    tmpE, d_tmpE = tEb[ii % 2]
                    tmpE2, d_tmpE2 = tE2b[ii % 2]
                    k.act(tmpE, pA[:, 0:512], AF.Exp, [pAd], [d_tmpE])
                    k.tt(PTf[:, 0:512], tmpE, TABf[:, (h2 * 21 + t0) * 128:(h2 * 21 + t0 + 4) * 128], ALU.mult,
                         [d_tmpE, d_TAB], [d_PT])
                    if nl == 5:
                        k.act(tmpE2, pB[:, 0:128], AF.Exp, [pBd], [d_tmpE2])
                        k.tt(PTf[:, 512:640], tmpE2, TABf[:, (h2 * 21 + 4) * 128:(h2 * 21 + 5) * 128], ALU.mult,
                             [d_tmpE2, d_TAB], [d_PT])
                    k.act(PTf[:, nl * 128:(nl + 2) * 128], pB[:, c_:c_ + 256], AF.Exp, [pBd], [d_PT])
                    if h2 == 0:
                        po, pod = k.ps()
                    c0 = h2 * 66
                    ntile = nl + 2
                    for j in range(ntile):
                        if j < nl:
                            rhs_ = vtok[:, kts[j], h2, :]
                            rd = d_vt
                        else:
                            rhs_ = vctx[:, j - nl, h2, :]
                            rd = d_vc
                        k.mm(po[:, c0:c0 + 66], PTf[:, j * 128:(j + 1) * 128], rhs_, j == 0, j == ntile - 1, [d_PT, rd], [pod])
                    if h2 == 1:
                        for hh in range(2):
                            cc0 = hh * 66
                            k.rcp(rden[:, hh:hh + 1], po[:, cc0 + 64:cc0 + 65], [pod], [d_rden])
                            k.ts(otok[:, hh * 64:(hh + 1) * 64], po[:, cc0:cc0 + 64], rden[:, hh:hh + 1], None, ALU.mult, None,
                                 [pod, d_rden], [d_otok])
                        pb, pd = k.ps()
                        k.tr(pb[:, 0:128], otok, ident[:], [d_otok, d_ident], [pd])
                        k.tt(yT, pb[:, 0:128], zs[:, qs], ALU.mult, [pd, d_zs], [d_yT])
                        P.dma(k.q(), yscr_v[4 + hp, :, qs], yT, reads=[d_yT], writes=[d_yscr[4 + hp]], r32=True)

        for l in range(depth):
            last = (l == depth - 1)
            fz = fence()
            cv = Carve(fz)
            bcol, d_bcol = cv.get(16)
            gpre_col, _d = cv.get(8)
            brow, _d = cv.get(D)
            gpost_row, _d = cv.get(D)
            ggt, d_ggt = cv.get(512)
            wada_f, d_wada = cv.get(8 * 512, F32R)
            wada = wada_f.rearrange("p (c n) -> p c n", c=8)
            sbc_f, d_sbc = cv.get(8 * 2 * 128, F32R)
            siluc_bc = sbc_f.rearrange("p (c j n) -> p c j n", c=8, j=2)
            for kc in range(8):
                for j in range(2):
                    k.cp(siluc_bc[:, kc, j, :], silucT[:, kc, j:j + 1].bitcast(F32).to_broadcast([128, 128]), [d_siluc], [d_sbc])
            P.dma("sp", bcol, b_ada.ap()[l, 0:2048].rearrange("(c p) -> p c", p=128), writes=[d_bcol], slow=True)
            P.dma("act", gpre_col, g_pre.ap()[l].rearrange("(c p) -> p c", p=128), writes=[d_bcol], slow=True)
            P.dma("sp", brow, b_ada.ap()[l, 2048:3072].partition_broadcast(128), writes=[d_bcol])
            P.dma("act", gpost_row, g_post.ap()[l].partition_broadcast(128), writes=[d_bcol])
            for blk in range(6):
                P.dma(k.q(), wada, w_ada.ap()[l, :, blk * 512:(blk + 1) * 512].rearrange("(c p) n -> p c n", p=128),
                      writes=[d_wada], r32=True)
                if blk < 4:
                    pb, pd = k.ps()
                    for cc in range(4):
                        for kc in range(8):
                            k.mm(pb[:, cc * 2:cc * 2 + 2], wada[:, kc, cc * 128:(cc + 1) * 128], silucT[:, kc, :],
                                 kc == 0, kc == 7, [d_wada, d_siluc], [pd])
                    k.tt(modcol[:, blk * 4:blk * 4 + 4, :], pb[:, 0:8].rearrange("p (c j) -> p c j", j=2),
                         bcol[:, blk * 4:blk * 4 + 4].unsqueeze(2).to_broadcast([128, 4, 2]), ALU.add,
                         [pd, d_bcol], [d_modcol])
                else:
                    for j in range(2):
                        pb, pd = k.ps()
                        for kc in range(8):
                            k.mm(pb[:], siluc_bc[:, kc, j, :], wada[:, kc, :], kc == 0, kc == 7, [d_wada, d_sbc], [pd])
                        c0 = (blk - 4) * 512
                        k.tt(ggt[:, 0:512], pb[:], brow[:, c0:c0 + 512], ALU.add, [pd, d_bcol], [d_ggt])
                        k.tt(ggt[:, 0:512], ggt[:, 0:512], gpost_row[:, c0:c0 + 512], ALU.mult, [d_ggt, d_bcol], [d_ggt])
                        P.dma("sp", ggscr.ap()[j, :, c0:c0 + 512], ggt[:, 0:512], reads=[d_ggt], writes=[d_gg])
            for j in range(2):
                k.stt(s1col[:, :, j], modcol[:, 8:16, j], 1.0, gpre_col, ALU.add, ALU.mult, [d_modcol, d_bcol], [d_modcol])

            def xio(kind, si):
                if l == 0:
                    xin = x_p.ap()[si] if kind == "p" else x_s.ap()
                    d_xin = d_x0
                else:
                    xin = xs_p[(l - 1) % 2].ap()[si] if kind == "p" else xs_s[(l - 1) % 2].ap()
                    d_xin = d_xs_p[(l - 1) % 2][si] if kind == "p" else d_xs_s[(l - 1) % 2]
                if last:
                    xout = y_p.ap()[si] if kind == "p" else y_s.ap()
                    d_xout = d_y0
                else:
                    xout = xs_p[l % 2].ap()[si] if kind == "p" else xs_s[l % 2].ap()
                    d_xout = d_xs_p[l % 2][si] if kind == "p" else d_xs_s[l % 2]
                return xin, d_xin, xout, d_xout

            d_x0 = Dep()
            d_y0 = Dep()
            for grp in ("p", "s"):
              for (kind, si, T) in [q_ for q_ in seqs if q_[0] == grp]:
                cj = 0 if kind == "p" else 1
                TT = min(T, 512)
                NTL = T // TT
                xin, d_xin, xout, d_xout = xio(kind, si)
                ho = si * SEQ if kind == "p" else 0
                hTv = hT[:, :, ho:ho + T]
                yscr_v = yscr.ap()[:, :, ho:ho + T]

                fz = fence()
                cv = Carve(fz)
                _x0, _d0 = cv.get(D)
                _x1, _d1 = cv.get(D)
                xt = [_x0, _x1]
                d_xt = [_d0, _d1]
                xn, d_xn = cv.get(D)
                for i in range(T // 128):
                    b = i % 2
                    P.dma(k.q(), xt[b], xin[i * 128:(i + 1) * 128, :], reads=[d_xin], writes=[d_xt[b]])
                    k.act(xn, xt[b], AF.Square, [d_xt[b]], [d_xn, d_small], accum=small[:, 0:1])
                    k.act(small[:, 1:2], small[:, 0:1], AF.Sqrt, [d_small], [d_small], bias=EPS, scale=1.0 / D)
                    k.rcp(small[:, 2:3], small[:, 1:2], [d_small], [d_small])
                    k.ts(xn, xt[b], small[:, 2:3], None, ALU.mult, None, [d_xt[b], d_small], [d_xn])
                    for half in range(2):
                        pb, pd = k.ps()
                        for c4 in range(4):
                            kc = half * 4 + c4
                            k.tr(pb[:, c4 * 128:(c4 + 1) * 128], xn[:, kc * 128:(kc + 1) * 128], ident[:], [d_xn, d_ident], [pd])
                        for c4 in range(4):
                            kc = half * 4 + c4
                            k.ts(hTv[:, kc, i * 128:(i + 1) * 128], pb[:, c4 * 128:(c4 + 1) * 128],
                                 s1col[:, kc, cj:cj + 1], modcol[:, kc, cj:cj + 1], ALU.mult, ALU.add,
                                 [pd, d_modcol], [d_hT], eng=("dve" if c4 % 2 == 0 else "pool") if False else "dve")

                fz = fence()
                cv = Carve(fz)
                zbuf, d_z = cv.get(TT, F32R)
                zsrc, d_zs0 = cv.get(TT)
                k.ms(zsrc, 0.0, [d_zs0])
                k.cp(zbuf, zsrc, [d_zs0], [d_z])
                for ch in range(8):
                    if (ch < 4 and not do_dn) or (ch >= 4 and not do_na):
                        for t in range(NTL):
                            P.dma(k.q(), yscr_v[ch, :, t * TT:(t + 1) * TT], zbuf, reads=[d_z], writes=[d_yscr[ch]], r32=True)

                if do_dn:
                    for h in range(4):
                        dn_unit(l, kind, si, T, h)

                if do_na:
                    for hp in range(4):
                        na_unit(l, kind, si, T, hp)

                for g in range(4):
                    win = POOL_WINDOWS[g]
                    fz = fence()
                    cv = Carve(fz)
                    wu, _dl = cv.getw(0)
                    d_wu = _dl[0]
                    wz, _dl = cv.getw(1)
                    d_wz = _dl[0]
                    pw, d_pw = cv.get(128, F32R)
                    psc, d_psc = cv.get(1)
                    U, d_U = cv.get(T + 16)
                    zs, d_zs = cv.get(T)
                    s_a, d_sa = cv.get(T + 16)
                    s_b, d_sb = cv.get(T + 16)
                    icnt, d_icnt = cv.get(T)
                    pooled, d_pooled = cv.get(T, F32R)
                    yT, d_yT = cv.get(TT, F32R)
                    wu3 = wu.rearrange("p (c n) -> p c n", c=8)
                    wz3 = wz.rearrange("p (c n) -> p c n", c=8)
                    cu = OFF_PL_U + g * 128
                    cz = OFF_PL_Z + g * 128
                    P.dma("sp", wu3, w_in.ap()[l, :, cu:cu + 128].rearrange("(c p) n -> p c n", p=128), writes=[d_wu], r32=True)
                    P.dma("act", wz3, w_in.ap()[l, :, cz:cz + 128].rearrange("(c p) n -> p c n", p=128), writes=[d_wz], r32=True)
                    P.dma("sp", pw, pool_w.ap()[l, g], writes=[d_pw], r32=True)
                    P.dma("act", psc, pool_scale.ap()[l, g * 128:(g + 1) * 128].rearrange("(p o) -> p o", o=1), writes=[d_psc], slow=True)
                    ic_src = (invcnt_p if kind == "p" else invcnt_s).ap()[g]
                    P.dma("sp", icnt, ic_src.partition_broadcast(128), writes=[d_icnt])
                    k.ms(U[:, 0:8], 0.0, [d_U])
                    k.ms(U[:, T + 8:T + 16], 0.0, [d_U])
                    for t in range(NTL):
                        pb, pd = k.ps()
                        for kc in range(8):
                            k.mm(pb[:, 0:TT], wu3[:, kc, :], hTv[:, kc, t * TT:(t + 1) * TT], kc == 0, kc == 7, [d_wu, d_hT], [pd])
                        k.cp(U[:, 8 + t * TT:8 + (t + 1) * TT], pb[:, 0:TT], [pd], [d_U])
                        pb, pd = k.ps()
                        for kc in range(8):
                            k.mm(pb[:, 0:TT], wz3[:, kc, :], hTv[:, kc, t * TT:(t + 1) * TT], kc == 0, kc == 7, [d_wz, d_hT], [pd])
                        k.act(zs[:, t * TT:(t + 1) * TT], pb[:, 0:TT], AF.Silu, [pd], [d_zs])
                    cur, dcur, curlen = U, d_U, T + 16
                    step = 1
                    bufs = [(s_a, d_sa), (s_b, d_sb)]
                    bi = 0
                    while step < win:
                        nb, dnb = bufs[bi]
                        bi ^= 1
                        nlen = curlen - step
                        k.tt(nb[:, 0:nlen], cur[:, 0:nlen], cur[:, step:step + nlen], ALU.add, [dcur], [dnb])
                        cur, dcur, curlen = nb, dnb, nlen
                        step *= 2
                    o0 = 8 - win // 2
                    nb, dnb = bufs[bi]
                    k.tt(nb[:, 0:T], cur[:, o0:o0 + T], icnt, ALU.mult, [dcur, d_icnt], [dnb])
                    k.tt(pooled, nb[:, 0:T], U[:, 8:8 + T], ALU.subtract, [dnb, d_U], [d_pooled])
                    for t in range(NTL):
                        pb, pd = k.ps()
                        k.mm(pb[:, 0:TT], pw, pooled[:, t * TT:(t + 1) * TT], True, True, [d_pw, d_pooled], [pd])
                        k.stt(yT, pb[:, 0:TT], psc[:, 0:1], zs[:, t * TT:(t + 1) * TT], ALU.mult, ALU.mult, [pd, d_psc, d_zs], [d_yT])
                        P.dma(k.q(), yscr_v[8 + g, :, t * TT:(t + 1) * TT], yT, reads=[d_yT], writes=[d_yscr[8 + g]], r32=True)

              if True:
                kind = grp
                cj = 0 if kind == "p" else 1
                T = NPS * SEQ if kind == "p" else DSEQ
                TT = 512
                NTL = T // TT
                hTv = hT[:, :, 0:T]
                yscr_v = yscr.ap()[:, :, 0:T]
                gscr_v = gscr.ap()[:, :, 0:T]
                mscr_v = mscr.ap()[:, :, 0:T]

                def xrows(sub):
                    if kind == "p":
                        xi, dxi, xo_, dxo = xio("p", sub // 2)
                        rr = (sub % 2) * 128
                    else:
                        xi, dxi, xo_, dxo = xio("s", 0)
                        rr = sub * 128
                    return xi[rr:rr + 128, :], dxi, xo_[rr:rr + 128, :], dxo

                for u in range(6):
                    fz = fence()
                    cv = Carve(fz)
                    wgu_f, d_wgl = cv.getw(0, 4)
                    wgu = wgu_f.rearrange("p (c n) -> p c n", c=8)
                    c0 = OFF_GATE + u * 512
                    P.dma(k.q(), wgu, w_in.ap()[l, :, c0:c0 + 512].rearrange("(c p) n -> p c n", p=128), writes=d_wgl, r32=True)
                    gsbs = [cv.get(TT) for _ in range(2)]
                    gi = 0
                    for t in range(NTL):
                        for c4 in range(4):
                            gsb, d_gsb = gsbs[gi % 2]
                            gi += 1
                            pg_, pgd_ = k.ps()
                            for kc in range(8):
                                k.mm(pg_[:, 0:TT], wgu[:, kc, c4 * 128:(c4 + 1) * 128], hTv[:, kc, t * TT:(t + 1) * TT], kc == 0, kc == 7,
                                     list(d_wgl) + [d_hT], [pgd_])
                            k.act(gsb, pg_[:, 0:TT], AF.Sigmoid, [pgd_], [d_gsb])
                            P.dma(k.q(), gscr_v[u * 4 + c4, :, t * TT:(t + 1) * TT], gsb, reads=[d_gsb], writes=[d_gscr[u * 4 + c4]])

                fz = fence()
                cv = Carve(fz)
                ysb_l = []
                wbr_l = []
                gin_l = []
                mo_l = []
                _wa, _wd = cv.get(4 * D, F32R)
                wbr_single = (_wa.rearrange("p (c n) -> p c n", c=4), _wd)
                for _i in range(2):
                    _a, _d = cv.get(4 * TT, F32R)
                    ysb_l.append((_a.rearrange("p (c n) -> p c n", c=4), _d))
                    wbr_l.append(wbr_single)
                    _a, _d = cv.get(8 * TT)
                    gin_l.append((_a.rearrange("p (c n) -> p c n", c=8), _d))
                    mo_l.append(cv.get(TT, F32R))
                accf, d_acc = cv.get(8 * TT)
                acc3 = accf.rearrange("p (c n) -> p c n", c=8)
                tmp, d_tmp = cv.get(TT)
                bi = 0
                mi = 0
                for t in range(NTL):
                    tsl_ = slice(t * TT, (t + 1) * TT)
                    for br in range(3):
                        ysb3, d_ysb = ysb_l[bi % 2]
                        wbr3, d_wbr = wbr_l[bi % 2]
                        gin3, d_gin = gin_l[bi % 2]
                        bi += 1
                        P.dma("sp", ysb3, yscr_v[br * 4:(br + 1) * 4, :, tsl_].rearrange("c p n -> p c n"),
                              reads=list(d_yscr[br * 4:(br + 1) * 4]), writes=[d_ysb], r32=True)
                        P.dma("act", gin3, gscr_v[br * 8:(br + 1) * 8, :, tsl_].rearrange("c p n -> p c n"),
                              reads=list(d_gscr[br * 8:(br + 1) * 8]), writes=[d_gin])
                        P.dma("sp", wbr3, w_br[br].ap()[l].rearrange("(c p) n -> p c n", p=128), writes=[d_wbr], r32=True)
                        for dc in range(8):
                            pa, pad = k.ps()
                            for wc in range(4):
                                k.mm(pa[:, 0:TT], wbr3[:, wc, dc * 128:(dc + 1) * 128], ysb3[:, wc, :], wc == 0, wc == 3, [d_wbr, d_ysb], [pad])
                            if br == 0:
                                k.tt(acc3[:, dc, :], gin3[:, dc, :], pa[:, 0:TT], ALU.mult, [d_gin, pad], [d_acc])
                            elif br == 1:
                                k.tt(tmp, gin3[:, dc, :], pa[:, 0:TT], ALU.mult, [d_gin, pad], [d_tmp])
                                k.tt(acc3[:, dc, :], acc3[:, dc, :], tmp, ALU.add, [d_acc, d_tmp], [d_acc], eng="pool")
                            else:
                                mo, d_mo = mo_l[mi % 2]
                                mi += 1
                                k.tt(tmp, gin3[:, dc, :], pa[:, 0:TT], ALU.mult, [d_gin, pad], [d_tmp])
                                k.tt(mo, acc3[:, dc, :], tmp, ALU.add, [d_acc, d_tmp], [d_mo], eng="pool")
                                P.dma("act", mscr_v[dc, :, tsl_], mo, reads=[d_mo], writes=[d_mscr], r32=True)

                fz = fence()
                cv = Carve(fz)
                wo, d_wo = cv.get(8 * D, F32R)
                wo3 = wo.rearrange("p (c n) -> p c n", c=8)
                mt_l = []
                for _i in range(2):
                    _a, _d = cv.get(8 * 128, F32R)
                    mt_l.append((_a.rearrange("p (c n) -> p c n", c=8), _d))
                xr_l = [cv.get(D) for _ in range(2)]
                xo_l = [cv.get(D) for _ in range(2)]
                xn, d_xn = cv.get(512)
                ggr, d_ggr = cv.get(D)
                P.dma("act", ggr, ggscr.ap()[cj], reads=[d_gg], writes=[d_ggr])
                P.dma("sp", wo3, w_out.ap()[l].rearrange("(c p) n -> p c n", p=128), writes=[d_wo], r32=True)
                for sub in range(T // 128):
                    r0 = sub * 128
                    mt3, d_mt = mt_l[sub % 2]
                    xr, d_xr = xr_l[sub % 2]
                    xo, d_xo = xo_l[sub % 2]
                    P.dma("sp", mt3, mscr_v[:, :, r0:r0 + 128].rearrange("c p n -> p c n"), reads=[d_mscr], writes=[d_mt], r32=True)
                    xi_rows, d_xin, xo_rows, d_xout = xrows(sub)
                    P.dma("act", xr, xi_rows, reads=[d_xin], writes=[d_xr])
                    pos = []
                    for half in range(2):
                        po, pod = k.ps()
                        for kc in range(8):
                            k.mm(po[:], mt3[:, kc, :], wo3[:, kc, half * 512:(half + 1) * 512], kc == 0, kc == 7, [d_mt, d_wo], [pod])
                        k.act(xn[:, 0:512], po[:], AF.Square, [pod], [d_xn, d_small], accum=small[:, 4 + half:5 + half])
                        pos.append((po, pod))
                    k.tt(small[:, 6:7], small[:, 4:5], small[:, 5:6], ALU.add, [d_small], [d_small])
                    k.act(small[:, 7:8], small[:, 6:7], AF.Sqrt, [d_small], [d_small], bias=EPS, scale=1.0 / D)
                    k.rcp(small[:, 8:9], small[:, 7:8], [d_small], [d_small])
                    for half in range(2):
                        po, pod = pos[half]
                        hs = slice(half * 512, (half + 1) * 512)
                        k.stt(xo[:, hs], po[:], small[:, 8:9], ggr[:, hs], ALU.mult, ALU.mult, [pod, d_small, d_ggr], [d_xo])
                        k.tt(xo[:, hs], xo[:, hs], xr[:, hs], ALU.add, [d_xo, d_xr], [d_xo], eng="pool")
                    P.dma("sp", xo_rows, xo, reads=[d_xo], writes=[d_xout])
        P.emit()
    return nc, k


_CACHE = {}


def _consts():
    def invcnt(T):
        out = np.zeros((4, T), np.float32)
        pos = np.arange(T)
        for gi, win in enumerate(POOL_WINDOWS):
            lo = np.maximum(pos - win // 2, 0)
            hi = np.minimum(pos + win // 2 - 1, T - 1)
            out[gi] = 1.0 / (hi - lo + 1).astype(np.float32)
        return out

    t = np.arange(128)
    same = (t[:, None] // 64) == (t[None, :] // 64)
    cst = np.zeros((128, NCST), np.float32)
    cst[:, 0:128] = same & (t[:, None] <= t[None, :])
    cst[:, 128:256] = same & (t[:, None] >= t[None, :])
    cst[:, 256:384] = same
    cst[:, 384:512] = 1.0
    f = np.arange(64)
    pm = t % 64
    cst[:, 512:576] = (pm[:, None] == f[None, :])
    cst[:, 576] = (pm == 63)
    cst[:, 577] = (pm == 0)
    cst[:, 578] = (t == 63)
    cst[:, 579] = (t == 127)
    cst[:, 580] = (t == 0)
    cst[:, 581] = (t == 64)
    P_ = pm[:, None]
    F_ = f[None, :]
    valid = [[F_ >= P_, F_ <= P_], [F_ > P_, F_ < P_], [F_ < P_, F_ > P_]]
    for ty in range(3):
        for d_ in range(2):
            sign = 1.0 if ty == 2 else -1.0
            cst[:, 582 + (ty * 2 + d_) * 64: 582 + (ty * 2 + d_ + 1) * 64] = np.where(valid[ty][d_], 0.0, sign * BIG)
    cq = np.arange(64)
    csq = np.clip(cq - 8, 0, 48)
    cm = (f[:, None] >= csq[None, :]) & (f[:, None] < csq[None, :] + 16)
    cst[:, 582 + 384:582 + 384 + 64] = np.concatenate([cm, cm], axis=0)
    Pf = t[:, None]
    Ff = t[None, :]
    validb = [[Ff >= Pf, Ff <= Pf], [Ff > Pf, Ff < Pf], [Ff < Pf, Ff > Pf]]
    for ty in range(3):
        for d_ in range(2):
            sign = 1.0 if ty == 2 else -1.0
            c0 = 1030 + (ty * 2 + d_) * 128
            cst[:, c0:c0 + 128] = np.where(validb[ty][d_] & same, 0.0, sign * BIG)
    return {"invcnt_p": invcnt(SEQ), "invcnt_s": invcnt(DSEQ), "ident": np.eye(128, dtype=np.float32), "dncst": cst}


def kernel(x_prompt, x_sample, c, cache_k_na, cache_v_na, state_dn, c_ctx, w_ada, b_ada, g_pre, g_post,
           w_in, conv_dn, a_log_dn, dt_bias_dn, g_norm_dn, na_bias, pool_w, pool_scale,
           w_br_dn, w_br_na, w_br_pl, w_out, _depth=DEPTH, _dn=True, _na=True):
    f = lambda a: np.ascontiguousarray(np.asarray(a, dtype=np.float32))
    key = (_depth, _dn, _na)
    if key not in _CACHE:
        _CACHE[key] = build_program(_depth, _dn, _na)
    nc, k = _CACHE[key]
    cs = _consts()
    dd = _depth
    shared = {"w_ada": f(w_ada[:dd]), "b_ada": f(b_ada[:dd]), "g_pre": f(g_pre[:dd]), "g_post": f(g_post[:dd]), "w_in": f(w_in[:dd]),
              "pool_w": f(pool_w[:dd]), "pool_scale": f(pool_scale[:dd]), "w_br_dn": f(w_br_dn[:dd]), "w_br_na": f(w_br_na[:dd]),
              "w_br_pl": f(w_br_pl[:dd]), "w_out": f(w_out[:dd]), "conv_dn": f(conv_dn[:dd]),
              "a_log": f(a_log_dn[:dd]).reshape(dd, 8), "dt_bias": f(dt_bias_dn[:dd]).reshape(dd, 8), "g_norm": f(g_norm_dn[:dd])}
    shared.update(cs)
    rpad = np.zeros((dd, 8, 15, 127), np.float32)
    rpad[..., 48:79] = f(na_bias[:dd])[..., ::-1]
    x_prompt = f(x_prompt)
    x_sample = f(x_sample)
    in_maps = []
    for core in range(8):
        b = core // 4
        m = dict(shared)
        m["x_p"] = x_prompt[core * NPS:(core + 1) * NPS]
        m["x_s"] = x_sample[b]
        m["cvec"] = np.stack([f(c_ctx), f(c)[b]])
        m["sdn"] = f(state_dn[b, :dd])
        m["ck"] = f(cache_k_na[b, :dd]).reshape(dd, 256, 512)
        m["cvv"] = f(cache_v_na[b, :dd]).reshape(dd, 256, 512)
        m["rpad"] = rpad
        in_maps.append({n: m[n] for n in k.din})
    res = run_bass_kernel_spmd(nc, in_maps, core_ids=list(range(8)))
    r = res.results
    y_p = np.concatenate([r[i]["y_p"] for i in range(8)], axis=0)
    y_s = np.stack([r[0]["y_s"], r[4]["y_s"]])
    n_k = np.concatenate([r[i]["nk"] for i in range(8)], axis=0).reshape(32, dd, SEQ, 8, 64)
    n_v = np.concatenate([r[i]["nv"] for i in range(8)], axis=0).reshape(32, dd, SEQ, 8, 64)
    n_s = np.concatenate([r[i]["nst"] for i in range(8)], axis=0)
    return y_p, y_s, n_k, n_v, n_s
```

```python
import contextlib
import numpy as np
import concourse.bass as bass
import concourse.mybir as mybir
from concourse.ap import AP
from concourse.bass_utils import run_bass_kernel_spmd

F32 = mybir.dt.float32
F32R = mybir.dt.float32r
ALU = mybir.AluOpType
AF = mybir.ActivationFunctionType

ENGS = ("pe", "act", "dve", "pool", "sp")
NDSEM = 12

D = 1024
DEPTH = 4
SEQ = 256
DSEQ = 2048
NIN = 8208
OFF_DN_Z = 1536
OFF_DN_BETA = 2048
OFF_DN_A = 2056
OFF_NA_Q = 2064
OFF_NA_K = 2576
OFF_NA_V = 3088
OFF_NA_Z = 3600
OFF_PL_U = 4112
OFF_PL_Z = 4624
OFF_GATE = 5136
EPS = 1e-6
POOL_WINDOWS = (2, 4, 8, 16)
NPS = 4
NCST = 582 + 6 * 64 + 64 + 6 * 128
BIG = 30000.0


class Dep:
    __slots__ = ("w", "r", "excl")

    def __init__(self, w=None, excl=False):
        self.w = w
        self.r = []
        self.excl = excl


class Op:
    __slots__ = ("eng", "fn", "waits", "signal", "count", "is_dma", "dsem", "dval", "dprev")

    def __init__(self, eng, fn, is_dma=False):
        self.eng = eng
        self.fn = fn
        self.waits = []
        self.signal = False
        self.count = None
        self.is_dma = is_dma
        self.dsem = None
        self.dval = None
        self.dprev = None


class Prog:
    def __init__(self, nc):
        self.nc = nc
        self.ops = {e: [] for e in ENGS}
        self.ndma = {e: 0 for e in ENGS}
        self.dtot = {e: [0] * NDSEM for e in ENGS}
        self.nops = 0

    def _mk(self, eng, fn, reads, writes, is_dma):
        o = Op(eng, fn, is_dma)
        ex = [t for t in reads if t.excl]
        if ex:
            reads = [t for t in reads if not t.excl]
            writes = list(writes) + [t for t in ex if t not in writes]
        deps = []
        seen = set()

        def add(d):
            if d is None or id(d) in seen:
                return
            if (not d.is_dma) and d.eng == "pe" and eng == "pe" and not is_dma:
                return
            seen.add(id(d))
            deps.append(d)

        for t in reads:
            add(t.w)
        for t in writes:
            add(t.w)
            for r in t.r:
                add(r)
        o.waits = deps
        for d in deps:
            d.signal = True
        for t in reads:
            if not is_dma:
                t.r = [x for x in t.r if x.is_dma or x.eng != eng]
            t.r.append(o)
        for t in writes:
            t.w = o
            t.r = []
        self.ops[eng].append(o)
        self.nops += 1
        return o

    def op(self, eng, fn, reads=(), writes=()):
        return self._mk(eng, fn, reads, writes, False)

    def dma(self, eng, out, in_, reads=(), writes=(), r32=False, slow=False):
        nc = self.nc
        eng = "pool" if type(out.tensor).__name__.startswith("DRam") else "sp"

        def fn(e):
            kw = {}
            if slow:
                kw["allow_slow_non_contiguous"] = True
            if r32:
                nc.dge_precook = False
            ins = e.dma_start(out=out, in_=in_, **kw)
            if r32:
                nc.dge_precook = True
            return ins

        o = self._mk(eng, fn, reads, writes, True)
        i = self.ndma[eng] % NDSEM
        self.ndma[eng] += 1
        o.dsem = i
        o.dprev = self.dtot[eng][i]
        self.dtot[eng][i] += 16
        o.dval = self.dtot[eng][i]
        o.signal = True
        return o

    def emit(self):
        nc = self.nc
        for e in ENGS:
            c = 0
            for o in self.ops[e]:
                if not o.is_dma and o.signal:
                    c += 1
                    o.count = c
        nsig = {e: sum(1 for o in self.ops[e] if (not o.is_dma and o.signal)) for e in ENGS}
        with contextlib.ExitStack() as st:
            esem = {e: st.enter_context(nc.semaphore("s_" + e)) for e in ENGS}
            dsem = {
                e: [st.enter_context(nc.semaphore("d_%s_%d" % (e, i))) for i in range(NDSEM)]
                for e in ("sp", "act", "pool")
            }
            block = st.enter_context(nc.Block())
            ops = self.ops
            dtot = self.dtot

            def run(e, engobj, final=False):
                seen_e = {x: 0 for x in ENGS}
                seen_d = {}
                for o in ops[e]:
                    for d in o.waits:
                        if d.is_dma:
                            key = (d.eng, d.dsem)
                            if seen_d.get(key, 0) >= d.dval:
                                continue
                            engobj.wait_ge(dsem[d.eng][d.dsem], d.dval)
                            seen_d[key] = d.dval
                        else:
                            if seen_e[d.eng] >= d.count:
                                continue
                            engobj.wait_ge(esem[d.eng], d.count)
                            seen_e[d.eng] = d.count
                    if o.is_dma:
                        key = (e, o.dsem)
                        if o.dprev > 0 and seen_d.get(key, 0) < o.dprev:
                            engobj.wait_ge(dsem[e][o.dsem], o.dprev)
                            seen_d[key] = o.dprev
                        ins = o.fn(engobj)
                        ins.then_inc(dsem[e][o.dsem], 16)
                    else:
                        ins = o.fn(engobj)
                        if o.signal:
                            ins.then_inc(esem[e], 1)
                if final:
                    for x in ENGS:
                        if x != e and nsig[x] > 0:
                            engobj.wait_ge(esem[x], nsig[x])
                    for q in ("sp", "act", "pool"):
                        for i in range(NDSEM):
                            if dtot[q][i] > 0:
                                engobj.wait_ge(dsem[q][i], dtot[q][i])

            @block.tensor
            def _(eng):
                run("pe", eng)

            @block.vector
            def _(eng):
                run("dve", eng)

            @block.scalar
            def _(eng):
                run("act", eng)

            @block.gpsimd
            def _(eng):
                run("pool", eng)

            @block.sync
            def _(eng):
                run("sp", eng, final=True)


class K:
    def __init__(self, nc, st):
        self.nc = nc
        self.st = st
        self.P = Prog(nc)
        self.din = {}
        self.dout = {}
        self.psb = [st.enter_context(nc.psum_tensor("psb%d" % i, [128, 512], F32)) for i in range(8)]
        self.psd = [Dep(excl=True) for _ in range(8)]
        self.psi = 0
        self.dq = 0

    def inp(self, name, shape, dt=F32):
        t = self.nc.dram_tensor(name, list(shape), dt, kind="ExternalInput")
        self.din[name] = t
        return t

    def outp(self, name, shape):
        t = self.nc.dram_tensor(name, list(shape), F32, kind="ExternalOutput")
        self.dout[name] = t
        return t

    def scr(self, name, shape, dt=F32):
        return self.nc.dram_tensor(name, list(shape), dt, kind="Internal")

    def sb(self, name, shape, dt=F32):
        return self.st.enter_context(self.nc.sbuf_tensor(name, list(shape), dt))

    def ps(self):
        i = self.psi
        self.psi = (i + 1) % 8
        return self.psb[i], self.psd[i]

    def q(self):
        self.dq ^= 1
        return "sp" if self.dq else "act"

    def mm(self, out, lhsT, rhs, start, stop, reads, writes):
        self.P.op("pe", lambda e: e.matmul(out, lhsT=lhsT, rhs=rhs, start=start, stop=stop), reads, writes)

    def tr(self, out, in_, ident, reads, writes):
        self.P.op("pe", lambda e: e.transpose(out, in_, ident), reads, writes)

    def act(self, out, in_, func, reads, writes, bias=None, scale=1.0, accum=None):
        def fn(e):
            kw = {}
            if bias is not None:
                kw["bias"] = bias
            if accum is not None:
                kw["accum_out"] = accum
            return e.activation(out=out, in_=in_, func=func, scale=scale, **kw)

        self.P.op("act", fn, reads, writes)

    def tt(self, out, in0, in1, op, reads, writes, eng="dve"):
        self.P.op(eng, lambda e: e.tensor_tensor(out=out, in0=in0, in1=in1, op=op), reads, writes)

    def ts(self, out, in0, s1, s2, op0, op1, reads, writes, eng="dve"):
        if s2 is None:
            self.P.op(eng, lambda e: e.tensor_scalar(out=out, in0=in0, scalar1=s1, scalar2=None, op0=op0), reads, writes)
        else:
            self.P.op(eng, lambda e: e.tensor_scalar(out=out, in0=in0, scalar1=s1, scalar2=s2, op0=op0, op1=op1), reads, writes)

    def stt(self, out, in0, scalar, in1, op0, op1, reads, writes, eng="dve"):
        self.P.op(eng, lambda e: e.scalar_tensor_tensor(out=out, in0=in0, scalar=scalar, in1=in1, op0=op0, op1=op1),
                  reads, writes)

    def rcp(self, out, in_, reads, writes):
        self.P.op("dve", lambda e: e.reciprocal(out=out, in_=in_), reads, writes)

    def cp(self, out, in_, reads, writes, eng="dve"):
        self.P.op(eng, lambda e: e.tensor_copy(out=out, in_=in_), reads, writes)

    def ms(self, ap, val, writes, eng="pool"):
        self.P.op(eng, lambda e: e.memset(ap, val), (), writes)


def build_program(depth=DEPTH, do_dn=True, do_na=True):
    nc = bass.Bass("TRN2", target_bir_lowering=False)
    st = contextlib.ExitStack()
    with st:
        k = K(nc, st)
        P = k.P
        x_p = k.inp("x_p", [NPS, SEQ, D])
        x_s = k.inp("x_s", [DSEQ, D])
        cvec = k.inp("cvec", [2, D])
        w_ada = k.inp("w_ada", [depth, D, 3 * D], F32R)
        b_ada = k.inp("b_ada", [depth, 3 * D])
        g_pre = k.inp("g_pre", [depth, D])
        g_post = k.inp("g_post", [depth, D])
        w_in = k.inp("w_in", [depth, D, NIN], F32R)
        pool_w = k.inp("pool_w", [depth, 4, 128, 128], F32R)
        pool_scale = k.inp("pool_scale", [depth, 512])
        w_br = [k.inp(n, [depth, 512, D], F32R) for n in ("w_br_dn", "w_br_na", "w_br_pl")]
        w_out = k.inp("w_out", [depth, D, D], F32R)
        invcnt_p = k.inp("invcnt_p", [4, SEQ])
        invcnt_s = k.inp("invcnt_s", [4, DSEQ])
        ident_in = k.inp("ident", [128, 128])
        conv_dn = k.inp("conv_dn", [depth, 3, 1536])
        a_log = k.inp("a_log", [depth, 8])
        dt_bias = k.inp("dt_bias", [depth, 8])
        g_norm = k.inp("g_norm", [depth, 128])
        sdn = k.inp("sdn", [depth, 2, 4, 128, 128])
        dncst_in = k.inp("dncst", [128, NCST])
        ck_in = k.inp("ck", [depth, 256, 512])
        cv_in = k.inp("cvv", [depth, 256, 512], F32R)
        rpad_in = k.inp("rpad", [depth, 8, 15, 127])

        y_p = k.outp("y_p", [NPS, SEQ, D])
        y_s = k.outp("y_s", [DSEQ, D])
        nk = k.outp("nk", [NPS, depth, SEQ, 512])
        nv = k.outp("nv", [NPS, depth, SEQ, 512])
        d_nkv = Dep()
        nst = k.outp("nst", [NPS, depth, 2, 4, 128, 128])
        d_nst = Dep()

        xs_p = [k.scr("xs_p%d" % i, [NPS, SEQ, D]) for i in range(2)]
        xs_s = [k.scr("xs_s%d" % i, [DSEQ, D]) for i in range(2)]
        yscr = k.scr("yscr", [12, 128, DSEQ], F32R)
        d_xs_p = [[Dep() for _ in range(NPS)] for _ in range(2)]
        d_xs_s = [Dep() for _ in range(2)]
        d_yscr = [Dep() for _ in range(12)]
        gscr = k.scr("gscr", [24, 128, DSEQ])
        d_gscr = [Dep() for _ in range(24)]
        mscr = k.scr("mscr", [8, 128, DSEQ], F32R)
        d_mscr = Dep()

        ident = k.sb("ident_sb", [128, 128])
        d_ident = Dep()
        P.dma("sp", ident[:], ident_in.ap(), writes=[d_ident])
        cst = k.sb("dncst_sb", [128, NCST])
        d_cst = Dep()
        P.dma("act", cst[:], dncst_in.ap(), writes=[d_cst])
        TRI = [cst[:, 0:128], cst[:, 128:256]]
        BLK = cst[:, 256:384]
        ONES = cst[:, 384:512]
        I2 = cst[:, 512:576]
        SEL2 = cst[:, 576:578]
        SELLAST = cst[:, 578:582]
        MASK = [[cst[:, 582 + (ty * 2 + d_) * 64: 582 + (ty * 2 + d_ + 1) * 64] for d_ in range(2)] for ty in range(3)]
        CM = cst[:, 582 + 384:582 + 384 + 64]
        MASKB = [[cst[:, 1030 + (ty * 2 + d_) * 128: 1030 + (ty * 2 + d_ + 1) * 128] for d_ in range(2)] for ty in range(3)]
        hT = k.sb("hT", [128, 8, DSEQ], F32R)
        d_hT = Dep()
        small = k.sb("small", [128, 16])
        d_small = Dep()
        silucT = k.sb("silucT", [128, 8, 2], F32R)
        d_siluc = Dep()
        modcol = k.sb("modcol", [128, 16, 2])
        s1col = k.sb("s1col", [128, 8, 2])
        d_modcol = Dep()
        ggscr = k.scr("ggscr", [2, 128, D])
        d_gg = Dep()
        RSZ = 15 * 1024
        FSZ = 18 * 1024 + 512
        arenaR = k.sb("arenaR", [128, RSZ], F32R)
        arenaF = k.sb("arenaF", [128, FSZ])
        arena_deps = []
        WOFF = RSZ - 4096
        wdeps = [Dep() for _ in range(4)]

        def fence():
            f = P.op("dve", lambda e: e.memset(small[:, 15:16], 0.0), reads=(), writes=list(arena_deps))
            arena_deps.clear()
            return f

        class Carve:
            def __init__(self, seed):
                self.offR = 0
                self.offF = 0
                self.seed = seed

            def getw(self, j, n=1):
                a = arenaR[:, WOFF + j * 1024:WOFF + (j + n) * 1024]
                return a, wdeps[j:j + n]

            def get(self, cols, dt=F32):
                if dt == F32R:
                    a = arenaR[:, self.offR:self.offR + cols]
                    self.offR += cols
                    assert self.offR <= WOFF, self.offR
                else:
                    a = arenaF[:, self.offF:self.offF + cols]
                    self.offF += cols
                    assert self.offF <= FSZ, self.offF
                d = Dep(self.seed)
                arena_deps.append(d)
                return a, d

        craw = k.sb("craw", [128, 8, 2])
        for j in range(2):
            P.dma("sp", craw[:, :, j], cvec.ap()[j].rearrange("(c p) -> p c", p=128), writes=[d_siluc], slow=True)
        k.act(silucT[:], craw[:], AF.Silu, [d_siluc], [d_siluc])

        seqs = [("p", i, SEQ) for i in range(NPS)] + [("s", 0, DSEQ)]
        zscr = k.scr("zscr", [depth, 120, 64, 127])
        d_zscr = [Dep() for _ in range(depth)]
        if do_na:
            for l_ in range(depth):
                P.dma("sp", zscr.ap()[l_], AP(rpad_in, l_ * 120 * 127, [[127, 120], [0, 64], [1, 127]]), writes=[d_zscr[l_]])

        def dn_unit(l, kind, si, T, h):
            NT = T // 128
            TT = min(T, 512)
            NTL = T // TT
            fz = fence()
            cv = Carve(fz)
            ws = []
            for off in (0, 512, 1024, OFF_DN_Z):
                wa, dwl = cv.getw(len(ws))
                dw = dwl[0]
                w3 = wa.rearrange("p (c n) -> p c n", c=8)
                c0 = off + h * 128
                P.dma(k.q(), w3, w_in.ap()[l, :, c0:c0 + 128].rearrange("(c p) n -> p c n", p=128), writes=[dw], r32=True)
                ws.append((w3, dw))
            (wq, d_wq), (wk, d_wk), (wv, d_wv), (wz, d_wz) = ws
            wba_f, d_wba = cv.get(8 * 4, F32R)
            wba = wba_f.rearrange("p (c n) -> p c n", c=8)
            for j4 in range(4):
                cj4 = OFF_DN_BETA + 4 * j4 + h
                P.dma("sp", wba[:, :, j4], w_in.ap()[l, :, cj4].rearrange("(c p) -> p c", p=128), writes=[d_wba], r32=True, slow=True)
            raw, d_raw = cv.get(T + 2)
            qf, d_qf = cv.get(T)
            kf, d_kf = cv.get(T)
            vf, d_vf = cv.get(T)
            oacc, _ = cv.get(T)
            d_oacc = [Dep(fz) for _ in range(NT)]
            arena_deps.extend(d_oacc)
            tmpb, d_tmpb = cv.get(512)
            cw, d_cw = cv.get(9)
            for idx in range(3):
                c0 = idx * 512 + h * 128
                P.dma("act", cw[:, idx * 3:(idx + 1) * 3], conv_dn.ap()[l, :, c0:c0 + 128].rearrange("t p -> p t"),
                      writes=[d_cw], slow=True)
            k.ms(raw[:, 0:1], 0.0, [d_raw])
            k.ms(raw[:, T + 1:T + 2], 0.0, [d_raw])
            for i in range(NT):
                k.ms(oacc[:, i * 128:(i + 1) * 128], 0.0, [d_oacc[i]])
            dsts = [(qf, d_qf), (kf, d_kf), (vf, d_vf)]
            for idx in range(3):
                w3, dw = ws[idx]
                dst, dd = dsts[idx]
                for t in range(NTL):
                    pb, pd = k.ps()
                    for kc in range(8):
                        k.mm(pb[:, 0:TT], w3[:, kc, :], hTv[:, kc, t * TT:(t + 1) * TT], kc == 0, kc == 7, [dw, d_hT], [pd])
                    k.cp(raw[:, 1 + t * TT:1 + (t + 1) * TT], pb[:, 0:TT], [pd], [d_raw])
                for t in range(NTL):
                    a = t * TT
                    k.ts(dst[:, a:a + TT], raw[:, a:a + TT], cw[:, idx * 3:idx * 3 + 1], None, ALU.mult, None, [d_raw, d_cw], [dd])
                    k.stt(tmpb[:, 0:TT], raw[:, a + 1:a + 1 + TT], cw[:, idx * 3 + 1:idx * 3 + 2], dst[:, a:a + TT],
                          ALU.mult, ALU.add, [d_raw, d_cw, dd], [d_tmpb])
                    k.stt(dst[:, a:a + TT], raw[:, a + 2:a + 2 + TT], cw[:, idx * 3 + 2:idx * 3 + 3], tmpb[:, 0:TT],
                          ALU.mult, ALU.add, [d_raw, d_cw, d_tmpb], [dd])
                    k.act(dst[:, a:a + TT], dst[:, a:a + TT], AF.Silu, [dd], [dd])
            for idx in range(2):
                dst, dd = dsts[idx]
                for t in range(NTL):
                    a = t * TT
                    k.act(tmpb[:, 0:TT], dst[:, a:a + TT], AF.Square, [dd], [d_tmpb])
                    pb, pd = k.ps()
                    k.mm(pb[:, 0:TT], ONES, tmpb[:, 0:TT], True, True, [d_cst, d_tmpb], [pd])
                    k.act(tmpb[:, 0:TT], pb[:, 0:TT], AF.Sqrt, [pd], [d_tmpb], bias=EPS)
                    k.rcp(tmpb[:, 0:TT], tmpb[:, 0:TT], [d_tmpb], [d_tmpb])
                    if idx == 0:
                        k.stt(dst[:, a:a + TT], dst[:, a:a + TT], 128.0 ** -0.5, tmpb[:, 0:TT], ALU.mult, ALU.mult, [dd, d_tmpb], [dd])
                    else:
                        k.tt(dst[:, a:a + TT], dst[:, a:a + TT], tmpb[:, 0:TT], ALU.mult, [dd, d_tmpb], [dd])
            zs = raw
            d_zs = d_raw
            for t in range(NTL):
                pb, pd = k.ps()
                for kc in range(8):
                    k.mm(pb[:, 0:TT], wz[:, kc, :], hTv[:, kc, t * TT:(t + 1) * TT], kc == 0, kc == 7, [d_wz, d_hT], [pd])
                k.act(zs[:, t * TT:(t + 1) * TT], pb[:, 0:TT], AF.Silu, [pd], [d_zs])
            d_g = Dep(fz)
            arena_deps.append(d_g)
            G_ = [d_g]
            ba, _ = cv.get(NT * 4)
            ba3 = ba.rearrange("p (t n) -> p t n", n=4)
            pb, pd = k.ps()
            for i in range(NT):
                for kc in range(8):
                    k.mm(pb[:, i * 4:(i + 1) * 4], hTv[:, kc, i * 128:(i + 1) * 128], wba[:, kc, :], kc == 0, kc == 7, [d_wba, d_hT], [pd])
            k.cp(ba, pb[:, 0:NT * 4], [pd], G_)
            NK = 10
            GA, _ = cv.get(NT * NK * 2)
            GA4 = GA.rearrange("p (t k d) -> p t k d", k=NK, d=2)
            K_BETA, K_NEGB, K_G, K_GB, K_EG, K_BG, K_EGL, K_GL = range(8)
            gk = lambda kk: GA4[:, :, kk, :]

            def g2():
                a_, _ = cv.get(NT * 2)
                return a_, a_.rearrange("p (t n) -> p t n", n=2)

            lnb, lnb3 = g2()
            la, la3 = g2()
            gt, gt3 = g2()
            rowc, _ = cv.get(4)
            gn_row, d_gn = cv.get(128)
            P.dma("sp", rowc[:, 0:1], dt_bias.ap()[l, h:h + 1].partition_broadcast(128), writes=G_)
            P.dma("sp", rowc[:, 1:2], dt_bias.ap()[l, 4 + h:5 + h].partition_broadcast(128), writes=G_)
            P.dma("act", rowc[:, 2:3], a_log.ap()[l, h:h + 1].partition_broadcast(128), writes=G_)
            P.dma("act", rowc[:, 3:4], a_log.ap()[l, 4 + h:5 + h].partition_broadcast(128), writes=G_)
            P.dma("sp", gn_row, g_norm.ap()[l].partition_broadcast(128), writes=[d_gn])
            k.act(rowc[:, 2:4], rowc[:, 2:4], AF.Exp, G_, G_)
            k.ts(rowc[:, 2:4], rowc[:, 2:4], -1.0, None, ALU.mult, None, G_, G_)
            k.act(gk(K_BETA), ba3[:, :, 0:2], AF.Sigmoid, G_, G_)
            k.ts(gk(K_NEGB), gk(K_BETA), -1.0, None, ALU.mult, None, G_, G_)
            k.act(gt3, ba3[:, :, 0:2], AF.Exp, G_, G_, scale=-1.0)
            k.act(gt, gt, AF.Ln, G_, G_, bias=1.0)
            k.ts(lnb, gt, -1.0, None, ALU.mult, None, G_, G_)
            k.tt(gt3, ba3[:, :, 2:4], rowc[:, 0:2].unsqueeze(1).to_broadcast([128, NT, 2]), ALU.add, G_, G_)
            k.act(gt, gt, AF.Exp, G_, G_)
            k.act(gt, gt, AF.Ln, G_, G_, bias=1.0)
            k.tt(la3, gt3, rowc[:, 2:4].unsqueeze(1).to_broadcast([128, NT, 2]), ALU.mult, G_, G_)
            pb, pd = k.ps()
            for i in range(NT):
                k.mm(pb[:, i * 2:(i + 1) * 2], TRI[0], la3[:, i, :], True, True, [d_cst, d_g], [pd])
                k.mm(pb[:, 64 + i * 2:64 + (i + 1) * 2], TRI[1], la3[:, i, :], True, True, [d_cst, d_g], [pd])
            k.cp(gk(K_G)[:, :, 0:1], pb[:, 0:NT * 2].rearrange("p (t n) -> p t n", n=2)[:, :, 0:1], [pd], G_)
            k.cp(gk(K_G)[:, :, 1:2], pb[:, 64:64 + NT * 2].rearrange("p (t n) -> p t n", n=2)[:, :, 1:2], [pd], G_)
            k.tt(gk(K_GB), gk(K_G), lnb3, ALU.add, G_, G_)
            k.act(gk(K_EG), gk(K_G), AF.Exp, G_, G_)
            k.tt(gk(K_BG), gk(K_BETA), gk(K_EG), ALU.mult, G_, G_)
            k.tt(gt3, gk(K_G), SEL2.unsqueeze(1).to_broadcast([128, NT, 2]), ALU.mult, G_ + [d_cst], G_)
            pb, pd = k.ps()
            k.mm(pb[:, 0:NT * 2], BLK, gt, True, True, [d_cst, d_g], [pd])
            k.tt(gt3, pb[:, 0:NT * 2].rearrange("p (t n) -> p t n", n=2), gk(K_G), ALU.subtract, [pd, d_g], G_)
            k.act(gk(K_EGL), gt3, AF.Exp, G_, G_)
            gsel2, _ = cv.get(NT * 4)
            k.tt(gsel2.rearrange("p (t d c) -> p t d c", d=2, c=2), gk(K_G).unsqueeze(3).to_broadcast([128, NT, 2, 2]),
                 SELLAST.rearrange("p (d c) -> p d c", d=2).unsqueeze(1).to_broadcast([128, NT, 2, 2]), ALU.mult,
                 G_ + [d_cst], G_)
            pb, pd = k.ps()
            k.mm(pb[:, 0:NT * 4], ONES, gsel2, True, True, [d_cst, d_g], [pd])
            for cp in range(2):
                k.act(gk(K_GL + cp), pb[:, 0:NT * 4].rearrange("p (t d c) -> p t d c", d=2, c=2)[:, :, :, cp], AF.Exp, [pd], G_)
            S = []
            for d_ in range(2):
                s_, ds_ = cv.get(128)
                if kind == "p":
                    k.ms(s_, 0.0, [ds_])
                else:
                    P.dma(k.q(), s_, sdn.ap()[l, d_, h], writes=[ds_])
                S.append((s_, ds_))
            X, d_X = cv.get(256)
            vb, d_vb = cv.get(256)
            Gd, d_Gd = cv.get(256)
            Gbd, d_Gbd = cv.get(256)
            E1, d_E1 = cv.get(256)
            E2, d_E2 = cv.get(256)
            E3, d_E3 = cv.get(256)
            MML = [[cv.get(256), cv.get(256)] for _ in range(2)]
            PTL = [cv.get(128) for _ in range(2)]
            OB = []
            for _i in range(2):
                o_ = {}
                o_["attnT"], o_["d_at"] = cv.get(256)
                o_["kg"], o_["d_kg"] = cv.get(256)
                o_["uw"], o_["d_uw"] = cv.get(512)
                ls_, o_["d_LS"] = cv.get(NK * 2)
                o_["LS3"] = ls_.rearrange("p (k d) -> p k d", d=2)
                OB.append(o_)
            SC = []
            for d_ in range(2):
                SC.append((cv.get(128), cv.get(128), cv.get(128)))
            PB = k.psb
            PD = k.psd

            def lanes(s_):
                return (s_, NT - 1 - s_)

            def prep(s_):
                il = lanes(s_)
                ob = OB[s_ % 2]
                LS3, d_LS = ob["LS3"], ob["d_LS"]
                ls = lambda kk, d_: LS3[:, kk, d_:d_ + 1]
                tsl = [slice(il[d_] * 128, (il[d_] + 1) * 128) for d_ in range(2)]
                for d_ in range(2):
                    k.cp(LS3[:, :, d_], GA4[:, il[d_], :, d_], [d_g], [d_LS], eng="pool")
                pk, pkd = PB[0], PD[0]
                for d_ in range(2):
                    k.tr(pk[:, d_ * 256:d_ * 256 + 128], kf[:, tsl[d_]], ident[:], [d_kf, d_ident], [pkd])
                    k.tr(pk[:, d_ * 256 + 128:d_ * 256 + 256], vf[:, tsl[d_]], ident[:], [d_vf, d_ident], [pkd])
                for d_ in range(2):
                    ds = slice(d_ * 128, (d_ + 1) * 128)
                    k.ts(ob["kg"][:, ds], pk[:, d_ * 256:d_ * 256 + 128], ls(K_EGL, d_), None, ALU.mult, None, [pkd, d_LS], [ob["d_kg"]])
                    k.act(X[:, ds], pk[:, d_ * 256:d_ * 256 + 128], AF.Copy, [pkd, d_LS], [d_X], scale=ls(K_BG, d_))
                    k.ts(vb[:, ds], pk[:, d_ * 256 + 128:d_ * 256 + 256], ls(K_BETA, d_), None, ALU.mult, None, [pkd, d_LS], [d_vb])
                yield
                pa, pad = PB[1], PD[1]
                for d_ in range(2):
                    k.mm(pa[:, d_ * 256:d_ * 256 + 128], kf[:, tsl[d_]], kf[:, tsl[d_]], True, True, [d_kf], [pad])
                    k.mm(pa[:, d_ * 256 + 128:d_ * 256 + 256], kf[:, tsl[d_]], qf[:, tsl[d_]], True, True, [d_kf, d_qf], [pad])
                for d_ in range(2):
                    ds = slice(d_ * 128, (d_ + 1) * 128)
                    k.ts(Gd[:, ds], ident[:], ls(K_G, d_), None, ALU.mult, None, [d_ident, d_LS], [d_Gd])
                    k.act(Gbd[:, ds], ident[:], AF.Copy, [d_ident, d_LS], [d_Gbd], scale=ls(K_GB, d_))
                pg, pgd = PB[2], PD[2]
                k.mm(pg[:, 0:256], ONES, Gd, True, True, [d_cst, d_Gd], [pgd])
                k.mm(pg[:, 256:512], ONES, Gbd, True, True, [d_cst, d_Gbd], [pgd])
                yield
                for d_ in range(2):
                    ds = slice(d_ * 128, (d_ + 1) * 128)
                    k.stt(E1[:, ds], pg[:, ds], ls(K_G, d_), MASKB[0][d_], ALU.subtract, ALU.add, [pgd, d_LS, d_cst], [d_E1])
                    k.stt(E2[:, ds], pg[:, 256 + d_ * 128:256 + (d_ + 1) * 128], ls(K_G, d_), MASKB[1][d_], ALU.subtract, ALU.add,
                          [pgd, d_LS, d_cst], [d_E2])
                    k.stt(E3[:, ds], pg[:, ds], ls(K_G, d_), MASKB[2][d_], ALU.subtract, ALU.add, [pgd, d_LS, d_cst], [d_E3])
                k.act(E1, E1, AF.Exp, [d_E1], [d_E1])
                k.act(E2, E2, AF.Exp, [d_E2], [d_E2])
                k.act(E3, E3, AF.Exp, [d_E3], [d_E3], scale=-1.0)
                yield
                cur = [MML[d_][0] for d_ in range(2)]
                nxt = [MML[d_][1] for d_ in range(2)]
                for d_ in range(2):
                    ds = slice(d_ * 128, (d_ + 1) * 128)
                    c_, dc_ = cur[d_]
                    k.tt(ob["attnT"][:, ds], pa[:, d_ * 256 + 128:d_ * 256 + 256], E1[:, ds], ALU.mult, [pad, d_E1], [ob["d_at"]])
                    k.stt(c_[:, 128:256], pa[:, d_ * 256:d_ * 256 + 128], -1.0, E2[:, ds], ALU.mult, ALU.mult, [pad, d_E2], [dc_])
                    k.stt(c_[:, 0:128], pa[:, d_ * 256:d_ * 256 + 128], ls(K_NEGB, d_), E3[:, ds], ALU.mult, ALU.mult,
                          [pad, d_E3, d_LS], [dc_])
                    k.tt(PTL[d_][0], ident[:], c_[:, 128:256], ALU.add, [d_ident, dc_], [PTL[d_][1]])
                yield
                pmb = (3, 2)
                ppb = (0, 1)
                for lev in range(5):
                    lastl = (lev == 4)
                    for d_ in range(2):
                        c_, dc_ = cur[d_]
                        n_, dn_ = nxt[d_]
                        pm, pmd = PB[pmb[d_]], PD[pmb[d_]]
                        k.mm(pm[:, 0:128], c_[:, 128:256], c_[:, 0:128], True, True, [dc_], [pmd])
                        if not lastl:
                            k.mm(pm[:, 128:256], c_[:, 0:128], c_[:, 128:256], True, True, [dc_], [pmd])
                        ncols = 128 if lastl else 256
                        if d_ == 0:
                            k.act(n_[:, 0:ncols], pm[:, 0:ncols], AF.Copy, [pmd], [dn_])
                        else:
                            k.cp(n_[:, 0:ncols], pm[:, 0:ncols], [pmd], [dn_])
                    yield
                    for d_ in range(2):
                        n_, dn_ = nxt[d_]
                        pt_, dpt_ = PTL[d_]
                        pp, ppd = PB[ppb[d_]], PD[ppb[d_]]
                        k.mm(pp[:, 0:128], n_[:, 0:128], pt_, True, True, [dn_, dpt_], [ppd])
                        k.tt(pt_, pt_, pp[:, 0:128], ALU.add, [dpt_, ppd], [dpt_])
                    yield
                    cur, nxt = nxt, cur
                pu, pud = PB[2], PD[2]
                for d_ in range(2):
                    ds = slice(d_ * 128, (d_ + 1) * 128)
                    pt_, dpt_ = PTL[d_]
                    k.mm(pu[:, d_ * 256:d_ * 256 + 128], pt_, vb[:, ds], True, True, [dpt_, d_vb], [pud])
                    k.mm(pu[:, d_ * 256 + 128:d_ * 256 + 256], X[:, ds], pt_, True, True, [d_X, dpt_], [pud])
                k.cp(ob["uw"], pu[:, 0:512], [pud], [ob["d_uw"]])
                yield

            def scan(s_, d_):
                i = lanes(s_)[d_]
                ob = OB[s_ % 2]
                LS3, d_LS = ob["LS3"], ob["d_LS"]
                ds = slice(d_ * 128, (d_ + 1) * 128)
                es = slice(d_ * 64, (d_ + 1) * 64)
                (vnew, d_vn), (oasb, d_oa), (otmp, d_ot) = SC[d_]
                s_t, ds_ = S[d_]
                p1, p1d = PB[4 + 2 * d_], PD[4 + 2 * d_]
                p2, p2d = PB[5 + 2 * d_], PD[5 + 2 * d_]
                for cp in ((0, 1) if d_ == 0 else (1, 0)):
                    bs = slice(cp * 64, (cp + 1) * 64)
                    cs = slice(i * 128 + cp * 64, i * 128 + (cp + 1) * 64)
                    k.mm(p1[bs, 0:128], ob["uw"][:, d_ * 256 + 128 + cp * 64:d_ * 256 + 128 + (cp + 1) * 64], s_t, True, True,
                         [ob["d_uw"], ds_], [p1d])
                    k.mm(p1[bs, 128:256], qf[:, cs], s_t, True, True, [d_qf, ds_], [p1d])
                    k.tt(vnew[bs, :], ob["uw"][bs, d_ * 256:d_ * 256 + 128], p1[bs, 0:128], ALU.subtract, [ob["d_uw"], p1d], [d_vn])
                    yield
                    k.mm(p2[bs, 0:128], ob["attnT"][bs, d_ * 128 + cp * 64:d_ * 128 + (cp + 1) * 64], vnew[bs, :], True, True, [ob["d_at"], d_vn], [p2d])
                    k.mm(p2[:, 128:256], ob["kg"][bs, ds], vnew[bs, :], True, True, [ob["d_kg"], d_vn], [p2d])
                    k.act(oasb[bs, :], p2[bs, 0:128], AF.Copy, [p2d], [d_oa])
                    k.stt(s_t, s_t, LS3[:, K_GL + cp, d_:d_ + 1], p2[:, 128:256], ALU.mult, ALU.add, [ds_, d_LS, p2d], [ds_])
                    yield
                    k.stt(otmp[bs, :], p1[bs, 128:256], LS3[bs, K_EG, d_:d_ + 1], oasb[bs, :], ALU.mult, ALU.add,
                          [p1d, d_LS, d_oa], [d_ot])
                    k.tt(oacc[bs, i * 128:(i + 1) * 128], oacc[bs, i * 128:(i + 1) * 128], otmp[bs, :], ALU.add,
                         [d_oacc[i], d_ot], [d_oacc[i]], eng="pool")
                    yield

            def run_gens(gens):
                gens = list(gens)
                while gens:
                    for g_ in list(gens):
                        try:
                            next(g_)
                        except StopIteration:
                            gens.remove(g_)

            run_gens([prep(0)])
            for s_ in range(NT):
                gl_ = [scan(s_, 0), scan(s_, 1)]
                if s_ + 1 < NT:
                    gl_.insert(0, prep(s_ + 1))
                run_gens(gl_)
            if kind == "p":
                for d_ in range(2):
                    P.dma(k.q(), nst.ap()[si, l, d_, h], S[d_][0], reads=[S[d_][1]], writes=[d_nst])
            rs, d_rs = cv.get(4)
            on, d_on = cv.get(128)
            yT, d_yT = cv.get(128, F32R)
            for i in range(NT):
                tsl = slice(i * 128, (i + 1) * 128)
                k.act(on, oacc[:, tsl], AF.Square, [d_oacc[i]], [d_on, d_rs], accum=rs[:, 0:1])
                k.act(rs[:, 1:2], rs[:, 0:1], AF.Sqrt, [d_rs], [d_rs], bias=EPS, scale=1.0 / 128)
                k.rcp(rs[:, 2:3], rs[:, 1:2], [d_rs], [d_rs])
                k.stt(on, oacc[:, tsl], rs[:, 2:3], gn_row, ALU.mult, ALU.mult, [d_oacc[i], d_rs, d_gn, d_on], [d_on])
                pt, ptd = k.ps()
                k.tr(pt[:, 0:128], on, ident[:], [d_on, d_ident], [ptd])
                k.tt(yT, pt[:, 0:128], zs[:, tsl], ALU.mult, [ptd, d_zs], [d_yT])
                P.dma(k.q(), yscr_v[h, :, tsl], yT, reads=[d_yT], writes=[d_yscr[h]], r32=True)

        def na_unit(l, kind, si, T, hp):
            NT = T // 128
            TT = min(T, 512)
            fz = fence()
            cv = Carve(fz)
            ws = []
            for off in (OFF_NA_Q, OFF_NA_K, OFF_NA_V, OFF_NA_Z):
                wa, dwl = cv.getw(len(ws))
                dw = dwl[0]
                w3 = wa.rearrange("p (c n) -> p c n", c=8)
                c0 = off + hp * 128
                P.dma(k.q(), w3, w_in.ap()[l, :, c0:c0 + 128].rearrange("(c p) n -> p c n", p=128), writes=[dw], r32=True)
                ws.append((w3, dw))
            (wq, d_wq), (wk, d_wk), (wv, d_wv), (wz, d_wz) = ws
            qT, d_qT = cv.get(T, F32R)
            kT, d_kT = cv.get(T, F32R)
            vt_f, d_vt = cv.get(NT * 132, F32R)
            vtok = vt_f.rearrange("p (t h e) -> p t h e", t=NT, h=2)
            zs, d_zs = cv.get(T)
            ones, d_ones = cv.get(2)
            stgs = [cv.get(256) for _ in range(2)]
            k.ms(ones, 1.0, [d_ones])
            for t in range(T // TT):
                ts_ = slice(t * TT, (t + 1) * TT)
                pb, pd = k.ps()
                for kc in range(8):
                    k.mm(pb[:, 0:TT], wq[:, kc, :], hTv[:, kc, ts_], kc == 0, kc == 7, [d_wq, d_hT], [pd])
                k.ts(qT[:, ts_], pb[:, 0:TT], 0.125, None, ALU.mult, None, [pd], [d_qT])
                pb, pd = k.ps()
                for kc in range(8):
                    k.mm(pb[:, 0:TT], wk[:, kc, :], hTv[:, kc, ts_], kc == 0, kc == 7, [d_wk, d_hT], [pd])
                k.cp(kT[:, ts_], pb[:, 0:TT], [pd], [d_kT])
                pb, pd = k.ps()
                for kc in range(8):
                    k.mm(pb[:, 0:TT], wz[:, kc, :], hTv[:, kc, ts_], kc == 0, kc == 7, [d_wz, d_hT], [pd])
                k.act(zs[:, ts_], pb[:, 0:TT], AF.Silu, [pd], [d_zs])
            for i in range(NT):
                is_ = slice(i * 128, (i + 1) * 128)
                pb, pd = k.ps()
                for kc in range(8):
                    k.mm(pb[:, 0:128], hTv[:, kc, is_], wv[:, kc, :], kc == 0, kc == 7, [d_wv, d_hT], [pd])
                k.cp(vtok[:, i, :, 0:64], pb[:, 0:128].rearrange("p (h e) -> p h e", h=2), [pd], [d_vt])
                k.cp(vtok[:, i, :, 64:66], ones[:, 0:2].unsqueeze(1).to_broadcast([128, 2, 2]), [d_ones], [d_vt], eng="pool")
                if kind == "p":
                    sq_ = i // 2
                    rs_ = slice((i % 2) * 128, (i % 2) * 128 + 128)
                    stg, d_stg = stgs[i % 2]
                    k.cp(stg[:, 0:128], pb[:, 0:128], [pd], [d_stg])
                    P.dma("act", nv.ap()[sq_, l, rs_, hp * 128:(hp + 1) * 128], stg[:, 0:128], reads=[d_stg], writes=[d_nkv])
                    pb, pd = k.ps()
                    for kc in range(8):
                        k.mm(pb[:, 0:128], hTv[:, kc, is_], wk[:, kc, :], kc == 0, kc == 7, [d_wk, d_hT], [pd])
                    k.cp(stg[:, 128:256], pb[:, 0:128], [pd], [d_stg])
                    P.dma("act", nk.ap()[sq_, l, rs_, hp * 128:(hp + 1) * 128], stg[:, 128:256], reads=[d_stg], writes=[d_nkv])
            otok, d_otok = cv.get(128)
            rden, d_rden = cv.get(2)
            yT, d_yT = cv.get(128, F32R)
            if kind == "p":
                PTs = [[cv.get(256, F32R) for _ in range(4)] for _ in range(2)]
                for sq_ in range(T // SEQ):
                    PT = PTs[sq_ % 2]
                    q0 = sq_ * SEQ
                    po, pod = k.ps()
                    for h2 in range(2):
                        hs = slice(h2 * 64, (h2 + 1) * 64)
                        for kt in range(2):
                            pt, d_pt = PT[h2 * 2 + kt]
                            pb, pd = k.ps()
                            k.mm(pb[:, 0:256], kT[hs, q0 + kt * 128:q0 + (kt + 1) * 128], qT[hs, q0:q0 + 256], True, True,
                                 [d_kT, d_qT], [pd])
                            k.act(pt, pb[:, 0:256], AF.Exp, [pd], [d_pt])
                        for qt in range(2):
                            for kt in range(2):
                                pt, d_pt = PT[h2 * 2 + kt]
                                c0 = qt * 132 + h2 * 66
                                k.mm(po[:, c0:c0 + 66], pt[:, qt * 128:(qt + 1) * 128], vtok[:, sq_ * 2 + kt, h2, :], kt == 0, kt == 1,
                                     [d_pt, d_vt], [pod])
                    for qt in range(2):
                        qs = slice(q0 + qt * 128, q0 + (qt + 1) * 128)
                        for h2 in range(2):
                            c0 = qt * 132 + h2 * 66
                            k.rcp(rden[:, h2:h2 + 1], po[:, c0 + 64:c0 + 65], [pod], [d_rden])
                            k.ts(otok[:, h2 * 64:(h2 + 1) * 64], po[:, c0:c0 + 64], rden[:, h2:h2 + 1], None, ALU.mult, None,
                                 [pod, d_rden], [d_otok])
                        pb, pd = k.ps()
                        k.tr(pb[:, 0:128], otok, ident[:], [d_otok, d_ident], [pd])
                        k.tt(yT, pb[:, 0:128], zs[:, qs], ALU.mult, [pd, d_zs], [d_yT])
                        P.dma(k.q(), yscr_v[4 + hp, :, qs], yT, reads=[d_yT], writes=[d_yscr[4 + hp]], r32=True)
            else:
                kcT, d_kcT = cv.get(256, F32R)
                vc_f, d_vc = cv.get(2 * 132, F32R)
                vctx = vc_f.rearrange("p (t h e) -> p t h e", t=2, h=2)
                cst_k, d_cstk = cv.get(256)
                P.dma("sp", cst_k.rearrange("p (t n) -> p t n", t=2),
                      ck_in.ap()[l, :, hp * 128:(hp + 1) * 128].rearrange("(t p) n -> p t n", p=128), writes=[d_cstk])
                pb, pd = k.ps()
                for t in range(2):
                    k.tr(pb[:, t * 128:(t + 1) * 128], cst_k[:, t * 128:(t + 1) * 128], ident[:], [d_cstk, d_ident], [pd])
                k.cp(kcT, pb[:, 0:256], [pd], [d_kcT])
                for t in range(2):
                    P.dma("act", vctx[:, t, :, 0:64],
                          cv_in.ap()[l, t * 128:(t + 1) * 128, hp * 128:(hp + 1) * 128].rearrange("p (h e) -> p h e", h=2),
                          writes=[d_vc], r32=True)
                    k.cp(vctx[:, t, :, 64:66], ones[:, 0:2].unsqueeze(1).to_broadcast([128, 2, 2]), [d_ones], [d_vc], eng="pool")
                E2f, d_E2 = cv.get(2 * 15 * 64)
                E2 = E2f.rearrange("p (h r c) -> p h r c", h=2, r=15)
                for a in range(2):
                    for h2 in range(2):
                        head = hp * 2 + h2
                        src = AP(zscr, ((l * 8 + head) * 15) * 8128 + 63, [[126, 64], [8128, 15], [1, 64]])
                        P.dma("sp" if a == 0 else "act", E2[a * 64:(a + 1) * 64, h2, :, :], src, reads=[d_zscr[l]], writes=[d_E2])
                k.act(E2f, E2f, AF.Exp, [d_E2], [d_E2])
                k.tt(E2f.rearrange("p (g c) -> p g c", c=64), E2f.rearrange("p (g c) -> p g c", c=64),
                     CM.unsqueeze(1).to_broadcast([128, 30, 64]), ALU.mult, [d_E2, d_cst], [d_E2])
                TABf, d_TAB = cv.get(2 * 21 * 128)
                TAB = TABf.rearrange("p (h t q) -> p h t q", h=2, t=21)
                k.ms(TABf, 0.0, [d_TAB])
                plans = {}
                tid = 0
                plans["int"] = []
                for j in range(5):
                    for a in range(2):
                        for b in range(2):
                            dr = 2 * j - 4 + a - b
                            if -4 <= dr <= 3:
                                plans["int"].append((tid, a, b, dr))
                    tid += 1
                tid0 = {"int": 0}
                for m_ in (0, 1, 14, 15):
                    tid0[m_] = tid
                    kt0 = 0 if m_ < 2 else 12
                    plans[m_] = []
                    for j in range(4):
                        for a in range(2):
                            for b in range(2):
                                kr = 2 * (kt0 + j) + a
                                r = 2 * m_ + b
                                rs_ = min(max(r - 4, 0), 24)
                                if rs_ <= kr <= rs_ + 7:
                                    plans[m_].append((tid, a, b, kr - r))
                        tid += 1
                assert tid == 21
                _ci = 0
                for key_, pl in plans.items():
                    for (tid_, a, b, dr) in pl:
                        k.cp(TAB[a * 64:(a + 1) * 64, :, tid_, b * 64:(b + 1) * 64], E2[a * 64:(a + 1) * 64, :, dr + 7, :],
                             [d_E2], [d_TAB], eng=("dve" if _ci % 2 == 0 else "pool"))
                        _ci += 1
                PTb = [cv.get(7 * 128, F32R) for _ in range(2)]
                tEb = [cv.get(512) for _ in range(2)]
                tE2b = [cv.get(128) for _ in range(2)]
                its = [(m_, h2) for m_ in range(16) for h2 in range(2)]

                def info(m_):
                    if 2 <= m_ <= 13:
                        return [m_ - 2 + j for j in range(5)], 0
                    return [(0 if m_ < 2 else 12) + j for j in range(4)], tid0[m_]

                def emit_scores(m_, h2):
                    kts, t0 = info(m_)
                    nl = len(kts)
                    qs = slice(m_ * 128, (m_ + 1) * 128)
                    hs = slice(h2 * 64, (h2 + 1) * 64)
                    pA, pAd = k.ps()
                    for j in range(4):
                        k.mm(pA[:, j * 128:(j + 1) * 128], kT[hs, kts[j] * 128:(kts[j] + 1) * 128], qT[hs, qs], True, True,
                             [d_kT, d_qT], [pAd])
                    pB, pBd = k.ps()
                    c_ = 0
                    if nl == 5:
                        k.mm(pB[:, 0:128], kT[hs, kts[4] * 128:(kts[4] + 1) * 128], qT[hs, qs], True, True, [d_kT, d_qT], [pBd])
                        c_ = 128
                    for t in range(2):
                        k.mm(pB[:, c_ + t * 128:c_ + (t + 1) * 128], kcT[hs, t * 128:(t + 1) * 128], qT[hs, qs], True, True,
                             [d_kcT, d_qT], [pBd])
                    return (pA, pAd, pB, pBd, c_)

                pend = emit_scores(*its[0])
                po, pod = None, None
                for ii, (m_, h2) in enumerate(its):
                    cur_sc = pend
                    if ii + 1 < len(its):
                        pend = emit_scores(*its[ii + 1])
                    pA, pAd, pB, pBd, c_ = cur_sc
                    kts, t0 = info(m_)
                    nl = len(kts)
                    qs = slice(m_ * 128, (m_ + 1) * 128)
                    PTf, d_PT = PTb[ii % 2]
                    tmpE, d_tmpE = tEb[ii % 2]
                    tmpE2, d_tmpE2 = tE2b[ii % 2]
                    k.act(tmpE, pA[:, 0:512], AF.Exp, [pAd], [d_tmpE])
                    k.tt(PTf[:, 0:512], tmpE, TABf[:, (h2 * 21 + t0) * 128:(h2 * 21 + t0 + 4) * 128], ALU.mult,
                         [d_tmpE, d_TAB], [d_PT])
                    if nl == 5:
                        k.act(tmpE2, pB[:, 0:128], AF.Exp, [pBd], [d_tmpE2])
                        k.tt(PTf[:, 512:640], tmpE2, TABf[:, (h2 * 21 + 4) * 128:(h2 * 21 + 5) * 128], ALU.mult,
                             [d_tmpE2, d_TAB], [d_PT])
                    k.act(PTf[:, nl * 128:(nl + 2) * 128], pB[:, c_:c_ + 256], AF.Exp, [pBd], [d_PT])
                    if h2 == 0:
                        po, pod = k.ps()
                    c0 = h2 * 66
                    ntile = nl + 2
                    for j in range(ntile):
                        if j < nl:
                            rhs_ = vtok[:, kts[j], h2, :]
                            rd = d_vt
                        else:
                            rhs_ = vctx[:, j - nl, h2, :]
                            rd = d_vc
                        k.mm(po[:, c0:c0 + 66], PTf[:, j * 128:(j + 1) * 128], rhs_, j == 0, j == ntile - 1, [d_PT, rd], [pod])
                    if h2 == 1:
                        for hh in range(2):
                            cc0 = hh * 66
                            k.rcp(rden[:, hh:hh + 1], po[:, cc0 + 64:cc0 + 65], [pod], [d_rden])
                            k.ts(otok[:, hh * 64:(hh + 1) * 64], po[:, cc0:cc0 + 64], rden[:, hh:hh + 1], None, ALU.mult, None,
                                 [pod, d_rden], [d_otok])
                        pb, pd = k.ps()
                        k.tr(pb[:, 0:128], otok, ident[:], [d_otok, d_ident], [pd])
                        k.tt(yT, pb[:, 0:128], zs[:, qs], ALU.mult, [pd, d_zs], [d_yT])
                        P.dma(k.q(), yscr_v[4 + hp, :, qs], yT, reads=[d_yT], writes=[d_yscr[4 + hp]], r32=True)

        for l in range(depth):
            last = (l == depth - 1)
            fz = fence()
            cv = Carve(fz)
            bcol, d_bcol = cv.get(16)
            gpre_col, _d = cv.get(8)
            brow, _d = cv.get(D)
            gpost_row, _d = cv.get(D)
            ggt, d_ggt = cv.get(512)
            wada_f, d_wada = cv.get(8 * 512, F32R)
            wada = wada_f.rearrange("p (c n) -> p c n", c=8)
            sbc_f, d_sbc = cv.get(8 * 2 * 128, F32R)
            siluc_bc = sbc_f.rearrange("p (c j n) -> p c j n", c=8, j=2)
            for kc in range(8):
                for j in range(2):
                    k.cp(siluc_bc[:, kc, j, :], silucT[:, kc, j:j + 1].bitcast(F32).to_broadcast([128, 128]), [d_siluc], [d_sbc])
            P.dma("sp", bcol, b_ada.ap()[l, 0:2048].rearrange("(c p) -> p c", p=128), writes=[d_bcol], slow=True)
            P.dma("act", gpre_col, g_pre.ap()[l].rearrange("(c p) -> p c", p=128), writes=[d_bcol], slow=True)
            P.dma("sp", brow, b_ada.ap()[l, 2048:3072].partition_broadcast(128), writes=[d_bcol])
            P.dma("act", gpost_row, g_post.ap()[l].partition_broadcast(128), writes=[d_bcol])
            for blk in range(6):
                P.dma(k.q(), wada, w_ada.ap()[l, :, blk * 512:(blk + 1) * 512].rearrange("(c p) n -> p c n", p=128),
                      writes=[d_wada], r32=True)
                if blk < 4:
                    pb, pd = k.ps()
                    for cc in range(4):
                        for kc in range(8):
                            k.mm(pb[:, cc * 2:cc * 2 + 2], wada[:, kc, cc * 128:(cc + 1) * 128], silucT[:, kc, :],
                                 kc == 0, kc == 7, [d_wada, d_siluc], [pd])
                    k.tt(modcol[:, blk * 4:blk * 4 + 4, :], pb[:, 0:8].rearrange("p (c j) -> p c j", j=2),
                         bcol[:, blk * 4:blk * 4 + 4].unsqueeze(2).to_broadcast([128, 4, 2]), ALU.add,
                         [pd, d_bcol], [d_modcol])
                else:
                    for j in range(2):
                        pb, pd = k.ps()
                        for kc in range(8):
                            k.mm(pb[:], siluc_bc[:, kc, j, :], wada[:, kc, :], kc == 0, kc == 7, [d_wada, d_sbc], [pd])
                        c0 = (blk - 4) * 512
                        k.tt(ggt[:, 0:512], pb[:], brow[:, c0:c0 + 512], ALU.add, [pd, d_bcol], [d_ggt])
                        k.tt(ggt[:, 0:512], ggt[:, 0:512], gpost_row[:, c0:c0 + 512], ALU.mult, [d_ggt, d_bcol], [d_ggt])
                        P.dma("sp", ggscr.ap()[j, :, c0:c0 + 512], ggt[:, 0:512], reads=[d_ggt], writes=[d_gg])
            for j in range(2):
                k.stt(s1col[:, :, j], modcol[:, 8:16, j], 1.0, gpre_col, ALU.add, ALU.mult, [d_modcol, d_bcol], [d_modcol])

            def xio(kind, si):
                if l == 0:
                    xin = x_p.ap()[si] if kind == "p" else x_s.ap()
                    d_xin = d_x0
                else:
                    xin = xs_p[(l - 1) % 2].ap()[si] if kind == "p" else xs_s[(l - 1) % 2].ap()
                    d_xin = d_xs_p[(l - 1) % 2][si] if kind == "p" else d_xs_s[(l - 1) % 2]
                if last:
                    xout = y_p.ap()[si] if kind == "p" else y_s.ap()
                    d_xout = d_y0
                else:
                    xout = xs_p[l % 2].ap()[si] if kind == "p" else xs_s[l % 2].ap()
                    d_xout = d_xs_p[l % 2][si] if kind == "p" else d_xs_s[l % 2]
                return xin, d_xin, xout, d_xout

            d_x0 = Dep()
            d_y0 = Dep()
            for grp in ("p", "s"):
              for (kind, si, T) in [q_ for q_ in seqs if q_[0] == grp]:
                cj = 0 if kind == "p" else 1
                TT = min(T, 512)
                NTL = T // TT
                xin, d_xin, xout, d_xout = xio(kind, si)
                ho = si * SEQ if kind == "p" else 0
                hTv = hT[:, :, ho:ho + T]
                yscr_v = yscr.ap()[:, :, ho:ho + T]

                fz = fence()
                cv = Carve(fz)
                _x0, _d0 = cv.get(D)
                _x1, _d1 = cv.get(D)
                xt = [_x0, _x1]
                d_xt = [_d0, _d1]
                xn, d_xn = cv.get(D)
                for i in range(T // 128):
                    b = i % 2
                    P.dma(k.q(), xt[b], xin[i * 128:(i + 1) * 128, :], reads=[d_xin], writes=[d_xt[b]])
                    k.act(xn, xt[b], AF.Square, [d_xt[b]], [d_xn, d_small], accum=small[:, 0:1])
                    k.act(small[:, 1:2], small[:, 0:1], AF.Sqrt, [d_small], [d_small], bias=EPS, scale=1.0 / D)
                    k.rcp(small[:, 2:3], small[:, 1:2], [d_small], [d_small])
                    k.ts(xn, xt[b], small[:, 2:3], None, ALU.mult, None, [d_xt[b], d_small], [d_xn])
                    for half in range(2):
                        pb, pd = k.ps()
                        for c4 in range(4):
                            kc = half * 4 + c4
                            k.tr(pb[:, c4 * 128:(c4 + 1) * 128], xn[:, kc * 128:(kc + 1) * 128], ident[:], [d_xn, d_ident], [pd])
                        for c4 in range(4):
                            kc = half * 4 + c4
                            k.ts(hTv[:, kc, i * 128:(i + 1) * 128], pb[:, c4 * 128:(c4 + 1) * 128],
                                 s1col[:, kc, cj:cj + 1], modcol[:, kc, cj:cj + 1], ALU.mult, ALU.add,
                                 [pd, d_modcol], [d_hT], eng=("dve" if c4 % 2 == 0 else "pool") if False else "dve")

                fz = fence()
                cv = Carve(fz)
                zbuf, d_z = cv.get(TT, F32R)
                zsrc, d_zs0 = cv.get(TT)
                k.ms(zsrc, 0.0, [d_zs0])
                k.cp(zbuf, zsrc, [d_zs0], [d_z])
                for ch in range(8):
                    if (ch < 4 and not do_dn) or (ch >= 4 and not do_na):
                        for t in range(NTL):
                            P.dma(k.q(), yscr_v[ch, :, t * TT:(t + 1) * TT], zbuf, reads=[d_z], writes=[d_yscr[ch]], r32=True)

                if do_dn:
                    for h in range(4):
                        dn_unit(l, kind, si, T, h)

              if True:
                kind = grp
                cj = 0 if kind == "p" else 1
                T = NPS * SEQ if kind == "p" else DSEQ
                TT = 512
                NTL = T // TT
                hTv = hT[:, :, 0:T]
                yscr_v = yscr.ap()[:, :, 0:T]
                gscr_v = gscr.ap()[:, :, 0:T]
                mscr_v = mscr.ap()[:, :, 0:T]

                def xrows(sub):
                    if kind == "p":
                        xi, dxi, xo_, dxo = xio("p", sub // 2)
                        rr = (sub % 2) * 128
                    else:
                        xi, dxi, xo_, dxo = xio("s", 0)
                        rr = sub * 128
                    return xi[rr:rr + 128, :], dxi, xo_[rr:rr + 128, :], dxo

                L = SEQ if kind == "p" else DSEQ
                nseq = T // L
                si = None

                if do_na:
                    for hp in range(4):
                        na_unit(l, kind, si, T, hp)

                for g in range(4):
                    win = POOL_WINDOWS[g]
                    fz = fence()
                    cv = Carve(fz)
                    wu, _dl = cv.getw(0)
                    d_wu = _dl[0]
                    wz, _dl = cv.getw(1)
                    d_wz = _dl[0]
                    pw, d_pw = cv.get(128, F32R)
                    psc, d_psc = cv.get(1)
                    LP = L + 16
                    U, d_U = cv.get(nseq * LP)
                    zs, d_zs = cv.get(T)
                    s_a, d_sa = cv.get(nseq * LP)
                    s_b, d_sb = cv.get(nseq * LP)
                    icnt, d_icnt = cv.get(L)
                    pooled, d_pooled = cv.get(T, F32R)
                    yT, d_yT = cv.get(TT, F32R)
                    U3 = U.rearrange("p (s n) -> p s n", s=nseq)
                    wu3 = wu.rearrange("p (c n) -> p c n", c=8)
                    wz3 = wz.rearrange("p (c n) -> p c n", c=8)
                    cu = OFF_PL_U + g * 128
                    cz = OFF_PL_Z + g * 128
                    P.dma("sp", wu3, w_in.ap()[l, :, cu:cu + 128].rearrange("(c p) n -> p c n", p=128), writes=[d_wu], r32=True)
                    P.dma("act", wz3, w_in.ap()[l, :, cz:cz + 128].rearrange("(c p) n -> p c n", p=128), writes=[d_wz], r32=True)
                    P.dma("sp", pw, pool_w.ap()[l, g], writes=[d_pw], r32=True)
                    P.dma("act", psc, pool_scale.ap()[l, g * 128:(g + 1) * 128].rearrange("(p o) -> p o", o=1), writes=[d_psc], slow=True)
                    ic_src = (invcnt_p if kind == "p" else invcnt_s).ap()[g]
                    P.dma("sp", icnt, ic_src.partition_broadcast(128), writes=[d_icnt])
                    k.ms(U3[:, :, 0:8], 0.0, [d_U])
                    k.ms(U3[:, :, L + 8:L + 16], 0.0, [d_U])
                    for t in range(NTL):
                        pb, pd = k.ps()
                        for kc in range(8):
                            k.mm(pb[:, 0:TT], wu3[:, kc, :], hTv[:, kc, t * TT:(t + 1) * TT], kc == 0, kc == 7, [d_wu, d_hT], [pd])
                        if L >= TT:
                            k.cp(U[:, 8 + t * TT:8 + (t + 1) * TT], pb[:, 0:TT], [pd], [d_U])
                        else:
                            spt = TT // L
                            k.cp(U3[:, t * spt:(t + 1) * spt, 8:8 + L], pb[:, 0:TT].rearrange("p (s n) -> p s n", s=spt), [pd], [d_U])
                        pb, pd = k.ps()
                        for kc in range(8):
                            k.mm(pb[:, 0:TT], wz3[:, kc, :], hTv[:, kc, t * TT:(t + 1) * TT], kc == 0, kc == 7, [d_wz, d_hT], [pd])
                        k.act(zs[:, t * TT:(t + 1) * TT], pb[:, 0:TT], AF.Silu, [pd], [d_zs])
                    cur, dcur, curlen = U, d_U, nseq * LP
                    step = 1
                    bufs = [(s_a, d_sa), (s_b, d_sb)]
                    bi = 0
                    while step < win:
                        nb, dnb = bufs[bi]
                        bi ^= 1
                        nlen = curlen - step
                        k.tt(nb[:, 0:nlen], cur[:, 0:nlen], cur[:, step:step + nlen], ALU.add, [dcur], [dnb])
                        cur, dcur, curlen = nb, dnb, nlen
                        step *= 2
                    o0 = 8 - win // 2
                    nb, dnb = bufs[bi]
                    nb3 = nb[:, 0:T].rearrange("p (s n) -> p s n", s=nseq)
                    k.tt(nb3, cur[:, 0:nseq * LP].rearrange("p (s n) -> p s n", s=nseq)[:, :, o0:o0 + L],
                         icnt.unsqueeze(1).to_broadcast([128, nseq, L]), ALU.mult, [dcur, d_icnt], [dnb])
                    k.tt(pooled.rearrange("p (s n) -> p s n", s=nseq), nb3, U3[:, :, 8:8 + L], ALU.subtract, [dnb, d_U], [d_pooled])
                    for t in range(NTL):
                        pb, pd = k.ps()
                        k.mm(pb[:, 0:TT], pw, pooled[:, t * TT:(t + 1) * TT], True, True, [d_pw, d_pooled], [pd])
                        k.stt(yT, pb[:, 0:TT], psc[:, 0:1], zs[:, t * TT:(t + 1) * TT], ALU.mult, ALU.mult, [pd, d_psc, d_zs], [d_yT])
                        P.dma(k.q(), yscr_v[8 + g, :, t * TT:(t + 1) * TT], yT, reads=[d_yT], writes=[d_yscr[8 + g]], r32=True)

                for u in range(6):
                    fz = fence()
                    cv = Carve(fz)
                    wgu_f, d_wgl = cv.getw(0, 4)
                    wgu = wgu_f.rearrange("p (c n) -> p c n", c=8)
                    c0 = OFF_GATE + u * 512
                    P.dma(k.q(), wgu, w_in.ap()[l, :, c0:c0 + 512].rearrange("(c p) n -> p c n", p=128), writes=d_wgl, r32=True)
                    gsbs = [cv.get(TT) for _ in range(2)]
                    gi = 0
                    for t in range(NTL):
                        for c4 in range(4):
                            gsb, d_gsb = gsbs[gi % 2]
                            gi += 1
                            pg_, pgd_ = k.ps()
                            for kc in range(8):
                                k.mm(pg_[:, 0:TT], wgu[:, kc, c4 * 128:(c4 + 1) * 128], hTv[:, kc, t * TT:(t + 1) * TT], kc == 0, kc == 7,
                                     list(d_wgl) + [d_hT], [pgd_])
                            k.act(gsb, pg_[:, 0:TT], AF.Sigmoid, [pgd_], [d_gsb])
                            P.dma(k.q(), gscr_v[u * 4 + c4, :, t * TT:(t + 1) * TT], gsb, reads=[d_gsb], writes=[d_gscr[u * 4 + c4]])

                fz = fence()
                cv = Carve(fz)
                ysb_l = []
                wbr_l = []
                gin_l = []
                mo_l = []
                _wa, _wd = cv.get(4 * D, F32R)
                wbr_single = (_wa.rearrange("p (c n) -> p c n", c=4), _wd)
                for _i in range(2):
                    _a, _d = cv.get(4 * TT, F32R)
                    ysb_l.append((_a.rearrange("p (c n) -> p c n", c=4), _d))
                    wbr_l.append(wbr_single)
                    _a, _d = cv.get(8 * TT)
                    gin_l.append((_a.rearrange("p (c n) -> p c n", c=8), _d))
                    mo_l.append(cv.get(TT, F32R))
                accf, d_acc = cv.get(8 * TT)
                acc3 = accf.rearrange("p (c n) -> p c n", c=8)
                tmp, d_tmp = cv.get(TT)
                bi = 0
                mi = 0
                for t in range(NTL):
                    tsl_ = slice(t * TT, (t + 1) * TT)
                    for br in range(3):
                        ysb3, d_ysb = ysb_l[bi % 2]
                        wbr3, d_wbr = wbr_l[bi % 2]
                        gin3, d_gin = gin_l[bi % 2]
                        bi += 1
                        P.dma("sp", ysb3, yscr_v[br * 4:(br + 1) * 4, :, tsl_].rearrange("c p n -> p c n"),
                              reads=list(d_yscr[br * 4:(br + 1) * 4]), writes=[d_ysb], r32=True)
                        P.dma("act", gin3, gscr_v[br * 8:(br + 1) * 8, :, tsl_].rearrange("c p n -> p c n"),
                              reads=list(d_gscr[br * 8:(br + 1) * 8]), writes=[d_gin])
                        P.dma("sp", wbr3, w_br[br].ap()[l].rearrange("(c p) n -> p c n", p=128), writes=[d_wbr], r32=True)
                        for dc in range(8):
                            pa, pad = k.ps()
                            for wc in range(4):
                                k.mm(pa[:, 0:TT], wbr3[:, wc, dc * 128:(dc + 1) * 128], ysb3[:, wc, :], wc == 0, wc == 3, [d_wbr, d_ysb], [pad])
                            if br == 0:
                                k.tt(acc3[:, dc, :], gin3[:, dc, :], pa[:, 0:TT], ALU.mult, [d_gin, pad], [d_acc])
                            elif br == 1:
                                k.tt(tmp, gin3[:, dc, :], pa[:, 0:TT], ALU.mult, [d_gin, pad], [d_tmp])
                                k.tt(acc3[:, dc, :], acc3[:, dc, :], tmp, ALU.add, [d_acc, d_tmp], [d_acc], eng="pool")
                            else:
                                mo, d_mo = mo_l[mi % 2]
                                mi += 1
                                k.tt(tmp, gin3[:, dc, :], pa[:, 0:TT], ALU.mult, [d_gin, pad], [d_tmp])
                                k.tt(mo, acc3[:, dc, :], tmp, ALU.add, [d_acc, d_tmp], [d_mo], eng="pool")
                                P.dma("act", mscr_v[dc, :, tsl_], mo, reads=[d_mo], writes=[d_mscr], r32=True)

                fz = fence()
                cv = Carve(fz)
                wo, d_wo = cv.get(8 * D, F32R)
                wo3 = wo.rearrange("p (c n) -> p c n", c=8)
                mt_l = []
                for _i in range(2):
                    _a, _d = cv.get(8 * 128, F32R)
                    mt_l.append((_a.rearrange("p (c n) -> p c n", c=8), _d))
                xr_l = [cv.get(D) for _ in range(2)]
                xo_l = [cv.get(D) for _ in range(2)]
                xn, d_xn = cv.get(512)
                ggr, d_ggr = cv.get(D)
                P.dma("act", ggr, ggscr.ap()[cj], reads=[d_gg], writes=[d_ggr])
                P.dma("sp", wo3, w_out.ap()[l].rearrange("(c p) n -> p c n", p=128), writes=[d_wo], r32=True)
                for sub in range(T // 128):
                    r0 = sub * 128
                    mt3, d_mt = mt_l[sub % 2]
                    xr, d_xr = xr_l[sub % 2]
                    xo, d_xo = xo_l[sub % 2]
                    P.dma("sp", mt3, mscr_v[:, :, r0:r0 + 128].rearrange("c p n -> p c n"), reads=[d_mscr], writes=[d_mt], r32=True)
                    xi_rows, d_xin, xo_rows, d_xout = xrows(sub)
                    P.dma("act", xr, xi_rows, reads=[d_xin], writes=[d_xr])
                    pos = []
                    for half in range(2):
                        po, pod = k.ps()
                        for kc in range(8):
                            k.mm(po[:], mt3[:, kc, :], wo3[:, kc, half * 512:(half + 1) * 512], kc == 0, kc == 7, [d_mt, d_wo], [pod])
                        k.act(xn[:, 0:512], po[:], AF.Square, [pod], [d_xn, d_small], accum=small[:, 4 + half:5 + half])
                        pos.append((po, pod))
                    k.tt(small[:, 6:7], small[:, 4:5], small[:, 5:6], ALU.add, [d_small], [d_small])
                    k.act(small[:, 7:8], small[:, 6:7], AF.Sqrt, [d_small], [d_small], bias=EPS, scale=1.0 / D)
                    k.rcp(small[:, 8:9], small[:, 7:8], [d_small], [d_small])
                    for half in range(2):
                        po, pod = pos[half]
                        hs = slice(half * 512, (half + 1) * 512)
                        k.stt(xo[:, hs], po[:], small[:, 8:9], ggr[:, hs], ALU.mult, ALU.mult, [pod, d_small, d_ggr], [d_xo])
                        k.tt(xo[:, hs], xo[:, hs], xr[:, hs], ALU.add, [d_xo, d_xr], [d_xo], eng="pool")
                    P.dma("sp", xo_rows, xo, reads=[d_xo], writes=[d_xout])
        P.emit()
    return nc, k


_CACHE = {}


def _consts():
    def invcnt(T):
        out = np.zeros((4, T), np.float32)
        pos = np.arange(T)
        for gi, win in enumerate(POOL_WINDOWS):
            lo = np.maximum(pos - win // 2, 0)
            hi = np.minimum(pos + win // 2 - 1, T - 1)
            out[gi] = 1.0 / (hi - lo + 1).astype(np.float32)
        return out

    t = np.arange(128)
    same = (t[:, None] // 64) == (t[None, :] // 64)
    cst = np.zeros((128, NCST), np.float32)
    cst[:, 0:128] = same & (t[:, None] <= t[None, :])
    cst[:, 128:256] = same & (t[:, None] >= t[None, :])
    cst[:, 256:384] = same
    cst[:, 384:512] = 1.0
    f = np.arange(64)
    pm = t % 64
    cst[:, 512:576] = (pm[:, None] == f[None, :])
    cst[:, 576] = (pm == 63)
    cst[:, 577] = (pm == 0)
    cst[:, 578] = (t == 63)
    cst[:, 579] = (t == 127)
    cst[:, 580] = (t == 0)
    cst[:, 581] = (t == 64)
    P_ = pm[:, None]
    F_ = f[None, :]
    valid = [[F_ >= P_, F_ <= P_], [F_ > P_, F_ < P_], [F_ < P_, F_ > P_]]
    for ty in range(3):
        for d_ in range(2):
            sign = 1.0 if ty == 2 else -1.0
            cst[:, 582 + (ty * 2 + d_) * 64: 582 + (ty * 2 + d_ + 1) * 64] = np.where(valid[ty][d_], 0.0, sign * BIG)
    cq = np.arange(64)
    csq = np.clip(cq - 8, 0, 48)
    cm = (f[:, None] >= csq[None, :]) & (f[:, None] < csq[None, :] + 16)
    cst[:, 582 + 384:582 + 384 + 64] = np.concatenate([cm, cm], axis=0)
    Pf = t[:, None]
    Ff = t[None, :]
    validb = [[Ff >= Pf, Ff <= Pf], [Ff > Pf, Ff < Pf], [Ff < Pf, Ff > Pf]]
    for ty in range(3):
        for d_ in range(2):
            sign = 1.0 if ty == 2 else -1.0
            c0 = 1030 + (ty * 2 + d_) * 128
            cst[:, c0:c0 + 128] = np.where(validb[ty][d_] & same, 0.0, sign * BIG)
    return {"invcnt_p": invcnt(SEQ), "invcnt_s": invcnt(DSEQ), "ident": np.eye(128, dtype=np.float32), "dncst": cst}


def kernel(x_prompt, x_sample, c, cache_k_na, cache_v_na, state_dn, c_ctx, w_ada, b_ada, g_pre, g_post,
           w_in, conv_dn, a_log_dn, dt_bias_dn, g_norm_dn, na_bias, pool_w, pool_scale,
           w_br_dn, w_br_na, w_br_pl, w_out, _depth=DEPTH, _dn=True, _na=True):
    f = lambda a: np.ascontiguousarray(np.asarray(a, dtype=np.float32))
    key = (_depth, _dn, _na)
    if key not in _CACHE:
        _CACHE[key] = build_program(_depth, _dn, _na)
    nc, k = _CACHE[key]
    cs = _consts()
    dd = _depth
    shared = {"w_ada": f(w_ada[:dd]), "b_ada": f(b_ada[:dd]), "g_pre": f(g_pre[:dd]), "g_post": f(g_post[:dd]), "w_in": f(w_in[:dd]),
              "pool_w": f(pool_w[:dd]), "pool_scale": f(pool_scale[:dd]), "w_br_dn": f(w_br_dn[:dd]), "w_br_na": f(w_br_na[:dd]),
              "w_br_pl": f(w_br_pl[:dd]), "w_out": f(w_out[:dd]), "conv_dn": f(conv_dn[:dd]),
              "a_log": f(a_log_dn[:dd]).reshape(dd, 8), "dt_bias": f(dt_bias_dn[:dd]).reshape(dd, 8), "g_norm": f(g_norm_dn[:dd])}
    shared.update(cs)
    rpad = np.zeros((dd, 8, 15, 127), np.float32)
    rpad[..., 48:79] = f(na_bias[:dd])[..., ::-1]
    x_prompt = f(x_prompt)
    x_sample = f(x_sample)
    in_maps = []
    for core in range(8):
        b = core // 4
        m = dict(shared)
        m["x_p"] = x_prompt[core * NPS:(core + 1) * NPS]
        m["x_s"] = x_sample[b]
        m["cvec"] = np.stack([f(c_ctx), f(c)[b]])
        m["sdn"] = f(state_dn[b, :dd])
        m["ck"] = f(cache_k_na[b, :dd]).reshape(dd, 256, 512)
        m["cvv"] = f(cache_v_na[b, :dd]).reshape(dd, 256, 512)
        m["rpad"] = rpad
        in_maps.append({n: m[n] for n in k.din})
    res = run_bass_kernel_spmd(nc, in_maps, core_ids=list(range(8)))
    r = res.results
    y_p = np.concatenate([r[i]["y_p"] for i in range(8)], axis=0)
    y_s = np.stack([r[0]["y_s"], r[4]["y_s"]])
    n_k = np.concatenate([r[i]["nk"] for i in range(8)], axis=0).reshape(32, dd, SEQ, 8, 64)
    n_v = np.concatenate([r[i]["nv"] for i in range(8)], axis=0).reshape(32, dd, SEQ, 8, 64)
    n_s = np.concatenate([r[i]["nst"] for i in range(8)], axis=0)
    return y_p, y_s, n_k, n_v, n_s
```

```python
import contextlib
import numpy as np
import concourse.bass as bass
import concourse.mybir as mybir
from concourse.ap import AP
from concourse.bass_utils import run_bass_kernel_spmd

F32 = mybir.dt.float32
F32R = mybir.dt.float32r
ALU = mybir.AluOpType
AF = mybir.ActivationFunctionType

ENGS = ("pe", "act", "dve", "pool", "sp")
NDSEM = 12

D = 1024
DEPTH = 4
SEQ = 256
DSEQ = 2048
NIN = 8208
OFF_DN_Z = 1536
OFF_DN_BETA = 2048
OFF_DN_A = 2056
OFF_NA_Q = 2064
OFF_NA_K = 2576
OFF_NA_V = 3088
OFF_NA_Z = 3600
OFF_PL_U = 4112
OFF_PL_Z = 4624
OFF_GATE = 5136
EPS = 1e-6
POOL_WINDOWS = (2, 4, 8, 16)
NPS = 4
NCST = 582 + 6 * 64 + 64 + 6 * 128
BIG = 30000.0


class Dep:
    __slots__ = ("w", "r", "excl")

    def __init__(self, w=None, excl=False):
        self.w = w
        self.r = []
        self.excl = excl


class Op:
    __slots__ = ("eng", "fn", "waits", "signal", "count", "is_dma", "dsem", "dval", "dprev")

    def __init__(self, eng, fn, is_dma=False):
        self.eng = eng
        self.fn = fn
        self.waits = []
        self.signal = False
        self.count = None
        self.is_dma = is_dma
        self.dsem = None
        self.dval = None
        self.dprev = None


class Prog:
    def __init__(self, nc):
        self.nc = nc
        self.ops = {e: [] for e in ENGS}
        self.ndma = {e: 0 for e in ENGS}
        self.dtot = {e: [0] * NDSEM for e in ENGS}
        self.nops = 0

    def _mk(self, eng, fn, reads, writes, is_dma):
        o = Op(eng, fn, is_dma)
        ex = [t for t in reads if t.excl]
        if ex:
            reads = [t for t in reads if not t.excl]
            writes = list(writes) + [t for t in ex if t not in writes]
        deps = []
        seen = set()

        def add(d):
            if d is None or id(d) in seen:
                return
            if (not d.is_dma) and d.eng == "pe" and eng == "pe" and not is_dma:
                return
            seen.add(id(d))
            deps.append(d)

        for t in reads:
            add(t.w)
        for t in writes:
            add(t.w)
            for r in t.r:
                add(r)
        o.waits = deps
        for d in deps:
            d.signal = True
        for t in reads:
            if not is_dma:
                t.r = [x for x in t.r if x.is_dma or x.eng != eng]
            t.r.append(o)
        for t in writes:
            t.w = o
            t.r = []
        self.ops[eng].append(o)
        self.nops += 1
        return o

    def op(self, eng, fn, reads=(), writes=()):
        return self._mk(eng, fn, reads, writes, False)

    def dma(self, eng, out, in_, reads=(), writes=(), r32=False, slow=False):
        nc = self.nc
        eng = "pool" if type(out.tensor).__name__.startswith("DRam") else "sp"

        def fn(e):
            kw = {}
            if slow:
                kw["allow_slow_non_contiguous"] = True
            if r32:
                nc.dge_precook = False
            ins = e.dma_start(out=out, in_=in_, **kw)
            if r32:
                nc.dge_precook = True
            return ins

        o = self._mk(eng, fn, reads, writes, True)
        i = self.ndma[eng] % NDSEM
        self.ndma[eng] += 1
        o.dsem = i
        o.dprev = self.dtot[eng][i]
        self.dtot[eng][i] += 16
        o.dval = self.dtot[eng][i]
        o.signal = True
        return o

    def emit(self):
        nc = self.nc
        for e in ENGS:
            c = 0
            for o in self.ops[e]:
                if not o.is_dma and o.signal:
                    c += 1
                    o.count = c
        nsig = {e: sum(1 for o in self.ops[e] if (not o.is_dma and o.signal)) for e in ENGS}
        with contextlib.ExitStack() as st:
            esem = {e: st.enter_context(nc.semaphore("s_" + e)) for e in ENGS}
            dsem = {
                e: [st.enter_context(nc.semaphore("d_%s_%d" % (e, i))) for i in range(NDSEM)]
                for e in ("sp", "act", "pool")
            }
            block = st.enter_context(nc.Block())
            ops = self.ops
            dtot = self.dtot

            def run(e, engobj, final=False):
                seen_e = {x: 0 for x in ENGS}
                seen_d = {}
                for o in ops[e]:
                    for d in o.waits:
                        if d.is_dma:
                            key = (d.eng, d.dsem)
                            if seen_d.get(key, 0) >= d.dval:
                                continue
                            engobj.wait_ge(dsem[d.eng][d.dsem], d.dval)
                            seen_d[key] = d.dval
                        else:
                            if seen_e[d.eng] >= d.count:
                                continue
                            engobj.wait_ge(esem[d.eng], d.count)
                            seen_e[d.eng] = d.count
                    if o.is_dma:
                        key = (e, o.dsem)
                        if o.dprev > 0 and seen_d.get(key, 0) < o.dprev:
                            engobj.wait_ge(dsem[e][o.dsem], o.dprev)
                            seen_d[key] = o.dprev
                        ins = o.fn(engobj)
                        ins.then_inc(dsem[e][o.dsem], 16)
                    else:
                        ins = o.fn(engobj)
                        if o.signal:
                            ins.then_inc(esem[e], 1)
                if final:
                    for x in ENGS:
                        if x != e and nsig[x] > 0:
                            engobj.wait_ge(esem[x], nsig[x])
                    for q in ("sp", "act", "pool"):
                        for i in range(NDSEM):
                            if dtot[q][i] > 0:
                                engobj.wait_ge(dsem[q][i], dtot[q][i])

            @block.tensor
            def _(eng):
                run("pe", eng)

            @block.vector
            def _(eng):
                run("dve", eng)

            @block.scalar
            def _(eng):
                run("act", eng)

            @block.gpsimd
            def _(eng):
                run("pool", eng)

            @block.sync
            def _(eng):
                run("sp", eng, final=True)


class K:
    def __init__(self, nc, st):
        self.nc = nc
        self.st = st
        self.P = Prog(nc)
        self.din = {}
        self.dout = {}
        self.psb = [st.enter_context(nc.psum_tensor("psb%d" % i, [128, 512], F32)) for i in range(8)]
        self.psd = [Dep(excl=True) for _ in range(8)]
        self.psi = 0
        self.dq = 0

    def inp(self, name, shape, dt=F32):
        t = self.nc.dram_tensor(name, list(shape), dt, kind="ExternalInput")
        self.din[name] = t
        return t

    def outp(self, name, shape):
        t = self.nc.dram_tensor(name, list(shape), F32, kind="ExternalOutput")
        self.dout[name] = t
        return t

    def scr(self, name, shape, dt=F32):
        return self.nc.dram_tensor(name, list(shape), dt, kind="Internal")

    def sb(self, name, shape, dt=F32):
        return self.st.enter_context(self.nc.sbuf_tensor(name, list(shape), dt))

    def ps(self):
        i = self.psi
        self.psi = (i + 1) % 8
        return self.psb[i], self.psd[i]

    def q(self):
        self.dq ^= 1
        return "sp" if self.dq else "act"

    def mm(self, out, lhsT, rhs, start, stop, reads, writes):
        self.P.op("pe", lambda e: e.matmul(out, lhsT=lhsT, rhs=rhs, start=start, stop=stop), reads, writes)

    def tr(self, out, in_, ident, reads, writes):
        self.P.op("pe", lambda e: e.transpose(out, in_, ident), reads, writes)

    def act(self, out, in_, func, reads, writes, bias=None, scale=1.0, accum=None):
        def fn(e):
            kw = {}
            if bias is not None:
                kw["bias"] = bias
            if accum is not None:
                kw["accum_out"] = accum
            return e.activation(out=out, in_=in_, func=func, scale=scale, **kw)

        self.P.op("act", fn, reads, writes)

    def tt(self, out, in0, in1, op, reads, writes, eng="dve"):
        self.P.op(eng, lambda e: e.tensor_tensor(out=out, in0=in0, in1=in1, op=op), reads, writes)

    def ts(self, out, in0, s1, s2, op0, op1, reads, writes, eng="dve"):
        if s2 is None:
            self.P.op(eng, lambda e: e.tensor_scalar(out=out, in0=in0, scalar1=s1, scalar2=None, op0=op0), reads, writes)
        else:
            self.P.op(eng, lambda e: e.tensor_scalar(out=out, in0=in0, scalar1=s1, scalar2=s2, op0=op0, op1=op1), reads, writes)

    def stt(self, out, in0, scalar, in1, op0, op1, reads, writes, eng="dve"):
        self.P.op(eng, lambda e: e.scalar_tensor_tensor(out=out, in0=in0, scalar=scalar, in1=in1, op0=op0, op1=op1),
                  reads, writes)

    def rcp(self, out, in_, reads, writes):
        self.P.op("dve", lambda e: e.reciprocal(out=out, in_=in_), reads, writes)

    def cp(self, out, in_, reads, writes, eng="dve"):
        self.P.op(eng, lambda e: e.tensor_copy(out=out, in_=in_), reads, writes)

    def ms(self, ap, val, writes, eng="pool"):
        self.P.op(eng, lambda e: e.memset(ap, val), (), writes)


def build_program(depth=DEPTH, do_dn=True, do_na=True):
    nc = bass.Bass("TRN2", target_bir_lowering=False)
    st = contextlib.ExitStack()
    with st:
        k = K(nc, st)
        P = k.P
        x_p = k.inp("x_p", [NPS, SEQ, D])
        x_s = k.inp("x_s", [DSEQ, D])
        cvec = k.inp("cvec", [2, D])
        w_ada = k.inp("w_ada", [depth, D, 3 * D], F32R)
        b_ada = k.inp("b_ada", [depth, 3 * D])
        g_pre = k.inp("g_pre", [depth, D])
        g_post = k.inp("g_post", [depth, D])
        w_in = k.inp("w_in", [depth, D, NIN], F32R)
        pool_w = k.inp("pool_w", [depth, 4, 128, 128], F32R)
        pool_scale = k.inp("pool_scale", [depth, 512])
        w_br = [k.inp(n, [depth, 512, D], F32R) for n in ("w_br_dn", "w_br_na", "w_br_pl")]
        w_out = k.inp("w_out", [depth, D, D], F32R)
        invcnt_p = k.inp("invcnt_p", [4, SEQ])
        invcnt_s = k.inp("invcnt_s", [4, DSEQ])
        ident_in = k.inp("ident", [128, 128])
        conv_dn = k.inp("conv_dn", [depth, 3, 1536])
        a_log = k.inp("a_log", [depth, 8])
        dt_bias = k.inp("dt_bias", [depth, 8])
        g_norm = k.inp("g_norm", [depth, 128])
        sdn = k.inp("sdn", [depth, 2, 4, 128, 128])
        dncst_in = k.inp("dncst", [128, NCST])
        ck_in = k.inp("ck", [depth, 256, 512])
        cv_in = k.inp("cvv", [depth, 256, 512], F32R)
        rpad_in = k.inp("rpad", [depth, 8, 15, 127])

        y_p = k.outp("y_p", [NPS, SEQ, D])
        y_s = k.outp("y_s", [DSEQ, D])
        nk = k.outp("nk", [NPS, depth, SEQ, 512])
        nv = k.outp("nv", [NPS, depth, SEQ, 512])
        d_nkv = Dep()
        nst = k.outp("nst", [NPS, depth, 2, 4, 128, 128])
        d_nst = Dep()

        xs_p = [k.scr("xs_p%d" % i, [NPS, SEQ, D]) for i in range(2)]
        xs_s = [k.scr("xs_s%d" % i, [DSEQ, D]) for i in range(2)]
        yscr = k.scr("yscr", [12, 128, DSEQ], F32R)
        d_xs_p = [[Dep() for _ in range(NPS)] for _ in range(2)]
        d_xs_s = [Dep() for _ in range(2)]
        d_yscr = [Dep() for _ in range(12)]
        gscr = k.scr("gscr", [24, 128, DSEQ])
        d_gscr = [Dep() for _ in range(24)]
        mscr = k.scr("mscr", [8, 128, DSEQ], F32R)
        d_mscr = Dep()

        ident = k.sb("ident_sb", [128, 128])
        d_ident = Dep()
        P.dma("sp", ident[:], ident_in.ap(), writes=[d_ident])
        cst = k.sb("dncst_sb", [128, NCST])
        d_cst = Dep()
        P.dma("act", cst[:], dncst_in.ap(), writes=[d_cst])
        TRI = [cst[:, 0:128], cst[:, 128:256]]
        BLK = cst[:, 256:384]
        ONES = cst[:, 384:512]
        I2 = cst[:, 512:576]
        SEL2 = cst[:, 576:578]
        SELLAST = cst[:, 578:582]
        MASK = [[cst[:, 582 + (ty * 2 + d_) * 64: 582 + (ty * 2 + d_ + 1) * 64] for d_ in range(2)] for ty in range(3)]
        CM = cst[:, 582 + 384:582 + 384 + 64]
        MASKB = [[cst[:, 1030 + (ty * 2 + d_) * 128: 1030 + (ty * 2 + d_ + 1) * 128] for d_ in range(2)] for ty in range(3)]
        hT = k.sb("hT", [128, 8, DSEQ], F32R)
        d_hT = Dep()
        small = k.sb("small", [128, 16])
        d_small = Dep()
        silucT = k.sb("silucT", [128, 8, 2], F32R)
        d_siluc = Dep()
        modcol = k.sb("modcol", [128, 16, 2])
        s1col = k.sb("s1col", [128, 8, 2])
        d_modcol = Dep()
        ggscr = k.scr("ggscr", [2, 128, D])
        d_gg = Dep()
        RSZ = 15 * 1024
        FSZ = 18 * 1024 + 512
        arenaR = k.sb("arenaR", [128, RSZ], F32R)
        arenaF = k.sb("arenaF", [128, FSZ])
        arena_deps = []
        WOFF = RSZ - 4096
        wdeps = [Dep() for _ in range(4)]

        def fence():
            f = P.op("dve", lambda e: e.memset(small[:, 15:16], 0.0), reads=(), writes=list(arena_deps))
            arena_deps.clear()
            return f

        class Carve:
            def __init__(self, seed):
                self.offR = 0
                self.offF = 0
                self.seed = seed

            def getw(self, j, n=1):
                a = arenaR[:, WOFF + j * 1024:WOFF + (j + n) * 1024]
                return a, wdeps[j:j + n]

            def get(self, cols, dt=F32):
                if dt == F32R:
                    a = arenaR[:, self.offR:self.offR + cols]
                    self.offR += cols
                    assert self.offR <= WOFF, self.offR
                else:
                    a = arenaF[:, self.offF:self.offF + cols]
                    self.offF += cols
                    assert self.offF <= FSZ, self.offF
                d = Dep(self.seed)
                arena_deps.append(d)
                return a, d

        craw = k.sb("craw", [128, 8, 2])
        for j in range(2):
            P.dma("sp", craw[:, :, j], cvec.ap()[j].rearrange("(c p) -> p c", p=128), writes=[d_siluc], slow=True)
        k.act(silucT[:], craw[:], AF.Silu, [d_siluc], [d_siluc])

        seqs = [("p", i, SEQ) for i in range(NPS)] + [("s", 0, DSEQ)]
        zscr = k.scr("zscr", [depth, 120, 64, 127])
        d_zscr = [Dep() for _ in range(depth)]
        if do_na:
            for l_ in range(depth):
                P.dma("sp", zscr.ap()[l_], AP(rpad_in, l_ * 120 * 127, [[127, 120], [0, 64], [1, 127]]), writes=[d_zscr[l_]])

        def dn_unit(l, kind, si, T, h, L):
            NT = T // 128
            TT = min(T, 512)
            NTL = T // TT
            nseq = T // L
            tps = L // 128
            LP2 = L + 2
            FL = nseq * LP2

            def tcol(ti):
                return (ti // tps) * LP2 + (ti % tps) * 128

            fz = fence()
            cv = Carve(fz)
            ws = []
            for off in (0, 512, 1024, OFF_DN_Z):
                wa, dwl = cv.getw(len(ws))
                dw = dwl[0]
                w3 = wa.rearrange("p (c n) -> p c n", c=8)
                c0 = off + h * 128
                P.dma(k.q(), w3, w_in.ap()[l, :, c0:c0 + 128].rearrange("(c p) n -> p c n", p=128), writes=[dw], r32=True)
                ws.append((w3, dw))
            (wq, d_wq), (wk, d_wk), (wv, d_wv), (wz, d_wz) = ws
            wba_f, d_wba = cv.get(8 * 4, F32R)
            wba = wba_f.rearrange("p (c n) -> p c n", c=8)
            for j4 in range(4):
                cj4 = OFF_DN_BETA + 4 * j4 + h
                P.dma("sp", wba[:, :, j4], w_in.ap()[l, :, cj4].rearrange("(c p) -> p c", p=128), writes=[d_wba], r32=True, slow=True)
            raw, d_raw = cv.get(FL)
            qf, d_qf = cv.get(FL)
            kf, d_kf = cv.get(FL)
            vf, d_vf = cv.get(FL)
            raw3 = raw.rearrange("p (s n) -> p s n", s=nseq)
            oacc, _ = cv.get(T)
            d_oacc = [Dep(fz) for _ in range(NT)]
            arena_deps.extend(d_oacc)
            tmpb, d_tmpb = cv.get(512)
            cw, d_cw = cv.get(9)
            for idx in range(3):
                c0 = idx * 512 + h * 128
                P.dma("act", cw[:, idx * 3:(idx + 1) * 3], conv_dn.ap()[l, :, c0:c0 + 128].rearrange("t p -> p t"),
                      writes=[d_cw], slow=True)
            k.ms(raw3[:, :, 0:1], 0.0, [d_raw])
            k.ms(raw3[:, :, L + 1:L + 2], 0.0, [d_raw])
            for i in range(NT):
                k.ms(oacc[:, i * 128:(i + 1) * 128], 0.0, [d_oacc[i]])
            fchunks = [(a_, min(512, FL - 2 - a_)) for a_ in range(0, FL - 2, 512)]
            fchunks2 = [(a_, min(512, FL - a_)) for a_ in range(0, FL, 512)]
            dsts = [(qf, d_qf), (kf, d_kf), (vf, d_vf)]
            for idx in range(3):
                w3, dw = ws[idx]
                dst, dd = dsts[idx]
                for t in range(NTL):
                    pb, pd = k.ps()
                    for kc in range(8):
                        k.mm(pb[:, 0:TT], w3[:, kc, :], hTv[:, kc, t * TT:(t + 1) * TT], kc == 0, kc == 7, [dw, d_hT], [pd])
                    if L >= TT:
                        k.cp(raw[:, 1 + t * TT:1 + (t + 1) * TT], pb[:, 0:TT], [pd], [d_raw])
                    else:
                        spt = TT // L
                        k.cp(raw3[:, t * spt:(t + 1) * spt, 1:1 + L], pb[:, 0:TT].rearrange("p (s n) -> p s n", s=spt), [pd], [d_raw])
                for (a, n_) in fchunks:
                    k.ts(dst[:, a:a + n_], raw[:, a:a + n_], cw[:, idx * 3:idx * 3 + 1], None, ALU.mult, None, [d_raw, d_cw], [dd])
                    k.stt(tmpb[:, 0:n_], raw[:, a + 1:a + 1 + n_], cw[:, idx * 3 + 1:idx * 3 + 2], dst[:, a:a + n_],
                          ALU.mult, ALU.add, [d_raw, d_cw, dd], [d_tmpb])
                    k.stt(dst[:, a:a + n_], raw[:, a + 2:a + 2 + n_], cw[:, idx * 3 + 2:idx * 3 + 3], tmpb[:, 0:n_],
                          ALU.mult, ALU.add, [d_raw, d_cw, d_tmpb], [dd])
                    k.act(dst[:, a:a + n_], dst[:, a:a + n_], AF.Silu, [dd], [dd])
                if FL - 2 < FL:
                    k.ms(dst[:, FL - 2:FL], 0.0, [dd])
            for idx in range(2):
                dst, dd = dsts[idx]
                for (a, n_) in fchunks2:
                    k.act(tmpb[:, 0:n_], dst[:, a:a + n_], AF.Square, [dd], [d_tmpb])
                    pb, pd = k.ps()
                    k.mm(pb[:, 0:n_], ONES, tmpb[:, 0:n_], True, True, [d_cst, d_tmpb], [pd])
                    k.act(tmpb[:, 0:n_], pb[:, 0:n_], AF.Sqrt, [pd], [d_tmpb], bias=EPS)
                    k.rcp(tmpb[:, 0:n_], tmpb[:, 0:n_], [d_tmpb], [d_tmpb])
                    if idx == 0:
                        k.stt(dst[:, a:a + n_], dst[:, a:a + n_], 128.0 ** -0.5, tmpb[:, 0:n_], ALU.mult, ALU.mult, [dd, d_tmpb], [dd])
                    else:
                        k.tt(dst[:, a:a + n_], dst[:, a:a + n_], tmpb[:, 0:n_], ALU.mult, [dd, d_tmpb], [dd])
            zs = raw
            d_zs = d_raw
            for t in range(NTL):
                pb, pd = k.ps()
                for kc in range(8):
                    k.mm(pb[:, 0:TT], wz[:, kc, :], hTv[:, kc, t * TT:(t + 1) * TT], kc == 0, kc == 7, [d_wz, d_hT], [pd])
                k.act(zs[:, t * TT:(t + 1) * TT], pb[:, 0:TT], AF.Silu, [pd], [d_zs])
            d_g = Dep(fz)
            arena_deps.append(d_g)
            G_ = [d_g]
            ba, _ = cv.get(NT * 4)
            ba3 = ba.rearrange("p (t n) -> p t n", n=4)
            pb, pd = k.ps()
            for i in range(NT):
                for kc in range(8):
                    k.mm(pb[:, i * 4:(i + 1) * 4], hTv[:, kc, i * 128:(i + 1) * 128], wba[:, kc, :], kc == 0, kc == 7, [d_wba, d_hT], [pd])
            k.cp(ba, pb[:, 0:NT * 4], [pd], G_)
            NK = 10
            GA, _ = cv.get(NT * NK * 2)
            GA4 = GA.rearrange("p (t k d) -> p t k d", k=NK, d=2)
            K_BETA, K_NEGB, K_G, K_GB, K_EG, K_BG, K_EGL, K_GL = range(8)
            gk = lambda kk: GA4[:, :, kk, :]

            def g2():
                a_, _ = cv.get(NT * 2)
                return a_, a_.rearrange("p (t n) -> p t n", n=2)

            lnb, lnb3 = g2()
            la, la3 = g2()
            gt, gt3 = g2()
            rowc, _ = cv.get(4)
            gn_row, d_gn = cv.get(128)
            P.dma("sp", rowc[:, 0:1], dt_bias.ap()[l, h:h + 1].partition_broadcast(128), writes=G_)
            P.dma("sp", rowc[:, 1:2], dt_bias.ap()[l, 4 + h:5 + h].partition_broadcast(128), writes=G_)
            P.dma("act", rowc[:, 2:3], a_log.ap()[l, h:h + 1].partition_broadcast(128), writes=G_)
            P.dma("act", rowc[:, 3:4], a_log.ap()[l, 4 + h:5 + h].partition_broadcast(128), writes=G_)
            P.dma("sp", gn_row, g_norm.ap()[l].partition_broadcast(128), writes=[d_gn])
            k.act(rowc[:, 2:4], rowc[:, 2:4], AF.Exp, G_, G_)
            k.ts(rowc[:, 2:4], rowc[:, 2:4], -1.0, None, ALU.mult, None, G_, G_)
            k.act(gk(K_BETA), ba3[:, :, 0:2], AF.Sigmoid, G_, G_)
            k.ts(gk(K_NEGB), gk(K_BETA), -1.0, None, ALU.mult, None, G_, G_)
            k.act(gt3, ba3[:, :, 0:2], AF.Exp, G_, G_, scale=-1.0)
            k.act(gt, gt, AF.Ln, G_, G_, bias=1.0)
            k.ts(lnb, gt, -1.0, None, ALU.mult, None, G_, G_)
            k.tt(gt3, ba3[:, :, 2:4], rowc[:, 0:2].unsqueeze(1).to_broadcast([128, NT, 2]), ALU.add, G_, G_)
            k.act(gt, gt, AF.Exp, G_, G_)
            k.act(gt, gt, AF.Ln, G_, G_, bias=1.0)
            k.tt(la3, gt3, rowc[:, 2:4].unsqueeze(1).to_broadcast([128, NT, 2]), ALU.mult, G_, G_)
            pb, pd = k.ps()
            for i in range(NT):
                k.mm(pb[:, i * 2:(i + 1) * 2], TRI[0], la3[:, i, :], True, True, [d_cst, d_g], [pd])
                k.mm(pb[:, 64 + i * 2:64 + (i + 1) * 2], TRI[1], la3[:, i, :], True, True, [d_cst, d_g], [pd])
            k.cp(gk(K_G)[:, :, 0:1], pb[:, 0:NT * 2].rearrange("p (t n) -> p t n", n=2)[:, :, 0:1], [pd], G_)
            k.cp(gk(K_G)[:, :, 1:2], pb[:, 64:64 + NT * 2].rearrange("p (t n) -> p t n", n=2)[:, :, 1:2], [pd], G_)
            k.tt(gk(K_GB), gk(K_G), lnb3, ALU.add, G_, G_)
            k.act(gk(K_EG), gk(K_G), AF.Exp, G_, G_)
            k.tt(gk(K_BG), gk(K_BETA), gk(K_EG), ALU.mult, G_, G_)
            k.tt(gt3, gk(K_G), SEL2.unsqueeze(1).to_broadcast([128, NT, 2]), ALU.mult, G_ + [d_cst], G_)
            pb, pd = k.ps()
            k.mm(pb[:, 0:NT * 2], BLK, gt, True, True, [d_cst, d_g], [pd])
            k.tt(gt3, pb[:, 0:NT * 2].rearrange("p (t n) -> p t n", n=2), gk(K_G), ALU.subtract, [pd, d_g], G_)
            k.act(gk(K_EGL), gt3, AF.Exp, G_, G_)
            gsel2, _ = cv.get(NT * 4)
            k.tt(gsel2.rearrange("p (t d c) -> p t d c", d=2, c=2), gk(K_G).unsqueeze(3).to_broadcast([128, NT, 2, 2]),
                 SELLAST.rearrange("p (d c) -> p d c", d=2).unsqueeze(1).to_broadcast([128, NT, 2, 2]), ALU.mult,
                 G_ + [d_cst], G_)
            pb, pd = k.ps()
            k.mm(pb[:, 0:NT * 4], ONES, gsel2, True, True, [d_cst, d_g], [pd])
            for cp in range(2):
                k.act(gk(K_GL + cp), pb[:, 0:NT * 4].rearrange("p (t d c) -> p t d c", d=2, c=2)[:, :, :, cp], AF.Exp, [pd], G_)
            S_all = []
            for sq_ in range(nseq):
                S_ = []
                for d_ in range(2):
                    s_, ds_ = cv.get(128)
                    if kind == "p":
                        k.ms(s_, 0.0, [ds_])
                    else:
                        P.dma(k.q(), s_, sdn.ap()[l, d_, h], writes=[ds_])
                    S_.append((s_, ds_))
                S_all.append(S_)
            X, d_X = cv.get(256)
            vb, d_vb = cv.get(256)
            Gd, d_Gd = cv.get(256)
            Gbd, d_Gbd = cv.get(256)
            E1, d_E1 = cv.get(256)
            E2, d_E2 = cv.get(256)
            E3, d_E3 = cv.get(256)
            MML = [[cv.get(256), cv.get(256)] for _ in range(2)]
            PTL = [cv.get(128) for _ in range(2)]
            OB = []
            for _i in range(2):
                o_ = {}
                o_["attnT"], o_["d_at"] = cv.get(256)
                o_["kg"], o_["d_kg"] = cv.get(256)
                o_["uw"], o_["d_uw"] = cv.get(512)
                ls_, o_["d_LS"] = cv.get(NK * 2)
                o_["LS3"] = ls_.rearrange("p (k d) -> p k d", d=2)
                OB.append(o_)
            SC = []
            for d_ in range(2):
                SC.append((cv.get(128), cv.get(128), cv.get(128)))
            PB = k.psb
            PD = k.psd

            def lanes(j_):
                sq_ = j_ // tps
                s_ = j_ % tps
                return (sq_ * tps + s_, sq_ * tps + tps - 1 - s_)

            def prep(s_):
                il = lanes(s_)
                ob = OB[s_ % 2]
                LS3, d_LS = ob["LS3"], ob["d_LS"]
                ls = lambda kk, d_: LS3[:, kk, d_:d_ + 1]
                tsl = [slice(tcol(il[d_]), tcol(il[d_]) + 128) for d_ in range(2)]
                for d_ in range(2):
                    k.cp(LS3[:, :, d_], GA4[:, il[d_], :, d_], [d_g], [d_LS], eng="pool")
                pk, pkd = PB[0], PD[0]
                for d_ in range(2):
                    k.tr(pk[:, d_ * 256:d_ * 256 + 128], kf[:, tsl[d_]], ident[:], [d_kf, d_ident], [pkd])
                    k.tr(pk[:, d_ * 256 + 128:d_ * 256 + 256], vf[:, tsl[d_]], ident[:], [d_vf, d_ident], [pkd])
                for d_ in range(2):
                    ds = slice(d_ * 128, (d_ + 1) * 128)
                    k.ts(ob["kg"][:, ds], pk[:, d_ * 256:d_ * 256 + 128], ls(K_EGL, d_), None, ALU.mult, None, [pkd, d_LS], [ob["d_kg"]])
                    k.act(X[:, ds], pk[:, d_ * 256:d_ * 256 + 128], AF.Copy, [pkd, d_LS], [d_X], scale=ls(K_BG, d_))
                    k.ts(vb[:, ds], pk[:, d_ * 256 + 128:d_ * 256 + 256], ls(K_BETA, d_), None, ALU.mult, None, [pkd, d_LS], [d_vb])
                yield
                pa, pad = PB[1], PD[1]
                for d_ in range(2):
                    k.mm(pa[:, d_ * 256:d_ * 256 + 128], kf[:, tsl[d_]], kf[:, tsl[d_]], True, True, [d_kf], [pad])
                    k.mm(pa[:, d_ * 256 + 128:d_ * 256 + 256], kf[:, tsl[d_]], qf[:, tsl[d_]], True, True, [d_kf, d_qf], [pad])
                for d_ in range(2):
                    ds = slice(d_ * 128, (d_ + 1) * 128)
                    k.ts(Gd[:, ds], ident[:], ls(K_G, d_), None, ALU.mult, None, [d_ident, d_LS], [d_Gd])
                    k.act(Gbd[:, ds], ident[:], AF.Copy, [d_ident, d_LS], [d_Gbd], scale=ls(K_GB, d_))
                pg, pgd = PB[2], PD[2]
                k.mm(pg[:, 0:256], ONES, Gd, True, True, [d_cst, d_Gd], [pgd])
                k.mm(pg[:, 256:512], ONES, Gbd, True, True, [d_cst, d_Gbd], [pgd])
                yield
                for d_ in range(2):
                    ds = slice(d_ * 128, (d_ + 1) * 128)
                    k.stt(E1[:, ds], pg[:, ds], ls(K_G, d_), MASKB[0][d_], ALU.subtract, ALU.add, [pgd, d_LS, d_cst], [d_E1])
                    k.stt(E2[:, ds], pg[:, 256 + d_ * 128:256 + (d_ + 1) * 128], ls(K_G, d_), MASKB[1][d_], ALU.subtract, ALU.add,
                          [pgd, d_LS, d_cst], [d_E2])
                    k.stt(E3[:, ds], pg[:, ds], ls(K_G, d_), MASKB[2][d_], ALU.subtract, ALU.add, [pgd, d_LS, d_cst], [d_E3])
                k.act(E1, E1, AF.Exp, [d_E1], [d_E1])
                k.act(E2, E2, AF.Exp, [d_E2], [d_E2])
                k.act(E3, E3, AF.Exp, [d_E3], [d_E3], scale=-1.0)
                yield
                cur = [MML[d_][0] for d_ in range(2)]
                nxt = [MML[d_][1] for d_ in range(2)]
                for d_ in range(2):
                    ds = slice(d_ * 128, (d_ + 1) * 128)
                    c_, dc_ = cur[d_]
                    k.tt(ob["attnT"][:, ds], pa[:, d_ * 256 + 128:d_ * 256 + 256], E1[:, ds], ALU.mult, [pad, d_E1], [ob["d_at"]])
                    k.stt(c_[:, 128:256], pa[:, d_ * 256:d_ * 256 + 128], -1.0, E2[:, ds], ALU.mult, ALU.mult, [pad, d_E2], [dc_])
                    k.stt(c_[:, 0:128], pa[:, d_ * 256:d_ * 256 + 128], ls(K_NEGB, d_), E3[:, ds], ALU.mult, ALU.mult,
                          [pad, d_E3, d_LS], [dc_])
                    k.tt(PTL[d_][0], ident[:], c_[:, 128:256], ALU.add, [d_ident, dc_], [PTL[d_][1]])
                yield
                pmb = (3, 2)
                ppb = (0, 1)
                for lev in range(5):
                    lastl = (lev == 4)
                    for d_ in range(2):
                        c_, dc_ = cur[d_]
                        n_, dn_ = nxt[d_]
                        pm, pmd = PB[pmb[d_]], PD[pmb[d_]]
                        k.mm(pm[:, 0:128], c_[:, 128:256], c_[:, 0:128], True, True, [dc_], [pmd])
                        if not lastl:
                            k.mm(pm[:, 128:256], c_[:, 0:128], c_[:, 128:256], True, True, [dc_], [pmd])
                        ncols = 128 if lastl else 256
                        if d_ == 0:
                            k.act(n_[:, 0:ncols], pm[:, 0:ncols], AF.Copy, [pmd], [dn_])
                        else:
                            k.cp(n_[:, 0:ncols], pm[:, 0:ncols], [pmd], [dn_])
                    yield
                    for d_ in range(2):
                        n_, dn_ = nxt[d_]
                        pt_, dpt_ = PTL[d_]
                        pp, ppd = PB[ppb[d_]], PD[ppb[d_]]
                        k.mm(pp[:, 0:128], n_[:, 0:128], pt_, True, True, [dn_, dpt_], [ppd])
                        k.tt(pt_, pt_, pp[:, 0:128], ALU.add, [dpt_, ppd], [dpt_])
                    yield
                    cur, nxt = nxt, cur
                pu, pud = PB[2], PD[2]
                for d_ in range(2):
                    ds = slice(d_ * 128, (d_ + 1) * 128)
                    pt_, dpt_ = PTL[d_]
                    k.mm(pu[:, d_ * 256:d_ * 256 + 128], pt_, vb[:, ds], True, True, [dpt_, d_vb], [pud])
                    k.mm(pu[:, d_ * 256 + 128:d_ * 256 + 256], X[:, ds], pt_, True, True, [d_X, dpt_], [pud])
                k.cp(ob["uw"], pu[:, 0:512], [pud], [ob["d_uw"]])
                yield

            def scan(s_, d_):
                i = lanes(s_)[d_]
                ob = OB[s_ % 2]
                LS3, d_LS = ob["LS3"], ob["d_LS"]
                ds = slice(d_ * 128, (d_ + 1) * 128)
                es = slice(d_ * 64, (d_ + 1) * 64)
                (vnew, d_vn), (oasb, d_oa), (otmp, d_ot) = SC[d_]
                s_t, ds_ = S_all[s_ // tps][d_]
                p1, p1d = PB[4 + 2 * d_], PD[4 + 2 * d_]
                p2, p2d = PB[5 + 2 * d_], PD[5 + 2 * d_]
                for cp in ((0, 1) if d_ == 0 else (1, 0)):
                    bs = slice(cp * 64, (cp + 1) * 64)
                    cs = slice(tcol(i) + cp * 64, tcol(i) + (cp + 1) * 64)
                    k.mm(p1[bs, 0:128], ob["uw"][:, d_ * 256 + 128 + cp * 64:d_ * 256 + 128 + (cp + 1) * 64], s_t, True, True,
                         [ob["d_uw"], ds_], [p1d])
                    k.mm(p1[bs, 128:256], qf[:, cs], s_t, True, True, [d_qf, ds_], [p1d])
                    k.tt(vnew[bs, :], ob["uw"][bs, d_ * 256:d_ * 256 + 128], p1[bs, 0:128], ALU.subtract, [ob["d_uw"], p1d], [d_vn])
                    yield
                    k.mm(p2[bs, 0:128], ob["attnT"][bs, d_ * 128 + cp * 64:d_ * 128 + (cp + 1) * 64], vnew[bs, :], True, True, [ob["d_at"], d_vn], [p2d])
                    k.mm(p2[:, 128:256], ob["kg"][bs, ds], vnew[bs, :], True, True, [ob["d_kg"], d_vn], [p2d])
                    k.act(oasb[bs, :], p2[bs, 0:128], AF.Copy, [p2d], [d_oa])
                    k.stt(s_t, s_t, LS3[:, K_GL + cp, d_:d_ + 1], p2[:, 128:256], ALU.mult, ALU.add, [ds_, d_LS, p2d], [ds_])
                    yield
                    k.stt(otmp[bs, :], p1[bs, 128:256], LS3[bs, K_EG, d_:d_ + 1], oasb[bs, :], ALU.mult, ALU.add,
                          [p1d, d_LS, d_oa], [d_ot])
                    k.tt(oacc[bs, i * 128:(i + 1) * 128], oacc[bs, i * 128:(i + 1) * 128], otmp[bs, :], ALU.add,
                         [d_oacc[i], d_ot], [d_oacc[i]], eng="pool")
                    yield

            def run_gens(gens):
                gens = list(gens)
                while gens:
                    for g_ in list(gens):
                        try:
                            next(g_)
                        except StopIteration:
                            gens.remove(g_)

            run_gens([prep(0)])
            for s_ in range(NT):
                gl_ = [scan(s_, 0), scan(s_, 1)]
                if s_ + 1 < NT:
                    gl_.insert(0, prep(s_ + 1))
                run_gens(gl_)
            if kind == "p":
                for sq_ in range(nseq):
                    for d_ in range(2):
                        P.dma(k.q(), nst.ap()[sq_, l, d_, h], S_all[sq_][d_][0], reads=[S_all[sq_][d_][1]], writes=[d_nst])
            rs, d_rs = cv.get(4)
            on, d_on = cv.get(128)
            yT, d_yT = cv.get(128, F32R)
            for i in range(NT):
                tsl = slice(i * 128, (i + 1) * 128)
                k.act(on, oacc[:, tsl], AF.Square, [d_oacc[i]], [d_on, d_rs], accum=rs[:, 0:1])
                k.act(rs[:, 1:2], rs[:, 0:1], AF.Sqrt, [d_rs], [d_rs], bias=EPS, scale=1.0 / 128)
                k.rcp(rs[:, 2:3], rs[:, 1:2], [d_rs], [d_rs])
                k.stt(on, oacc[:, tsl], rs[:, 2:3], gn_row, ALU.mult, ALU.mult, [d_oacc[i], d_rs, d_gn, d_on], [d_on])
                pt, ptd = k.ps()
                k.tr(pt[:, 0:128], on, ident[:], [d_on, d_ident], [ptd])
                k.tt(yT, pt[:, 0:128], zs[:, tsl], ALU.mult, [ptd, d_zs], [d_yT])
                P.dma(k.q(), yscr_v[h, :, tsl], yT, reads=[d_yT], writes=[d_yscr[h]], r32=True)

        def na_unit(l, kind, si, T, hp):
            NT = T // 128
            TT = min(T, 512)
            fz = fence()
            cv = Carve(fz)
            ws = []
            for off in (OFF_NA_Q, OFF_NA_K, OFF_NA_V, OFF_NA_Z):
                wa, dwl = cv.getw(len(ws))
                dw = dwl[0]
                w3 = wa.rearrange("p (c n) -> p c n", c=8)
                c0 = off + hp * 128
                P.dma(k.q(), w3, w_in.ap()[l, :, c0:c0 + 128].rearrange("(c p) n -> p c n", p=128), writes=[dw], r32=True)
                ws.append((w3, dw))
            (wq, d_wq), (wk, d_wk), (wv, d_wv), (wz, d_wz) = ws
            qT, d_qT = cv.get(T, F32R)
            kT, d_kT = cv.get(T, F32R)
            vt_f, d_vt = cv.get(NT * 132, F32R)
            vtok = vt_f.rearrange("p (t h e) -> p t h e", t=NT, h=2)
            zs, d_zs = cv.get(T)
            ones, d_ones = cv.get(2)
            stgs = [cv.get(256) for _ in range(2)]
            k.ms(ones, 1.0, [d_ones])
            for t in range(T // TT):
                ts_ = slice(t * TT, (t + 1) * TT)
                pb, pd = k.ps()
                for kc in range(8):
                    k.mm(pb[:, 0:TT], wq[:, kc, :], hTv[:, kc, ts_], kc == 0, kc == 7, [d_wq, d_hT], [pd])
                k.ts(qT[:, ts_], pb[:, 0:TT], 0.125, None, ALU.mult, None, [pd], [d_qT])
                pb, pd = k.ps()
                for kc in range(8):
                    k.mm(pb[:, 0:TT], wk[:, kc, :], hTv[:, kc, ts_], kc == 0, kc == 7, [d_wk, d_hT], [pd])
                k.cp(kT[:, ts_], pb[:, 0:TT], [pd], [d_kT])
                pb, pd = k.ps()
                for kc in range(8):
                    k.mm(pb[:, 0:TT], wz[:, kc, :], hTv[:, kc, ts_], kc == 0, kc == 7, [d_wz, d_hT], [pd])
                k.act(zs[:, ts_], pb[:, 0:TT], AF.Silu, [pd], [d_zs])
            for i in range(NT):
                is_ = slice(i * 128, (i + 1) * 128)
                pb, pd = k.ps()
                for kc in range(8):
                    k.mm(pb[:, 0:128], hTv[:, kc, is_], wv[:, kc, :], kc == 0, kc == 7, [d_wv, d_hT], [pd])
                k.cp(vtok[:, i, :, 0:64], pb[:, 0:128].rearrange("p (h e) -> p h e", h=2), [pd], [d_vt])
                k.cp(vtok[:, i, :, 64:66], ones[:, 0:2].unsqueeze(1).to_broadcast([128, 2, 2]), [d_ones], [d_vt], eng="pool")
                if kind == "p":
                    sq_ = i // 2
                    rs_ = slice((i % 2) * 128, (i % 2) * 128 + 128)
                    stg, d_stg = stgs[i % 2]
                    k.cp(stg[:, 0:128], pb[:, 0:128], [pd], [d_stg])
                    P.dma("act", nv.ap()[sq_, l, rs_, hp * 128:(hp + 1) * 128], stg[:, 0:128], reads=[d_stg], writes=[d_nkv])
                    pb, pd = k.ps()
                    for kc in range(8):
                        k.mm(pb[:, 0:128], hTv[:, kc, is_], wk[:, kc, :], kc == 0, kc == 7, [d_wk, d_hT], [pd])
                    k.cp(stg[:, 128:256], pb[:, 0:128], [pd], [d_stg])
                    P.dma("act", nk.ap()[sq_, l, rs_, hp * 128:(hp + 1) * 128], stg[:, 128:256], reads=[d_stg], writes=[d_nkv])
            otok, d_otok = cv.get(128)
            rden, d_rden = cv.get(2)
            yT, d_yT = cv.get(128, F32R)
            if kind == "p":
                PTs = [[cv.get(256, F32R) for _ in range(4)] for _ in range(2)]
                for sq_ in range(T // SEQ):
                    PT = PTs[sq_ % 2]
                    q0 = sq_ * SEQ
                    po, pod = k.ps()
                    for h2 in range(2):
                        hs = slice(h2 * 64, (h2 + 1) * 64)
                        for kt in range(2):
                            pt, d_pt = PT[h2 * 2 + kt]
                            pb, pd = k.ps()
                            k.mm(pb[:, 0:256], kT[hs, q0 + kt * 128:q0 + (kt + 1) * 128], qT[hs, q0:q0 + 256], True, True,
                                 [d_kT, d_qT], [pd])
                            k.act(pt, pb[:, 0:256], AF.Exp, [pd], [d_pt])
                        for qt in range(2):
                            for kt in range(2):
                                pt, d_pt = PT[h2 * 2 + kt]
                                c0 = qt * 132 + h2 * 66
                                k.mm(po[:, c0:c0 + 66], pt[:, qt * 128:(qt + 1) * 128], vtok[:, sq_ * 2 + kt, h2, :], kt == 0, kt == 1,
                                     [d_pt, d_vt], [pod])
                    for qt in range(2):
                        qs = slice(q0 + qt * 128, q0 + (qt + 1) * 128)
                        for h2 in range(2):
                            c0 = qt * 132 + h2 * 66
                            k.rcp(rden[:, h2:h2 + 1], po[:, c0 + 64:c0 + 65], [pod], [d_rden])
                            k.ts(otok[:, h2 * 64:(h2 + 1) * 64], po[:, c0:c0 + 64], rden[:, h2:h2 + 1], None, ALU.mult, None,
                                 [pod, d_rden], [d_otok])
                        pb, pd = k.ps()
                        k.tr(pb[:, 0:128], otok, ident[:], [d_otok, d_ident], [pd])
                        k.tt(yT, pb[:, 0:128], zs[:, qs], ALU.mult, [pd, d_zs], [d_yT])
                        P.dma(k.q(), yscr_v[4 + hp, :, qs], yT, reads=[d_yT], writes=[d_yscr[4 + hp]], r32=True)
            else:
                kcT, d_kcT = cv.get(256, F32R)
                vc_f, d_vc = cv.get(2 * 132, F32R)
                vctx = vc_f.rearrange("p (t h e) -> p t h e", t=2, h=2)
                cst_k, d_cstk = cv.get(256)
                P.dma("sp", cst_k.rearrange("p (t n) -> p t n", t=2),
                      ck_in.ap()[l, :, hp * 128:(hp + 1) * 128].rearrange("(t p) n -> p t n", p=128), writes=[d_cstk])
                pb, pd = k.ps()
                for t in range(2):
                    k.tr(pb[:, t * 128:(t + 1) * 128], cst_k[:, t * 128:(t + 1) * 128], ident[:], [d_cstk, d_ident], [pd])
                k.cp(kcT, pb[:, 0:256], [pd], [d_kcT])
                for t in range(2):
                    P.dma("act", vctx[:, t, :, 0:64],
                          cv_in.ap()[l, t * 128:(t + 1) * 128, hp * 128:(hp + 1) * 128].rearrange("p (h e) -> p h e", h=2),
                          writes=[d_vc], r32=True)
                    k.cp(vctx[:, t, :, 64:66], ones[:, 0:2].unsqueeze(1).to_broadcast([128, 2, 2]), [d_ones], [d_vc], eng="pool")
                E2f, d_E2 = cv.get(2 * 15 * 64)
                E2 = E2f.rearrange("p (h r c) -> p h r c", h=2, r=15)
                for a in range(2):
                    for h2 in range(2):
                        head = hp * 2 + h2
                        src = AP(zscr, ((l * 8 + head) * 15) * 8128 + 63, [[126, 64], [8128, 15], [1, 64]])
                        P.dma("sp" if a == 0 else "act", E2[a * 64:(a + 1) * 64, h2, :, :], src, reads=[d_zscr[l]], writes=[d_E2])
                k.act(E2f, E2f, AF.Exp, [d_E2], [d_E2])
                k.tt(E2f.rearrange("p (g c) -> p g c", c=64), E2f.rearrange("p (g c) -> p g c", c=64),
                     CM.unsqueeze(1).to_broadcast([128, 30, 64]), ALU.mult, [d_E2, d_cst], [d_E2])
                TABf, d_TAB = cv.get(2 * 21 * 128)
                TAB = TABf.rearrange("p (h t q) -> p h t q", h=2, t=21)
                k.ms(TABf, 0.0, [d_TAB])
                plans = {}
                tid = 0
                plans["int"] = []
                for j in range(5):
                    for a in range(2):
                        for b in range(2):
                            dr = 2 * j - 4 + a - b
                            if -4 <= dr <= 3:
                                plans["int"].append((tid, a, b, dr))
                    tid += 1
                tid0 = {"int": 0}
                for m_ in (0, 1, 14, 15):
                    tid0[m_] = tid
                    kt0 = 0 if m_ < 2 else 12
                    plans[m_] = []
                    for j in range(4):
                        for a in range(2):
                            for b in range(2):
                                kr = 2 * (kt0 + j) + a
                                r = 2 * m_ + b
                                rs_ = min(max(r - 4, 0), 24)
                                if rs_ <= kr <= rs_ + 7:
                                    plans[m_].append((tid, a, b, kr - r))
                        tid += 1
                assert tid == 21
                _ci = 0
                for key_, pl in plans.items():
                    for (tid_, a, b, dr) in pl:
                        k.cp(TAB[a * 64:(a + 1) * 64, :, tid_, b * 64:(b + 1) * 64], E2[a * 64:(a + 1) * 64, :, dr + 7, :],
                             [d_E2], [d_TAB], eng=("dve" if _ci % 2 == 0 else "pool"))
                        _ci += 1
                PTb = [cv.get(7 * 128, F32R) for _ in range(2)]
                tEb = [cv.get(512) for _ in range(2)]
                tE2b = [cv.get(128) for _ in range(2)]
                its = [(m_, h2) for m_ in range(16) for h2 in range(2)]

                def info(m_):
                    if 2 <= m_ <= 13:
                        return [m_ - 2 + j for j in range(5)], 0
                    return [(0 if m_ < 2 else 12) + j for j in range(4)], tid0[m_]

                def emit_scores(m_, h2):
                    kts, t0 = info(m_)
                    nl = len(kts)
                    qs = slice(m_ * 128, (m_ + 1) * 128)
                    hs = slice(h2 * 64, (h2 + 1) * 64)
                    pA, pAd = k.ps()
                    for j in range(4):
                        k.mm(pA[:, j * 128:(j + 1) * 128], kT[hs, kts[j] * 128:(kts[j] + 1) * 128], qT[hs, qs], True, True,
                             [d_kT, d_qT], [pAd])
                    pB, pBd = k.ps()
                    c_ = 0
                    if nl == 5:
                        k.mm(pB[:, 0:128], kT[hs, kts[4] * 128:(kts[4] + 1) * 128], qT[hs, qs], True, True, [d_kT, d_qT], [pBd])
                        c_ = 128
                    for t in range(2):
                        k.mm(pB[:, c_ + t * 128:c_ + (t + 1) * 128], kcT[hs, t * 128:(t + 1) * 128], qT[hs, qs], True, True,
                             [d_kcT, d_qT], [pBd])
                    return (pA, pAd, pB, pBd, c_)

                pend = emit_scores(*its[0])
                po, pod = None, None
                for ii, (m_, h2) in enumerate(its):
                    cur_sc = pend
                    if ii + 1 < len(its):
                        pend = emit_scores(*its[ii + 1])
                    pA, pAd, pB, pBd, c_ = cur_sc
                    kts, t0 = info(m_)
                    nl = len(kts)
                    qs = slice(m_ * 128, (m_ + 1) * 128)
                    PTf, d_PT = PTb[ii % 2]
                    tmpE, d_tmpE = tEb[ii % 2]
                    tmpE2, d_tmpE2 = tE2b[ii % 2]
                    k.act(tmpE, pA[:, 0:512], AF.Exp, [pAd], [d_tmpE])
                    k.tt(PTf[:, 0:512], tmpE, TABf[:, (h2 * 21 + t0) * 128:(h2 * 21 + t0 + 4) * 128], ALU.mult,
                         [d_tmpE, d_TAB], [d_PT])
                    if nl == 5:
                        k.act(tmpE2, pB[:, 0:128], AF.Exp, [pBd], [d_tmpE2])
                        k.tt(PTf[:, 512:640], tmpE2, TABf[:, (h2 * 21 + 4) * 128:(h2 * 21 + 5) * 128], ALU.mult,
                             [d_tmpE2, d_TAB], [d_PT])
                    k.act(PTf[:, nl * 128:(nl + 2) * 128], pB[:, c_:c_ + 256], AF.Exp, [pBd], [d_PT])
                    if h2 == 0:
                        po, pod = k.ps()
                    c0 = h2 * 66
                    ntile = nl + 2
                    for j in range(ntile):
                        if j < nl:
                            rhs_ = vtok[:, kts[j], h2, :]
                            rd = d_vt
                        else:
                            rhs_ = vctx[:, j - nl, h2, :]
                            rd = d_vc
                        k.mm(po[:, c0:c0 + 66], PTf[:, j * 128:(j + 1) * 128], rhs_, j == 0, j == ntile - 1, [d_PT, rd], [pod])
                    if h2 == 1:
                        for hh in range(2):
                            cc0 = hh * 66
                            k.rcp(rden[:, hh:hh + 1], po[:, cc0 + 64:cc0 + 65], [pod], [d_rden])
                            k.ts(otok[:, hh * 64:(hh + 1) * 64], po[:, cc0:cc0 + 64], rden[:, hh:hh + 1], None, ALU.mult, None,
                                 [pod, d_rden], [d_otok])
                        pb, pd = k.ps()
                        k.tr(pb[:, 0:128], otok, ident[:], [d_otok, d_ident], [pd])
                        k.tt(yT, pb[:, 0:128], zs[:, qs], ALU.mult, [pd, d_zs], [d_yT])
                        P.dma(k.q(), yscr_v[4 + hp, :, qs], yT, reads=[d_yT], writes=[d_yscr[4 + hp]], r32=True)

        for l in range(depth):
            last = (l == depth - 1)
            fz = fence()
            cv = Carve(fz)
            bcol, d_bcol = cv.get(16)
            gpre_col, _d = cv.get(8)
            brow, _d = cv.get(D)
            gpost_row, _d = cv.get(D)
            ggt, d_ggt = cv.get(512)
            wada_f, d_wada = cv.get(8 * 512, F32R)
            wada = wada_f.rearrange("p (c n) -> p c n", c=8)
            sbc_f, d_sbc = cv.get(8 * 2 * 128, F32R)
            siluc_bc = sbc_f.rearrange("p (c j n) -> p c j n", c=8, j=2)
            for kc in range(8):
                for j in range(2):
                    k.cp(siluc_bc[:, kc, j, :], silucT[:, kc, j:j + 1].bitcast(F32).to_broadcast([128, 128]), [d_siluc], [d_sbc])
            P.dma("sp", bcol, b_ada.ap()[l, 0:2048].rearrange("(c p) -> p c", p=128), writes=[d_bcol], slow=True)
            P.dma("act", gpre_col, g_pre.ap()[l].rearrange("(c p) -> p c", p=128), writes=[d_bcol], slow=True)
            P.dma("sp", brow, b_ada.ap()[l, 2048:3072].partition_broadcast(128), writes=[d_bcol])
            P.dma("act", gpost_row, g_post.ap()[l].partition_broadcast(128), writes=[d_bcol])
            for blk in range(6):
                P.dma(k.q(), wada, w_ada.ap()[l, :, blk * 512:(blk + 1) * 512].rearrange("(c p) n -> p c n", p=128),
                      writes=[d_wada], r32=True)
                if blk < 4:
                    pb, pd = k.ps()
                    for cc in range(4):
                        for kc in range(8):
                            k.mm(pb[:, cc * 2:cc * 2 + 2], wada[:, kc, cc * 128:(cc + 1) * 128], silucT[:, kc, :],
                                 kc == 0, kc == 7, [d_wada, d_siluc], [pd])
                    k.tt(modcol[:, blk * 4:blk * 4 + 4, :], pb[:, 0:8].rearrange("p (c j) -> p c j", j=2),
                         bcol[:, blk * 4:blk * 4 + 4].unsqueeze(2).to_broadcast([128, 4, 2]), ALU.add,
                         [pd, d_bcol], [d_modcol])
                else:
                    for j in range(2):
                        pb, pd = k.ps()
                        for kc in range(8):
                            k.mm(pb[:], siluc_bc[:, kc, j, :], wada[:, kc, :], kc == 0, kc == 7, [d_wada, d_sbc], [pd])
                        c0 = (blk - 4) * 512
                        k.tt(ggt[:, 0:512], pb[:], brow[:, c0:c0 + 512], ALU.add, [pd, d_bcol], [d_ggt])
                        k.tt(ggt[:, 0:512], ggt[:, 0:512], gpost_row[:, c0:c0 + 512], ALU.mult, [d_ggt, d_bcol], [d_ggt])
                        P.dma("sp", ggscr.ap()[j, :, c0:c0 + 512], ggt[:, 0:512], reads=[d_ggt], writes=[d_gg])
            for j in range(2):
                k.stt(s1col[:, :, j], modcol[:, 8:16, j], 1.0, gpre_col, ALU.add, ALU.mult, [d_modcol, d_bcol], [d_modcol])

            def xio(kind, si):
                if l == 0:
                    xin = x_p.ap()[si] if kind == "p" else x_s.ap()
                    d_xin = d_x0
                else:
                    xin = xs_p[(l - 1) % 2].ap()[si] if kind == "p" else xs_s[(l - 1) % 2].ap()
                    d_xin = d_xs_p[(l - 1) % 2][si] if kind == "p" else d_xs_s[(l - 1) % 2]
                if last:
                    xout = y_p.ap()[si] if kind == "p" else y_s.ap()
                    d_xout = d_y0
                else:
                    xout = xs_p[l % 2].ap()[si] if kind == "p" else xs_s[l % 2].ap()
                    d_xout = d_xs_p[l % 2][si] if kind == "p" else d_xs_s[l % 2]
                return xin, d_xin, xout, d_xout

            d_x0 = Dep()
            d_y0 = Dep()
            for grp in ("p", "s"):
              for (kind, si, T) in [q_ for q_ in seqs if q_[0] == grp]:
                cj = 0 if kind == "p" else 1
                TT = min(T, 512)
                NTL = T // TT
                xin, d_xin, xout, d_xout = xio(kind, si)
                ho = si * SEQ if kind == "p" else 0
                hTv = hT[:, :, ho:ho + T]
                yscr_v = yscr.ap()[:, :, ho:ho + T]

                fz = fence()
                cv = Carve(fz)
                _x0, _d0 = cv.get(D)
                _x1, _d1 = cv.get(D)
                xt = [_x0, _x1]
                d_xt = [_d0, _d1]
                xn, d_xn = cv.get(D)
                for i in range(T // 128):
                    b = i % 2
                    P.dma(k.q(), xt[b], xin[i * 128:(i + 1) * 128, :], reads=[d_xin], writes=[d_xt[b]])
                    k.act(xn, xt[b], AF.Square, [d_xt[b]], [d_xn, d_small], accum=small[:, 0:1])
                    k.act(small[:, 1:2], small[:, 0:1], AF.Sqrt, [d_small], [d_small], bias=EPS, scale=1.0 / D)
                    k.rcp(small[:, 2:3], small[:, 1:2], [d_small], [d_small])
                    k.ts(xn, xt[b], small[:, 2:3], None, ALU.mult, None, [d_xt[b], d_small], [d_xn])
                    for half in range(2):
                        pb, pd = k.ps()
                        for c4 in range(4):
                            kc = half * 4 + c4
                            k.tr(pb[:, c4 * 128:(c4 + 1) * 128], xn[:, kc * 128:(kc + 1) * 128], ident[:], [d_xn, d_ident], [pd])
                        for c4 in range(4):
                            kc = half * 4 + c4
                            k.ts(hTv[:, kc, i * 128:(i + 1) * 128], pb[:, c4 * 128:(c4 + 1) * 128],
                                 s1col[:, kc, cj:cj + 1], modcol[:, kc, cj:cj + 1], ALU.mult, ALU.add,
                                 [pd, d_modcol], [d_hT], eng=("dve" if c4 % 2 == 0 else "pool") if False else "dve")

                fz = fence()
                cv = Carve(fz)
                zbuf, d_z = cv.get(TT, F32R)
                zsrc, d_zs0 = cv.get(TT)
                k.ms(zsrc, 0.0, [d_zs0])
                k.cp(zbuf, zsrc, [d_zs0], [d_z])
                for ch in range(8):
                    if (ch < 4 and not do_dn) or (ch >= 4 and not do_na):
                        for t in range(NTL):
                            P.dma(k.q(), yscr_v[ch, :, t * TT:(t + 1) * TT], zbuf, reads=[d_z], writes=[d_yscr[ch]], r32=True)

              if True:
                kind = grp
                cj = 0 if kind == "p" else 1
                T = NPS * SEQ if kind == "p" else DSEQ
                TT = 512
                NTL = T // TT
                hTv = hT[:, :, 0:T]
                yscr_v = yscr.ap()[:, :, 0:T]
                gscr_v = gscr.ap()[:, :, 0:T]
                mscr_v = mscr.ap()[:, :, 0:T]

                def xrows(sub):
                    if kind == "p":
                        xi, dxi, xo_, dxo = xio("p", sub // 2)
                        rr = (sub % 2) * 128
                    else:
                        xi, dxi, xo_, dxo = xio("s", 0)
                        rr = sub * 128
                    return xi[rr:rr + 128, :], dxi, xo_[rr:rr + 128, :], dxo

                L = SEQ if kind == "p" else DSEQ
                nseq = T // L
                si = None

                if do_dn:
                    for h in range(4):
                        dn_unit(l, kind, si, T, h, L)

                if do_na:
                    for hp in range(4):
                        na_unit(l, kind, si, T, hp)

                for g in range(4):
                    win = POOL_WINDOWS[g]
                    fz = fence()
                    cv = Carve(fz)
                    wu, _dl = cv.getw(0)
                    d_wu = _dl[0]
                    wz, _dl = cv.getw(1)
                    d_wz = _dl[0]
                    pw, d_pw = cv.get(128, F32R)
                    psc, d_psc = cv.get(1)
                    LP = L + 16
                    U, d_U = cv.get(nseq * LP)
                    zs, d_zs = cv.get(T)
                    s_a, d_sa = cv.get(nseq * LP)
                    s_b, d_sb = cv.get(nseq * LP)
                    icnt, d_icnt = cv.get(L)
                    pooled, d_pooled = cv.get(T, F32R)
                    yT, d_yT = cv.get(TT, F32R)
                    U3 = U.rearrange("p (s n) -> p s n", s=nseq)
                    wu3 = wu.rearrange("p (c n) -> p c n", c=8)
                    wz3 = wz.rearrange("p (c n) -> p c n", c=8)
                    cu = OFF_PL_U + g * 128
                    cz = OFF_PL_Z + g * 128
                    P.dma("sp", wu3, w_in.ap()[l, :, cu:cu + 128].rearrange("(c p) n -> p c n", p=128), writes=[d_wu], r32=True)
                    P.dma("act", wz3, w_in.ap()[l, :, cz:cz + 128].rearrange("(c p) n -> p c n", p=128), writes=[d_wz], r32=True)
                    P.dma("sp", pw, pool_w.ap()[l, g], writes=[d_pw], r32=True)
                    P.dma("act", psc, pool_scale.ap()[l, g * 128:(g + 1) * 128].rearrange("(p o) -> p o", o=1), writes=[d_psc], slow=True)
                    ic_src = (invcnt_p if kind == "p" else invcnt_s).ap()[g]
                    P.dma("sp", icnt, ic_src.partition_broadcast(128), writes=[d_icnt])
                    k.ms(U3[:, :, 0:8], 0.0, [d_U])
                    k.ms(U3[:, :, L + 8:L + 16], 0.0, [d_U])
                    for t in range(NTL):
                        pb, pd = k.ps()
                        for kc in range(8):
                            k.mm(pb[:, 0:TT], wu3[:, kc, :], hTv[:, kc, t * TT:(t + 1) * TT], kc == 0, kc == 7, [d_wu, d_hT], [pd])
                        if L >= TT:
                            k.cp(U[:, 8 + t * TT:8 + (t + 1) * TT], pb[:, 0:TT], [pd], [d_U])
                        else:
                            spt = TT // L
                            k.cp(U3[:, t * spt:(t + 1) * spt, 8:8 + L], pb[:, 0:TT].rearrange("p (s n) -> p s n", s=spt), [pd], [d_U])
                        pb, pd = k.ps()
                        for kc in range(8):
                            k.mm(pb[:, 0:TT], wz3[:, kc, :], hTv[:, kc, t * TT:(t + 1) * TT], kc == 0, kc == 7, [d_wz, d_hT], [pd])
                        k.act(zs[:, t * TT:(t + 1) * TT], pb[:, 0:TT], AF.Silu, [pd], [d_zs])
                    cur, dcur, curlen = U, d_U, nseq * LP
                    step = 1
                    bufs = [(s_a, d_sa), (s_b, d_sb)]
                    bi = 0
                    while step < win:
                        nb, dnb = bufs[bi]
                        bi ^= 1
                        nlen = curlen - step
                        k.tt(nb[:, 0:nlen], cur[:, 0:nlen], cur[:, step:step + nlen], ALU.add, [dcur], [dnb])
                        cur, dcur, curlen = nb, dnb, nlen
                        step *= 2
                    o0 = 8 - win // 2
                    nb, dnb = bufs[bi]
                    nb3 = nb[:, 0:T].rearrange("p (s n) -> p s n", s=nseq)
                    k.tt(nb3, cur[:, 0:nseq * LP].rearrange("p (s n) -> p s n", s=nseq)[:, :, o0:o0 + L],
                         icnt.unsqueeze(1).to_broadcast([128, nseq, L]), ALU.mult, [dcur, d_icnt], [dnb])
                    k.tt(pooled.rearrange("p (s n) -> p s n", s=nseq), nb3, U3[:, :, 8:8 + L], ALU.subtract, [dnb, d_U], [d_pooled])
                    for t in range(NTL):
                        pb, pd = k.ps()
                        k.mm(pb[:, 0:TT], pw, pooled[:, t * TT:(t + 1) * TT], True, True, [d_pw, d_pooled], [pd])
                        k.stt(yT, pb[:, 0:TT], psc[:, 0:1], zs[:, t * TT:(t + 1) * TT], ALU.mult, ALU.mult, [pd, d_psc, d_zs], [d_yT])
                        P.dma(k.q(), yscr_v[8 + g, :, t * TT:(t + 1) * TT], yT, reads=[d_yT], writes=[d_yscr[8 + g]], r32=True)

                for u in range(6):
                    fz = fence()
                    cv = Carve(fz)
                    wgu_f, d_wgl = cv.getw(0, 4)
                    wgu = wgu_f.rearrange("p (c n) -> p c n", c=8)
                    c0 = OFF_GATE + u * 512
                    P.dma(k.q(), wgu, w_in.ap()[l, :, c0:c0 + 512].rearrange("(c p) n -> p c n", p=128), writes=d_wgl, r32=True)
                    gsbs = [cv.get(TT) for _ in range(2)]
                    gi = 0
                    for t in range(NTL):
                        for c4 in range(4):
                            gsb, d_gsb = gsbs[gi % 2]
                            gi += 1
                            pg_, pgd_ = k.ps()
                            for kc in range(8):
                                k.mm(pg_[:, 0:TT], wgu[:, kc, c4 * 128:(c4 + 1) * 128], hTv[:, kc, t * TT:(t + 1) * TT], kc == 0, kc == 7,
                                     list(d_wgl) + [d_hT], [pgd_])
                            k.act(gsb, pg_[:, 0:TT], AF.Sigmoid, [pgd_], [d_gsb])
                            P.dma(k.q(), gscr_v[u * 4 + c4, :, t * TT:(t + 1) * TT], gsb, reads=[d_gsb], writes=[d_gscr[u * 4 + c4]])

                fz = fence()
                cv = Carve(fz)
                ysb_l = []
                wbr_l = []
                gin_l = []
                mo_l = []
                _wa, _wd = cv.get(4 * D, F32R)
                wbr_single = (_wa.rearrange("p (c n) -> p c n", c=4), _wd)
                for _i in range(2):
                    _a, _d = cv.get(4 * TT, F32R)
                    ysb_l.append((_a.rearrange("p (c n) -> p c n", c=4), _d))
                    wbr_l.append(wbr_single)
                    _a, _d = cv.get(8 * TT)
                    gin_l.append((_a.rearrange("p (c n) -> p c n", c=8), _d))
                    mo_l.append(cv.get(TT, F32R))
                accf, d_acc = cv.get(8 * TT)
                acc3 = accf.rearrange("p (c n) -> p c n", c=8)
                tmp, d_tmp = cv.get(TT)
                bi = 0
                mi = 0
                for t in range(NTL):
                    tsl_ = slice(t * TT, (t + 1) * TT)
                    for br in range(3):
                        ysb3, d_ysb = ysb_l[bi % 2]
                        wbr3, d_wbr = wbr_l[bi % 2]
                        gin3, d_gin = gin_l[bi % 2]
                        bi += 1
                        P.dma("sp", ysb3, yscr_v[br * 4:(br + 1) * 4, :, tsl_].rearrange("c p n -> p c n"),
                              reads=list(d_yscr[br * 4:(br + 1) * 4]), writes=[d_ysb], r32=True)
                        P.dma("act", gin3, gscr_v[br * 8:(br + 1) * 8, :, tsl_].rearrange("c p n -> p c n"),
                              reads=list(d_gscr[br * 8:(br + 1) * 8]), writes=[d_gin])
                        P.dma("sp", wbr3, w_br[br].ap()[l].rearrange("(c p) n -> p c n", p=128), writes=[d_wbr], r32=True)
                        for dc in range(8):
                            pa, pad = k.ps()
                            for wc in range(4):
                                k.mm(pa[:, 0:TT], wbr3[:, wc, dc * 128:(dc + 1) * 128], ysb3[:, wc, :], wc == 0, wc == 3, [d_wbr, d_ysb], [pad])
                            if br == 0:
                                k.tt(acc3[:, dc, :], gin3[:, dc, :], pa[:, 0:TT], ALU.mult, [d_gin, pad], [d_acc])
                            elif br == 1:
                                k.tt(tmp, gin3[:, dc, :], pa[:, 0:TT], ALU.mult, [d_gin, pad], [d_tmp])
                                k.tt(acc3[:, dc, :], acc3[:, dc, :], tmp, ALU.add, [d_acc, d_tmp], [d_acc], eng="pool")
                            else:
                                mo, d_mo = mo_l[mi % 2]
                                mi += 1
                                k.tt(tmp, gin3[:, dc, :], pa[:, 0:TT], ALU.mult, [d_gin, pad], [d_tmp])
                                k.tt(mo, acc3[:, dc, :], tmp, ALU.add, [d_acc, d_tmp], [d_mo], eng="pool")
                                P.dma("act", mscr_v[dc, :, tsl_], mo, reads=[d_mo], writes=[d_mscr], r32=True)

                fz = fence()
                cv = Carve(fz)
                wo, d_wo = cv.get(8 * D, F32R)
                wo3 = wo.rearrange("p (c n) -> p c n", c=8)
                mt_l = []
                for _i in range(2):
                    _a, _d = cv.get(8 * 128, F32R)
                    mt_l.append((_a.rearrange("p (c n) -> p c n", c=8), _d))
                xr_l = [cv.get(D) for _ in range(2)]
                xo_l = [cv.get(D) for _ in range(2)]
                xn, d_xn = cv.get(512)
                ggr, d_ggr = cv.get(D)
                P.dma("act", ggr, ggscr.ap()[cj], reads=[d_gg], writes=[d_ggr])
                P.dma("sp", wo3, w_out.ap()[l].rearrange("(c p) n -> p c n", p=128), writes=[d_wo], r32=True)
                for sub in range(T // 128):
                    r0 = sub * 128
                    mt3, d_mt = mt_l[sub % 2]
                    xr, d_xr = xr_l[sub % 2]
                    xo, d_xo = xo_l[sub % 2]
                    P.dma("sp", mt3, mscr_v[:, :, r0:r0 + 128].rearrange("c p n -> p c n"), reads=[d_mscr], writes=[d_mt], r32=True)
                    xi_rows, d_xin, xo_rows, d_xout = xrows(sub)
                    P.dma("act", xr, xi_rows, reads=[d_xin], writes=[d_xr])
                    pos = []
                    for half in range(2):
                        po, pod = k.ps()
                        for kc in range(8):
                            k.mm(po[:], mt3[:, kc, :], wo3[:, kc, half * 512:(half + 1) * 512], kc == 0, kc == 7, [d_mt, d_wo], [pod])
                        k.act(xn[:, 0:512], po[:], AF.Square, [pod], [d_xn, d_small], accum=small[:, 4 + half:5 + half])
                        pos.append((po, pod))
                    k.tt(small[:, 6:7], small[:, 4:5], small[:, 5:6], ALU.add, [d_small], [d_small])
                    k.act(small[:, 7:8], small[:, 6:7], AF.Sqrt, [d_small], [d_small], bias=EPS, scale=1.0 / D)
                    k.rcp(small[:, 8:9], small[:, 7:8], [d_small], [d_small])
                    for half in range(2):
                        po, pod = pos[half]
                        hs = slice(half * 512, (half + 1) * 512)
                        k.stt(xo[:, hs], po[:], small[:, 8:9], ggr[:, hs], ALU.mult, ALU.mult, [pod, d_small, d_ggr], [d_xo])
                        k.tt(xo[:, hs], xo[:, hs], xr[:, hs], ALU.add, [d_xo, d_xr], [d_xo], eng="pool")
                    P.dma("sp", xo_rows, xo, reads=[d_xo], writes=[d_xout])
        P.emit()
    return nc, k


_CACHE = {}


def _consts():
    def invcnt(T):
        out = np.zeros((4, T), np.float32)
        pos = np.arange(T)
        for gi, win in enumerate(POOL_WINDOWS):
            lo = np.maximum(pos - win // 2, 0)
            hi = np.minimum(pos + win // 2 - 1, T - 1)
            out[gi] = 1.0 / (hi - lo + 1).astype(np.float32)
        return out

    t = np.arange(128)
    same = (t[:, None] // 64) == (t[None, :] // 64)
    cst = np.zeros((128, NCST), np.float32)
    cst[:, 0:128] = same & (t[:, None] <= t[None, :])
    cst[:, 128:256] = same & (t[:, None] >= t[None, :])
    cst[:, 256:384] = same
    cst[:, 384:512] = 1.0
    f = np.arange(64)
    pm = t % 64
    cst[:, 512:576] = (pm[:, None] == f[None, :])
    cst[:, 576] = (pm == 63)
    cst[:, 577] = (pm == 0)
    cst[:, 578] = (t == 63)
    cst[:, 579] = (t == 127)
    cst[:, 580] = (t == 0)
    cst[:, 581] = (t == 64)
    P_ = pm[:, None]
    F_ = f[None, :]
    valid = [[F_ >= P_, F_ <= P_], [F_ > P_, F_ < P_], [F_ < P_, F_ > P_]]
    for ty in range(3):
        for d_ in range(2):
            sign = 1.0 if ty == 2 else -1.0
            cst[:, 582 + (ty * 2 + d_) * 64: 582 + (ty * 2 + d_ + 1) * 64] = np.where(valid[ty][d_], 0.0, sign * BIG)
    cq = np.arange(64)
    csq = np.clip(cq - 8, 0, 48)
    cm = (f[:, None] >= csq[None, :]) & (f[:, None] < csq[None, :] + 16)
    cst[:, 582 + 384:582 + 384 + 64] = np.concatenate([cm, cm], axis=0)
    Pf = t[:, None]
    Ff = t[None, :]
    validb = [[Ff >= Pf, Ff <= Pf], [Ff > Pf, Ff < Pf], [Ff < Pf, Ff > Pf]]
    for ty in range(3):
        for d_ in range(2):
            sign = 1.0 if ty == 2 else -1.0
            c0 = 1030 + (ty * 2 + d_) * 128
            cst[:, c0:c0 + 128] = np.where(validb[ty][d_] & same, 0.0, sign * BIG)
    return {"invcnt_p": invcnt(SEQ), "invcnt_s": invcnt(DSEQ), "ident": np.eye(128, dtype=np.float32), "dncst": cst}


def kernel(x_prompt, x_sample, c, cache_k_na, cache_v_na, state_dn, c_ctx, w_ada, b_ada, g_pre, g_post,
           w_in, conv_dn, a_log_dn, dt_bias_dn, g_norm_dn, na_bias, pool_w, pool_scale,
           w_br_dn, w_br_na, w_br_pl, w_out, _depth=DEPTH, _dn=True, _na=True):
    f = lambda a: np.ascontiguousarray(np.asarray(a, dtype=np.float32))
    key = (_depth, _dn, _na)
    if key not in _CACHE:
        _CACHE[key] = build_program(_depth, _dn, _na)
    nc, k = _CACHE[key]
    cs = _consts()
    dd = _depth
    shared = {"w_ada": f(w_ada[:dd]), "b_ada": f(b_ada[:dd]), "g_pre": f(g_pre[:dd]), "g_post": f(g_post[:dd]), "w_in": f(w_in[:dd]),
              "pool_w": f(pool_w[:dd]), "pool_scale": f(pool_scale[:dd]), "w_br_dn": f(w_br_dn[:dd]), "w_br_na": f(w_br_na[:dd]),
              "w_br_pl": f(w_br_pl[:dd]), "w_out": f(w_out[:dd]), "conv_dn": f(conv_dn[:dd]),
              "a_log": f(a_log_dn[:dd]).reshape(dd, 8), "dt_bias": f(dt_bias_dn[:dd]).reshape(dd, 8), "g_norm": f(g_norm_dn[:dd])}
    shared.update(cs)
    rpad = np.zeros((dd, 8, 15, 127), np.float32)
    rpad[..., 48:79] = f(na_bias[:dd])[..., ::-1]
    x_prompt = f(x_prompt)
    x_sample = f(x_sample)
    in_maps = []
    for core in range(8):
        b = core // 4
        m = dict(shared)
        m["x_p"] = x_prompt[core * NPS:(core + 1) * NPS]
        m["x_s"] = x_sample[b]
        m["cvec"] = np.stack([f(c_ctx), f(c)[b]])
        m["sdn"] = f(state_dn[b, :dd])
        m["ck"] = f(cache_k_na[b, :dd]).reshape(dd, 256, 512)
        m["cvv"] = f(cache_v_na[b, :dd]).reshape(dd, 256, 512)
        m["rpad"] = rpad
        in_maps.append({n: m[n] for n in k.din})
    res = run_bass_kernel_spmd(nc, in_maps, core_ids=list(range(8)))
    r = res.results
    y_p = np.concatenate([r[i]["y_p"] for i in range(8)], axis=0)
    y_s = np.stack([r[0]["y_s"], r[4]["y_s"]])
    n_k = np.concatenate([r[i]["nk"] for i in range(8)], axis=0).reshape(32, dd, SEQ, 8, 64)
    n_v = np.concatenate([r[i]["nv"] for i in range(8)], axis=0).reshape(32, dd, SEQ, 8, 64)
    n_s = np.concatenate([r[i]["nst"] for i in range(8)], axis=0)
    return y_p, y_s, n_k, n_v, n_s
```

```python
import contextlib
import numpy as np
import concourse.bass as bass
import concourse.mybir as mybir
from concourse.ap import AP
from concourse.bass_utils import run_bass_kernel_spmd

F32 = mybir.dt.float32
F32R = mybir.dt.float32r
ALU = mybir.AluOpType
AF = mybir.ActivationFunctionType

ENGS = ("pe", "act", "dve", "pool", "sp")
NDSEM = 12

D = 1024
DEPTH = 4
SEQ = 256
DSEQ = 2048
NIN = 8208
OFF_DN_Z = 1536
OFF_DN_BETA = 2048
OFF_DN_A = 2056
OFF_NA_Q = 2064
OFF_NA_K = 2576
OFF_NA_V = 3088
OFF_NA_Z = 3600
OFF_PL_U = 4112
OFF_PL_Z = 4624
OFF_GATE = 5136
EPS = 1e-6
POOL_WINDOWS = (2, 4, 8, 16)
NPS = 4
NCST = 582 + 6 * 64 + 64 + 6 * 128
BIG = 30000.0


class Dep:
    __slots__ = ("w", "r", "excl")

    def __init__(self, w=None, excl=False):
        self.w = w
        self.r = []
        self.excl = excl


class Op:
    __slots__ = ("eng", "fn", "waits", "signal", "count", "is_dma", "dsem", "dval", "dprev")

    def __init__(self, eng, fn, is_dma=False):
        self.eng = eng
        self.fn = fn
        self.waits = []
        self.signal = False
        self.count = None
        self.is_dma = is_dma
        self.dsem = None
        self.dval = None
        self.dprev = None


class Prog:
    def __init__(self, nc):
        self.nc = nc
        self.ops = {e: [] for e in ENGS}
        self.ndma = {e: 0 for e in ENGS}
        self.dtot = {e: [0] * NDSEM for e in ENGS}
        self.nops = 0

    def _mk(self, eng, fn, reads, writes, is_dma):
        o = Op(eng, fn, is_dma)
        ex = [t for t in reads if t.excl]
        if ex:
            reads = [t for t in reads if not t.excl]
            writes = list(writes) + [t for t in ex if t not in writes]
        deps = []
        seen = set()

        def add(d):
            if d is None or id(d) in seen:
                return
            if (not d.is_dma) and d.eng == "pe" and eng == "pe" and not is_dma:
                return
            seen.add(id(d))
            deps.append(d)

        for t in reads:
            add(t.w)
        for t in writes:
            add(t.w)
            for r in t.r:
                add(r)
        o.waits = deps
        for d in deps:
            d.signal = True
        for t in reads:
            if not is_dma:
                t.r = [x for x in t.r if x.is_dma or x.eng != eng]
            t.r.append(o)
        for t in writes:
            t.w = o
            t.r = []
        self.ops[eng].append(o)
        self.nops += 1
        return o

    def op(self, eng, fn, reads=(), writes=()):
        return self._mk(eng, fn, reads, writes, False)

    def dma(self, eng, out, in_, reads=(), writes=(), r32=False, slow=False):
        nc = self.nc
        eng = "pool" if type(out.tensor).__name__.startswith("DRam") else "sp"

        def fn(e):
            kw = {}
            if slow:
                kw["allow_slow_non_contiguous"] = True
            if r32:
                nc.dge_precook = False
            ins = e.dma_start(out=out, in_=in_, **kw)
            if r32:
                nc.dge_precook = True
            return ins

        o = self._mk(eng, fn, reads, writes, True)
        i = self.ndma[eng] % NDSEM
        self.ndma[eng] += 1
        o.dsem = i
        o.dprev = self.dtot[eng][i]
        self.dtot[eng][i] += 16
        o.dval = self.dtot[eng][i]
        o.signal = True
        return o

    def emit(self):
        nc = self.nc
        for e in ENGS:
            c = 0
            for o in self.ops[e]:
                if not o.is_dma and o.signal:
                    c += 1
                    o.count = c
        nsig = {e: sum(1 for o in self.ops[e] if (not o.is_dma and o.signal)) for e in ENGS}
        with contextlib.ExitStack() as st:
            esem = {e: st.enter_context(nc.semaphore("s_" + e)) for e in ENGS}
            dsem = {
                e: [st.enter_context(nc.semaphore("d_%s_%d" % (e, i))) for i in range(NDSEM)]
                for e in ("sp", "act", "pool")
            }
            block = st.enter_context(nc.Block())
            ops = self.ops
            dtot = self.dtot

            def run(e, engobj, final=False):
                seen_e = {x: 0 for x in ENGS}
                seen_d = {}
                for o in ops[e]:
                    for d in o.waits:
                        if d.is_dma:
                            key = (d.eng, d.dsem)
                            if seen_d.get(key, 0) >= d.dval:
                                continue
                            engobj.wait_ge(dsem[d.eng][d.dsem], d.dval)
                            seen_d[key] = d.dval
                        else:
                            if seen_e[d.eng] >= d.count:
                                continue
                            engobj.wait_ge(esem[d.eng], d.count)
                            seen_e[d.eng] = d.count
                    if o.is_dma:
                        key = (e, o.dsem)
                        if o.dprev > 0 and seen_d.get(key, 0) < o.dprev:
                            engobj.wait_ge(dsem[e][o.dsem], o.dprev)
                            seen_d[key] = o.dprev
                        ins = o.fn(engobj)
                        ins.then_inc(dsem[e][o.dsem], 16)
                    else:
                        ins = o.fn(engobj)
                        if o.signal:
                            ins.then_inc(esem[e], 1)
                if final:
                    for x in ENGS:
                        if x != e and nsig[x] > 0:
                            engobj.wait_ge(esem[x], nsig[x])
                    for q in ("sp", "act", "pool"):
                        for i in range(NDSEM):
                            if dtot[q][i] > 0:
                                engobj.wait_ge(dsem[q][i], dtot[q][i])

            @block.tensor
            def _(eng):
                run("pe", eng)

            @block.vector
            def _(eng):
                run("dve", eng)

            @block.scalar
            def _(eng):
                run("act", eng)

            @block.gpsimd
            def _(eng):
                run("pool", eng)

            @block.sync
            def _(eng):
                run("sp", eng, final=True)


class K:
    def __init__(self, nc, st):
        self.nc = nc
        self.st = st
        self.P = Prog(nc)
        self.din = {}
        self.dout = {}
        self.psb = [st.enter_context(nc.psum_tensor("psb%d" % i, [128, 512], F32)) for i in range(8)]
        self.psd = [Dep(excl=True) for _ in range(8)]
        self.psi = 0
        self.dq = 0

    def inp(self, name, shape, dt=F32):
        t = self.nc.dram_tensor(name, list(shape), dt, kind="ExternalInput")
        self.din[name] = t
        return t

    def outp(self, name, shape):
        t = self.nc.dram_tensor(name, list(shape), F32, kind="ExternalOutput")
        self.dout[name] = t
        return t

    def scr(self, name, shape, dt=F32):
        return self.nc.dram_tensor(name, list(shape), dt, kind="Internal")

    def sb(self, name, shape, dt=F32):
        return self.st.enter_context(self.nc.sbuf_tensor(name, list(shape), dt))

    def ps(self):
        i = self.psi
        self.psi = (i + 1) % 8
        return self.psb[i], self.psd[i]

    def q(self):
        self.dq ^= 1
        return "sp" if self.dq else "act"

    def mm(self, out, lhsT, rhs, start, stop, reads, writes):
        self.P.op("pe", lambda e: e.matmul(out, lhsT=lhsT, rhs=rhs, start=start, stop=stop), reads, writes)

    def tr(self, out, in_, ident, reads, writes):
        self.P.op("pe", lambda e: e.transpose(out, in_, ident), reads, writes)

    def act(self, out, in_, func, reads, writes, bias=None, scale=1.0, accum=None):
        def fn(e):
            kw = {}
            if bias is not None:
                kw["bias"] = bias
            if accum is not None:
                kw["accum_out"] = accum
            return e.activation(out=out, in_=in_, func=func, scale=scale, **kw)

        self.P.op("act", fn, reads, writes)

    def tt(self, out, in0, in1, op, reads, writes, eng="dve"):
        self.P.op(eng, lambda e: e.tensor_tensor(out=out, in0=in0, in1=in1, op=op), reads, writes)

    def ts(self, out, in0, s1, s2, op0, op1, reads, writes, eng="dve"):
        if s2 is None:
            self.P.op(eng, lambda e: e.tensor_scalar(out=out, in0=in0, scalar1=s1, scalar2=None, op0=op0), reads, writes)
        else:
            self.P.op(eng, lambda e: e.tensor_scalar(out=out, in0=in0, scalar1=s1, scalar2=s2, op0=op0, op1=op1), reads, writes)

    def stt(self, out, in0, scalar, in1, op0, op1, reads, writes, eng="dve"):
        self.P.op(eng, lambda e: e.scalar_tensor_tensor(out=out, in0=in0, scalar=scalar, in1=in1, op0=op0, op1=op1),
                  reads, writes)

    def rcp(self, out, in_, reads, writes):
        self.P.op("dve", lambda e: e.reciprocal(out=out, in_=in_), reads, writes)

    def cp(self, out, in_, reads, writes, eng="dve"):
        self.P.op(eng, lambda e: e.tensor_copy(out=out, in_=in_), reads, writes)

    def ms(self, ap, val, writes, eng="pool"):
        self.P.op(eng, lambda e: e.memset(ap, val), (), writes)


def build_program(depth=DEPTH, do_dn=True, do_na=True):
    nc = bass.Bass("TRN2", target_bir_lowering=False)
    st = contextlib.ExitStack()
    with st:
        k = K(nc, st)
        P = k.P
        x_p = k.inp("x_p", [NPS, SEQ, D])
        x_s = k.inp("x_s", [DSEQ, D])
        cvec = k.inp("cvec", [2, D])
        w_ada = k.inp("w_ada", [depth, D, 3 * D], F32R)
        b_ada = k.inp("b_ada", [depth, 3 * D])
        g_pre = k.inp("g_pre", [depth, D])
        g_post = k.inp("g_post", [depth, D])
        w_in = k.inp("w_in", [depth, D, NIN], F32R)
        pool_w = k.inp("pool_w", [depth, 4, 128, 128], F32R)
        pool_scale = k.inp("pool_scale", [depth, 512])
        w_br = [k.inp(n, [depth, 512, D], F32R) for n in ("w_br_dn", "w_br_na", "w_br_pl")]
        w_out = k.inp("w_out", [depth, D, D], F32R)
        invcnt_p = k.inp("invcnt_p", [4, SEQ])
        invcnt_s = k.inp("invcnt_s", [4, DSEQ])
        ident_in = k.inp("ident", [128, 128])
        conv_dn = k.inp("conv_dn", [depth, 3, 1536])
        a_log = k.inp("a_log", [depth, 8])
        dt_bias = k.inp("dt_bias", [depth, 8])
        g_norm = k.inp("g_norm", [depth, 128])
        sdn = k.inp("sdn", [depth, 2, 4, 128, 128])
        dncst_in = k.inp("dncst", [128, NCST])
        ck_in = k.inp("ck", [depth, 256, 512])
        cv_in = k.inp("cvv", [depth, 256, 512], F32R)
        rpad_in = k.inp("rpad", [depth, 8, 15, 127])

        y_p = k.outp("y_p", [NPS, SEQ, D])
        y_s = k.outp("y_s", [DSEQ, D])
        nk = k.outp("nk", [NPS, depth, SEQ, 512])
        nv = k.outp("nv", [NPS, depth, SEQ, 512])
        d_nkv = Dep()
        nst = k.outp("nst", [NPS, depth, 2, 4, 128, 128])
        d_nst = Dep()

        xs_p = [k.scr("xs_p%d" % i, [NPS, SEQ, D]) for i in range(2)]
        xs_s = [k.scr("xs_s%d" % i, [DSEQ, D]) for i in range(2)]
        yscr = k.scr("yscr", [12, 128, DSEQ], F32R)
        d_xs_p = [[Dep() for _ in range(NPS)] for _ in range(2)]
        d_xs_s = [Dep() for _ in range(2)]
        d_yscr = [Dep() for _ in range(12)]
        gscr = k.scr("gscr", [24, 128, DSEQ])
        d_gscr = [Dep() for _ in range(24)]
        mscr = k.scr("mscr", [8, 128, DSEQ], F32R)
        d_mscr = Dep()

        ident = k.sb("ident_sb", [128, 128])
        d_ident = Dep()
        P.dma("sp", ident[:], ident_in.ap(), writes=[d_ident])
        cst = k.sb("dncst_sb", [128, NCST])
        d_cst = Dep()
        P.dma("act", cst[:], dncst_in.ap(), writes=[d_cst])
        TRI = [cst[:, 0:128], cst[:, 128:256]]
        BLK = cst[:, 256:384]
        ONES = cst[:, 384:512]
        I2 = cst[:, 512:576]
        SEL2 = cst[:, 576:578]
        SELLAST = cst[:, 578:582]
        MASK = [[cst[:, 582 + (ty * 2 + d_) * 64: 582 + (ty * 2 + d_ + 1) * 64] for d_ in range(2)] for ty in range(3)]
        CM = cst[:, 582 + 384:582 + 384 + 64]
        MASKB = [[cst[:, 1030 + (ty * 2 + d_) * 128: 1030 + (ty * 2 + d_ + 1) * 128] for d_ in range(2)] for ty in range(3)]
        hT = k.sb("hT", [128, 8, DSEQ], F32R)
        d_hT = Dep()
        small = k.sb("small", [128, 16])
        d_small = Dep()
        silucT = k.sb("silucT", [128, 8, 2], F32R)
        d_siluc = Dep()
        modcol = k.sb("modcol", [128, 16, 2])
        s1col = k.sb("s1col", [128, 8, 2])
        d_modcol = Dep()
        ggscr = k.scr("ggscr", [2, 128, D])
        d_gg = Dep()
        RSZ = 15 * 1024
        FSZ = 18 * 1024 + 512
        arenaR = k.sb("arenaR", [128, RSZ], F32R)
        arenaF = k.sb("arenaF", [128, FSZ])
        arena_deps = []
        WOFF = RSZ - 4096
        wdeps = [Dep() for _ in range(4)]

        def fence():
            f = P.op("dve", lambda e: e.memset(small[:, 15:16], 0.0), reads=(), writes=list(arena_deps))
            arena_deps.clear()
            return f

        class Carve:
            def __init__(self, seed):
                self.offR = 0
                self.offF = 0
                self.seed = seed

            def getw(self, j, n=1):
                a = arenaR[:, WOFF + j * 1024:WOFF + (j + n) * 1024]
                return a, wdeps[j:j + n]

            def get(self, cols, dt=F32):
                if dt == F32R:
                    a = arenaR[:, self.offR:self.offR + cols]
                    self.offR += cols
                    assert self.offR <= WOFF, self.offR
                else:
                    a = arenaF[:, self.offF:self.offF + cols]
                    self.offF += cols
                    assert self.offF <= FSZ, self.offF
                d = Dep(self.seed)
                arena_deps.append(d)
                return a, d

        craw = k.sb("craw", [128, 8, 2])
        for j in range(2):
            P.dma("sp", craw[:, :, j], cvec.ap()[j].rearrange("(c p) -> p c", p=128), writes=[d_siluc], slow=True)
        k.act(silucT[:], craw[:], AF.Silu, [d_siluc], [d_siluc])

        seqs = [("p", i, SEQ) for i in range(NPS)] + [("s", 0, DSEQ)]
        zscr = k.scr("zscr", [depth, 120, 64, 127])
        d_zscr = [Dep() for _ in range(depth)]
        if do_na:
            for l_ in range(depth):
                P.dma("sp", zscr.ap()[l_], AP(rpad_in, l_ * 120 * 127, [[127, 120], [0, 64], [1, 127]]), writes=[d_zscr[l_]])

        def dn_unit(l, kind, si, T, h, L):
            NT = T // 128
            TT = min(T, 512)
            NTL = T // TT
            nseq = T // L
            tps = L // 128
            LP2 = L + 2
            FL = nseq * LP2

            def tcol(ti):
                return (ti // tps) * LP2 + (ti % tps) * 128

            fz = fence()
            cv = Carve(fz)
            ws = []
            for off in (0, 512, 1024, OFF_DN_Z):
                wa, dwl = cv.getw(len(ws))
                dw = dwl[0]
                w3 = wa.rearrange("p (c n) -> p c n", c=8)
                c0 = off + h * 128
                P.dma(k.q(), w3, w_in.ap()[l, :, c0:c0 + 128].rearrange("(c p) n -> p c n", p=128), writes=[dw], r32=True)
                ws.append((w3, dw))
            (wq, d_wq), (wk, d_wk), (wv, d_wv), (wz, d_wz) = ws
            wba_f, d_wba = cv.get(8 * 4, F32R)
            wba = wba_f.rearrange("p (c n) -> p c n", c=8)
            for j4 in range(4):
                cj4 = OFF_DN_BETA + 4 * j4 + h
                P.dma("sp", wba[:, :, j4], w_in.ap()[l, :, cj4].rearrange("(c p) -> p c", p=128), writes=[d_wba], r32=True, slow=True)
            raw, d_raw = cv.get(FL)
            qf, d_qf = cv.get(FL)
            kf, d_kf = cv.get(FL)
            vf, d_vf = cv.get(FL)
            raw3 = raw.rearrange("p (s n) -> p s n", s=nseq)
            oacc, _ = cv.get(T)
            d_oacc = [Dep(fz) for _ in range(NT)]
            arena_deps.extend(d_oacc)
            tmpb, d_tmpb = cv.get(512)
            cw, d_cw = cv.get(9)
            for idx in range(3):
                c0 = idx * 512 + h * 128
                P.dma("act", cw[:, idx * 3:(idx + 1) * 3], conv_dn.ap()[l, :, c0:c0 + 128].rearrange("t p -> p t"),
                      writes=[d_cw], slow=True)
            k.ms(raw3[:, :, 0:1], 0.0, [d_raw])
            k.ms(raw3[:, :, L + 1:L + 2], 0.0, [d_raw])
            for i in range(NT):
                k.ms(oacc[:, i * 128:(i + 1) * 128], 0.0, [d_oacc[i]])
            fchunks = [(a_, min(512, FL - 2 - a_)) for a_ in range(0, FL - 2, 512)]
            fchunks2 = [(a_, min(512, FL - a_)) for a_ in range(0, FL, 512)]
            dsts = [(qf, d_qf), (kf, d_kf), (vf, d_vf)]
            for idx in range(3):
                w3, dw = ws[idx]
                dst, dd = dsts[idx]
                for t in range(NTL):
                    pb, pd = k.ps()
                    for kc in range(8):
                        k.mm(pb[:, 0:TT], w3[:, kc, :], hTv[:, kc, t * TT:(t + 1) * TT], kc == 0, kc == 7, [dw, d_hT], [pd])
                    if L >= TT:
                        k.cp(raw[:, 1 + t * TT:1 + (t + 1) * TT], pb[:, 0:TT], [pd], [d_raw])
                    else:
                        spt = TT // L
                        k.cp(raw3[:, t * spt:(t + 1) * spt, 1:1 + L], pb[:, 0:TT].rearrange("p (s n) -> p s n", s=spt), [pd], [d_raw])
                for (a, n_) in fchunks:
                    k.ts(dst[:, a:a + n_], raw[:, a:a + n_], cw[:, idx * 3:idx * 3 + 1], None, ALU.mult, None, [d_raw, d_cw], [dd])
                    k.stt(tmpb[:, 0:n_], raw[:, a + 1:a + 1 + n_], cw[:, idx * 3 + 1:idx * 3 + 2], dst[:, a:a + n_],
                          ALU.mult, ALU.add, [d_raw, d_cw, dd], [d_tmpb])
                    k.stt(dst[:, a:a + n_], raw[:, a + 2:a + 2 + n_], cw[:, idx * 3 + 2:idx * 3 + 3], tmpb[:, 0:n_],
                          ALU.mult, ALU.add, [d_raw, d_cw, d_tmpb], [dd])
                    k.act(dst[:, a:a + n_], dst[:, a:a + n_], AF.Silu, [dd], [dd])
                if FL - 2 < FL:
                    k.ms(dst[:, FL - 2:FL], 0.0, [dd])
            for idx in range(2):
                dst, dd = dsts[idx]
                for (a, n_) in fchunks2:
                    k.act(tmpb[:, 0:n_], dst[:, a:a + n_], AF.Square, [dd], [d_tmpb])
                    pb, pd = k.ps()
                    k.mm(pb[:, 0:n_], ONES, tmpb[:, 0:n_], True, True, [d_cst, d_tmpb], [pd])
                    k.act(tmpb[:, 0:n_], pb[:, 0:n_], AF.Sqrt, [pd], [d_tmpb], bias=EPS)
                    k.rcp(tmpb[:, 0:n_], tmpb[:, 0:n_], [d_tmpb], [d_tmpb])
                    if idx == 0:
                        k.stt(dst[:, a:a + n_], dst[:, a:a + n_], 128.0 ** -0.5, tmpb[:, 0:n_], ALU.mult, ALU.mult, [dd, d_tmpb], [dd])
                    else:
                        k.tt(dst[:, a:a + n_], dst[:, a:a + n_], tmpb[:, 0:n_], ALU.mult, [dd, d_tmpb], [dd])
            zs = raw
            d_zs = d_raw
            for t in range(NTL):
                pb, pd = k.ps()
                for kc in range(8):
                    k.mm(pb[:, 0:TT], wz[:, kc, :], hTv[:, kc, t * TT:(t + 1) * TT], kc == 0, kc == 7, [d_wz, d_hT], [pd])
                k.act(zs[:, t * TT:(t + 1) * TT], pb[:, 0:TT], AF.Silu, [pd], [d_zs])
            d_g = Dep(fz)
            arena_deps.append(d_g)
            G_ = [d_g]
            ba, _ = cv.get(NT * 4)
            ba3 = ba.rearrange("p (t n) -> p t n", n=4)
            pb, pd = k.ps()
            for i in range(NT):
                for kc in range(8):
                    k.mm(pb[:, i * 4:(i + 1) * 4], hTv[:, kc, i * 128:(i + 1) * 128], wba[:, kc, :], kc == 0, kc == 7, [d_wba, d_hT], [pd])
            k.cp(ba, pb[:, 0:NT * 4], [pd], G_)
            NK = 10
            GA, _ = cv.get(NT * NK * 2)
            GA4 = GA.rearrange("p (t k d) -> p t k d", k=NK, d=2)
            K_BETA, K_NEGB, K_G, K_GB, K_EG, K_BG, K_EGL, K_GL = range(8)
            gk = lambda kk: GA4[:, :, kk, :]

            def g2():
                a_, _ = cv.get(NT * 2)
                return a_, a_.rearrange("p (t n) -> p t n", n=2)

            lnb, lnb3 = g2()
            la, la3 = g2()
            gt, gt3 = g2()
            rowc, _ = cv.get(4)
            gn_row, d_gn = cv.get(128)
            P.dma("sp", rowc[:, 0:1], dt_bias.ap()[l, h:h + 1].partition_broadcast(128), writes=G_)
            P.dma("sp", rowc[:, 1:2], dt_bias.ap()[l, 4 + h:5 + h].partition_broadcast(128), writes=G_)
            P.dma("act", rowc[:, 2:3], a_log.ap()[l, h:h + 1].partition_broadcast(128), writes=G_)
            P.dma("act", rowc[:, 3:4], a_log.ap()[l, 4 + h:5 + h].partition_broadcast(128), writes=G_)
            P.dma("sp", gn_row, g_norm.ap()[l].partition_broadcast(128), writes=[d_gn])
            k.act(rowc[:, 2:4], rowc[:, 2:4], AF.Exp, G_, G_)
            k.ts(rowc[:, 2:4], rowc[:, 2:4], -1.0, None, ALU.mult, None, G_, G_)
            k.act(gk(K_BETA), ba3[:, :, 0:2], AF.Sigmoid, G_, G_)
            k.ts(gk(K_NEGB), gk(K_BETA), -1.0, None, ALU.mult, None, G_, G_)
            k.act(gt3, ba3[:, :, 0:2], AF.Exp, G_, G_, scale=-1.0)
            k.act(gt, gt, AF.Ln, G_, G_, bias=1.0)
            k.ts(lnb, gt, -1.0, None, ALU.mult, None, G_, G_)
            k.tt(gt3, ba3[:, :, 2:4], rowc[:, 0:2].unsqueeze(1).to_broadcast([128, NT, 2]), ALU.add, G_, G_)
            k.act(gt, gt, AF.Exp, G_, G_)
            k.act(gt, gt, AF.Ln, G_, G_, bias=1.0)
            k.tt(la3, gt3, rowc[:, 2:4].unsqueeze(1).to_broadcast([128, NT, 2]), ALU.mult, G_, G_)
            pb, pd = k.ps()
            for i in range(NT):
                k.mm(pb[:, i * 2:(i + 1) * 2], TRI[0], la3[:, i, :], True, True, [d_cst, d_g], [pd])
                k.mm(pb[:, 64 + i * 2:64 + (i + 1) * 2], TRI[1], la3[:, i, :], True, True, [d_cst, d_g], [pd])
            k.cp(gk(K_G)[:, :, 0:1], pb[:, 0:NT * 2].rearrange("p (t n) -> p t n", n=2)[:, :, 0:1], [pd], G_)
            k.cp(gk(K_G)[:, :, 1:2], pb[:, 64:64 + NT * 2].rearrange("p (t n) -> p t n", n=2)[:, :, 1:2], [pd], G_)
            k.tt(gk(K_GB), gk(K_G), lnb3, ALU.add, G_, G_)
            k.act(gk(K_EG), gk(K_G), AF.Exp, G_, G_)
            k.tt(gk(K_BG), gk(K_BETA), gk(K_EG), ALU.mult, G_, G_)
            k.tt(gt3, gk(K_G), SEL2.unsqueeze(1).to_broadcast([128, NT, 2]), ALU.mult, G_ + [d_cst], G_)
            pb, pd = k.ps()
            k.mm(pb[:, 0:NT * 2], BLK, gt, True, True, [d_cst, d_g], [pd])
            k.tt(gt3, pb[:, 0:NT * 2].rearrange("p (t n) -> p t n", n=2), gk(K_G), ALU.subtract, [pd, d_g], G_)
            k.act(gk(K_EGL), gt3, AF.Exp, G_, G_)
            gsel2, _ = cv.get(NT * 4)
            k.tt(gsel2.rearrange("p (t d c) -> p t d c", d=2, c=2), gk(K_G).unsqueeze(3).to_broadcast([128, NT, 2, 2]),
                 SELLAST.rearrange("p (d c) -> p d c", d=2).unsqueeze(1).to_broadcast([128, NT, 2, 2]), ALU.mult,
                 G_ + [d_cst], G_)
            pb, pd = k.ps()
            k.mm(pb[:, 0:NT * 4], ONES, gsel2, True, True, [d_cst, d_g], [pd])
            for cp in range(2):
                k.act(gk(K_GL + cp), pb[:, 0:NT * 4].rearrange("p (t d c) -> p t d c", d=2, c=2)[:, :, :, cp], AF.Exp, [pd], G_)
            S_all = []
            for sq_ in range(nseq):
                S_ = []
                for d_ in range(2):
                    s_, ds_ = cv.get(128)
                    if kind == "p":
                        k.ms(s_, 0.0, [ds_])
                    else:
                        P.dma(k.q(), s_, sdn.ap()[l, d_, h], writes=[ds_])
                    S_.append((s_, ds_))
                S_all.append(S_)
            X, d_X = cv.get(256)
            vb, d_vb = cv.get(256)
            Gd, d_Gd = cv.get(256)
            Gbd, d_Gbd = cv.get(256)
            E1, d_E1 = cv.get(256)
            E2, d_E2 = cv.get(256)
            E3, d_E3 = cv.get(256)
            MML = [[cv.get(256), cv.get(256)] for _ in range(2)]
            PTL = [cv.get(128) for _ in range(2)]
            OB = []
            for _i in range(2):
                o_ = {}
                o_["attnT"], o_["d_at"] = cv.get(256)
                o_["kg"], o_["d_kg"] = cv.get(256)
                o_["uw"], o_["d_uw"] = cv.get(512)
                ls_, o_["d_LS"] = cv.get(NK * 2)
                o_["LS3"] = ls_.rearrange("p (k d) -> p k d", d=2)
                OB.append(o_)
            SC = []
            for d_ in range(2):
                SC.append((cv.get(128), cv.get(128), cv.get(128)))
            PB = k.psb
            PD = k.psd

            def lanes(j_):
                sq_ = j_ // tps
                s_ = j_ % tps
                return (sq_ * tps + s_, sq_ * tps + tps - 1 - s_)

            def prep(s_):
                il = lanes(s_)
                ob = OB[s_ % 2]
                LS3, d_LS = ob["LS3"], ob["d_LS"]
                ls = lambda kk, d_: LS3[:, kk, d_:d_ + 1]
                tsl = [slice(tcol(il[d_]), tcol(il[d_]) + 128) for d_ in range(2)]
                for d_ in range(2):
                    k.cp(LS3[:, :, d_], GA4[:, il[d_], :, d_], [d_g], [d_LS])
                pk, pkd = PB[0], PD[0]
                for d_ in range(2):
                    k.tr(pk[:, d_ * 256:d_ * 256 + 128], kf[:, tsl[d_]], ident[:], [d_kf, d_ident], [pkd])
                    k.tr(pk[:, d_ * 256 + 128:d_ * 256 + 256], vf[:, tsl[d_]], ident[:], [d_vf, d_ident], [pkd])
                for d_ in range(2):
                    ds = slice(d_ * 128, (d_ + 1) * 128)
                    k.ts(ob["kg"][:, ds], pk[:, d_ * 256:d_ * 256 + 128], ls(K_EGL, d_), None, ALU.mult, None, [pkd, d_LS], [ob["d_kg"]])
                    k.act(X[:, ds], pk[:, d_ * 256:d_ * 256 + 128], AF.Copy, [pkd, d_LS], [d_X], scale=ls(K_BG, d_))
                    k.ts(vb[:, ds], pk[:, d_ * 256 + 128:d_ * 256 + 256], ls(K_BETA, d_), None, ALU.mult, None, [pkd, d_LS], [d_vb])
                yield
                pa, pad = PB[1], PD[1]
                for d_ in range(2):
                    k.mm(pa[:, d_ * 256:d_ * 256 + 128], kf[:, tsl[d_]], kf[:, tsl[d_]], True, True, [d_kf], [pad])
                    k.mm(pa[:, d_ * 256 + 128:d_ * 256 + 256], kf[:, tsl[d_]], qf[:, tsl[d_]], True, True, [d_kf, d_qf], [pad])
                for d_ in range(2):
                    ds = slice(d_ * 128, (d_ + 1) * 128)
                    k.ts(Gd[:, ds], ident[:], ls(K_G, d_), None, ALU.mult, None, [d_ident, d_LS], [d_Gd])
                    k.act(Gbd[:, ds], ident[:], AF.Copy, [d_ident, d_LS], [d_Gbd], scale=ls(K_GB, d_))
                pg, pgd = PB[2], PD[2]
                k.mm(pg[:, 0:256], ONES, Gd, True, True, [d_cst, d_Gd], [pgd])
                k.mm(pg[:, 256:512], ONES, Gbd, True, True, [d_cst, d_Gbd], [pgd])
                yield
                for d_ in range(2):
                    ds = slice(d_ * 128, (d_ + 1) * 128)
                    k.stt(E1[:, ds], pg[:, ds], ls(K_G, d_), MASKB[0][d_], ALU.subtract, ALU.add, [pgd, d_LS, d_cst], [d_E1])
                    k.stt(E2[:, ds], pg[:, 256 + d_ * 128:256 + (d_ + 1) * 128], ls(K_G, d_), MASKB[1][d_], ALU.subtract, ALU.add,
                          [pgd, d_LS, d_cst], [d_E2])
                    k.stt(E3[:, ds], pg[:, ds], ls(K_G, d_), MASKB[2][d_], ALU.subtract, ALU.add, [pgd, d_LS, d_cst], [d_E3])
                k.act(E1, E1, AF.Exp, [d_E1], [d_E1])
                k.act(E2, E2, AF.Exp, [d_E2], [d_E2])
                k.act(E3, E3, AF.Exp, [d_E3], [d_E3], scale=-1.0)
                yield
                cur = [MML[d_][0] for d_ in range(2)]
                nxt = [MML[d_][1] for d_ in range(2)]
                for d_ in range(2):
                    ds = slice(d_ * 128, (d_ + 1) * 128)
                    c_, dc_ = cur[d_]
                    k.tt(ob["attnT"][:, ds], pa[:, d_ * 256 + 128:d_ * 256 + 256], E1[:, ds], ALU.mult, [pad, d_E1], [ob["d_at"]])
                    k.stt(c_[:, 128:256], pa[:, d_ * 256:d_ * 256 + 128], -1.0, E2[:, ds], ALU.mult, ALU.mult, [pad, d_E2], [dc_])
                    k.stt(c_[:, 0:128], pa[:, d_ * 256:d_ * 256 + 128], ls(K_NEGB, d_), E3[:, ds], ALU.mult, ALU.mult,
                          [pad, d_E3, d_LS], [dc_])
                    k.tt(PTL[d_][0], ident[:], c_[:, 128:256], ALU.add, [d_ident, dc_], [PTL[d_][1]])
                yield
                pmb = (3, 2)
                ppb = (0, 1)
                for lev in range(5):
                    lastl = (lev == 4)
                    for d_ in range(2):
                        c_, dc_ = cur[d_]
                        n_, dn_ = nxt[d_]
                        pm, pmd = PB[pmb[d_]], PD[pmb[d_]]
                        k.mm(pm[:, 0:128], c_[:, 128:256], c_[:, 0:128], True, True, [dc_], [pmd])
                        if not lastl:
                            k.mm(pm[:, 128:256], c_[:, 0:128], c_[:, 128:256], True, True, [dc_], [pmd])
                        ncols = 128 if lastl else 256
                        if d_ == 0:
                            k.act(n_[:, 0:ncols], pm[:, 0:ncols], AF.Copy, [pmd], [dn_])
                        else:
                            k.cp(n_[:, 0:ncols], pm[:, 0:ncols], [pmd], [dn_])
                    yield
                    for d_ in range(2):
                        n_, dn_ = nxt[d_]
                        pt_, dpt_ = PTL[d_]
                        pp, ppd = PB[ppb[d_]], PD[ppb[d_]]
                        k.mm(pp[:, 0:128], n_[:, 0:128], pt_, True, True, [dn_, dpt_], [ppd])
                        k.tt(pt_, pt_, pp[:, 0:128], ALU.add, [dpt_, ppd], [dpt_])
                    yield
                    cur, nxt = nxt, cur
                pu, pud = PB[2], PD[2]
                for d_ in range(2):
                    ds = slice(d_ * 128, (d_ + 1) * 128)
                    pt_, dpt_ = PTL[d_]
                    k.mm(pu[:, d_ * 256:d_ * 256 + 128], pt_, vb[:, ds], True, True, [dpt_, d_vb], [pud])
                    k.mm(pu[:, d_ * 256 + 128:d_ * 256 + 256], X[:, ds], pt_, True, True, [d_X, dpt_], [pud])
                k.cp(ob["uw"], pu[:, 0:512], [pud], [ob["d_uw"]])
                yield

            def scan(s_, d_):
                i = lanes(s_)[d_]
                ob = OB[s_ % 2]
                LS3, d_LS = ob["LS3"], ob["d_LS"]
                ds = slice(d_ * 128, (d_ + 1) * 128)
                es = slice(d_ * 64, (d_ + 1) * 64)
                (vnew, d_vn), (oasb, d_oa), (otmp, d_ot) = SC[d_]
                s_t, ds_ = S_all[s_ // tps][d_]
                p1, p1d = PB[4 + 2 * d_], PD[4 + 2 * d_]
                p2, p2d = PB[5 + 2 * d_], PD[5 + 2 * d_]
                for cp in ((0, 1) if d_ == 0 else (1, 0)):
                    bs = slice(cp * 64, (cp + 1) * 64)
                    cs = slice(tcol(i) + cp * 64, tcol(i) + (cp + 1) * 64)
                    k.mm(p1[bs, 0:128], ob["uw"][:, d_ * 256 + 128 + cp * 64:d_ * 256 + 128 + (cp + 1) * 64], s_t, True, True,
                         [ob["d_uw"], ds_], [p1d])
                    k.mm(p1[bs, 128:256], qf[:, cs], s_t, True, True, [d_qf, ds_], [p1d])
                    k.tt(vnew[bs, :], ob["uw"][bs, d_ * 256:d_ * 256 + 128], p1[bs, 0:128], ALU.subtract, [ob["d_uw"], p1d], [d_vn])
                    yield
                    k.mm(p2[bs, 0:128], ob["attnT"][bs, d_ * 128 + cp * 64:d_ * 128 + (cp + 1) * 64], vnew[bs, :], True, True, [ob["d_at"], d_vn], [p2d])
                    k.mm(p2[:, 128:256], ob["kg"][bs, ds], vnew[bs, :], True, True, [ob["d_kg"], d_vn], [p2d])
                    k.act(oasb[bs, :], p2[bs, 0:128], AF.Copy, [p2d], [d_oa])
                    k.stt(s_t, s_t, LS3[:, K_GL + cp, d_:d_ + 1], p2[:, 128:256], ALU.mult, ALU.add, [ds_, d_LS, p2d], [ds_])
                    yield
                    k.stt(otmp[bs, :], p1[bs, 128:256], LS3[bs, K_EG, d_:d_ + 1], oasb[bs, :], ALU.mult, ALU.add,
                          [p1d, d_LS, d_oa], [d_ot])
                    k.tt(oacc[bs, i * 128:(i + 1) * 128], oacc[bs, i * 128:(i + 1) * 128], otmp[bs, :], ALU.add,
                         [d_oacc[i], d_ot], [d_oacc[i]], eng="pool")
                    yield

            def run_gens(gens):
                gens = list(gens)
                while gens:
                    for g_ in list(gens):
                        try:
                            next(g_)
                        except StopIteration:
                            gens.remove(g_)

            run_gens([prep(0)])
            for s_ in range(NT):
                gl_ = [scan(s_, 0), scan(s_, 1)]
                if s_ + 1 < NT:
                    gl_.insert(0, prep(s_ + 1))
                run_gens(gl_)
            if kind == "p":
                for sq_ in range(nseq):
                    for d_ in range(2):
                        P.dma(k.q(), nst.ap()[sq_, l, d_, h], S_all[sq_][d_][0], reads=[S_all[sq_][d_][1]], writes=[d_nst])
            rs, d_rs = cv.get(4)
            on, d_on = cv.get(128)
            yT, d_yT = cv.get(128, F32R)
            for i in range(NT):
                tsl = slice(i * 128, (i + 1) * 128)
                k.act(on, oacc[:, tsl], AF.Square, [d_oacc[i]], [d_on, d_rs], accum=rs[:, 0:1])
                k.act(rs[:, 1:2], rs[:, 0:1], AF.Sqrt, [d_rs], [d_rs], bias=EPS, scale=1.0 / 128)
                k.rcp(rs[:, 2:3], rs[:, 1:2], [d_rs], [d_rs])
                k.stt(on, oacc[:, tsl], rs[:, 2:3], gn_row, ALU.mult, ALU.mult, [d_oacc[i], d_rs, d_gn, d_on], [d_on])
                pt, ptd = k.ps()
                k.tr(pt[:, 0:128], on, ident[:], [d_on, d_ident], [ptd])
                k.tt(yT, pt[:, 0:128], zs[:, tsl], ALU.mult, [ptd, d_zs], [d_yT])
                P.dma(k.q(), yscr_v[h, :, tsl], yT, reads=[d_yT], writes=[d_yscr[h]], r32=True)

        def na_unit(l, kind, si, T, hp):
            NT = T // 128
            TT = min(T, 512)
            fz = fence()
            cv = Carve(fz)
            ws = []
            for off in (OFF_NA_Q, OFF_NA_K, OFF_NA_V, OFF_NA_Z):
                wa, dwl = cv.getw(len(ws))
                dw = dwl[0]
                w3 = wa.rearrange("p (c n) -> p c n", c=8)
                c0 = off + hp * 128
                P.dma(k.q(), w3, w_in.ap()[l, :, c0:c0 + 128].rearrange("(c p) n -> p c n", p=128), writes=[dw], r32=True)
                ws.append((w3, dw))
            (wq, d_wq), (wk, d_wk), (wv, d_wv), (wz, d_wz) = ws
            qT, d_qT = cv.get(T, F32R)
            kT, d_kT = cv.get(T, F32R)
            vt_f, d_vt = cv.get(NT * 132, F32R)
            vtok = vt_f.rearrange("p (t h e) -> p t h e", t=NT, h=2)
            zs, d_zs = cv.get(T)
            ones, d_ones = cv.get(2)
            stgs = [cv.get(256) for _ in range(2)]
            k.ms(ones, 1.0, [d_ones])
            for t in range(T // TT):
                ts_ = slice(t * TT, (t + 1) * TT)
                pb, pd = k.ps()
                for kc in range(8):
                    k.mm(pb[:, 0:TT], wq[:, kc, :], hTv[:, kc, ts_], kc == 0, kc == 7, [d_wq, d_hT], [pd])
                k.ts(qT[:, ts_], pb[:, 0:TT], 0.125, None, ALU.mult, None, [pd], [d_qT])
                pb, pd = k.ps()
                for kc in range(8):
                    k.mm(pb[:, 0:TT], wk[:, kc, :], hTv[:, kc, ts_], kc == 0, kc == 7, [d_wk, d_hT], [pd])
                k.cp(kT[:, ts_], pb[:, 0:TT], [pd], [d_kT])
                pb, pd = k.ps()
                for kc in range(8):
                    k.mm(pb[:, 0:TT], wz[:, kc, :], hTv[:, kc, ts_], kc == 0, kc == 7, [d_wz, d_hT], [pd])
                k.act(zs[:, ts_], pb[:, 0:TT], AF.Silu, [pd], [d_zs])
            for i in range(NT):
                is_ = slice(i * 128, (i + 1) * 128)
                pb, pd = k.ps()
                for kc in range(8):
                    k.mm(pb[:, 0:128], hTv[:, kc, is_], wv[:, kc, :], kc == 0, kc == 7, [d_wv, d_hT], [pd])
                k.cp(vtok[:, i, :, 0:64], pb[:, 0:128].rearrange("p (h e) -> p h e", h=2), [pd], [d_vt])
                k.cp(vtok[:, i, :, 64:66], ones[:, 0:2].unsqueeze(1).to_broadcast([128, 2, 2]), [d_ones], [d_vt], eng="pool")
                if kind == "p":
                    sq_ = i // 2
                    rs_ = slice((i % 2) * 128, (i % 2) * 128 + 128)
                    stg, d_stg = stgs[i % 2]
                    k.cp(stg[:, 0:128], pb[:, 0:128], [pd], [d_stg])
                    P.dma("act", nv.ap()[sq_, l, rs_, hp * 128:(hp + 1) * 128], stg[:, 0:128], reads=[d_stg], writes=[d_nkv])
                    pb, pd = k.ps()
                    for kc in range(8):
                        k.mm(pb[:, 0:128], hTv[:, kc, is_], wk[:, kc, :], kc == 0, kc == 7, [d_wk, d_hT], [pd])
                    k.cp(stg[:, 128:256], pb[:, 0:128], [pd], [d_stg])
                    P.dma("act", nk.ap()[sq_, l, rs_, hp * 128:(hp + 1) * 128], stg[:, 128:256], reads=[d_stg], writes=[d_nkv])
            otok, d_otok = cv.get(128)
            rden, d_rden = cv.get(2)
            yT, d_yT = cv.get(128, F32R)
            if kind == "p":
                PTs = [[cv.get(256, F32R) for _ in range(4)] for _ in range(2)]
                for sq_ in range(T // SEQ):
                    PT = PTs[sq_ % 2]
                    q0 = sq_ * SEQ
                    po, pod = k.ps()
                    for h2 in range(2):
                        hs = slice(h2 * 64, (h2 + 1) * 64)
                        for kt in range(2):
                            pt, d_pt = PT[h2 * 2 + kt]
                            pb, pd = k.ps()
                            k.mm(pb[:, 0:256], kT[hs, q0 + kt * 128:q0 + (kt + 1) * 128], qT[hs, q0:q0 + 256], True, True,
                                 [d_kT, d_qT], [pd])
                            k.act(pt, pb[:, 0:256], AF.Exp, [pd], [d_pt])
                        for qt in range(2):
                            for kt in range(2):
                                pt, d_pt = PT[h2 * 2 + kt]
                                c0 = qt * 132 + h2 * 66
                                k.mm(po[:, c0:c0 + 66], pt[:, qt * 128:(qt + 1) * 128], vtok[:, sq_ * 2 + kt, h2, :], kt == 0, kt == 1,
                                     [d_pt, d_vt], [pod])
                    for qt in range(2):
                        qs = slice(q0 + qt * 128, q0 + (qt + 1) * 128)
                        for h2 in range(2):
                            c0 = qt * 132 + h2 * 66
                            k.rcp(rden[:, h2:h2 + 1], po[:, c0 + 64:c0 + 65], [pod], [d_rden])
                            k.ts(otok[:, h2 * 64:(h2 + 1) * 64], po[:, c0:c0 + 64], rden[:, h2:h2 + 1], None, ALU.mult, None,
                                 [pod, d_rden], [d_otok])
                        pb, pd = k.ps()
                        k.tr(pb[:, 0:128], otok, ident[:], [d_otok, d_ident], [pd])
                        k.tt(yT, pb[:, 0:128], zs[:, qs], ALU.mult, [pd, d_zs], [d_yT])
                        P.dma(k.q(), yscr_v[4 + hp, :, qs], yT, reads=[d_yT], writes=[d_yscr[4 + hp]], r32=True)
            else:
                kcT, d_kcT = cv.get(256, F32R)
                vc_f, d_vc = cv.get(2 * 132, F32R)
                vctx = vc_f.rearrange("p (t h e) -> p t h e", t=2, h=2)
                cst_k, d_cstk = cv.get(256)
                P.dma("sp", cst_k.rearrange("p (t n) -> p t n", t=2),
                      ck_in.ap()[l, :, hp * 128:(hp + 1) * 128].rearrange("(t p) n -> p t n", p=128), writes=[d_cstk])
                pb, pd = k.ps()
                for t in range(2):
                    k.tr(pb[:, t * 128:(t + 1) * 128], cst_k[:, t * 128:(t + 1) * 128], ident[:], [d_cstk, d_ident], [pd])
                k.cp(kcT, pb[:, 0:256], [pd], [d_kcT])
                for t in range(2):
                    P.dma("act", vctx[:, t, :, 0:64],
                          cv_in.ap()[l, t * 128:(t + 1) * 128, hp * 128:(hp + 1) * 128].rearrange("p (h e) -> p h e", h=2),
                          writes=[d_vc], r32=True)
                    k.cp(vctx[:, t, :, 64:66], ones[:, 0:2].unsqueeze(1).to_broadcast([128, 2, 2]), [d_ones], [d_vc], eng="pool")
                E2f, d_E2 = cv.get(2 * 15 * 64)
                E2 = E2f.rearrange("p (h r c) -> p h r c", h=2, r=15)
                for a in range(2):
                    for h2 in range(2):
                        head = hp * 2 + h2
                        src = AP(zscr, ((l * 8 + head) * 15) * 8128 + 63, [[126, 64], [8128, 15], [1, 64]])
                        P.dma("sp" if a == 0 else "act", E2[a * 64:(a + 1) * 64, h2, :, :], src, reads=[d_zscr[l]], writes=[d_E2])
                k.act(E2f, E2f, AF.Exp, [d_E2], [d_E2])
                k.tt(E2f.rearrange("p (g c) -> p g c", c=64), E2f.rearrange("p (g c) -> p g c", c=64),
                     CM.unsqueeze(1).to_broadcast([128, 30, 64]), ALU.mult, [d_E2, d_cst], [d_E2])
                TABf, d_TAB = cv.get(2 * 21 * 128)
                TAB = TABf.rearrange("p (h t q) -> p h t q", h=2, t=21)
                k.ms(TABf, 0.0, [d_TAB])
                plans = {}
                tid = 0
                plans["int"] = []
                for j in range(5):
                    for a in range(2):
                        for b in range(2):
                            dr = 2 * j - 4 + a - b
                            if -4 <= dr <= 3:
                                plans["int"].append((tid, a, b, dr))
                    tid += 1
                tid0 = {"int": 0}
                for m_ in (0, 1, 14, 15):
                    tid0[m_] = tid
                    kt0 = 0 if m_ < 2 else 12
                    plans[m_] = []
                    for j in range(4):
                        for a in range(2):
                            for b in range(2):
                                kr = 2 * (kt0 + j) + a
                                r = 2 * m_ + b
                                rs_ = min(max(r - 4, 0), 24)
                                if rs_ <= kr <= rs_ + 7:
                                    plans[m_].append((tid, a, b, kr - r))
                        tid += 1
                assert tid == 21
                _ci = 0
                for key_, pl in plans.items():
                    for (tid_, a, b, dr) in pl:
                        k.cp(TAB[a * 64:(a + 1) * 64, :, tid_, b * 64:(b + 1) * 64], E2[a * 64:(a + 1) * 64, :, dr + 7, :],
                             [d_E2], [d_TAB], eng=("dve" if _ci % 2 == 0 else "pool"))
                        _ci += 1
                PTb = [cv.get(7 * 128, F32R) for _ in range(2)]
                tEb = [cv.get(512) for _ in range(2)]
                tE2b = [cv.get(128) for _ in range(2)]
                its = [(m_, h2) for m_ in range(16) for h2 in range(2)]

                def info(m_):
                    if 2 <= m_ <= 13:
                        return [m_ - 2 + j for j in range(5)], 0
                    return [(0 if m_ < 2 else 12) + j for j in range(4)], tid0[m_]

                def emit_scores(m_, h2):
                    kts, t0 = info(m_)
                    nl = len(kts)
                    qs = slice(m_ * 128, (m_ + 1) * 128)
                    hs = slice(h2 * 64, (h2 + 1) * 64)
                    pA, pAd = k.ps()
                    for j in range(4):
                        k.mm(pA[:, j * 128:(j + 1) * 128], kT[hs, kts[j] * 128:(kts[j] + 1) * 128], qT[hs, qs], True, True,
                             [d_kT, d_qT], [pAd])
                    pB, pBd = k.ps()
                    c_ = 0
                    if nl == 5:
                        k.mm(pB[:, 0:128], kT[hs, kts[4] * 128:(kts[4] + 1) * 128], qT[hs, qs], True, True, [d_kT, d_qT], [pBd])
                        c_ = 128
                    for t in range(2):
                        k.mm(pB[:, c_ + t * 128:c_ + (t + 1) * 128], kcT[hs, t * 128:(t + 1) * 128], qT[hs, qs], True, True,
                             [d_kcT, d_qT], [pBd])
                    return (pA, pAd, pB, pBd, c_)

                pend = emit_scores(*its[0])
                po, pod = None, None
                for ii, (m_, h2) in enumerate(its):
                    cur_sc = pend
                    if ii + 1 < len(its):
                        pend = emit_scores(*its[ii + 1])
                    pA, pAd, pB, pBd, c_ = cur_sc
                    kts, t0 = info(m_)
                    nl = len(kts)
                    qs = slice(m_ * 128, (m_ + 1) * 128)
                    PTf, d_PT = PTb[ii % 2]
                    tmpE, d_tmpE = tEb[ii % 2]
                    tmpE2, d_tmpE2 = tE2b[ii % 2]
                    k.act(tmpE, pA[:, 0:512], AF.Exp, [pAd], [d_tmpE])
                    k.tt(PTf[:, 0:512], tmpE, TABf[:, (h2 * 21 + t0) * 128:(h2 * 21 + t0 + 4) * 128], ALU.mult,
                         [d_tmpE, d_TAB], [d_PT])
                    if nl == 5:
                        k.act(tmpE2, pB[:, 0:128], AF.Exp, [pBd], [d_tmpE2])
                        k.tt(PTf[:, 512:640], tmpE2, TABf[:, (h2 * 21 + 4) * 128:(h2 * 21 + 5) * 128], ALU.mult,
                             [d_tmpE2, d_TAB], [d_PT])
                    k.act(PTf[:, nl * 128:(nl + 2) * 128], pB[:, c_:c_ + 256], AF.Exp, [pBd], [d_PT])
                    if h2 == 0:
                        po, pod = k.ps()
                    c0 = h2 * 66
                    ntile = nl + 2
                    for j in range(ntile):
                        if j < nl:
                            rhs_ = vtok[:, kts[j], h2, :]
                            rd = d_vt
                        else:
                            rhs_ = vctx[:, j - nl, h2, :]
                            rd = d_vc
                        k.mm(po[:, c0:c0 + 66], PTf[:, j * 128:(j + 1) * 128], rhs_, j == 0, j == ntile - 1, [d_PT, rd], [pod])
                    if h2 == 1:
                        for hh in range(2):
                            cc0 = hh * 66
                            k.rcp(rden[:, hh:hh + 1], po[:, cc0 + 64:cc0 + 65], [pod], [d_rden])
                            k.ts(otok[:, hh * 64:(hh + 1) * 64], po[:, cc0:cc0 + 64], rden[:, hh:hh + 1], None, ALU.mult, None,
                                 [pod, d_rden], [d_otok])
                        pb, pd = k.ps()
                        k.tr(pb[:, 0:128], otok, ident[:], [d_otok, d_ident], [pd])
                        k.tt(yT, pb[:, 0:128], zs[:, qs], ALU.mult, [pd, d_zs], [d_yT])
                        P.dma(k.q(), yscr_v[4 + hp, :, qs], yT, reads=[d_yT], writes=[d_yscr[4 + hp]], r32=True)

        for l in range(depth):
            last = (l == depth - 1)
            fz = fence()
            cv = Carve(fz)
            bcol, d_bcol = cv.get(16)
            gpre_col, _d = cv.get(8)
            brow, _d = cv.get(D)
            gpost_row, _d = cv.get(D)
            ggt, d_ggt = cv.get(512)
            wada_f, d_wada = cv.get(8 * 512, F32R)
            wada = wada_f.rearrange("p (c n) -> p c n", c=8)
            sbc_f, d_sbc = cv.get(8 * 2 * 128, F32R)
            siluc_bc = sbc_f.rearrange("p (c j n) -> p c j n", c=8, j=2)
            for kc in range(8):
                for j in range(2):
                    k.cp(siluc_bc[:, kc, j, :], silucT[:, kc, j:j + 1].bitcast(F32).to_broadcast([128, 128]), [d_siluc], [d_sbc])
            P.dma("sp", bcol, b_ada.ap()[l, 0:2048].rearrange("(c p) -> p c", p=128), writes=[d_bcol], slow=True)
            P.dma("act", gpre_col, g_pre.ap()[l].rearrange("(c p) -> p c", p=128), writes=[d_bcol], slow=True)
            P.dma("sp", brow, b_ada.ap()[l, 2048:3072].partition_broadcast(128), writes=[d_bcol])
            P.dma("act", gpost_row, g_post.ap()[l].partition_broadcast(128), writes=[d_bcol])
            for blk in range(6):
                P.dma(k.q(), wada, w_ada.ap()[l, :, blk * 512:(blk + 1) * 512].rearrange("(c p) n -> p c n", p=128),
                      writes=[d_wada], r32=True)
                if blk < 4:
                    pb, pd = k.ps()
                    for cc in range(4):
                        for kc in range(8):
                            k.mm(pb[:, cc * 2:cc * 2 + 2], wada[:, kc, cc * 128:(cc + 1) * 128], silucT[:, kc, :],
                                 kc == 0, kc == 7, [d_wada, d_siluc], [pd])
                    k.tt(modcol[:, blk * 4:blk * 4 + 4, :], pb[:, 0:8].rearrange("p (c j) -> p c j", j=2),
                         bcol[:, blk * 4:blk * 4 + 4].unsqueeze(2).to_broadcast([128, 4, 2]), ALU.add,
                         [pd, d_bcol], [d_modcol])
                else:
                    for j in range(2):
                        pb, pd = k.ps()
                        for kc in range(8):
                            k.mm(pb[:], siluc_bc[:, kc, j, :], wada[:, kc, :], kc == 0, kc == 7, [d_wada, d_sbc], [pd])
                        c0 = (blk - 4) * 512
                        k.tt(ggt[:, 0:512], pb[:], brow[:, c0:c0 + 512], ALU.add, [pd, d_bcol], [d_ggt])
                        k.tt(ggt[:, 0:512], ggt[:, 0:512], gpost_row[:, c0:c0 + 512], ALU.mult, [d_ggt, d_bcol], [d_ggt])
                        P.dma("sp", ggscr.ap()[j, :, c0:c0 + 512], ggt[:, 0:512], reads=[d_ggt], writes=[d_gg])
            for j in range(2):
                k.stt(s1col[:, :, j], modcol[:, 8:16, j], 1.0, gpre_col, ALU.add, ALU.mult, [d_modcol, d_bcol], [d_modcol])

            def xio(kind, si):
                if l == 0:
                    xin = x_p.ap()[si] if kind == "p" else x_s.ap()
                    d_xin = d_x0
                else:
                    xin = xs_p[(l - 1) % 2].ap()[si] if kind == "p" else xs_s[(l - 1) % 2].ap()
                    d_xin = d_xs_p[(l - 1) % 2][si] if kind == "p" else d_xs_s[(l - 1) % 2]
                if last:
                    xout = y_p.ap()[si] if kind == "p" else y_s.ap()
                    d_xout = d_y0
                else:
                    xout = xs_p[l % 2].ap()[si] if kind == "p" else xs_s[l % 2].ap()
                    d_xout = d_xs_p[l % 2][si] if kind == "p" else d_xs_s[l % 2]
                return xin, d_xin, xout, d_xout

            d_x0 = Dep()
            d_y0 = Dep()
            for grp in ("p", "s"):
              for (kind, si, T) in [q_ for q_ in seqs if q_[0] == grp]:
                cj = 0 if kind == "p" else 1
                TT = min(T, 512)
                NTL = T // TT
                xin, d_xin, xout, d_xout = xio(kind, si)
                ho = si * SEQ if kind == "p" else 0
                hTv = hT[:, :, ho:ho + T]
                yscr_v = yscr.ap()[:, :, ho:ho + T]

                fz = fence()
                cv = Carve(fz)
                _x0, _d0 = cv.get(D)
                _x1, _d1 = cv.get(D)
                xt = [_x0, _x1]
                d_xt = [_d0, _d1]
                xn, d_xn = cv.get(D)
                for i in range(T // 128):
                    b = i % 2
                    P.dma(k.q(), xt[b], xin[i * 128:(i + 1) * 128, :], reads=[d_xin], writes=[d_xt[b]])
                    k.act(xn, xt[b], AF.Square, [d_xt[b]], [d_xn, d_small], accum=small[:, 0:1])
                    k.act(small[:, 1:2], small[:, 0:1], AF.Sqrt, [d_small], [d_small], bias=EPS, scale=1.0 / D)
                    k.rcp(small[:, 2:3], small[:, 1:2], [d_small], [d_small])
                    k.ts(xn, xt[b], small[:, 2:3], None, ALU.mult, None, [d_xt[b], d_small], [d_xn])
                    for half in range(2):
                        pb, pd = k.ps()
                        for c4 in range(4):
                            kc = half * 4 + c4
                            k.tr(pb[:, c4 * 128:(c4 + 1) * 128], xn[:, kc * 128:(kc + 1) * 128], ident[:], [d_xn, d_ident], [pd])
                        for c4 in range(4):
                            kc = half * 4 + c4
                            k.ts(hTv[:, kc, i * 128:(i + 1) * 128], pb[:, c4 * 128:(c4 + 1) * 128],
                                 s1col[:, kc, cj:cj + 1], modcol[:, kc, cj:cj + 1], ALU.mult, ALU.add,
                                 [pd, d_modcol], [d_hT], eng=("dve" if c4 % 2 == 0 else "pool") if False else "dve")

                fz = fence()
                cv = Carve(fz)
                zbuf, d_z = cv.get(TT, F32R)
                zsrc, d_zs0 = cv.get(TT)
                k.ms(zsrc, 0.0, [d_zs0])
                k.cp(zbuf, zsrc, [d_zs0], [d_z])
                for ch in range(8):
                    if (ch < 4 and not do_dn) or (ch >= 4 and not do_na):
                        for t in range(NTL):
                            P.dma(k.q(), yscr_v[ch, :, t * TT:(t + 1) * TT], zbuf, reads=[d_z], writes=[d_yscr[ch]], r32=True)

              if True:
                kind = grp
                cj = 0 if kind == "p" else 1
                T = NPS * SEQ if kind == "p" else DSEQ
                TT = 512
                NTL = T // TT
                hTv = hT[:, :, 0:T]
                yscr_v = yscr.ap()[:, :, 0:T]
                gscr_v = gscr.ap()[:, :, 0:T]
                mscr_v = mscr.ap()[:, :, 0:T]

                def xrows(sub):
                    if kind == "p":
                        xi, dxi, xo_, dxo = xio("p", sub // 2)
                        rr = (sub % 2) * 128
                    else:
                        xi, dxi, xo_, dxo = xio("s", 0)
                        rr = sub * 128
                    return xi[rr:rr + 128, :], dxi, xo_[rr:rr + 128, :], dxo

                L = SEQ if kind == "p" else DSEQ
                nseq = T // L
                si = None

                if do_dn:
                    for h in range(4):
                        dn_unit(l, kind, si, T, h, L)

                if do_na:
                    for hp in range(4):
                        na_unit(l, kind, si, T, hp)

                for g in range(4):
                    win = POOL_WINDOWS[g]
                    fz = fence()
                    cv = Carve(fz)
                    wu, _dl = cv.getw(0)
                    d_wu = _dl[0]
                    wz, _dl = cv.getw(1)
                    d_wz = _dl[0]
                    pw, d_pw = cv.get(128, F32R)
                    psc, d_psc = cv.get(1)
                    LP = L + 16
                    U, d_U = cv.get(nseq * LP)
                    zs, d_zs = cv.get(T)
                    s_a, d_sa = cv.get(nseq * LP)
                    s_b, d_sb = cv.get(nseq * LP)
                    icnt, d_icnt = cv.get(L)
                    pooled, d_pooled = cv.get(T, F32R)
                    yT, d_yT = cv.get(TT, F32R)
                    U3 = U.rearrange("p (s n) -> p s n", s=nseq)
                    wu3 = wu.rearrange("p (c n) -> p c n", c=8)
                    wz3 = wz.rearrange("p (c n) -> p c n", c=8)
                    cu = OFF_PL_U + g * 128
                    cz = OFF_PL_Z + g * 128
                    P.dma("sp", wu3, w_in.ap()[l, :, cu:cu + 128].rearrange("(c p) n -> p c n", p=128), writes=[d_wu], r32=True)
                    P.dma("act", wz3, w_in.ap()[l, :, cz:cz + 128].rearrange("(c p) n -> p c n", p=128), writes=[d_wz], r32=True)
                    P.dma("sp", pw, pool_w.ap()[l, g], writes=[d_pw], r32=True)
                    P.dma("act", psc, pool_scale.ap()[l, g * 128:(g + 1) * 128].rearrange("(p o) -> p o", o=1), writes=[d_psc], slow=True)
                    ic_src = (invcnt_p if kind == "p" else invcnt_s).ap()[g]
                    P.dma("sp", icnt, ic_src.partition_broadcast(128), writes=[d_icnt])
                    k.ms(U3[:, :, 0:8], 0.0, [d_U])
                    k.ms(U3[:, :, L + 8:L + 16], 0.0, [d_U])
                    for t in range(NTL):
                        pb, pd = k.ps()
                        for kc in range(8):
                            k.mm(pb[:, 0:TT], wu3[:, kc, :], hTv[:, kc, t * TT:(t + 1) * TT], kc == 0, kc == 7, [d_wu, d_hT], [pd])
                        if L >= TT:
                            k.cp(U[:, 8 + t * TT:8 + (t + 1) * TT], pb[:, 0:TT], [pd], [d_U])
                        else:
                            spt = TT // L
                            k.cp(U3[:, t * spt:(t + 1) * spt, 8:8 + L], pb[:, 0:TT].rearrange("p (s n) -> p s n", s=spt), [pd], [d_U])
                        pb, pd = k.ps()
                        for kc in range(8):
                            k.mm(pb[:, 0:TT], wz3[:, kc, :], hTv[:, kc, t * TT:(t + 1) * TT], kc == 0, kc == 7, [d_wz, d_hT], [pd])
                        k.act(zs[:, t * TT:(t + 1) * TT], pb[:, 0:TT], AF.Silu, [pd], [d_zs])
                    cur, dcur, curlen = U, d_U, nseq * LP
                    step = 1
                    bufs = [(s_a, d_sa), (s_b, d_sb)]
                    bi = 0
                    while step < win:
                        nb, dnb = bufs[bi]
                        bi ^= 1
                        nlen = curlen - step
                        k.tt(nb[:, 0:nlen], cur[:, 0:nlen], cur[:, step:step + nlen], ALU.add, [dcur], [dnb])
                        cur, dcur, curlen = nb, dnb, nlen
                        step *= 2
                    o0 = 8 - win // 2
                    nb, dnb = bufs[bi]
                    nb3 = nb[:, 0:T].rearrange("p (s n) -> p s n", s=nseq)
                    k.tt(nb3, cur[:, 0:nseq * LP].rearrange("p (s n) -> p s n", s=nseq)[:, :, o0:o0 + L],
                         icnt.unsqueeze(1).to_broadcast([128, nseq, L]), ALU.mult, [dcur, d_icnt], [dnb])
                    k.tt(pooled.rearrange("p (s n) -> p s n", s=nseq), nb3, U3[:, :, 8:8 + L], ALU.subtract, [dnb, d_U], [d_pooled])
                    for t in range(NTL):
                        pb, pd = k.ps()
                        k.mm(pb[:, 0:TT], pw, pooled[:, t * TT:(t + 1) * TT], True, True, [d_pw, d_pooled], [pd])
                        k.stt(yT, pb[:, 0:TT], psc[:, 0:1], zs[:, t * TT:(t + 1) * TT], ALU.mult, ALU.mult, [pd, d_psc, d_zs], [d_yT])
                        P.dma(k.q(), yscr_v[8 + g, :, t * TT:(t + 1) * TT], yT, reads=[d_yT], writes=[d_yscr[8 + g]], r32=True)

                for u in range(6):
                    fz = fence()
                    cv = Carve(fz)
                    wgu_f, d_wgl = cv.getw(0, 4)
                    wgu = wgu_f.rearrange("p (c n) -> p c n", c=8)
                    c0 = OFF_GATE + u * 512
                    P.dma(k.q(), wgu, w_in.ap()[l, :, c0:c0 + 512].rearrange("(c p) n -> p c n", p=128), writes=d_wgl, r32=True)
                    gsbs = [cv.get(TT) for _ in range(2)]
                    gi = 0
                    for t in range(NTL):
                        for c4 in range(4):
                            gsb, d_gsb = gsbs[gi % 2]
                            gi += 1
                            pg_, pgd_ = k.ps()
                            for kc in range(8):
                                k.mm(pg_[:, 0:TT], wgu[:, kc, c4 * 128:(c4 + 1) * 128], hTv[:, kc, t * TT:(t + 1) * TT], kc == 0, kc == 7,
                                     list(d_wgl) + [d_hT], [pgd_])
                            k.act(gsb, pg_[:, 0:TT], AF.Sigmoid, [pgd_], [d_gsb])
                            P.dma(k.q(), gscr_v[u * 4 + c4, :, t * TT:(t + 1) * TT], gsb, reads=[d_gsb], writes=[d_gscr[u * 4 + c4]])

                fz = fence()
                cv = Carve(fz)
                ysb_l = []
                wbr_l = []
                gin_l = []
                mo_l = []
                _wa, _wd = cv.get(4 * D, F32R)
                wbr_single = (_wa.rearrange("p (c n) -> p c n", c=4), _wd)
                for _i in range(2):
                    _a, _d = cv.get(4 * TT, F32R)
                    ysb_l.append((_a.rearrange("p (c n) -> p c n", c=4), _d))
                    wbr_l.append(wbr_single)
                    _a, _d = cv.get(8 * TT)
                    gin_l.append((_a.rearrange("p (c n) -> p c n", c=8), _d))
                    mo_l.append(cv.get(TT, F32R))
                accf, d_acc = cv.get(8 * TT)
                acc3 = accf.rearrange("p (c n) -> p c n", c=8)
                tmp, d_tmp = cv.get(TT)
                bi = 0
                mi = 0
                for t in range(NTL):
                    tsl_ = slice(t * TT, (t + 1) * TT)
                    for br in range(3):
                        ysb3, d_ysb = ysb_l[bi % 2]
                        wbr3, d_wbr = wbr_l[bi % 2]
                        gin3, d_gin = gin_l[bi % 2]
                        bi += 1
                        P.dma("sp", ysb3, yscr_v[br * 4:(br + 1) * 4, :, tsl_].rearrange("c p n -> p c n"),
                              reads=list(d_yscr[br * 4:(br + 1) * 4]), writes=[d_ysb], r32=True)
                        P.dma("act", gin3, gscr_v[br * 8:(br + 1) * 8, :, tsl_].rearrange("c p n -> p c n"),
                              reads=list(d_gscr[br * 8:(br + 1) * 8]), writes=[d_gin])
                        P.dma("sp", wbr3, w_br[br].ap()[l].rearrange("(c p) n -> p c n", p=128), writes=[d_wbr], r32=True)
                        for dc in range(8):
                            pa, pad = k.ps()
                            for wc in range(4):
                                k.mm(pa[:, 0:TT], wbr3[:, wc, dc * 128:(dc + 1) * 128], ysb3[:, wc, :], wc == 0, wc == 3, [d_wbr, d_ysb], [pad])
                            if br == 0:
                                k.tt(acc3[:, dc, :], gin3[:, dc, :], pa[:, 0:TT], ALU.mult, [d_gin, pad], [d_acc])
                            elif br == 1:
                                k.tt(tmp, gin3[:, dc, :], pa[:, 0:TT], ALU.mult, [d_gin, pad], [d_tmp])
                                k.tt(acc3[:, dc, :], acc3[:, dc, :], tmp, ALU.add, [d_acc, d_tmp], [d_acc], eng="pool")
                            else:
                                mo, d_mo = mo_l[mi % 2]
                                mi += 1
                                k.tt(tmp, gin3[:, dc, :], pa[:, 0:TT], ALU.mult, [d_gin, pad], [d_tmp])
                                k.tt(mo, acc3[:, dc, :], tmp, ALU.add, [d_acc, d_tmp], [d_mo], eng="pool")
                                P.dma("act", mscr_v[dc, :, tsl_], mo, reads=[d_mo], writes=[d_mscr], r32=True)

                fz = fence()
                cv = Carve(fz)
                wo, d_wo = cv.get(8 * D, F32R)
                wo3 = wo.rearrange("p (c n) -> p c n", c=8)
                mt_l = []
                for _i in range(2):
                    _a, _d = cv.get(8 * 128, F32R)
                    mt_l.append((_a.rearrange("p (c n) -> p c n", c=8), _d))
                xr_l = [cv.get(D) for _ in range(2)]
                xo_l = [cv.get(D) for _ in range(2)]
                xn, d_xn = cv.get(512)
                ggr, d_ggr = cv.get(D)
                P.dma("act", ggr, ggscr.ap()[cj], reads=[d_gg], writes=[d_ggr])
                P.dma("sp", wo3, w_out.ap()[l].rearrange("(c p) n -> p c n", p=128), writes=[d_wo], r32=True)
                for sub in range(T // 128):
                    r0 = sub * 128
                    mt3, d_mt = mt_l[sub % 2]
                    xr, d_xr = xr_l[sub % 2]
                    xo, d_xo = xo_l[sub % 2]
                    P.dma("sp", mt3, mscr_v[:, :, r0:r0 + 128].rearrange("c p n -> p c n"), reads=[d_mscr], writes=[d_mt], r32=True)
                    xi_rows, d_xin, xo_rows, d_xout = xrows(sub)
                    P.dma("act", xr, xi_rows, reads=[d_xin], writes=[d_xr])
                    pos = []
                    for half in range(2):
                        po, pod = k.ps()
                        for kc in range(8):
                            k.mm(po[:], mt3[:, kc, :], wo3[:, kc, half * 512:(half + 1) * 512], kc == 0, kc == 7, [d_mt, d_wo], [pod])
                        k.act(xn[:, 0:512], po[:], AF.Square, [pod], [d_xn, d_small], accum=small[:, 4 + half:5 + half])
                        pos.append((po, pod))
                    k.tt(small[:, 6:7], small[:, 4:5], small[:, 5:6], ALU.add, [d_small], [d_small])
                    k.act(small[:, 7:8], small[:, 6:7], AF.Sqrt, [d_small], [d_small], bias=EPS, scale=1.0 / D)
                    k.rcp(small[:, 8:9], small[:, 7:8], [d_small], [d_small])
                    for half in range(2):
                        po, pod = pos[half]
                        hs = slice(half * 512, (half + 1) * 512)
                        k.stt(xo[:, hs], po[:], small[:, 8:9], ggr[:, hs], ALU.mult, ALU.mult, [pod, d_small, d_ggr], [d_xo])
                        k.tt(xo[:, hs], xo[:, hs], xr[:, hs], ALU.add, [d_xo, d_xr], [d_xo], eng="pool")
                    P.dma("sp", xo_rows, xo, reads=[d_xo], writes=[d_xout])
        P.emit()
    return nc, k


_CACHE = {}


def _consts():
    def invcnt(T):
        out = np.zeros((4, T), np.float32)
        pos = np.arange(T)
        for gi, win in enumerate(POOL_WINDOWS):
            lo = np.maximum(pos - win // 2, 0)
            hi = np.minimum(pos + win // 2 - 1, T - 1)
            out[gi] = 1.0 / (hi - lo + 1).astype(np.float32)
        return out

    t = np.arange(128)
    same = (t[:, None] // 64) == (t[None, :] // 64)
    cst = np.zeros((128, NCST), np.float32)
    cst[:, 0:128] = same & (t[:, None] <= t[None, :])
    cst[:, 128:256] = same & (t[:, None] >= t[None, :])
    cst[:, 256:384] = same
    cst[:, 384:512] = 1.0
    f = np.arange(64)
    pm = t % 64
    cst[:, 512:576] = (pm[:, None] == f[None, :])
    cst[:, 576] = (pm == 63)
    cst[:, 577] = (pm == 0)
    cst[:, 578] = (t == 63)
    cst[:, 579] = (t == 127)
    cst[:, 580] = (t == 0)
    cst[:, 581] = (t == 64)
    P_ = pm[:, None]
    F_ = f[None, :]
    valid = [[F_ >= P_, F_ <= P_], [F_ > P_, F_ < P_], [F_ < P_, F_ > P_]]
    for ty in range(3):
        for d_ in range(2):
            sign = 1.0 if ty == 2 else -1.0
            cst[:, 582 + (ty * 2 + d_) * 64: 582 + (ty * 2 + d_ + 1) * 64] = np.where(valid[ty][d_], 0.0, sign * BIG)
    cq = np.arange(64)
    csq = np.clip(cq - 8, 0, 48)
    cm = (f[:, None] >= csq[None, :]) & (f[:, None] < csq[None, :] + 16)
    cst[:, 582 + 384:582 + 384 + 64] = np.concatenate([cm, cm], axis=0)
    Pf = t[:, None]
    Ff = t[None, :]
    validb = [[Ff >= Pf, Ff <= Pf], [Ff > Pf, Ff < Pf], [Ff < Pf, Ff > Pf]]
    for ty in range(3):
        for d_ in range(2):
            sign = 1.0 if ty == 2 else -1.0
            c0 = 1030 + (ty * 2 + d_) * 128
            cst[:, c0:c0 + 128] = np.where(validb[ty][d_] & same, 0.0, sign * BIG)
    return {"invcnt_p": invcnt(SEQ), "invcnt_s": invcnt(DSEQ), "ident": np.eye(128, dtype=np.float32), "dncst": cst}


def kernel(x_prompt, x_sample, c, cache_k_na, cache_v_na, state_dn, c_ctx, w_ada, b_ada, g_pre, g_post,
           w_in, conv_dn, a_log_dn, dt_bias_dn, g_norm_dn, na_bias, pool_w, pool_scale,
           w_br_dn, w_br_na, w_br_pl, w_out, _depth=DEPTH, _dn=True, _na=True):
    f = lambda a: np.ascontiguousarray(np.asarray(a, dtype=np.float32))
    key = (_depth, _dn, _na)
    if key not in _CACHE:
        _CACHE[key] = build_program(_depth, _dn, _na)
    nc, k = _CACHE[key]
    cs = _consts()
    dd = _depth
    shared = {"w_ada": f(w_ada[:dd]), "b_ada": f(b_ada[:dd]), "g_pre": f(g_pre[:dd]), "g_post": f(g_post[:dd]), "w_in": f(w_in[:dd]),
              "pool_w": f(pool_w[:dd]), "pool_scale": f(pool_scale[:dd]), "w_br_dn": f(w_br_dn[:dd]), "w_br_na": f(w_br_na[:dd]),
              "w_br_pl": f(w_br_pl[:dd]), "w_out": f(w_out[:dd]), "conv_dn": f(conv_dn[:dd]),
              "a_log": f(a_log_dn[:dd]).reshape(dd, 8), "dt_bias": f(dt_bias_dn[:dd]).reshape(dd, 8), "g_norm": f(g_norm_dn[:dd])}
    shared.update(cs)
    rpad = np.zeros((dd, 8, 15, 127), np.float32)
    rpad[..., 48:79] = f(na_bias[:dd])[..., ::-1]
    x_prompt = f(x_prompt)
    x_sample = f(x_sample)
    in_maps = []
    for core in range(8):
        b = core // 4
        m = dict(shared)
        m["x_p"] = x_prompt[core * NPS:(core + 1) * NPS]
        m["x_s"] = x_sample[b]
        m["cvec"] = np.stack([f(c_ctx), f(c)[b]])
        m["sdn"] = f(state_dn[b, :dd])
        m["ck"] = f(cache_k_na[b, :dd]).reshape(dd, 256, 512)
        m["cvv"] = f(cache_v_na[b, :dd]).reshape(dd, 256, 512)
        m["rpad"] = rpad
        in_maps.append({n: m[n] for n in k.din})
    res = run_bass_kernel_spmd(nc, in_maps, core_ids=list(range(8)))
    r = res.results
    y_p = np.concatenate([r[i]["y_p"] for i in range(8)], axis=0)
    y_s = np.stack([r[0]["y_s"], r[4]["y_s"]])
    n_k = np.concatenate([r[i]["nk"] for i in range(8)], axis=0).reshape(32, dd, SEQ, 8, 64)
    n_v = np.concatenate([r[i]["nv"] for i in range(8)], axis=0).reshape(32, dd, SEQ, 8, 64)
    n_s = np.concatenate([r[i]["nst"] for i in range(8)], axis=0)
    return y_p, y_s, n_k, n_v, n_s
```

```python
import contextlib
import numpy as np
import concourse.bass as bass
import concourse.mybir as mybir
from concourse.ap import AP
from concourse.bass_utils import run_bass_kernel_spmd

F32 = mybir.dt.float32
F32R = mybir.dt.float32r
ALU = mybir.AluOpType
AF = mybir.ActivationFunctionType

ENGS = ("pe", "act", "dve", "pool", "sp")
NDSEM = 12

D = 1024
DEPTH = 4
SEQ = 256
DSEQ = 2048
NIN = 8208
OFF_DN_Z = 1536
OFF_DN_BETA = 2048
OFF_DN_A = 2056
OFF_NA_Q = 2064
OFF_NA_K = 2576
OFF_NA_V = 3088
OFF_NA_Z = 3600
OFF_PL_U = 4112
OFF_PL_Z = 4624
OFF_GATE = 5136
EPS = 1e-6
POOL_WINDOWS = (2, 4, 8, 16)
NPS = 4
NCST = 582 + 6 * 64 + 64 + 6 * 128
BIG = 30000.0


class Dep:
    __slots__ = ("w", "r", "excl")

    def __init__(self, w=None, excl=False):
        self.w = w
        self.r = []
        self.excl = excl


class Op:
    __slots__ = ("eng", "fn", "waits", "signal", "count", "is_dma", "dsem", "dval", "dprev")

    def __init__(self, eng, fn, is_dma=False):
        self.eng = eng
        self.fn = fn
        self.waits = []
        self.signal = False
        self.count = None
        self.is_dma = is_dma
        self.dsem = None
        self.dval = None
        self.dprev = None


class Prog:
    def __init__(self, nc):
        self.nc = nc
        self.ops = {e: [] for e in ENGS}
        self.ndma = {e: 0 for e in ENGS}
        self.dtot = {e: [0] * NDSEM for e in ENGS}
        self.nops = 0

    def _mk(self, eng, fn, reads, writes, is_dma):
        o = Op(eng, fn, is_dma)
        ex = [t for t in reads if t.excl]
        if ex:
            reads = [t for t in reads if not t.excl]
            writes = list(writes) + [t for t in ex if t not in writes]
        deps = []
        seen = set()

        def add(d):
            if d is None or id(d) in seen:
                return
            if (not d.is_dma) and d.eng == "pe" and eng == "pe" and not is_dma:
                return
            seen.add(id(d))
            deps.append(d)

        for t in reads:
            add(t.w)
        for t in writes:
            add(t.w)
            for r in t.r:
                add(r)
        o.waits = deps
        for d in deps:
            d.signal = True
        for t in reads:
            if not is_dma:
                t.r = [x for x in t.r if x.is_dma or x.eng != eng]
            t.r.append(o)
        for t in writes:
            t.w = o
            t.r = []
        self.ops[eng].append(o)
        self.nops += 1
        return o

    def op(self, eng, fn, reads=(), writes=()):
        return self._mk(eng, fn, reads, writes, False)

    def dma(self, eng, out, in_, reads=(), writes=(), r32=False, slow=False):
        nc = self.nc
        eng = "pool" if type(out.tensor).__name__.startswith("DRam") else "sp"

        def fn(e):
            kw = {}
            if slow:
                kw["allow_slow_non_contiguous"] = True
            if r32:
                nc.dge_precook = False
            ins = e.dma_start(out=out, in_=in_, **kw)
            if r32:
                nc.dge_precook = True
            return ins

        o = self._mk(eng, fn, reads, writes, True)
        i = self.ndma[eng] % NDSEM
        self.ndma[eng] += 1
        o.dsem = i
        o.dprev = self.dtot[eng][i]
        self.dtot[eng][i] += 16
        o.dval = self.dtot[eng][i]
        o.signal = True
        return o

    def emit(self):
        nc = self.nc
        for e in ENGS:
            c = 0
            for o in self.ops[e]:
                if not o.is_dma and o.signal:
                    c += 1
                    o.count = c
        nsig = {e: sum(1 for o in self.ops[e] if (not o.is_dma and o.signal)) for e in ENGS}
        with contextlib.ExitStack() as st:
            esem = {e: st.enter_context(nc.semaphore("s_" + e)) for e in ENGS}
            dsem = {
                e: [st.enter_context(nc.semaphore("d_%s_%d" % (e, i))) for i in range(NDSEM)]
                for e in ("sp", "act", "pool")
            }
            block = st.enter_context(nc.Block())
            ops = self.ops
            dtot = self.dtot

            def run(e, engobj, final=False):
                seen_e = {x: 0 for x in ENGS}
                seen_d = {}
                for o in ops[e]:
                    for d in o.waits:
                        if d.is_dma:
                            key = (d.eng, d.dsem)
                            if seen_d.get(key, 0) >= d.dval:
                                continue
                            engobj.wait_ge(dsem[d.eng][d.dsem], d.dval)
                            seen_d[key] = d.dval
                        else:
                            if seen_e[d.eng] >= d.count:
                                continue
                            engobj.wait_ge(esem[d.eng], d.count)
                            seen_e[d.eng] = d.count
                    if o.is_dma:
                        key = (e, o.dsem)
                        if o.dprev > 0 and seen_d.get(key, 0) < o.dprev:
                            engobj.wait_ge(dsem[e][o.dsem], o.dprev)
                            seen_d[key] = o.dprev
                        ins = o.fn(engobj)
                        ins.then_inc(dsem[e][o.dsem], 16)
                    else:
                        ins = o.fn(engobj)
                        if o.signal:
                            ins.then_inc(esem[e], 1)
                if final:
                    for x in ENGS:
                        if x != e and nsig[x] > 0:
                            engobj.wait_ge(esem[x], nsig[x])
                    for q in ("sp", "act", "pool"):
                        for i in range(NDSEM):
                            if dtot[q][i] > 0:
                                engobj.wait_ge(dsem[q][i], dtot[q][i])

            @block.tensor
            def _(eng):
                run("pe", eng)

            @block.vector
            def _(eng):
                run("dve", eng)

            @block.scalar
            def _(eng):
                run("act", eng)

            @block.gpsimd
            def _(eng):
                run("pool", eng)

            @block.sync
            def _(eng):
                run("sp", eng, final=True)


class K:
    def __init__(self, nc, st):
        self.nc = nc
        self.st = st
        self.P = Prog(nc)
        self.din = {}
        self.dout = {}
        self.psb = [st.enter_context(nc.psum_tensor("psb%d" % i, [128, 512], F32)) for i in range(8)]
        self.psd = [Dep(excl=True) for _ in range(8)]
        self.psi = 0
        self.dq = 0

    def inp(self, name, shape, dt=F32):
        t = self.nc.dram_tensor(name, list(shape), dt, kind="ExternalInput")
        self.din[name] = t
        return t

    def outp(self, name, shape):
        t = self.nc.dram_tensor(name, list(shape), F32, kind="ExternalOutput")
        self.dout[name] = t
        return t

    def scr(self, name, shape, dt=F32):
        return self.nc.dram_tensor(name, list(shape), dt, kind="Internal")

    def sb(self, name, shape, dt=F32):
        return self.st.enter_context(self.nc.sbuf_tensor(name, list(shape), dt))

    def ps(self):
        i = self.psi
        self.psi = (i + 1) % 8
        return self.psb[i], self.psd[i]

    def q(self):
        self.dq ^= 1
        return "sp" if self.dq else "act"

    def mm(self, out, lhsT, rhs, start, stop, reads, writes):
        self.P.op("pe", lambda e: e.matmul(out, lhsT=lhsT, rhs=rhs, start=start, stop=stop), reads, writes)

    def tr(self, out, in_, ident, reads, writes):
        self.P.op("pe", lambda e: e.transpose(out, in_, ident), reads, writes)

    def act(self, out, in_, func, reads, writes, bias=None, scale=1.0, accum=None):
        def fn(e):
            kw = {}
            if bias is not None:
                kw["bias"] = bias
            if accum is not None:
                kw["accum_out"] = accum
            return e.activation(out=out, in_=in_, func=func, scale=scale, **kw)

        self.P.op("act", fn, reads, writes)

    def tt(self, out, in0, in1, op, reads, writes, eng="dve"):
        self.P.op(eng, lambda e: e.tensor_tensor(out=out, in0=in0, in1=in1, op=op), reads, writes)

    def ts(self, out, in0, s1, s2, op0, op1, reads, writes, eng="dve"):
        if s2 is None:
            self.P.op(eng, lambda e: e.tensor_scalar(out=out, in0=in0, scalar1=s1, scalar2=None, op0=op0), reads, writes)
        else:
            self.P.op(eng, lambda e: e.tensor_scalar(out=out, in0=in0, scalar1=s1, scalar2=s2, op0=op0, op1=op1), reads, writes)

    def stt(self, out, in0, scalar, in1, op0, op1, reads, writes, eng="dve"):
        self.P.op(eng, lambda e: e.scalar_tensor_tensor(out=out, in0=in0, scalar=scalar, in1=in1, op0=op0, op1=op1),
                  reads, writes)

    def rcp(self, out, in_, reads, writes):
        self.P.op("dve", lambda e: e.reciprocal(out=out, in_=in_), reads, writes)

    def cp(self, out, in_, reads, writes, eng="dve"):
        self.P.op(eng, lambda e: e.tensor_copy(out=out, in_=in_), reads, writes)

    def ms(self, ap, val, writes, eng="pool"):
        self.P.op(eng, lambda e: e.memset(ap, val), (), writes)


def build_program(depth=DEPTH, do_dn=True, do_na=True):
    nc = bass.Bass("TRN2", target_bir_lowering=False)
    st = contextlib.ExitStack()
    with st:
        k = K(nc, st)
        P = k.P
        x_p = k.inp("x_p", [NPS, SEQ, D])
        x_s = k.inp("x_s", [DSEQ, D])
        cvec = k.inp("cvec", [2, D])
        w_ada = k.inp("w_ada", [depth, D, 3 * D], F32R)
        b_ada = k.inp("b_ada", [depth, 3 * D])
        g_pre = k.inp("g_pre", [depth, D])
        g_post = k.inp("g_post", [depth, D])
        w_in = k.inp("w_in", [depth, D, NIN], F32R)
        pool_w = k.inp("pool_w", [depth, 4, 128, 128], F32R)
        pool_scale = k.inp("pool_scale", [depth, 512])
        w_br = [k.inp(n, [depth, 512, D], F32R) for n in ("w_br_dn", "w_br_na", "w_br_pl")]
        w_out = k.inp("w_out", [depth, D, D], F32R)
        invcnt_p = k.inp("invcnt_p", [4, SEQ])
        invcnt_s = k.inp("invcnt_s", [4, DSEQ])
        ident_in = k.inp("ident", [128, 128])
        conv_dn = k.inp("conv_dn", [depth, 3, 1536])
        a_log = k.inp("a_log", [depth, 8])
        dt_bias = k.inp("dt_bias", [depth, 8])
        g_norm = k.inp("g_norm", [depth, 128])
        sdn = k.inp("sdn", [depth, 2, 4, 128, 128])
        dncst_in = k.inp("dncst", [128, NCST])
        ck_in = k.inp("ck", [depth, 256, 512])
        cv_in = k.inp("cvv", [depth, 256, 512], F32R)
        rpad_in = k.inp("rpad", [depth, 8, 15, 127])

        y_p = k.outp("y_p", [NPS, SEQ, D])
        y_s = k.outp("y_s", [DSEQ, D])
        nk = k.outp("nk", [NPS, depth, SEQ, 512])
        nv = k.outp("nv", [NPS, depth, SEQ, 512])
        d_nkv = Dep()
        nst = k.outp("nst", [NPS, depth, 2, 4, 128, 128])
        d_nst = Dep()

        xs_p = [k.scr("xs_p%d" % i, [NPS, SEQ, D]) for i in range(2)]
        xs_s = [k.scr("xs_s%d" % i, [DSEQ, D]) for i in range(2)]
        yscr = k.scr("yscr", [12, 128, DSEQ], F32R)
        d_xs_p = [[Dep() for _ in range(NPS)] for _ in range(2)]
        d_xs_s = [Dep() for _ in range(2)]
        d_yscr = [Dep() for _ in range(12)]
        gscr = k.scr("gscr", [24, 128, DSEQ])
        d_gscr = [Dep() for _ in range(24)]
        mscr = k.scr("mscr", [8, 128, DSEQ], F32R)
        d_mscr = Dep()

        ident = k.sb("ident_sb", [128, 128])
        d_ident = Dep()
        P.dma("sp", ident[:], ident_in.ap(), writes=[d_ident])
        cst = k.sb("dncst_sb", [128, NCST])
        d_cst = Dep()
        P.dma("act", cst[:], dncst_in.ap(), writes=[d_cst])
        TRI = [cst[:, 0:128], cst[:, 128:256]]
        BLK = cst[:, 256:384]
        ONES = cst[:, 384:512]
        I2 = cst[:, 512:576]
        SEL2 = cst[:, 576:578]
        SELLAST = cst[:, 578:582]
        MASK = [[cst[:, 582 + (ty * 2 + d_) * 64: 582 + (ty * 2 + d_ + 1) * 64] for d_ in range(2)] for ty in range(3)]
        CM = cst[:, 582 + 384:582 + 384 + 64]
        MASKB = [[cst[:, 1030 + (ty * 2 + d_) * 128: 1030 + (ty * 2 + d_ + 1) * 128] for d_ in range(2)] for ty in range(3)]
        hT = k.sb("hT", [128, 8, DSEQ], F32R)
        d_hT = Dep()
        small = k.sb("small", [128, 16])
        d_small = Dep()
        silucT = k.sb("silucT", [128, 8, 2], F32R)
        d_siluc = Dep()
        modcol = k.sb("modcol", [128, 16, 2])
        s1col = k.sb("s1col", [128, 8, 2])
        d_modcol = Dep()
        ggscr = k.scr("ggscr", [2, 128, D])
        d_gg = Dep()
        RSZ = 15 * 1024
        FSZ = 18 * 1024 + 512
        arenaR = k.sb("arenaR", [128, RSZ], F32R)
        arenaF = k.sb("arenaF", [128, FSZ])
        arena_deps = []
        WOFF = RSZ - 4096
        wdeps = [Dep() for _ in range(4)]

        def fence():
            f = P.op("dve", lambda e: e.memset(small[:, 15:16], 0.0), reads=(), writes=list(arena_deps))
            arena_deps.clear()
            return f

        class Carve:
            def __init__(self, seed):
                self.offR = 0
                self.offF = 0
                self.seed = seed

            def getw(self, j, n=1):
                a = arenaR[:, WOFF + j * 1024:WOFF + (j + n) * 1024]
                return a, wdeps[j:j + n]

            def get(self, cols, dt=F32):
                if dt == F32R:
                    a = arenaR[:, self.offR:self.offR + cols]
                    self.offR += cols
                    assert self.offR <= WOFF, self.offR
                else:
                    a = arenaF[:, self.offF:self.offF + cols]
                    self.offF += cols
                    assert self.offF <= FSZ, self.offF
                d = Dep(self.seed)
                arena_deps.append(d)
                return a, d

        craw = k.sb("craw", [128, 8, 2])
        for j in range(2):
            P.dma("sp", craw[:, :, j], cvec.ap()[j].rearrange("(c p) -> p c", p=128), writes=[d_siluc], slow=True)
        k.act(silucT[:], craw[:], AF.Silu, [d_siluc], [d_siluc])

        seqs = [("p", i, SEQ) for i in range(NPS)] + [("s", 0, DSEQ)]
        zscr = k.scr("zscr", [depth, 120, 64, 127])
        d_zscr = [Dep() for _ in range(depth)]
        if do_na:
            for l_ in range(depth):
                P.dma("sp", zscr.ap()[l_], AP(rpad_in, l_ * 120 * 127, [[127, 120], [0, 64], [1, 127]]), writes=[d_zscr[l_]])

        def dn_unit(l, kind, si, T, h, L):
            NT = T // 128
            TT = min(T, 512)
            NTL = T // TT
            nseq = T // L
            tps = L // 128
            LP2 = L + 2
            FL = nseq * LP2

            def tcol(ti):
                return (ti // tps) * LP2 + (ti % tps) * 128

            fz = fence()
            cv = Carve(fz)
            ws = []
            for off in (0, 512, 1024, OFF_DN_Z):
                wa, dwl = cv.getw(len(ws))
                dw = dwl[0]
                w3 = wa.rearrange("p (c n) -> p c n", c=8)
                c0 = off + h * 128
                P.dma(k.q(), w3, w_in.ap()[l, :, c0:c0 + 128].rearrange("(c p) n -> p c n", p=128), writes=[dw], r32=True)
                ws.append((w3, dw))
            (wq, d_wq), (wk, d_wk), (wv, d_wv), (wz, d_wz) = ws
            wba_f, d_wba = cv.get(8 * 4, F32R)
            wba = wba_f.rearrange("p (c n) -> p c n", c=8)
            for j4 in range(4):
                cj4 = OFF_DN_BETA + 4 * j4 + h
                P.dma("sp", wba[:, :, j4], w_in.ap()[l, :, cj4].rearrange("(c p) -> p c", p=128), writes=[d_wba], r32=True, slow=True)
            raw, d_raw = cv.get(FL)
            qf, d_qf = cv.get(FL)
            kf, d_kf = cv.get(FL)
            vf, d_vf = cv.get(FL)
            raw3 = raw.rearrange("p (s n) -> p s n", s=nseq)
            oacc, _ = cv.get(T)
            d_oacc = [Dep(fz) for _ in range(NT)]
            arena_deps.extend(d_oacc)
            tmpb, d_tmpb = cv.get(512)
            cw, d_cw = cv.get(9)
            for idx in range(3):
                c0 = idx * 512 + h * 128
                P.dma("act", cw[:, idx * 3:(idx + 1) * 3], conv_dn.ap()[l, :, c0:c0 + 128].rearrange("t p -> p t"),
                      writes=[d_cw], slow=True)
            k.ms(raw3[:, :, 0:1], 0.0, [d_raw])
            k.ms(raw3[:, :, L + 1:L + 2], 0.0, [d_raw])
            for i in range(NT):
                k.ms(oacc[:, i * 128:(i + 1) * 128], 0.0, [d_oacc[i]])
            fchunks = [(a_, min(512, FL - 2 - a_)) for a_ in range(0, FL - 2, 512)]
            fchunks2 = [(a_, min(512, FL - a_)) for a_ in range(0, FL, 512)]
            dsts = [(qf, d_qf), (kf, d_kf), (vf, d_vf)]
            for idx in range(3):
                w3, dw = ws[idx]
                dst, dd = dsts[idx]
                for t in range(NTL):
                    pb, pd = k.ps()
                    for kc in range(8):
                        k.mm(pb[:, 0:TT], w3[:, kc, :], hTv[:, kc, t * TT:(t + 1) * TT], kc == 0, kc == 7, [dw, d_hT], [pd])
                    if L >= TT:
                        k.cp(raw[:, 1 + t * TT:1 + (t + 1) * TT], pb[:, 0:TT], [pd], [d_raw])
                    else:
                        spt = TT // L
                        k.cp(raw3[:, t * spt:(t + 1) * spt, 1:1 + L], pb[:, 0:TT].rearrange("p (s n) -> p s n", s=spt), [pd], [d_raw])
                for (a, n_) in fchunks:
                    k.ts(dst[:, a:a + n_], raw[:, a:a + n_], cw[:, idx * 3:idx * 3 + 1], None, ALU.mult, None, [d_raw, d_cw], [dd])
                    k.stt(tmpb[:, 0:n_], raw[:, a + 1:a + 1 + n_], cw[:, idx * 3 + 1:idx * 3 + 2], dst[:, a:a + n_],
                          ALU.mult, ALU.add, [d_raw, d_cw, dd], [d_tmpb])
                    k.stt(dst[:, a:a + n_], raw[:, a + 2:a + 2 + n_], cw[:, idx * 3 + 2:idx * 3 + 3], tmpb[:, 0:n_],
                          ALU.mult, ALU.add, [d_raw, d_cw, d_tmpb], [dd])
                    k.act(dst[:, a:a + n_], dst[:, a:a + n_], AF.Silu, [dd], [dd])
                if FL - 2 < FL:
                    k.ms(dst[:, FL - 2:FL], 0.0, [dd])
            for idx in range(2):
                dst, dd = dsts[idx]
                for (a, n_) in fchunks2:
                    k.act(tmpb[:, 0:n_], dst[:, a:a + n_], AF.Square, [dd], [d_tmpb])
                    pb, pd = k.ps()
                    k.mm(pb[:, 0:n_], ONES, tmpb[:, 0:n_], True, True, [d_cst, d_tmpb], [pd])
                    k.act(tmpb[:, 0:n_], pb[:, 0:n_], AF.Sqrt, [pd], [d_tmpb], bias=EPS)
                    k.rcp(tmpb[:, 0:n_], tmpb[:, 0:n_], [d_tmpb], [d_tmpb])
                    if idx == 0:
                        k.stt(dst[:, a:a + n_], dst[:, a:a + n_], 128.0 ** -0.5, tmpb[:, 0:n_], ALU.mult, ALU.mult, [dd, d_tmpb], [dd])
                    else:
                        k.tt(dst[:, a:a + n_], dst[:, a:a + n_], tmpb[:, 0:n_], ALU.mult, [dd, d_tmpb], [dd])
            zs = raw
            d_zs = d_raw
            for t in range(NTL):
                pb, pd = k.ps()
                for kc in range(8):
                    k.mm(pb[:, 0:TT], wz[:, kc, :], hTv[:, kc, t * TT:(t + 1) * TT], kc == 0, kc == 7, [d_wz, d_hT], [pd])
                k.act(zs[:, t * TT:(t + 1) * TT], pb[:, 0:TT], AF.Silu, [pd], [d_zs])
            d_g = Dep(fz)
            arena_deps.append(d_g)
            G_ = [d_g]
            ba, _ = cv.get(NT * 4)
            ba3 = ba.rearrange("p (t n) -> p t n", n=4)
            pb, pd = k.ps()
            for i in range(NT):
                for kc in range(8):
                    k.mm(pb[:, i * 4:(i + 1) * 4], hTv[:, kc, i * 128:(i + 1) * 128], wba[:, kc, :], kc == 0, kc == 7, [d_wba, d_hT], [pd])
            k.cp(ba, pb[:, 0:NT * 4], [pd], G_)
            NK = 10
            GA, _ = cv.get(NT * NK * 2)
            GA4 = GA.rearrange("p (t k d) -> p t k d", k=NK, d=2)
            K_BETA, K_NEGB, K_G, K_GB, K_EG, K_BG, K_EGL, K_GL = range(8)
            gk = lambda kk: GA4[:, :, kk, :]

            def g2():
                a_, _ = cv.get(NT * 2)
                return a_, a_.rearrange("p (t n) -> p t n", n=2)

            lnb, lnb3 = g2()
            la, la3 = g2()
            gt, gt3 = g2()
            rowc, _ = cv.get(4)
            gn_row, d_gn = cv.get(128)
            P.dma("sp", rowc[:, 0:1], dt_bias.ap()[l, h:h + 1].partition_broadcast(128), writes=G_)
            P.dma("sp", rowc[:, 1:2], dt_bias.ap()[l, 4 + h:5 + h].partition_broadcast(128), writes=G_)
            P.dma("act", rowc[:, 2:3], a_log.ap()[l, h:h + 1].partition_broadcast(128), writes=G_)
            P.dma("act", rowc[:, 3:4], a_log.ap()[l, 4 + h:5 + h].partition_broadcast(128), writes=G_)
            P.dma("sp", gn_row, g_norm.ap()[l].partition_broadcast(128), writes=[d_gn])
            k.act(rowc[:, 2:4], rowc[:, 2:4], AF.Exp, G_, G_)
            k.ts(rowc[:, 2:4], rowc[:, 2:4], -1.0, None, ALU.mult, None, G_, G_)
            k.act(gk(K_BETA), ba3[:, :, 0:2], AF.Sigmoid, G_, G_)
            k.ts(gk(K_NEGB), gk(K_BETA), -1.0, None, ALU.mult, None, G_, G_)
            k.act(gt3, ba3[:, :, 0:2], AF.Exp, G_, G_, scale=-1.0)
            k.act(gt, gt, AF.Ln, G_, G_, bias=1.0)
            k.ts(lnb, gt, -1.0, None, ALU.mult, None, G_, G_)
            k.tt(gt3, ba3[:, :, 2:4], rowc[:, 0:2].unsqueeze(1).to_broadcast([128, NT, 2]), ALU.add, G_, G_)
            k.act(gt, gt, AF.Exp, G_, G_)
            k.act(gt, gt, AF.Ln, G_, G_, bias=1.0)
            k.tt(la3, gt3, rowc[:, 2:4].unsqueeze(1).to_broadcast([128, NT, 2]), ALU.mult, G_, G_)
            pb, pd = k.ps()
            for i in range(NT):
                k.mm(pb[:, i * 2:(i + 1) * 2], TRI[0], la3[:, i, :], True, True, [d_cst, d_g], [pd])
                k.mm(pb[:, 64 + i * 2:64 + (i + 1) * 2], TRI[1], la3[:, i, :], True, True, [d_cst, d_g], [pd])
            k.cp(gk(K_G)[:, :, 0:1], pb[:, 0:NT * 2].rearrange("p (t n) -> p t n", n=2)[:, :, 0:1], [pd], G_)
            k.cp(gk(K_G)[:, :, 1:2], pb[:, 64:64 + NT * 2].rearrange("p (t n) -> p t n", n=2)[:, :, 1:2], [pd], G_)
            k.tt(gk(K_GB), gk(K_G), lnb3, ALU.add, G_, G_)
            k.act(gk(K_EG), gk(K_G), AF.Exp, G_, G_)
            k.tt(gk(K_BG), gk(K_BETA), gk(K_EG), ALU.mult, G_, G_)
            k.tt(gt3, gk(K_G), SEL2.unsqueeze(1).to_broadcast([128, NT, 2]), ALU.mult, G_ + [d_cst], G_)
            pb, pd = k.ps()
            k.mm(pb[:, 0:NT * 2], BLK, gt, True, True, [d_cst, d_g], [pd])
            k.tt(gt3, pb[:, 0:NT * 2].rearrange("p (t n) -> p t n", n=2), gk(K_G), ALU.subtract, [pd, d_g], G_)
            k.act(gk(K_EGL), gt3, AF.Exp, G_, G_)
            gsel2, _ = cv.get(NT * 4)
            k.tt(gsel2.rearrange("p (t d c) -> p t d c", d=2, c=2), gk(K_G).unsqueeze(3).to_broadcast([128, NT, 2, 2]),
                 SELLAST.rearrange("p (d c) -> p d c", d=2).unsqueeze(1).to_broadcast([128, NT, 2, 2]), ALU.mult,
                 G_ + [d_cst], G_)
            pb, pd = k.ps()
            k.mm(pb[:, 0:NT * 4], ONES, gsel2, True, True, [d_cst, d_g], [pd])
            for cp in range(2):
                k.act(gk(K_GL + cp), pb[:, 0:NT * 4].rearrange("p (t d c) -> p t d c", d=2, c=2)[:, :, :, cp], AF.Exp, [pd], G_)
            S_all = []
            for sq_ in range(nseq):
                S_ = []
                for d_ in range(2):
                    s_, ds_ = cv.get(128)
                    if kind == "p":
                        k.ms(s_, 0.0, [ds_])
                    else:
                        P.dma(k.q(), s_, sdn.ap()[l, d_, h], writes=[ds_])
                    S_.append((s_, ds_))
                S_all.append(S_)
            X, d_X = cv.get(256)
            vb, d_vb = cv.get(256)
            Gd, d_Gd = cv.get(256)
            Gbd, d_Gbd = cv.get(256)
            E1, d_E1 = cv.get(256)
            E2, d_E2 = cv.get(256)
            E3, d_E3 = cv.get(256)
            MML = [[cv.get(256), cv.get(256)] for _ in range(2)]
            PTL = [cv.get(128) for _ in range(2)]
            OB = []
            for _i in range(2):
                o_ = {}
                o_["attnT"], o_["d_at"] = cv.get(256)
                o_["kg"], o_["d_kg"] = cv.get(256)
                o_["uw"], o_["d_uw"] = cv.get(512)
                ls_, o_["d_LS"] = cv.get(NK * 2)
                o_["LS3"] = ls_.rearrange("p (k d) -> p k d", d=2)
                OB.append(o_)
            SC = []
            for d_ in range(2):
                SC.append((cv.get(128), cv.get(128), cv.get(128)))
            PB = k.psb
            PD = k.psd

            def lanes(j_):
                sq_ = j_ // tps
                s_ = j_ % tps
                return (sq_ * tps + s_, sq_ * tps + tps - 1 - s_)

            def prep(s_):
                il = lanes(s_)
                ob = OB[s_ % 2]
                LS3, d_LS = ob["LS3"], ob["d_LS"]
                ls = lambda kk, d_: LS3[:, kk, d_:d_ + 1]
                tsl = [slice(tcol(il[d_]), tcol(il[d_]) + 128) for d_ in range(2)]
                for d_ in range(2):
                    k.cp(LS3[:, :, d_], GA4[:, il[d_], :, d_], [d_g], [d_LS], eng="pool")
                pk, pkd = PB[0], PD[0]
                for d_ in range(2):
                    k.tr(pk[:, d_ * 256:d_ * 256 + 128], kf[:, tsl[d_]], ident[:], [d_kf, d_ident], [pkd])
                    k.tr(pk[:, d_ * 256 + 128:d_ * 256 + 256], vf[:, tsl[d_]], ident[:], [d_vf, d_ident], [pkd])
                for d_ in range(2):
                    ds = slice(d_ * 128, (d_ + 1) * 128)
                    k.ts(ob["kg"][:, ds], pk[:, d_ * 256:d_ * 256 + 128], ls(K_EGL, d_), None, ALU.mult, None, [pkd, d_LS], [ob["d_kg"]])
                    k.act(X[:, ds], pk[:, d_ * 256:d_ * 256 + 128], AF.Copy, [pkd, d_LS], [d_X], scale=ls(K_BG, d_))
                    k.ts(vb[:, ds], pk[:, d_ * 256 + 128:d_ * 256 + 256], ls(K_BETA, d_), None, ALU.mult, None, [pkd, d_LS], [d_vb])
                yield
                pa, pad = PB[1], PD[1]
                for d_ in range(2):
                    k.mm(pa[:, d_ * 256:d_ * 256 + 128], kf[:, tsl[d_]], kf[:, tsl[d_]], True, True, [d_kf], [pad])
                    k.mm(pa[:, d_ * 256 + 128:d_ * 256 + 256], kf[:, tsl[d_]], qf[:, tsl[d_]], True, True, [d_kf, d_qf], [pad])
                for d_ in range(2):
                    ds = slice(d_ * 128, (d_ + 1) * 128)
                    k.ts(Gd[:, ds], ident[:], ls(K_G, d_), None, ALU.mult, None, [d_ident, d_LS], [d_Gd])
                    k.act(Gbd[:, ds], ident[:], AF.Copy, [d_ident, d_LS], [d_Gbd], scale=ls(K_GB, d_))
                pg, pgd = PB[2], PD[2]
                k.mm(pg[:, 0:256], ONES, Gd, True, True, [d_cst, d_Gd], [pgd])
                k.mm(pg[:, 256:512], ONES, Gbd, True, True, [d_cst, d_Gbd], [pgd])
                yield
                for d_ in range(2):
                    ds = slice(d_ * 128, (d_ + 1) * 128)
                    k.stt(E1[:, ds], pg[:, ds], ls(K_G, d_), MASKB[0][d_], ALU.subtract, ALU.add, [pgd, d_LS, d_cst], [d_E1])
                    k.stt(E2[:, ds], pg[:, 256 + d_ * 128:256 + (d_ + 1) * 128], ls(K_G, d_), MASKB[1][d_], ALU.subtract, ALU.add,
                          [pgd, d_LS, d_cst], [d_E2])
                    k.stt(E3[:, ds], pg[:, ds], ls(K_G, d_), MASKB[2][d_], ALU.subtract, ALU.add, [pgd, d_LS, d_cst], [d_E3])
                k.act(E1, E1, AF.Exp, [d_E1], [d_E1])
                k.act(E2, E2, AF.Exp, [d_E2], [d_E2])
                k.act(E3, E3, AF.Exp, [d_E3], [d_E3], scale=-1.0)
                yield
                cur = [MML[d_][0] for d_ in range(2)]
                nxt = [MML[d_][1] for d_ in range(2)]
                for d_ in range(2):
                    ds = slice(d_ * 128, (d_ + 1) * 128)
                    c_, dc_ = cur[d_]
                    k.tt(ob["attnT"][:, ds], pa[:, d_ * 256 + 128:d_ * 256 + 256], E1[:, ds], ALU.mult, [pad, d_E1], [ob["d_at"]])
                    k.stt(c_[:, 128:256], pa[:, d_ * 256:d_ * 256 + 128], -1.0, E2[:, ds], ALU.mult, ALU.mult, [pad, d_E2], [dc_])
                    k.stt(c_[:, 0:128], pa[:, d_ * 256:d_ * 256 + 128], ls(K_NEGB, d_), E3[:, ds], ALU.mult, ALU.mult,
                          [pad, d_E3, d_LS], [dc_])
                    k.tt(PTL[d_][0], ident[:], c_[:, 128:256], ALU.add, [d_ident, dc_], [PTL[d_][1]])
                yield
                pmb = (3, 2)
                ppb = (0, 1)
                for lev in range(5):
                    lastl = (lev == 4)
                    for d_ in range(2):
                        c_, dc_ = cur[d_]
                        n_, dn_ = nxt[d_]
                        pm, pmd = PB[pmb[d_]], PD[pmb[d_]]
                        k.mm(pm[:, 0:128], c_[:, 128:256], c_[:, 0:128], True, True, [dc_], [pmd])
                        if not lastl:
                            k.mm(pm[:, 128:256], c_[:, 0:128], c_[:, 128:256], True, True, [dc_], [pmd])
                        ncols = 128 if lastl else 256
                        if d_ == 0:
                            k.act(n_[:, 0:ncols], pm[:, 0:ncols], AF.Copy, [pmd], [dn_])
                        else:
                            k.cp(n_[:, 0:ncols], pm[:, 0:ncols], [pmd], [dn_])
                    yield
                    for d_ in range(2):
                        n_, dn_ = nxt[d_]
                        pt_, dpt_ = PTL[d_]
                        pp, ppd = PB[ppb[d_]], PD[ppb[d_]]
                        k.mm(pp[:, 0:128], n_[:, 0:128], pt_, True, True, [dn_, dpt_], [ppd])
                        k.tt(pt_, pt_, pp[:, 0:128], ALU.add, [dpt_, ppd], [dpt_])
                    yield
                    cur, nxt = nxt, cur
                pu, pud = PB[2], PD[2]
                for d_ in range(2):
                    ds = slice(d_ * 128, (d_ + 1) * 128)
                    pt_, dpt_ = PTL[d_]
                    k.mm(pu[:, d_ * 256:d_ * 256 + 128], pt_, vb[:, ds], True, True, [dpt_, d_vb], [pud])
                    k.mm(pu[:, d_ * 256 + 128:d_ * 256 + 256], X[:, ds], pt_, True, True, [d_X, dpt_], [pud])
                k.cp(ob["uw"], pu[:, 0:512], [pud], [ob["d_uw"]])
                yield

            def scan(s_, d_):
                i = lanes(s_)[d_]
                ob = OB[s_ % 2]
                LS3, d_LS = ob["LS3"], ob["d_LS"]
                ds = slice(d_ * 128, (d_ + 1) * 128)
                es = slice(d_ * 64, (d_ + 1) * 64)
                (vnew, d_vn), (oasb, d_oa), (otmp, d_ot) = SC[d_]
                s_t, ds_ = S_all[s_ // tps][d_]
                p1, p1d = PB[4 + 2 * d_], PD[4 + 2 * d_]
                p2, p2d = PB[5 + 2 * d_], PD[5 + 2 * d_]
                for cp in ((0, 1) if d_ == 0 else (1, 0)):
                    bs = slice(cp * 64, (cp + 1) * 64)
                    cs = slice(tcol(i) + cp * 64, tcol(i) + (cp + 1) * 64)
                    k.mm(p1[bs, 0:128], ob["uw"][:, d_ * 256 + 128 + cp * 64:d_ * 256 + 128 + (cp + 1) * 64], s_t, True, True,
                         [ob["d_uw"], ds_], [p1d])
                    k.mm(p1[bs, 128:256], qf[:, cs], s_t, True, True, [d_qf, ds_], [p1d])
                    k.tt(vnew[bs, :], ob["uw"][bs, d_ * 256:d_ * 256 + 128], p1[bs, 0:128], ALU.subtract, [ob["d_uw"], p1d], [d_vn])
                    yield
                    k.mm(p2[bs, 0:128], ob["attnT"][bs, d_ * 128 + cp * 64:d_ * 128 + (cp + 1) * 64], vnew[bs, :], True, True, [ob["d_at"], d_vn], [p2d])
                    k.mm(p2[:, 128:256], ob["kg"][bs, ds], vnew[bs, :], True, True, [ob["d_kg"], d_vn], [p2d])
                    k.act(oasb[bs, :], p2[bs, 0:128], AF.Copy, [p2d], [d_oa])
                    k.stt(s_t, s_t, LS3[:, K_GL + cp, d_:d_ + 1], p2[:, 128:256], ALU.mult, ALU.add, [ds_, d_LS, p2d], [ds_])
                    yield
                    k.stt(otmp[bs, :], p1[bs, 128:256], LS3[bs, K_EG, d_:d_ + 1], oasb[bs, :], ALU.mult, ALU.add,
                          [p1d, d_LS, d_oa], [d_ot])
                    k.tt(oacc[bs, i * 128:(i + 1) * 128], oacc[bs, i * 128:(i + 1) * 128], otmp[bs, :], ALU.add,
                         [d_oacc[i], d_ot], [d_oacc[i]], eng="pool")
                    yield

            def run_gens(gens):
                gens = list(gens)
                while gens:
                    for g_ in list(gens):
                        try:
                            next(g_)
                        except StopIteration:
                            gens.remove(g_)

            run_gens([prep(0)])
            for s_ in range(NT):
                gl_ = [scan(s_, 0), scan(s_, 1)]
                if s_ + 1 < NT:
                    gl_.insert(0, prep(s_ + 1))
                run_gens(gl_)
            if kind == "p":
                for sq_ in range(nseq):
                    for d_ in range(2):
                        P.dma(k.q(), nst.ap()[sq_, l, d_, h], S_all[sq_][d_][0], reads=[S_all[sq_][d_][1]], writes=[d_nst])
            rs, d_rs = cv.get(4)
            on, d_on = cv.get(128)
            yT, d_yT = cv.get(128, F32R)
            for i in range(NT):
                tsl = slice(i * 128, (i + 1) * 128)
                k.act(on, oacc[:, tsl], AF.Square, [d_oacc[i]], [d_on, d_rs], accum=rs[:, 0:1])
                k.act(rs[:, 1:2], rs[:, 0:1], AF.Sqrt, [d_rs], [d_rs], bias=EPS, scale=1.0 / 128)
                k.rcp(rs[:, 2:3], rs[:, 1:2], [d_rs], [d_rs])
                k.stt(on, oacc[:, tsl], rs[:, 2:3], gn_row, ALU.mult, ALU.mult, [d_oacc[i], d_rs, d_gn, d_on], [d_on])
                pt, ptd = k.ps()
                k.tr(pt[:, 0:128], on, ident[:], [d_on, d_ident], [ptd])
                k.tt(yT, pt[:, 0:128], zs[:, tsl], ALU.mult, [ptd, d_zs], [d_yT])
                P.dma(k.q(), yscr_v[h, :, tsl], yT, reads=[d_yT], writes=[d_yscr[h]], r32=True)

        def na_unit(l, kind, si, T, hp):
            NT = T // 128
            TT = min(T, 512)
            fz = fence()
            cv = Carve(fz)
            ws = []
            for off in (OFF_NA_Q, OFF_NA_K, OFF_NA_V, OFF_NA_Z):
                wa, dwl = cv.getw(len(ws))
                dw = dwl[0]
                w3 = wa.rearrange("p (c n) -> p c n", c=8)
                c0 = off + hp * 128
                P.dma(k.q(), w3, w_in.ap()[l, :, c0:c0 + 128].rearrange("(c p) n -> p c n", p=128), writes=[dw], r32=True)
                ws.append((w3, dw))
            (wq, d_wq), (wk, d_wk), (wv, d_wv), (wz, d_wz) = ws
            qT, d_qT = cv.get(T, F32R)
            kT, d_kT = cv.get(T, F32R)
            vt_f, d_vt = cv.get(NT * 132, F32R)
            vtok = vt_f.rearrange("p (t h e) -> p t h e", t=NT, h=2)
            zs, d_zs = cv.get(T)
            ones, d_ones = cv.get(2)
            stgs = [cv.get(256) for _ in range(2)]
            k.ms(ones, 1.0, [d_ones])
            for t in range(T // TT):
                ts_ = slice(t * TT, (t + 1) * TT)
                pb, pd = k.ps()
                for kc in range(8):
                    k.mm(pb[:, 0:TT], wq[:, kc, :], hTv[:, kc, ts_], kc == 0, kc == 7, [d_wq, d_hT], [pd])
                k.ts(qT[:, ts_], pb[:, 0:TT], 0.125, None, ALU.mult, None, [pd], [d_qT])
                pb, pd = k.ps()
                for kc in range(8):
                    k.mm(pb[:, 0:TT], wk[:, kc, :], hTv[:, kc, ts_], kc == 0, kc == 7, [d_wk, d_hT], [pd])
                k.cp(kT[:, ts_], pb[:, 0:TT], [pd], [d_kT])
                pb, pd = k.ps()
                for kc in range(8):
                    k.mm(pb[:, 0:TT], wz[:, kc, :], hTv[:, kc, ts_], kc == 0, kc == 7, [d_wz, d_hT], [pd])
                k.act(zs[:, ts_], pb[:, 0:TT], AF.Silu, [pd], [d_zs])
            for i in range(NT):
                is_ = slice(i * 128, (i + 1) * 128)
                pb, pd = k.ps()
                for kc in range(8):
                    k.mm(pb[:, 0:128], hTv[:, kc, is_], wv[:, kc, :], kc == 0, kc == 7, [d_wv, d_hT], [pd])
                k.cp(vtok[:, i, :, 0:64], pb[:, 0:128].rearrange("p (h e) -> p h e", h=2), [pd], [d_vt])
                k.cp(vtok[:, i, :, 64:66], ones[:, 0:2].unsqueeze(1).to_broadcast([128, 2, 2]), [d_ones], [d_vt], eng="pool")
                if kind == "p":
                    sq_ = i // 2
                    rs_ = slice((i % 2) * 128, (i % 2) * 128 + 128)
                    stg, d_stg = stgs[i % 2]
                    k.cp(stg[:, 0:128], pb[:, 0:128], [pd], [d_stg])
                    P.dma("act", nv.ap()[sq_, l, rs_, hp * 128:(hp + 1) * 128], stg[:, 0:128], reads=[d_stg], writes=[d_nkv])
                    pb, pd = k.ps()
                    for kc in range(8):
                        k.mm(pb[:, 0:128], hTv[:, kc, is_], wk[:, kc, :], kc == 0, kc == 7, [d_wk, d_hT], [pd])
                    k.cp(stg[:, 128:256], pb[:, 0:128], [pd], [d_stg])
                    P.dma("act", nk.ap()[sq_, l, rs_, hp * 128:(hp + 1) * 128], stg[:, 128:256], reads=[d_stg], writes=[d_nkv])
            otok, d_otok = cv.get(128)
            rden, d_rden = cv.get(2)
            yT, d_yT = cv.get(128, F32R)
            if kind == "p":
                PTs = [[cv.get(256, F32R) for _ in range(4)] for _ in range(2)]
                for sq_ in range(T // SEQ):
                    PT = PTs[sq_ % 2]
                    q0 = sq_ * SEQ
                    po, pod = k.ps()
                    for h2 in range(2):
                        hs = slice(h2 * 64, (h2 + 1) * 64)
                        for kt in range(2):
                            pt, d_pt = PT[h2 * 2 + kt]
                            pb, pd = k.ps()
                            k.mm(pb[:, 0:256], kT[hs, q0 + kt * 128:q0 + (kt + 1) * 128], qT[hs, q0:q0 + 256], True, True,
                                 [d_kT, d_qT], [pd])
                            k.act(pt, pb[:, 0:256], AF.Exp, [pd], [d_pt])
                        for qt in range(2):
                            for kt in range(2):
                                pt, d_pt = PT[h2 * 2 + kt]
                                c0 = qt * 132 + h2 * 66
                                k.mm(po[:, c0:c0 + 66], pt[:, qt * 128:(qt + 1) * 128], vtok[:, sq_ * 2 + kt, h2, :], kt == 0, kt == 1,
                                     [d_pt, d_vt], [pod])
                    for qt in range(2):
                        qs = slice(q0 + qt * 128, q0 + (qt + 1) * 128)
                        for h2 in range(2):
                            c0 = qt * 132 + h2 * 66
                            k.rcp(rden[:, h2:h2 + 1], po[:, c0 + 64:c0 + 65], [pod], [d_rden])
                            k.ts(otok[:, h2 * 64:(h2 + 1) * 64], po[:, c0:c0 + 64], rden[:, h2:h2 + 1], None, ALU.mult, None,
                                 [pod, d_rden], [d_otok])
                        pb, pd = k.ps()
                        k.tr(pb[:, 0:128], otok, ident[:], [d_otok, d_ident], [pd])
                        k.tt(yT, pb[:, 0:128], zs[:, qs], ALU.mult, [pd, d_zs], [d_yT])
                        P.dma(k.q(), yscr_v[4 + hp, :, qs], yT, reads=[d_yT], writes=[d_yscr[4 + hp]], r32=True)
            else:
                kcT, d_kcT = cv.get(256, F32R)
                vc_f, d_vc = cv.get(2 * 132, F32R)
                vctx = vc_f.rearrange("p (t h e) -> p t h e", t=2, h=2)
                cst_k, d_cstk = cv.get(256)
                P.dma("sp", cst_k.rearrange("p (t n) -> p t n", t=2),
                      ck_in.ap()[l, :, hp * 128:(hp + 1) * 128].rearrange("(t p) n -> p t n", p=128), writes=[d_cstk])
                pb, pd = k.ps()
                for t in range(2):
                    k.tr(pb[:, t * 128:(t + 1) * 128], cst_k[:, t * 128:(t + 1) * 128], ident[:], [d_cstk, d_ident], [pd])
                k.cp(kcT, pb[:, 0:256], [pd], [d_kcT])
                for t in range(2):
                    P.dma("act", vctx[:, t, :, 0:64],
                          cv_in.ap()[l, t * 128:(t + 1) * 128, hp * 128:(hp + 1) * 128].rearrange("p (h e) -> p h e", h=2),
                          writes=[d_vc], r32=True)
                    k.cp(vctx[:, t, :, 64:66], ones[:, 0:2].unsqueeze(1).to_broadcast([128, 2, 2]), [d_ones], [d_vc], eng="pool")
                E2f, d_E2 = cv.get(2 * 15 * 64)
                E2 = E2f.rearrange("p (h r c) -> p h r c", h=2, r=15)
                for a in range(2):
                    for h2 in range(2):
                        head = hp * 2 + h2
                        src = AP(zscr, ((l * 8 + head) * 15) * 8128 + 63, [[126, 64], [8128, 15], [1, 64]])
                        P.dma("sp" if a == 0 else "act", E2[a * 64:(a + 1) * 64, h2, :, :], src, reads=[d_zscr[l]], writes=[d_E2])
                k.act(E2f, E2f, AF.Exp, [d_E2], [d_E2])
                k.tt(E2f.rearrange("p (g c) -> p g c", c=64), E2f.rearrange("p (g c) -> p g c", c=64),
                     CM.unsqueeze(1).to_broadcast([128, 30, 64]), ALU.mult, [d_E2, d_cst], [d_E2])
                TABf, d_TAB = cv.get(2 * 21 * 128)
                TAB = TABf.rearrange("p (h t q) -> p h t q", h=2, t=21)
                k.ms(TABf, 0.0, [d_TAB])
                plans = {}
                tid = 0
                plans["int"] = []
                for j in range(5):
                    for a in range(2):
                        for b in range(2):
                            dr = 2 * j - 4 + a - b
                            if -4 <= dr <= 3:
                                plans["int"].append((tid, a, b, dr))
                    tid += 1
                tid0 = {"int": 0}
                for m_ in (0, 1, 14, 15):
                    tid0[m_] = tid
                    kt0 = 0 if m_ < 2 else 12
                    plans[m_] = []
                    for j in range(4):
                        for a in range(2):
                            for b in range(2):
                                kr = 2 * (kt0 + j) + a
                                r = 2 * m_ + b
                                rs_ = min(max(r - 4, 0), 24)
                                if rs_ <= kr <= rs_ + 7:
                                    plans[m_].append((tid, a, b, kr - r))
                        tid += 1
                assert tid == 21
                _ci = 0
                for key_, pl in plans.items():
                    for (tid_, a, b, dr) in pl:
                        k.cp(TAB[a * 64:(a + 1) * 64, :, tid_, b * 64:(b + 1) * 64], E2[a * 64:(a + 1) * 64, :, dr + 7, :],
                             [d_E2], [d_TAB], eng="dve")
                        _ci += 1
                PTb = [cv.get(7 * 128, F32R) for _ in range(2)]
                tEb = [cv.get(512) for _ in range(2)]
                tE2b = [cv.get(128) for _ in range(2)]
                its = [(m_, h2) for m_ in range(16) for h2 in range(2)]

                def info(m_):
                    if 2 <= m_ <= 13:
                        return [m_ - 2 + j for j in range(5)], 0
                    return [(0 if m_ < 2 else 12) + j for j in range(4)], tid0[m_]

                def emit_scores(m_, h2):
                    kts, t0 = info(m_)
                    nl = len(kts)
                    qs = slice(m_ * 128, (m_ + 1) * 128)
                    hs = slice(h2 * 64, (h2 + 1) * 64)
                    pA, pAd = k.ps()
                    for j in range(4):
                        k.mm(pA[:, j * 128:(j + 1) * 128], kT[hs, kts[j] * 128:(kts[j] + 1) * 128], qT[hs, qs], True, True,
                             [d_kT, d_qT], [pAd])
                    pB, pBd = k.ps()
                    c_ = 0
                    if nl == 5:
                        k.mm(pB[:, 0:128], kT[hs, kts[4] * 128:(kts[4] + 1) * 128], qT[hs, qs], True, True, [d_kT, d_qT], [pBd])
                        c_ = 128
                    for t in range(2):
                        k.mm(pB[:, c_ + t * 128:c_ + (t + 1) * 128], kcT[hs, t * 128:(t + 1) * 128], qT[hs, qs], True, True,
                             [d_kcT, d_qT], [pBd])
                    return (pA, pAd, pB, pBd, c_)

                pend = emit_scores(*its[0])
                po, pod = None, None
                for ii, (m_, h2) in enumerate(its):
                    cur_sc = pend
                    if ii + 1 < len(its):
                        pend = emit_scores(*its[ii + 1])
                    pA, pAd, pB, pBd, c_ = cur_sc
                    kts, t0 = info(m_)
                    nl = len(kts)
                    qs = slice(m_ * 128, (m_ + 1) * 128)
                    PTf, d_PT = PTb[ii % 2]
                    tmpE, d_tmpE = tEb[ii % 2]
                    tmpE2, d_tmpE2 = tE2b[ii % 2]
                    k.act(tmpE, pA[:, 0:512], AF.Exp, [pAd], [d_tmpE])
                    k.tt(PTf[:, 0:512], tmpE, TABf[:, (h2 * 21 + t0) * 128:(h2 * 21 + t0 + 4) * 128], ALU.mult,
                         [d_tmpE, d_TAB], [d_PT])
                    if nl == 5:
                        k.act(tmpE2, pB[:, 0:128], AF.Exp, [pBd], [d_tmpE2])
                        k.tt(PTf[:, 512:640], tmpE2, TABf[:, (h2 * 21 + 4) * 128:(h2 * 21 + 5) * 128], ALU.mult,
                             [d_tmpE2, d_TAB], [d_PT])
                    k.act(PTf[:, nl * 128:(nl + 2) * 128], pB[:, c_:c_ + 256], AF.Exp, [pBd], [d_PT])
                    if h2 == 0:
                        po, pod = k.ps()
                    c0 = h2 * 66
                    ntile = nl + 2
                    for j in range(ntile):
                        if j < nl:
                            rhs_ = vtok[:, kts[j], h2, :]
                            rd = d_vt
                        else:
                            rhs_ = vctx[:, j - nl, h2, :]
                            rd = d_vc
                        k.mm(po[:, c0:c0 + 66], PTf[:, j * 128:(j + 1) * 128], rhs_, j == 0, j == ntile - 1, [d_PT, rd], [pod])
                    if h2 == 1:
                        for hh in range(2):
                            cc0 = hh * 66
                            k.rcp(rden[:, hh:hh + 1], po[:, cc0 + 64:cc0 + 65], [pod], [d_rden])
                            k.ts(otok[:, hh * 64:(hh + 1) * 64], po[:, cc0:cc0 + 64], rden[:, hh:hh + 1], None, ALU.mult, None,
                                 [pod, d_rden], [d_otok])
                        pb, pd = k.ps()
                        k.tr(pb[:, 0:128], otok, ident[:], [d_otok, d_ident], [pd])
                        k.tt(yT, pb[:, 0:128], zs[:, qs], ALU.mult, [pd, d_zs], [d_yT])
                        P.dma(k.q(), yscr_v[4 + hp, :, qs], yT, reads=[d_yT], writes=[d_yscr[4 + hp]], r32=True)

        for l in range(depth):
            last = (l == depth - 1)
            fz = fence()
            cv = Carve(fz)
            bcol, d_bcol = cv.get(16)
            gpre_col, _d = cv.get(8)
            brow, _d = cv.get(D)
            gpost_row, _d = cv.get(D)
            ggt, d_ggt = cv.get(512)
            wada_f, d_wada = cv.get(8 * 512, F32R)
            wada = wada_f.rearrange("p (c n) -> p c n", c=8)
            sbc_f, d_sbc = cv.get(8 * 2 * 128, F32R)
            siluc_bc = sbc_f.rearrange("p (c j n) -> p c j n", c=8, j=2)
            for kc in range(8):
                for j in range(2):
                    k.cp(siluc_bc[:, kc, j, :], silucT[:, kc, j:j + 1].bitcast(F32).to_broadcast([128, 128]), [d_siluc], [d_sbc])
            P.dma("sp", bcol, b_ada.ap()[l, 0:2048].rearrange("(c p) -> p c", p=128), writes=[d_bcol], slow=True)
            P.dma("act", gpre_col, g_pre.ap()[l].rearrange("(c p) -> p c", p=128), writes=[d_bcol], slow=True)
            P.dma("sp", brow, b_ada.ap()[l, 2048:3072].partition_broadcast(128), writes=[d_bcol])
            P.dma("act", gpost_row, g_post.ap()[l].partition_broadcast(128), writes=[d_bcol])
            for blk in range(6):
                P.dma(k.q(), wada, w_ada.ap()[l, :, blk * 512:(blk + 1) * 512].rearrange("(c p) n -> p c n", p=128),
                      writes=[d_wada], r32=True)
                if blk < 4:
                    pb, pd = k.ps()
                    for cc in range(4):
                        for kc in range(8):
                            k.mm(pb[:, cc * 2:cc * 2 + 2], wada[:, kc, cc * 128:(cc + 1) * 128], silucT[:, kc, :],
                                 kc == 0, kc == 7, [d_wada, d_siluc], [pd])
                    k.tt(modcol[:, blk * 4:blk * 4 + 4, :], pb[:, 0:8].rearrange("p (c j) -> p c j", j=2),
                         bcol[:, blk * 4:blk * 4 + 4].unsqueeze(2).to_broadcast([128, 4, 2]), ALU.add,
                         [pd, d_bcol], [d_modcol])
                else:
                    for j in range(2):
                        pb, pd = k.ps()
                        for kc in range(8):
                            k.mm(pb[:], siluc_bc[:, kc, j, :], wada[:, kc, :], kc == 0, kc == 7, [d_wada, d_sbc], [pd])
                        c0 = (blk - 4) * 512
                        k.tt(ggt[:, 0:512], pb[:], brow[:, c0:c0 + 512], ALU.add, [pd, d_bcol], [d_ggt])
                        k.tt(ggt[:, 0:512], ggt[:, 0:512], gpost_row[:, c0:c0 + 512], ALU.mult, [d_ggt, d_bcol], [d_ggt])
                        P.dma("sp", ggscr.ap()[j, :, c0:c0 + 512], ggt[:, 0:512], reads=[d_ggt], writes=[d_gg])
            for j in range(2):
                k.stt(s1col[:, :, j], modcol[:, 8:16, j], 1.0, gpre_col, ALU.add, ALU.mult, [d_modcol, d_bcol], [d_modcol])

            def xio(kind, si):
                if l == 0:
                    xin = x_p.ap()[si] if kind == "p" else x_s.ap()
                    d_xin = d_x0
                else:
                    xin = xs_p[(l - 1) % 2].ap()[si] if kind == "p" else xs_s[(l - 1) % 2].ap()
                    d_xin = d_xs_p[(l - 1) % 2][si] if kind == "p" else d_xs_s[(l - 1) % 2]
                if last:
                    xout = y_p.ap()[si] if kind == "p" else y_s.ap()
                    d_xout = d_y0
                else:
                    xout = xs_p[l % 2].ap()[si] if kind == "p" else xs_s[l % 2].ap()
                    d_xout = d_xs_p[l % 2][si] if kind == "p" else d_xs_s[l % 2]
                return xin, d_xin, xout, d_xout

            d_x0 = Dep()
            d_y0 = Dep()
            for grp in ("p", "s"):
              for (kind, si, T) in [q_ for q_ in seqs if q_[0] == grp]:
                cj = 0 if kind == "p" else 1
                TT = min(T, 512)
                NTL = T // TT
                xin, d_xin, xout, d_xout = xio(kind, si)
                ho = si * SEQ if kind == "p" else 0
                hTv = hT[:, :, ho:ho + T]
                yscr_v = yscr.ap()[:, :, ho:ho + T]

                fz = fence()
                cv = Carve(fz)
                _x0, _d0 = cv.get(D)
                _x1, _d1 = cv.get(D)
                xt = [_x0, _x1]
                d_xt = [_d0, _d1]
                xn, d_xn = cv.get(D)
                for i in range(T // 128):
                    b = i % 2
                    P.dma(k.q(), xt[b], xin[i * 128:(i + 1) * 128, :], reads=[d_xin], writes=[d_xt[b]])
                    k.act(xn, xt[b], AF.Square, [d_xt[b]], [d_xn, d_small], accum=small[:, 0:1])
                    k.act(small[:, 1:2], small[:, 0:1], AF.Sqrt, [d_small], [d_small], bias=EPS, scale=1.0 / D)
                    k.rcp(small[:, 2:3], small[:, 1:2], [d_small], [d_small])
                    k.ts(xn, xt[b], small[:, 2:3], None, ALU.mult, None, [d_xt[b], d_small], [d_xn])
                    for half in range(2):
                        pb, pd = k.ps()
                        for c4 in range(4):
                            kc = half * 4 + c4
                            k.tr(pb[:, c4 * 128:(c4 + 1) * 128], xn[:, kc * 128:(kc + 1) * 128], ident[:], [d_xn, d_ident], [pd])
                        for c4 in range(4):
                            kc = half * 4 + c4
                            k.ts(hTv[:, kc, i * 128:(i + 1) * 128], pb[:, c4 * 128:(c4 + 1) * 128],
                                 s1col[:, kc, cj:cj + 1], modcol[:, kc, cj:cj + 1], ALU.mult, ALU.add,
                                 [pd, d_modcol], [d_hT], eng=("dve" if c4 % 2 == 0 else "pool") if False else "dve")

                fz = fence()
                cv = Carve(fz)
                zbuf, d_z = cv.get(TT, F32R)
                zsrc, d_zs0 = cv.get(TT)
                k.ms(zsrc, 0.0, [d_zs0])
                k.cp(zbuf, zsrc, [d_zs0], [d_z])
                for ch in range(8):
                    if (ch < 4 and not do_dn) or (ch >= 4 and not do_na):
                        for t in range(NTL):
                            P.dma(k.q(), yscr_v[ch, :, t * TT:(t + 1) * TT], zbuf, reads=[d_z], writes=[d_yscr[ch]], r32=True)

              if True:
                kind = grp
                cj = 0 if kind == "p" else 1
                T = NPS * SEQ if kind == "p" else DSEQ
                TT = 512
                NTL = T // TT
                hTv = hT[:, :, 0:T]
                yscr_v = yscr.ap()[:, :, 0:T]
                gscr_v = gscr.ap()[:, :, 0:T]
                mscr_v = mscr.ap()[:, :, 0:T]

                def xrows(sub):
                    if kind == "p":
                        xi, dxi, xo_, dxo = xio("p", sub // 2)
                        rr = (sub % 2) * 128
                    else:
                        xi, dxi, xo_, dxo = xio("s", 0)
                        rr = sub * 128
                    return xi[rr:rr + 128, :], dxi, xo_[rr:rr + 128, :], dxo

                L = SEQ if kind == "p" else DSEQ
                nseq = T // L
                si = None

                if do_dn:
                    for h in range(4):
                        dn_unit(l, kind, si, T, h, L)

                if do_na:
                    for hp in range(4):
                        na_unit(l, kind, si, T, hp)

                for g in range(4):
                    win = POOL_WINDOWS[g]
                    fz = fence()
                    cv = Carve(fz)
                    wu, _dl = cv.getw(0)
                    d_wu = _dl[0]
                    wz, _dl = cv.getw(1)
                    d_wz = _dl[0]
                    pw, d_pw = cv.get(128, F32R)
                    psc, d_psc = cv.get(1)
                    LP = L + 16
                    U, d_U = cv.get(nseq * LP)
                    zs, d_zs = cv.get(T)
                    s_a, d_sa = cv.get(nseq * LP)
                    s_b, d_sb = cv.get(nseq * LP)
                    icnt, d_icnt = cv.get(L)
                    pooled, d_pooled = cv.get(T, F32R)
                    yT, d_yT = cv.get(TT, F32R)
                    U3 = U.rearrange("p (s n) -> p s n", s=nseq)
                    wu3 = wu.rearrange("p (c n) -> p c n", c=8)
                    wz3 = wz.rearrange("p (c n) -> p c n", c=8)
                    cu = OFF_PL_U + g * 128
                    cz = OFF_PL_Z + g * 128
                    P.dma("sp", wu3, w_in.ap()[l, :, cu:cu + 128].rearrange("(c p) n -> p c n", p=128), writes=[d_wu], r32=True)
                    P.dma("act", wz3, w_in.ap()[l, :, cz:cz + 128].rearrange("(c p) n -> p c n", p=128), writes=[d_wz], r32=True)
                    P.dma("sp", pw, pool_w.ap()[l, g], writes=[d_pw], r32=True)
                    P.dma("act", psc, pool_scale.ap()[l, g * 128:(g + 1) * 128].rearrange("(p o) -> p o", o=1), writes=[d_psc], slow=True)
                    ic_src = (invcnt_p if kind == "p" else invcnt_s).ap()[g]
                    P.dma("sp", icnt, ic_src.partition_broadcast(128), writes=[d_icnt])
                    k.ms(U3[:, :, 0:8], 0.0, [d_U])
                    k.ms(U3[:, :, L + 8:L + 16], 0.0, [d_U])
                    for t in range(NTL):
                        pb, pd = k.ps()
                        for kc in range(8):
                            k.mm(pb[:, 0:TT], wu3[:, kc, :], hTv[:, kc, t * TT:(t + 1) * TT], kc == 0, kc == 7, [d_wu, d_hT], [pd])
                        if L >= TT:
                            k.cp(U[:, 8 + t * TT:8 + (t + 1) * TT], pb[:, 0:TT], [pd], [d_U])
                        else:
                            spt = TT // L
                            k.cp(U3[:, t * spt:(t + 1) * spt, 8:8 + L], pb[:, 0:TT].rearrange("p (s n) -> p s n", s=spt), [pd], [d_U])
                        pb, pd = k.ps()
                        for kc in range(8):
                            k.mm(pb[:, 0:TT], wz3[:, kc, :], hTv[:, kc, t * TT:(t + 1) * TT], kc == 0, kc == 7, [d_wz, d_hT], [pd])
                        k.act(zs[:, t * TT:(t + 1) * TT], pb[:, 0:TT], AF.Silu, [pd], [d_zs])
                    cur, dcur, curlen = U, d_U, nseq * LP
                    step = 1
                    bufs = [(s_a, d_sa), (s_b, d_sb)]
                    bi = 0
                    while step < win:
                        nb, dnb = bufs[bi]
                        bi ^= 1
                        nlen = curlen - step
                        k.tt(nb[:, 0:nlen], cur[:, 0:nlen], cur[:, step:step + nlen], ALU.add, [dcur], [dnb])
                        cur, dcur, curlen = nb, dnb, nlen
                        step *= 2
                    o0 = 8 - win // 2
                    nb, dnb = bufs[bi]
                    nb3 = nb[:, 0:T].rearrange("p (s n) -> p s n", s=nseq)
                    k.tt(nb3, cur[:, 0:nseq * LP].rearrange("p (s n) -> p s n", s=nseq)[:, :, o0:o0 + L],
                         icnt.unsqueeze(1).to_broadcast([128, nseq, L]), ALU.mult, [dcur, d_icnt], [dnb])
                    k.tt(pooled.rearrange("p (s n) -> p s n", s=nseq), nb3, U3[:, :, 8:8 + L], ALU.subtract, [dnb, d_U], [d_pooled])
                    for t in range(NTL):
                        pb, pd = k.ps()
                        k.mm(pb[:, 0:TT], pw, pooled[:, t * TT:(t + 1) * TT], True, True, [d_pw, d_pooled], [pd])
                        k.stt(yT, pb[:, 0:TT], psc[:, 0:1], zs[:, t * TT:(t + 1) * TT], ALU.mult, ALU.mult, [pd, d_psc, d_zs], [d_yT])
                        P.dma(k.q(), yscr_v[8 + g, :, t * TT:(t + 1) * TT], yT, reads=[d_yT], writes=[d_yscr[8 + g]], r32=True)

                for u in range(6):
                    fz = fence()
                    cv = Carve(fz)
                    wgu_f, d_wgl = cv.getw(0, 4)
                    wgu = wgu_f.rearrange("p (c n) -> p c n", c=8)
                    c0 = OFF_GATE + u * 512
                    P.dma(k.q(), wgu, w_in.ap()[l, :, c0:c0 + 512].rearrange("(c p) n -> p c n", p=128), writes=d_wgl, r32=True)
                    gsbs = [cv.get(TT) for _ in range(2)]
                    gi = 0
                    for t in range(NTL):
                        for c4 in range(4):
                            gsb, d_gsb = gsbs[gi % 2]
                            gi += 1
                            pg_, pgd_ = k.ps()
                            for kc in range(8):
                                k.mm(pg_[:, 0:TT], wgu[:, kc, c4 * 128:(c4 + 1) * 128], hTv[:, kc, t * TT:(t + 1) * TT], kc == 0, kc == 7,
                                     list(d_wgl) + [d_hT], [pgd_])
                            k.act(gsb, pg_[:, 0:TT], AF.Sigmoid, [pgd_], [d_gsb])
                            P.dma(k.q(), gscr_v[u * 4 + c4, :, t * TT:(t + 1) * TT], gsb, reads=[d_gsb], writes=[d_gscr[u * 4 + c4]])

                fz = fence()
                cv = Carve(fz)
                ysb_l = []
                wbr_l = []
                gin_l = []
                mo_l = []
                _wa, _wd = cv.get(4 * D, F32R)
                wbr_single = (_wa.rearrange("p (c n) -> p c n", c=4), _wd)
                for _i in range(2):
                    _a, _d = cv.get(4 * TT, F32R)
                    ysb_l.append((_a.rearrange("p (c n) -> p c n", c=4), _d))
                    if _i == 0:
                        wbr_l.append((wbr_single[0], [wbr_single[1]]))
                    else:
                        _sa, _sd = cv.getw(0, 4)
                        wbr_l.append((_sa.rearrange("p (c n) -> p c n", c=4), list(_sd)))
                    _a, _d = cv.get(8 * TT)
                    gin_l.append((_a.rearrange("p (c n) -> p c n", c=8), _d))
                    mo_l.append(cv.get(TT, F32R))
                accf, d_acc = cv.get(8 * TT)
                acc3 = accf.rearrange("p (c n) -> p c n", c=8)
                tmp, d_tmp = cv.get(TT)
                bi = 0
                mi = 0
                for t in range(NTL):
                    tsl_ = slice(t * TT, (t + 1) * TT)
                    for br in range(3):
                        ysb3, d_ysb = ysb_l[bi % 2]
                        wbr3, d_wbr = wbr_l[bi % 2]
                        gin3, d_gin = gin_l[bi % 2]
                        bi += 1
                        P.dma("sp", ysb3, yscr_v[br * 4:(br + 1) * 4, :, tsl_].rearrange("c p n -> p c n"),
                              reads=list(d_yscr[br * 4:(br + 1) * 4]), writes=[d_ysb], r32=True)
                        P.dma("act", gin3, gscr_v[br * 8:(br + 1) * 8, :, tsl_].rearrange("c p n -> p c n"),
                              reads=list(d_gscr[br * 8:(br + 1) * 8]), writes=[d_gin])
                        P.dma("sp", wbr3, w_br[br].ap()[l].rearrange("(c p) n -> p c n", p=128), writes=d_wbr, r32=True)
                        for dc in range(8):
                            pa, pad = k.ps()
                            for wc in range(4):
                                k.mm(pa[:, 0:TT], wbr3[:, wc, dc * 128:(dc + 1) * 128], ysb3[:, wc, :], wc == 0, wc == 3, d_wbr + [d_ysb], [pad])
                            if br == 0:
                                k.tt(acc3[:, dc, :], gin3[:, dc, :], pa[:, 0:TT], ALU.mult, [d_gin, pad], [d_acc])
                            elif br == 1:
                                k.tt(tmp, gin3[:, dc, :], pa[:, 0:TT], ALU.mult, [d_gin, pad], [d_tmp])
                                k.tt(acc3[:, dc, :], acc3[:, dc, :], tmp, ALU.add, [d_acc, d_tmp], [d_acc], eng="pool")
                            else:
                                mo, d_mo = mo_l[mi % 2]
                                mi += 1
                                k.tt(tmp, gin3[:, dc, :], pa[:, 0:TT], ALU.mult, [d_gin, pad], [d_tmp])
                                k.tt(mo, acc3[:, dc, :], tmp, ALU.add, [d_acc, d_tmp], [d_mo], eng="pool")
                                P.dma("act", mscr_v[dc, :, tsl_], mo, reads=[d_mo], writes=[d_mscr], r32=True)

                fz = fence()
                cv = Carve(fz)
                wo, d_wo = cv.get(8 * D, F32R)
                wo3 = wo.rearrange("p (c n) -> p c n", c=8)
                mt_l = []
                for _i in range(2):
                    _a, _d = cv.get(8 * 128, F32R)
                    mt_l.append((_a.rearrange("p (c n) -> p c n", c=8), _d))
                xr_l = [cv.get(D) for _ in range(2)]
                xo_l = [cv.get(D) for _ in range(2)]
                xn, d_xn = cv.get(512)
                ggr, d_ggr = cv.get(D)
                P.dma("act", ggr, ggscr.ap()[cj], reads=[d_gg], writes=[d_ggr])
                P.dma("sp", wo3, w_out.ap()[l].rearrange("(c p) n -> p c n", p=128), writes=[d_wo], r32=True)
                for sub in range(T // 128):
                    r0 = sub * 128
                    mt3, d_mt = mt_l[sub % 2]
                    xr, d_xr = xr_l[sub % 2]
                    xo, d_xo = xo_l[sub % 2]
                    P.dma("sp", mt3, mscr_v[:, :, r0:r0 + 128].rearrange("c p n -> p c n"), reads=[d_mscr], writes=[d_mt], r32=True)
                    xi_rows, d_xin, xo_rows, d_xout = xrows(sub)
                    P.dma("act", xr, xi_rows, reads=[d_xin], writes=[d_xr])
                    pos = []
                    for half in range(2):
                        po, pod = k.ps()
                        for kc in range(8):
                            k.mm(po[:], mt3[:, kc, :], wo3[:, kc, half * 512:(half + 1) * 512], kc == 0, kc == 7, [d_mt, d_wo], [pod])
                        k.act(xn[:, 0:512], po[:], AF.Square, [pod], [d_xn, d_small], accum=small[:, 4 + half:5 + half])
                        pos.append((po, pod))
                    k.tt(small[:, 6:7], small[:, 4:5], small[:, 5:6], ALU.add, [d_small], [d_small])
                    k.act(small[:, 7:8], small[:, 6:7], AF.Sqrt, [d_small], [d_small], bias=EPS, scale=1.0 / D)
                    k.rcp(small[:, 8:9], small[:, 7:8], [d_small], [d_small])
                    for half in range(2):
                        po, pod = pos[half]
                        hs = slice(half * 512, (half + 1) * 512)
                        k.stt(xo[:, hs], po[:], small[:, 8:9], ggr[:, hs], ALU.mult, ALU.mult, [pod, d_small, d_ggr], [d_xo])
                        k.tt(xo[:, hs], xo[:, hs], xr[:, hs], ALU.add, [d_xo, d_xr], [d_xo], eng="pool")
                    P.dma("sp", xo_rows, xo, reads=[d_xo], writes=[d_xout])
        P.emit()
    return nc, k


_CACHE = {}


def _consts():
    def invcnt(T):
        out = np.zeros((4, T), np.float32)
        pos = np.arange(T)
        for gi, win in enumerate(POOL_WINDOWS):
            lo = np.maximum(pos - win // 2, 0)
            hi = np.minimum(pos + win // 2 - 1, T - 1)
            out[gi] = 1.0 / (hi - lo + 1).astype(np.float32)
        return out

    t = np.arange(128)
    same = (t[:, None] // 64) == (t[None, :] // 64)
    cst = np.zeros((128, NCST), np.float32)
    cst[:, 0:128] = same & (t[:, None] <= t[None, :])
    cst[:, 128:256] = same & (t[:, None] >= t[None, :])
    cst[:, 256:384] = same
    cst[:, 384:512] = 1.0
    f = np.arange(64)
    pm = t % 64
    cst[:, 512:576] = (pm[:, None] == f[None, :])
    cst[:, 576] = (pm == 63)
    cst[:, 577] = (pm == 0)
    cst[:, 578] = (t == 63)
    cst[:, 579] = (t == 127)
    cst[:, 580] = (t == 0)
    cst[:, 581] = (t == 64)
    P_ = pm[:, None]
    F_ = f[None, :]
    valid = [[F_ >= P_, F_ <= P_], [F_ > P_, F_ < P_], [F_ < P_, F_ > P_]]
    for ty in range(3):
        for d_ in range(2):
            sign = 1.0 if ty == 2 else -1.0
            cst[:, 582 + (ty * 2 + d_) * 64: 582 + (ty * 2 + d_ + 1) * 64] = np.where(valid[ty][d_], 0.0, sign * BIG)
    cq = np.arange(64)
    csq = np.clip(cq - 8, 0, 48)
    cm = (f[:, None] >= csq[None, :]) & (f[:, None] < csq[None, :] + 16)
    cst[:, 582 + 384:582 + 384 + 64] = np.concatenate([cm, cm], axis=0)
    Pf = t[:, None]
    Ff = t[None, :]
    validb = [[Ff >= Pf, Ff <= Pf], [Ff > Pf, Ff < Pf], [Ff < Pf, Ff > Pf]]
    for ty in range(3):
        for d_ in range(2):
            sign = 1.0 if ty == 2 else -1.0
            c0 = 1030 + (ty * 2 + d_) * 128
            cst[:, c0:c0 + 128] = np.where(validb[ty][d_] & same, 0.0, sign * BIG)
    return {"invcnt_p": invcnt(SEQ), "invcnt_s": invcnt(DSEQ), "ident": np.eye(128, dtype=np.float32), "dncst": cst}


def kernel(x_prompt, x_sample, c, cache_k_na, cache_v_na, state_dn, c_ctx, w_ada, b_ada, g_pre, g_post,
           w_in, conv_dn, a_log_dn, dt_bias_dn, g_norm_dn, na_bias, pool_w, pool_scale,
           w_br_dn, w_br_na, w_br_pl, w_out, _depth=DEPTH, _dn=True, _na=True):
    f = lambda a: np.ascontiguousarray(np.asarray(a, dtype=np.float32))
    key = (_depth, _dn, _na)
    if key not in _CACHE:
        _CACHE[key] = build_program(_depth, _dn, _na)
    nc, k = _CACHE[key]
    cs = _consts()
    dd = _depth
    shared = {"w_ada": f(w_ada[:dd]), "b_ada": f(b_ada[:dd]), "g_pre": f(g_pre[:dd]), "g_post": f(g_post[:dd]), "w_in": f(w_in[:dd]),
              "pool_w": f(pool_w[:dd]), "pool_scale": f(pool_scale[:dd]), "w_br_dn": f(w_br_dn[:dd]), "w_br_na": f(w_br_na[:dd]),
              "w_br_pl": f(w_br_pl[:dd]), "w_out": f(w_out[:dd]), "conv_dn": f(conv_dn[:dd]),
              "a_log": f(a_log_dn[:dd]).reshape(dd, 8), "dt_bias": f(dt_bias_dn[:dd]).reshape(dd, 8), "g_norm": f(g_norm_dn[:dd])}
    shared.update(cs)
    rpad = np.zeros((dd, 8, 15, 127), np.float32)
    rpad[..., 48:79] = f(na_bias[:dd])[..., ::-1]
    x_prompt = f(x_prompt)
    x_sample = f(x_sample)
    in_maps = []
    for core in range(8):
        b = core // 4
        m = dict(shared)
        m["x_p"] = x_prompt[core * NPS:(core + 1) * NPS]
        m["x_s"] = x_sample[b]
        m["cvec"] = np.stack([f(c_ctx), f(c)[b]])
        m["sdn"] = f(state_dn[b, :dd])
        m["ck"] = f(cache_k_na[b, :dd]).reshape(dd, 256, 512)
        m["cvv"] = f(cache_v_na[b, :dd]).reshape(dd, 256, 512)
        m["rpad"] = rpad
        in_maps.append({n: m[n] for n in k.din})
    res = run_bass_kernel_spmd(nc, in_maps, core_ids=list(range(8)))
    r = res.results
    y_p = np.concatenate([r[i]["y_p"] for i in range(8)], axis=0)
    y_s = np.stack([r[0]["y_s"], r[4]["y_s"]])
    n_k = np.concatenate([r[i]["nk"] for i in range(8)], axis=0).reshape(32, dd, SEQ, 8, 64)
    n_v = np.concatenate([r[i]["nv"] for i in range(8)], axis=0).reshape(32, dd, SEQ, 8, 64)
    n_s = np.concatenate([r[i]["nst"] for i in range(8)], axis=0)
    return y_p, y_s, n_k, n_v, n_s
```
